# Optimizing a Trainium2 kernel written in Bass

```python
import jax
import jax.numpy as jnp
from jax import lax
import numpy as np

D_MODEL = 1024
BATCH = 2
SEQ = 16384
DEPTH = 2

GRID_W = 64
CTX_LEN = 256

FNET_GROUPS = 4
FNET_GROUP_DIM = 128
FNET_WIDTH = FNET_GROUPS * FNET_GROUP_DIM

NA_HEADS = 8
NA_HEAD_DIM = 64
NA_WIDTH = NA_HEADS * NA_HEAD_DIM
NA_KH = 8
NA_KW = 16
NA_SCALE = NA_HEAD_DIM ** -0.5

MLA_HEADS = 8
MLA_Q_LORA = 256
MLA_KV_LORA = 128
MLA_NOPE_DIM = 64
MLA_ROPE_DIM = 32
MLA_V_DIM = 64
MLA_WIDTH = MLA_HEADS * MLA_V_DIM
MLA_SCALE = (MLA_NOPE_DIM + MLA_ROPE_DIM) ** -0.5
ATTN_BLOCK = 128
ROPE_BASE = 10000.0

N_BRANCH = 3
BRANCH_DIM = 512

FFN_HIDDEN = -(-8 * D_MODEL // (3 * 256)) * 256

NORM_EPS = 1e-5
DEEPNORM_ALPHA = (2 * DEPTH) ** 0.25
DEEPNORM_BETA = (8 * DEPTH) ** -0.25

IN_SIZES = (FNET_WIDTH, NA_WIDTH, NA_WIDTH, NA_WIDTH, MLA_Q_LORA, MLA_KV_LORA, MLA_ROPE_DIM, N_BRANCH * D_MODEL)
IN_DIM = int(sum(IN_SIZES))
IN_OFFSETS = [int(v) for v in np.cumsum(IN_SIZES)[:-1]]

kernel_name = 'hybrid_fnet_natten_mla_deepnorm_prefix'


def layer_norm(x, g, b):
    xf = x.astype(jnp.float32)
    mu = jnp.mean(xf, axis=-1, keepdims=True)
    var = jnp.mean(jnp.square(xf - mu), axis=-1, keepdims=True)
    y = (xf - mu) * lax.rsqrt(var + NORM_EPS) * g.astype(jnp.float32) + b.astype(jnp.float32)
    return y.astype(x.dtype)


def rms_norm(x, g):
    xf = x.astype(jnp.float32)
    y = xf * lax.rsqrt(jnp.mean(jnp.square(xf), axis=-1, keepdims=True) + NORM_EPS) * g.astype(jnp.float32)
    return y.astype(x.dtype)


def rope_tables(n):
    t = jnp.arange(n, dtype=jnp.int32)
    rows = (t // GRID_W).astype(jnp.float32)
    cols = (t % GRID_W).astype(jnp.float32)
    n_freq = MLA_ROPE_DIM // 4
    inv_freq = ROPE_BASE ** (-jnp.arange(n_freq, dtype=jnp.float32) / n_freq)
    ang = jnp.concatenate([rows[:, None] * inv_freq, cols[:, None] * inv_freq], axis=-1)
    return jnp.cos(ang), jnp.sin(ang)


def axial_rope(x, cos, sin):
    xf = x.astype(jnp.float32)
    half = x.shape[-1] // 2
    x1, x2 = xf[..., :half], xf[..., half:]
    return jnp.concatenate([x1 * cos - x2 * sin, x2 * cos + x1 * sin], axis=-1).astype(x.dtype)


def split_heads(z, n_heads):
    b, n, w = z.shape
    return z.reshape(b, n, n_heads, w // n_heads)


def fourier_mix(z):
    b, n, _ = z.shape
    zg = z.reshape(b, n, FNET_GROUPS, FNET_GROUP_DIM).astype(jnp.float32)
    y = jnp.fft.fft2(zg, axes=(1, 3), norm='ortho').real
    return y.reshape(b, n, FNET_WIDTH).astype(z.dtype)


def softmax_attend(q, k, v, scale):
    s = jnp.einsum('bqhd,bkhd->bhqk', q, k, preferred_element_type=jnp.float32) * scale
    p = jax.nn.softmax(s, axis=-1).astype(v.dtype)
    return jnp.einsum('bhqk,bkhd->bqhd', p, v)


def blocked_attend(q, k, v, scale):
    b, n, h, dq = q.shape
    qb = q.reshape(b, n // ATTN_BLOCK, ATTN_BLOCK, h, dq).transpose(1, 0, 2, 3, 4)
    out = lax.map(lambda qi: softmax_attend(qi, k, v, scale), qb)
    return out.transpose(1, 0, 2, 3, 4).reshape(b, n, h, v.shape[-1])


def neighbourhood_attention(q, k, v, k_ctx, v_ctx, rpb, n_rows):
    b, n, h, dh = q.shape
    kh = min(NA_KH, n_rows)
    kw = NA_KW
    qg = q.reshape(b, n_rows, GRID_W, h, dh).transpose(1, 0, 2, 3, 4)
    kg = k.reshape(b, n_rows, GRID_W, h, dh)
    vg = v.reshape(b, n_rows, GRID_W, h, dh)
    cols = jnp.arange(GRID_W, dtype=jnp.int32)
    col_start = jnp.clip(cols - kw // 2, 0, GRID_W - kw)
    col_idx = col_start[:, None] + jnp.arange(kw, dtype=jnp.int32)
    dc = col_idx - cols[:, None] + (NA_KW - 1)
    rpb_c = rpb.astype(jnp.float32)[:, :, dc]
    n_loc = kh * kw

    def row_fn(args):
        r, q_r = args
        rs = jnp.clip(r - kh // 2, 0, n_rows - kh)
        k_rows = lax.dynamic_slice_in_dim(kg, rs, kh, axis=1)
        v_rows = lax.dynamic_slice_in_dim(vg, rs, kh, axis=1)
        k_win = k_rows[:, :, col_idx]
        v_win = v_rows[:, :, col_idx]
        dr = rs + jnp.arange(kh, dtype=jnp.int32) - r + (NA_KH - 1)
        bias = jnp.take(rpb_c, dr, axis=1).transpose(0, 2, 1, 3)
        s_loc = jnp.einsum('bwhd,bkwjhd->bhwkj', q_r, k_win, preferred_element_type=jnp.float32) * NA_SCALE + bias[None]
        s_ctx = jnp.einsum('bwhd,bchd->bhwc', q_r, k_ctx, preferred_element_type=jnp.float32) * NA_SCALE
        s = jnp.concatenate([s_loc.reshape(b, h, GRID_W, n_loc), s_ctx], axis=-1)
        p = jax.nn.softmax(s, axis=-1).astype(v.dtype)
        p_loc = p[..., :n_loc].reshape(b, h, GRID_W, kh, kw)
        p_ctx = p[..., n_loc:]
        return (jnp.einsum('bhwkj,bkwjhd->bwhd', p_loc, v_win)
                + jnp.einsum('bhwc,bchd->bwhd', p_ctx, v_ctx))

    out = lax.map(row_fn, (jnp.arange(n_rows, dtype=jnp.int32), qg))
    return out.transpose(1, 0, 2, 3, 4).reshape(b, n, h * dh)


def mla_queries(cq, g_q, w_uq, w_qr, rope):
    b, n, _ = cq.shape
    cq = rms_norm(cq, g_q)
    q_nope = (cq @ w_uq).reshape(b, n, MLA_HEADS, MLA_NOPE_DIM)
    q_rope = (cq @ w_qr).reshape(b, n, MLA_HEADS, MLA_ROPE_DIM)
    if rope is not None:
        q_rope = axial_rope(q_rope, rope[0][:, None, :], rope[1][:, None, :])
    return jnp.concatenate([q_nope, q_rope], axis=-1)


def mla_keys_values(ckv, kr, g_kv, w_uk, w_uv, rope):
    b, n, _ = ckv.shape
    ckv = rms_norm(ckv, g_kv)
    k_nope = (ckv @ w_uk).reshape(b, n, MLA_HEADS, MLA_NOPE_DIM)
    v = (ckv @ w_uv).reshape(b, n, MLA_HEADS, MLA_V_DIM)
    if rope is not None:
        kr = axial_rope(kr, rope[0], rope[1])
    k_rope = jnp.broadcast_to(kr[:, :, None, :], (b, n, MLA_HEADS, MLA_ROPE_DIM))
    return jnp.concatenate([k_nope, k_rope], axis=-1), v


def merge_branches(ys, g, w_branch, w_out):
    b, n, _ = g.shape
    y = jnp.stack(ys, axis=2)
    proj = jnp.einsum('bngi,gio->bngo', y, w_branch)
    gate = jax.nn.sigmoid(g.reshape(b, n, N_BRANCH, D_MODEL).astype(jnp.float32)).astype(proj.dtype)
    return jnp.sum(gate * proj, axis=2) @ w_out


def mixer(h, hc, lp, rope, n_rows, need_ctx):
    b, n, _ = h.shape
    f, nq, nk, nv, cq, ckv, kr, g = jnp.split(h @ lp['w_in'], IN_OFFSETS, axis=-1)
    fc, nqc, nkc, nvc, cqc, ckvc, krc, gc = jnp.split(hc @ lp['w_in'], IN_OFFSETS, axis=-1)
    na_kc, na_vc = split_heads(nkc, NA_HEADS), split_heads(nvc, NA_HEADS)
    mla_kc, mla_vc = mla_keys_values(ckvc, krc, lp['g_kv'], lp['w_uk'], lp['w_uv'], None)
    y_f = fourier_mix(f)
    y_na = neighbourhood_attention(split_heads(nq, NA_HEADS), split_heads(nk, NA_HEADS), split_heads(nv, NA_HEADS),
                                   na_kc, na_vc, lp['rpb'], n_rows)
    mla_k, mla_v = mla_keys_values(ckv, kr, lp['g_kv'], lp['w_uk'], lp['w_uv'], rope)
    mla_q = mla_queries(cq, lp['g_q'], lp['w_uq'], lp['w_qr'], rope)
    y_mla = blocked_attend(mla_q, jnp.concatenate([mla_kc, mla_k], axis=1),
                           jnp.concatenate([mla_vc, mla_v], axis=1), MLA_SCALE).reshape(b, n, MLA_WIDTH)
    out = merge_branches([y_f, y_na, y_mla], g, lp['w_branch'], lp['w_out'])
    if not need_ctx:
        return out, None
    n_ctx = hc.shape[1]
    yc_f = fourier_mix(fc)
    yc_na = softmax_attend(split_heads(nqc, NA_HEADS), na_kc, na_vc, NA_SCALE).reshape(b, n_ctx, NA_WIDTH)
    mla_qc = mla_queries(cqc, lp['g_q'], lp['w_uq'], lp['w_qr'], None)
    yc_mla = softmax_attend(mla_qc, mla_kc, mla_vc, MLA_SCALE).reshape(b, n_ctx, MLA_WIDTH)
    out_c = merge_branches([yc_f, yc_na, yc_mla], gc, lp['w_branch'], lp['w_out'])
    return out, out_c


def swiglu(h, w_gate, w_up, w_down):
    return (jax.nn.silu(h @ w_gate) * (h @ w_up)) @ w_down


def setup_inputs(seed: int = 0) -> dict:
    key = jax.random.key(seed)
    ks = jax.random.split(key, 32)

    def nrm(k, shape, scale):
        return jax.random.normal(k, shape, dtype=jnp.float32) * scale

    L, D, F = DEPTH, D_MODEL, FFN_HIDDEN
    return {
        'x': nrm(ks[0], (BATCH, SEQ, D), 1.0),
        'c': nrm(ks[1], (BATCH, D), 1.0),
        'ctx': nrm(ks[2], (BATCH, CTX_LEN, D), 1.0),
        'c_ctx': nrm(ks[3], (D,), 1.0),
        'ln_in_g': 1.0 + nrm(ks[4], (D,), 0.02),
        'ln_in_b': nrm(ks[5], (D,), 0.02),
        'w_mod': nrm(ks[6], (L, D, 6 * D), 0.1 * D ** -0.5),
        'b_mod': nrm(ks[7], (L, 6 * D), 0.02),
        'w_in': nrm(ks[8], (L, D, IN_DIM), D ** -0.5),
        'mla_q_norm_g': 1.0 + nrm(ks[9], (L, MLA_Q_LORA), 0.02),
        'mla_kv_norm_g': 1.0 + nrm(ks[10], (L, MLA_KV_LORA), 0.02),
        'w_uq': nrm(ks[11], (L, MLA_Q_LORA, MLA_HEADS * MLA_NOPE_DIM), MLA_Q_LORA ** -0.5),
        'w_qr': nrm(ks[12], (L, MLA_Q_LORA, MLA_HEADS * MLA_ROPE_DIM), MLA_Q_LORA ** -0.5),
        'w_uk': nrm(ks[13], (L, MLA_KV_LORA, MLA_HEADS * MLA_NOPE_DIM), MLA_KV_LORA ** -0.5),
        'w_uv': nrm(ks[14], (L, MLA_KV_LORA, MLA_HEADS * MLA_V_DIM), MLA_KV_LORA ** -0.5),
        'na_rpb': nrm(ks[15], (L, NA_HEADS, 2 * NA_KH - 1, 2 * NA_KW - 1), 0.1),
        'w_branch': nrm(ks[16], (L, N_BRANCH, BRANCH_DIM, D), BRANCH_DIM ** -0.5),
        'w_out': nrm(ks[17], (L, D, D), DEEPNORM_BETA * D ** -0.5),
        'ln1_g': 1.0 + nrm(ks[18], (L, D), 0.02),
        'ln1_b': nrm(ks[19], (L, D), 0.02),
        'ln2_g': 1.0 + nrm(ks[20], (L, D), 0.02),
        'ln2_b': nrm(ks[21], (L, D), 0.02),
        'w_ffn_gate': nrm(ks[22], (L, D, F), D ** -0.5),
        'w_ffn_up': nrm(ks[23], (L, D, F), D ** -0.5),
        'w_ffn_down': nrm(ks[24], (L, F, D), DEEPNORM_BETA * F ** -0.5),
    }


def reference(x, c, ctx, c_ctx, ln_in_g, ln_in_b, w_mod, b_mod, w_in, mla_q_norm_g, mla_kv_norm_g,
              w_uq, w_qr, w_uk, w_uv, na_rpb, w_branch, w_out, ln1_g, ln1_b, ln2_g, ln2_b,
              w_ffn_gate, w_ffn_up, w_ffn_down):
    n = x.shape[1]
    n_rows = n // GRID_W
    rope = rope_tables(n)
    x_lat = layer_norm(x, ln_in_g, ln_in_b)
    x_ctx = layer_norm(ctx, ln_in_g, ln_in_b)
    silu_c = jax.nn.silu(c)
    silu_cc = jax.nn.silu(c_ctx)
    for l in range(DEPTH):
        last = l == DEPTH - 1
        lp = {'w_in': w_in[l], 'g_q': mla_q_norm_g[l], 'g_kv': mla_kv_norm_g[l], 'w_uq': w_uq[l],
              'w_qr': w_qr[l], 'w_uk': w_uk[l], 'w_uv': w_uv[l], 'rpb': na_rpb[l],
              'w_branch': w_branch[l], 'w_out': w_out[l]}
        sh1, sc1, g1, sh2, sc2, g2 = [m[:, None, :] for m in jnp.split(silu_c @ w_mod[l] + b_mod[l], 6, axis=-1)]
        csh1, csc1, cg1, csh2, csc2, cg2 = jnp.split(silu_cc @ w_mod[l] + b_mod[l], 6, axis=-1)
        a_lat = x_lat * (1 + sc1) + sh1
        a_ctx = x_ctx * (1 + csc1) + csh1
        m_lat, m_ctx = mixer(a_lat, a_ctx, lp, rope, n_rows, not last)
        x_lat = layer_norm(DEEPNORM_ALPHA * x_lat + (1 + g1) * m_lat, ln1_g[l], ln1_b[l])
        f_lat = swiglu(x_lat * (1 + sc2) + sh2, w_ffn_gate[l], w_ffn_up[l], w_ffn_down[l])
        x_lat = layer_norm(DEEPNORM_ALPHA * x_lat + (1 + g2) * f_lat, ln2_g[l], ln2_b[l])
        if not last:
            x_ctx = layer_norm(DEEPNORM_ALPHA * x_ctx + (1 + cg1) * m_ctx, ln1_g[l], ln1_b[l])
            f_ctx = swiglu(x_ctx * (1 + csc2) + csh2, w_ffn_gate[l], w_ffn_up[l], w_ffn_down[l])
            x_ctx = layer_norm(DEEPNORM_ALPHA * x_ctx + (1 + cg2) * f_ctx, ln2_g[l], ln2_b[l])
    return x_lat
```

```python
from contextlib import ExitStack
import ml_dtypes
from concourse.bass_utils import run_bass_kernel_spmd
import numpy as np
import concourse.bass as bass
import concourse.mybir as mybir

F32 = mybir.dt.float32
BF16 = mybir.dt.bfloat16
U8 = mybir.dt.uint8
ALU = mybir.AluOpType
AF = mybir.ActivationFunctionType
AX = mybir.AxisListType


class Buf:
    __slots__ = ("name", "w", "r", "dsem", "dcnt")

    def __init__(self, name):
        self.name = name
        self.w = None
        self.r = {}
        self.dsem = None
        self.dcnt = 0


class Prog:
    ENGS = ("pe", "act", "dve", "pool", "sp")

    def __init__(self, nc, stack):
        self.nc = nc
        self.stack = stack
        self.lists = {e: [] for e in self.ENGS}
        self.sems = {}
        self.cnt = {}
        self.waited = {e: {} for e in self.ENGS}
        for e in self.ENGS:
            self._newsem("eng_" + e)
        self._newsem("coll")
        self.pool_keys = []
        self.pool_idx = 0
        self.attached = []

    def _newsem(self, key):
        self.sems[key] = self.stack.enter_context(self.nc.semaphore(key))
        self.cnt[key] = 0
        return key

    def buf(self, name):
        return Buf(name)

    def _deps(self, reads, writes):
        deps = {}
        def add(t):
            if t is None:
                return
            k, v = t
            if deps.get(k, 0) < v:
                deps[k] = v
        for b in reads:
            add(b.w)
        for b in writes:
            add(b.w)
            for k, v in b.r.items():
                add((k, v))
        return deps

    def _waits(self, eng, deps):
        ws = []
        wd = self.waited[eng]
        for k, v in deps.items():
            if wd.get(k, 0) < v:
                wd[k] = v
                ws.append((k, v))
        return ws

    def _mark(self, tok, reads, writes):
        k, v = tok
        for b in reads:
            if b.r.get(k, 0) < v:
                b.r[k] = v
        for b in writes:
            b.w = tok
            b.r = {}

    def _dsem(self, b):
        if b.dsem is None:
            if self.pool_idx >= len(self.pool_keys):
                self.pool_keys.append(self._newsem("dq%d" % len(self.pool_keys)))
            b.dsem = self.pool_keys[self.pool_idx]
            self.pool_idx += 1
            self.attached.append(b)
        return b.dsem

    def op(self, eng, fn, reads=(), writes=(), inc=True):
        deps = self._deps(reads, writes)
        if eng == "pe":
            deps.pop("eng_pe", None)
        ws = self._waits(eng, deps)
        key = "eng_" + eng
        if inc:
            self.cnt[key] += 1
            tok = (key, self.cnt[key])
        else:
            tok = (key, self.cnt[key] + 1)
        self.lists[eng].append((ws, fn, (key, 1) if inc else None))
        self._mark(tok, reads, writes)
        if inc:
            self.waited[eng][key] = max(self.waited[eng].get(key, 0), 0)
        return tok

    def dma(self, q, out_ap, in_ap, reads, writes, own=None, **kw):
        if own is None:
            own = writes[0]
        key = self._dsem(own)
        ws = self._waits(q, self._deps(reads, writes))
        self.cnt[key] += 16
        tok = (key, self.cnt[key])
        fn = lambda e, o=out_ap, i=in_ap, kw=kw: e.dma_start(out=o, in_=i, **kw)
        self.lists[q].append((ws, fn, (key, 16)))
        self._mark(tok, reads, writes)
        return tok

    def collective(self, kind, in_ap, out_ap, groups, reads, writes):
        key = "coll"
        ws = self._waits("pool", self._deps(reads, writes))
        self.cnt[key] += 1
        tok = (key, self.cnt[key])
        fn = lambda e: e.collective_compute(kind, ALU.bypass, replica_groups=groups,
                                            ins=[in_ap], outs=[out_ap])
        self.lists["pool"].append((ws, fn, (key, 1)))
        self._mark(tok, reads, writes)
        return tok

    def barrier(self):
        for e in self.ENGS:
            ws = self._waits(e, {k: v for k, v in self.cnt.items() if v > 0})
            if ws:
                self.lists[e].append((ws, None, None))
        for b in self.attached:
            b.dsem = None
        self.attached = []
        self.pool_idx = 0

    def emit(self):
        nc = self.nc
        sems = self.sems
        lists = self.lists
        with nc.Block() as block:
            def run(eng_obj, lst):
                for ws, fn, inc in lst:
                    for k, v in ws:
                        eng_obj.wait_ge(sems[k], v)
                    if fn is not None:
                        ins = fn(eng_obj)
                        if inc is not None:
                            ins.then_inc(sems[inc[0]], inc[1])

            @block.tensor
            def _(e):
                run(e, lists["pe"])

            @block.scalar
            def _(e):
                run(e, lists["act"])

            @block.vector
            def _(e):
                run(e, lists["dve"])

            @block.gpsimd
            def _(e):
                run(e, lists["pool"])

            @block.sync
            def _(e):
                run(e, lists["sp"])


class Arena:
    def __init__(self, ap_u8, limit):
        self.ap = ap_u8
        self.limit = limit
        self.off = 0

    def reset(self, off=0):
        self.off = off

    def alloc(self, shape, dt, parts=128):
        esz = 4 if dt == F32 else 2
        n = int(np.prod(shape))
        nbytes = (n * esz + 63) // 64 * 64
        assert self.off + nbytes <= self.limit, ("arena overflow", self.off, nbytes, self.limit)
        a = self.ap[0:parts, self.off:self.off + n * esz].bitcast(dt)
        self.off += nbytes
        if len(shape) == 2:
            a = a.rearrange("p (a b) -> p a b", a=shape[0])
        elif len(shape) == 3:
            a = a.rearrange("p (a b c) -> p a b c", a=shape[0], b=shape[1])
        return a
D = 1024
CTX = 256
GW = 64
FH = 2816
L = 2
EPS = 1e-5
ALPHA = (2 * L) ** 0.25
NA_SCALE = 64 ** -0.5
MLA_SCALE = 96 ** -0.5
MASKVAL = -30000.0
O_F, O_NQ, O_NK, O_NV, O_CQ, O_CKV, O_KR, O_G = 0, 512, 1024, 1536, 2048, 2304, 2432, 2464
IN_DIM = 5536


class Cfg:
    def __init__(self, R=64, debug=(), stop=99):
        self.stop = stop
        self.R = R
        self.T = R * GW
        self.N = 4 * self.T
        self.N1 = self.N // 128
        self.NT = 512
        self.NTL = self.T // 512
        self.NQB = R // 8
        self.debug = tuple(debug)


def build_program(cfg):
    R, T, N, N1, NT, NTL, NQB = cfg.R, cfg.T, cfg.N, cfg.N1, cfg.NT, cfg.NTL, cfg.NQB
    TW = T + 512
    nc = bass.Bass("TRN2", target_bir_lowering=False)

    def din(name, shape, dt=F32):
        return nc.dram_tensor(name, list(shape), dt, kind="ExternalInput").ap()

    dbg = {}

    def dscr(name, shape, dt=BF16):
        if name in cfg.debug and name not in ('NOOUT', 'ONLYP3', 'L1'):
            t = nc.dram_tensor(name, list(shape), dt, kind="ExternalOutput")
            dbg[name] = t
        else:
            t = nc.dram_tensor(name, list(shape), dt)
        return t.ap()

    x_in = din("x", [T, D]); ctx_in = din("ctx", [CTX, D]); c_in = din("c", [128, 8]); cc_in = din("c_ctx", [128, 8])
    lng_in = din("ln_in_g", [128, 8]); lnb_in = din("ln_in_b", [128, 8])
    w_mod = din("w_mod", [L, D, 6 * D]); b_mod = din("b_mod", [L, 128, 48]); w_in = din("w_in", [L, D, IN_DIM])
    gq_in = din("mla_q_norm_g", [L, 128, 2]); gkv_in = din("mla_kv_norm_g", [L, 128, 1])
    w_uq = din("w_uq", [L, 256, 512]); w_qr = din("w_qr", [L, 256, 256]); w_uk = din("w_uk", [L, 128, 512]); w_uv = din("w_uv", [L, 128, 512])
    rpbT = din("rpbT", [L, 8, 23, 64, 64])
    w_br = din("w_branch", [L, 1536, D]); w_out = din("w_out", [L, D, D])
    ln1g = din("ln1_g", [L, 128, 8]); ln1b = din("ln1_b", [L, 128, 8]); ln2g = din("ln2_g", [L, 128, 8]); ln2b = din("ln2_b", [L, 128, 8])
    w_fg = din("w_ffn_gate", [L, D, FH]); w_fu = din("w_ffn_up", [L, D, FH]); w_fd = din("w_ffn_down", [L, FH, D])
    k_id = din("k_ident", [128, 128]); k_cs = din("k_cs128", [128, 256], BF16)
    k_c1 = din("k_c1", [N1, N1], BF16); k_s1 = din("k_s1", [N1, N1], BF16); k_ns1 = din("k_ns1", [N1, N1], BF16)
    k_mre = din("k_mre", [128, N1 * 32], BF16); k_mim = din("k_mimn", [128, N1 * 32], BF16)
    k_cc = din("k_cc256", [256, 256], BF16); k_sc = din("k_scn256", [256, 256], BF16)
    k_rope = din("k_rope", [2, 32, T]); k_rv = din("k_rv", [128, 3 * 64]); k_sel = din("k_sel", [128, 8])
    out_d = nc.dram_tensor("out", [T, D], F32, kind="ExternalOutput").ap()

    WP = {}
    CASTS = []

    def wpiece(key, src2d, r0, kch, c0, pc):
        WP[key] = (dscr("W_%s" % "_".join(str(x) for x in key), [128, kch * pc]), kch, pc)
        CASTS.append((key, src2d, r0, kch, c0, pc))

    for l in range(L):
        for nm, c0, pc in (("f", O_F, 512), ("nq", O_NQ, 512), ("nk", O_NK, 512), ("nv", O_NV, 512), ("cq", O_CQ, 256), ("ckv", O_CKV, 128)):
            wpiece((l, nm), w_in[l], 0, 8, c0, pc)
        for gi in range(3):
            for hb in range(2):
                wpiece((l, "g", gi, hb), w_in[l], 0, 8, O_G + gi * D + hb * 512, 512)
                wpiece((l, "br", gi, hb), w_br[l], gi * 512, 4, hb * 512, 512)
        for hb in range(2):
            wpiece((l, "o", hb), w_out[l], 0, 8, hb * 512, 512)
        for j in range(6):
            pc = 512 if j < 5 else 256
            wpiece((l, "fg", j), w_fg[l], 0, 8, j * 512, pc)
            wpiece((l, "fu", j), w_fu[l], 0, 8, j * 512, pc)
        for j in range(4):
            wpiece((l, "fd", j), w_fd[l], 0, 22, j * 256, 256)
    XL = [dscr("XL%d" % i, [NTL, 128, 8 * NT], F32) for i in range(2)]
    XC = [dscr("XC%d" % i, [128, 8 * CTX], F32) for i in range(2)]
    NAQT = dscr("NAQT", [4, 128, T]); NAKW = dscr("NAKW", [4, 128, TW]); NBL = T // 128; NWB = TW // 128
    NAVW = dscr("NAVW", [8 * 128, NWB * 65])
    HK = dscr("HK", [512, 512]); HKA = dscr("HKA", [4 * 512, 512]); HV = dscr("HV", [512, 512]); HVA = dscr("HVA", [4 * 512, 512])
    MLAQT = dscr("MLAQT", [8, 96, T]); MLAKT = dscr("MLAKT", [768, T]); MLAKTA = [dscr("MLAKTA%d" % h, [4 * 96, T]) for h in range(8)]
    MLAV = dscr("MLAV", [8 * 128, NBL * 65])
    MLAVA = [dscr("MLAVA%d" % h, [4 * 128, NBL * 65]) for h in range(8)]
    HF = 2 if T * 512 > (1 << 20) else 1; CH = T // HF
    FAB = dscr("FAB", [4 * T, 256]); FABA = [dscr("FABA%d" % j, [4 * CH, 256]) for j in range(4 * HF)]
    NAQTc = dscr("NAQTc", [4, 128, CTX]); NAKTc = dscr("NAKTc", [4, 128, CTX]); NAVc = dscr("NAVc", [8 * 128, 2 * 65])
    MLAQTc = dscr("MLAQTc", [8, 96, CTX]); MLAKTc = dscr("MLAKTc", [8, 96, CTX]); MLAVc = dscr("MLAVc", [8 * 128, 2 * 65])
    FABc = dscr("FABc", [CTX, 4, 256])
    YF = dscr("YF", [NTL, 128, 4 * NT]); YNA = dscr("YNA", [NTL, 128, 4 * NT]); YMLA = dscr("YMLA", [NTL, 128, 4 * NT])
    YFc = dscr("YFc", [4, 128, CTX]); YNAc = dscr("YNAc", [4, 128, CTX]); YMLAc = dscr("YMLAc", [4, 128, CTX])
    EBT = [dscr("EBT%d" % l, [8, 64, 23 * 64]) for l in range(L)]

    GROUPS = [[0, 1, 2, 3], [4, 5, 6, 7]]
    ARENA = 196 * 1024

    with ExitStack() as st:
        P = Prog(nc, st)
        ar_ap = st.enter_context(nc.sbuf_tensor("arena", [128, ARENA], U8))
        ps_ap = st.enter_context(nc.psum_tensor("ps", [128, 8, 512], F32))
        ar = Arena(ar_ap, ARENA)
        psb = [P.buf("ps%d" % i) for i in range(8)]
        PS = lambda i: ps_ap[:, i, :]
        dram = {}

        def db(name):
            if name not in dram:
                dram[name] = P.buf(name)
            return dram[name]

        rr = {"lin": 0, "ev": 0, "bank": 0}

        def xview(i, lat, ti):
            if lat:
                return XL[i][ti].rearrange("p (k t) -> p k t", k=8)
            return XC[i].rearrange("p (k t) -> p k t", k=8)

        def evac_eng():
            rr["ev"] ^= 1
            return "act" if rr["ev"] else "dve"

        def copy_op(eng, out, in_, rd, wr):
            if eng == "act":
                P.op("act", lambda e: e.copy(out=out, in_=in_), rd, wr)
            else:
                P.op(eng, lambda e: e.tensor_copy(out=out, in_=in_), rd, wr)

        def pt(name, shape, dt, parts=128):
            return ar.alloc(shape, dt, parts=parts), P.buf(name)

        ident, b_ident = pt("ident", [128], F32)
        ones, b_ones = pt("ones", [128], F32)
        epsb, b_eps = pt("eps", [1], F32)
        cs128, b_cs = pt("cs128", [256], BF16)
        c1, b_c1 = pt("c1", [N1], BF16); s1, b_s1 = pt("s1", [N1], BF16); ns1, b_ns1 = pt("ns1", [N1], BF16)
        mre, b_mre = pt("mre", [N1, 32], BF16); mimn, b_mim = pt("mimn", [N1, 32], BF16)
        cc256, b_cc = pt("cc256", [2, 256], BF16); sc256, b_sc = pt("sc256", [2, 256], BF16)
        rv, b_rv = pt("rv", [3, 8, 8], F32); sel, b_sel = pt("sel", [8], F32)
        lnin, b_lnin = pt("lnin", [2, 8], F32)
        modv, b_mod_ = pt("modv", [L, 48, 2], F32); mod1, b_mod1 = pt("mod1", [L, 48, 2], F32)
        lnp, b_lnp = pt("lnp", [L, 4, 8], F32)
        gq, b_gq = pt("gq", [L, 2], F32); gkv, b_gkv = pt("gkv", [L, 1], F32)
        wkr, b_wkr = pt("wkr", [8, 2, 96], BF16)
        wq, b_wq = pt("wq", [2, 8, 96], BF16); wqs, b_wqs = pt("wqs", [2, 8, 96], BF16)
        wk, b_wk = pt("wk", [8, 64], BF16); wuv, b_wuv = pt("wuv", [512], BF16)
        PERSIST = ar.off

        P.dma("sp", ident, k_id, [], [b_ident])
        P.op("pool", lambda e: e.memset(ones, 1.0), [], [b_ones])
        P.op("pool", lambda e: e.memset(epsb, EPS), [], [b_eps])
        P.dma("sp", cs128, k_cs, [], [b_cs])
        P.dma("sp", c1[0:N1], k_c1, [], [b_c1]); P.dma("sp", s1[0:N1], k_s1, [], [b_s1]); P.dma("sp", ns1[0:N1], k_ns1, [], [b_ns1])
        P.dma("sp", mre, k_mre.rearrange("p (a b) -> p a b", b=32), [], [b_mre])
        P.dma("sp", mimn, k_mim.rearrange("p (a b) -> p a b", b=32), [], [b_mim])
        P.dma("sp", cc256, k_cc.rearrange("(a p) k -> p a k", p=128), [], [b_cc])
        P.dma("sp", sc256, k_sc.rearrange("(a p) k -> p a k", p=128), [], [b_sc])
        P.dma("sp", rv, k_rv.rearrange("p (t a b) -> p t a b", t=3, a=8), [], [b_rv])
        P.dma("sp", sel, k_sel, [], [b_sel])
        P.dma("sp", lnin[:, 0, :], lng_in, [], [b_lnin])
        P.dma("sp", lnin[:, 1, :], lnb_in, [], [b_lnin])
        for l in range(L):
            for i, src in enumerate((ln1g, ln1b, ln2g, ln2b)):
                P.dma("sp", lnp[:, l, i, :], src[l], [], [b_lnp])
            P.dma("sp", gq[:, l, :], gq_in[l], [], [b_gq])
            P.dma("sp", gkv[:, l, :], gkv_in[l], [], [b_gkv])

        b_wcast = P.buf("wcast")
        for (key, src2d, r0, kch, c0, pc) in CASTS:
            dstp = WP[key][0]
            for k in range(kch):
                P.dma("pool", dstp[:, k * pc:(k + 1) * pc], src2d[r0 + k * 128:r0 + (k + 1) * 128, c0:c0 + pc], [], [b_wcast])

        ar.reset(PERSIST)
        cs_t, b_cst = pt("cs_t", [2, 8], F32)
        bm_t, b_bmt = pt("bm_t", [L, 48], F32)
        P.dma("sp", cs_t[:, 0, :], c_in, [], [b_cst])
        P.dma("sp", cs_t[:, 1, :], cc_in, [], [b_cst])
        P.op("act", lambda e: e.activation(out=cs_t, in_=cs_t, func=AF.Silu), [b_cst], [b_cst])
        for l in range(L):
            P.dma("sp", bm_t[:, l, :], b_mod[l], [], [b_bmt])
        wm = [pt("wm%d" % i, [8, 512], F32) for i in range(2)]
        for l in range(L):
            for j in range(12):
                wt_, wb_ = wm[j % 2]
                P.dma("sp", wt_, w_mod[l][:, j * 512:(j + 1) * 512].rearrange("(k p) m -> p k m", p=128), [], [wb_])
                for cc in range(4):
                    ch = j * 4 + cc
                    bank = ch % 4
                    for k in range(8):
                        P.op("pe", lambda e, wt_=wt_, k=k, cc=cc, bank=bank: e.matmul(
                            ps_ap[:, bank, 0:2], lhsT=wt_[:, k, cc * 128:(cc + 1) * 128], rhs=cs_t[:, :, k],
                            start=(k == 0), stop=(k == 7)), [wb_, b_cst], [psb[bank]], inc=(k == 7))
                    P.op("dve", lambda e, l=l, ch=ch, bank=bank: e.tensor_scalar(
                        out=modv[:, l, ch, :], in0=ps_ap[:, bank, 0:2], scalar1=bm_t[:, l, ch:ch + 1], scalar2=None,
                        op0=ALU.add), [psb[bank], b_bmt], [b_mod_])
        P.op("dve", lambda e: e.tensor_scalar(out=mod1, in0=modv, scalar1=1.0, scalar2=None, op0=ALU.add), [b_mod_], [b_mod1])

        def MOD(l, idx, s, plus1):
            src = mod1 if plus1 else modv
            return src[:, l, idx, s:s + 1]

        eb_t = [pt("ebt%d" % i, [23, 64], F32, parts=64) for i in range(2)]
        eb_o = [pt("ebo%d" % i, [23, 64], BF16, parts=64) for i in range(2)]
        for l in range(L):
            for h in range(8):
                (ti, tb), (oi, ob) = eb_t[h % 2], eb_o[h % 2]
                P.dma("sp", ti, rpbT[l, h].rearrange("e k q -> k e q"), [], [tb])
                P.op("act", lambda e, ti=ti, oi=oi: e.activation(out=oi, in_=ti, func=AF.Exp), [tb], [ob])
                P.dma("pool", EBT[l][h], oi.rearrange("p a b -> p (a b)"), [ob], [db("EBT%d" % l)], own=ob)
        P.barrier()
        ar.reset(PERSIST)

        def load_layer_small(l):
            st_uq, b1 = pt("st_uq", [2, 512], F32); st_qr, b2 = pt("st_qr", [2, 256], F32)
            P.dma("sp", st_uq, w_uq[l].rearrange("(k p) m -> p k m", p=128), [], [b1])
            P.dma("sp", st_qr, w_qr[l].rearrange("(k p) m -> p k m", p=128), [], [b2])
            P.op("pool", lambda e: e.memset(wqs, 0.0), [], [b_wqs])
            P.op("pool", lambda e: e.memset(wkr, 0.0), [], [b_wkr])
            uqv = st_uq.rearrange("p k (h d) -> p k h d", h=8); qrv = st_qr.rearrange("p k (h d) -> p k h d", h=8)
            P.op("dve", lambda e: e.tensor_copy(out=wq[:, :, :, 0:64], in_=uqv), [b1], [b_wq])
            P.op("dve", lambda e: e.tensor_copy(out=wq[:, :, :, 64:96], in_=qrv), [b2], [b_wq])
            P.op("dve", lambda e: e.tensor_copy(out=wqs[:, :, :, 64:80], in_=qrv[:, :, :, 16:32]), [b2], [b_wqs])
            P.op("dve", lambda e: e.tensor_copy(out=wqs[:, :, :, 80:96], in_=qrv[:, :, :, 0:16]), [b2], [b_wqs])
            st_kr, b3 = pt("st_kr", [8, 32], F32)
            P.dma("sp", st_kr, w_in[l][:, O_KR:O_KR + 32].rearrange("(k p) m -> p k m", p=128), [], [b3])
            P.op("dve", lambda e: e.tensor_copy(out=wkr[:, :, 0, 64:96], in_=st_kr), [b3], [b_wkr])
            P.op("dve", lambda e: e.tensor_copy(out=wkr[:, :, 1, 64:80], in_=st_kr[:, :, 16:32]), [b3], [b_wkr])
            P.op("dve", lambda e: e.tensor_copy(out=wkr[:, :, 1, 80:96], in_=st_kr[:, :, 0:16]), [b3], [b_wkr])
            st_uk, b4 = pt("st_uk", [512], F32); st_uv, b5 = pt("st_uv", [512], F32)
            P.dma("sp", st_uk, w_uk[l], [], [b4]); P.dma("sp", st_uv, w_uv[l], [], [b5])
            P.op("dve", lambda e: e.tensor_copy(out=wk, in_=st_uk.rearrange("p (h d) -> p h d", h=8)), [b4], [b_wk])
            P.op("dve", lambda e: e.tensor_copy(out=wuv, in_=st_uv), [b5], [b_wuv])

        wslots = []

        def load_piece(key):
            wd, kch, pc = WP[key]
            slot, sbuf_ = wslots[rr["lin"] % len(wslots)]
            rr["lin"] += 1
            P.dma("sp", slot[:, 0:kch * pc], wd, [b_wcast], [sbuf_])
            return slot[:, 0:kch * pc].rearrange("p (k m) -> p k m", k=kch), sbuf_, kch, pc

        def mm_chunk(piece, c0, rows, act, abuf, nt):
            sv, sbuf_, kch, pc = piece
            bank = rr["bank"] % 4; rr["bank"] += 1
            for k in range(kch):
                P.op("pe", lambda e, k=k: e.matmul(ps_ap[0:rows, bank, 0:nt], lhsT=sv[:, k, c0:c0 + rows], rhs=act[:, k, 0:nt],
                                                   start=(k == 0), stop=(k == kch - 1)), [sbuf_, abuf], [psb[bank]], inc=(k == kch - 1))
            return ps_ap[0:rows, bank, 0:nt], psb[bank]

        def linear_fm(keys, act, abuf, nt, consume):
            ci = 0
            for key in keys:
                piece = load_piece(key)
                for c0 in range(0, piece[3], 128):
                    rows = min(128, piece[3] - c0)
                    ps, pb = mm_chunk(piece, c0, rows, act, abuf, nt)
                    consume(ci, ps, pb, rows)
                    ci += 1

        def ln_fm(r, rbuf, nt, gap, bap, sq, sqbuf, st4, st4buf):
            for k in range(8):
                if k % 2 == 0:
                    P.op("act", lambda e, k=k: e.activation(out=sq[:, k, 0:nt], in_=r[:, k, 0:nt], func=AF.Square), [rbuf], [sqbuf])
                else:
                    P.op("pool", lambda e, k=k: e.tensor_tensor(out=sq[:, k, 0:nt], in0=r[:, k, 0:nt], in1=r[:, k, 0:nt], op=ALU.mult), [rbuf], [sqbuf])
            for k in range(8):
                P.op("pe", lambda e, k=k: e.matmul(ps_ap[:, 4, 0:nt], lhsT=ones, rhs=r[:, k, 0:nt], start=(k == 0), stop=(k == 7)),
                     [b_ones, rbuf], [psb[4]], inc=(k == 7))
            for k in range(8):
                P.op("pe", lambda e, k=k: e.matmul(ps_ap[:, 5, 0:nt], lhsT=ones, rhs=sq[:, k, 0:nt], start=(k == 0), stop=(k == 7)),
                     [b_ones, sqbuf], [psb[5]], inc=(k == 7))
            mean = st4[:, 0, 0:nt]; msq = st4[:, 1, 0:nt]; rstd = st4[:, 2, 0:nt]
            P.op("dve", lambda e: e.tensor_scalar(out=mean, in0=ps_ap[:, 4, 0:nt], scalar1=1.0 / D, scalar2=None, op0=ALU.mult), [psb[4]], [st4buf])
            P.op("dve", lambda e: e.tensor_tensor(out=msq, in0=mean, in1=mean, op=ALU.mult), [st4buf], [st4buf])
            P.op("dve", lambda e: e.scalar_tensor_tensor(out=rstd, in0=ps_ap[:, 5, 0:nt], scalar=1.0 / D, in1=msq, op0=ALU.mult, op1=ALU.subtract), [psb[5], st4buf], [st4buf])
            P.op("act", lambda e: e.activation(out=rstd, in_=rstd, func=AF.Ln, bias=epsb[:, 0:1], scale=1.0), [st4buf, b_eps], [st4buf])
            P.op("act", lambda e: e.activation(out=rstd, in_=rstd, func=AF.Exp, scale=-0.5), [st4buf], [st4buf])
            for k in range(8):
                rk = r[:, k, 0:nt]
                P.op("pool", lambda e, rk=rk: e.tensor_tensor(out=rk, in0=rk, in1=mean, op=ALU.subtract), [rbuf, st4buf], [rbuf])
                P.op("dve", lambda e, rk=rk: e.tensor_tensor(out=rk, in0=rk, in1=rstd, op=ALU.mult), [rbuf, st4buf], [rbuf])
                P.op("act", lambda e, rk=rk, k=k: e.activation(out=rk, in_=rk, func=AF.Identity, bias=bap[:, k:k + 1], scale=gap[:, k:k + 1]), [rbuf, b_lnp, b_lnin], [rbuf])

        def rms_fm(src, sbuf_, kch, nt, gsc, dst, dbuf, sq, sqbuf, st4, st4buf):
            for k in range(kch):
                P.op("pool", lambda e, k=k: e.tensor_tensor(out=sq[:, k, 0:nt], in0=src[:, k, 0:nt], in1=src[:, k, 0:nt], op=ALU.mult), [sbuf_], [sqbuf])
            for k in range(kch):
                P.op("pe", lambda e, k=k: e.matmul(ps_ap[:, 5, 0:nt], lhsT=ones, rhs=sq[:, k, 0:nt], start=(k == 0), stop=(k == kch - 1)),
                     [b_ones, sqbuf], [psb[5]], inc=(k == kch - 1))
            rstd = st4[:, 3, 0:nt]
            P.op("act", lambda e: e.activation(out=rstd, in_=ps_ap[:, 5, 0:nt], func=AF.Ln, bias=epsb[:, 0:1], scale=1.0 / (kch * 128)), [psb[5], b_eps], [st4buf])
            P.op("act", lambda e: e.activation(out=rstd, in_=rstd, func=AF.Exp, scale=-0.5), [st4buf], [st4buf])
            for k in range(kch):
                P.op("dve", lambda e, k=k: e.scalar_tensor_tensor(out=dst[:, k, 0:nt], in0=src[:, k, 0:nt], scalar=gsc[:, k:k + 1], in1=rstd, op0=ALU.mult, op1=ALU.mult),
                     [sbuf_, st4buf, b_gq, b_gkv], [dbuf])

        att_state = {}

        def attend(QT, qbuf, kbs, scale, nq, ytile, ybuf):
            ptl = att_state["ptl"]; rs, rsb = att_state["rs"]; bcs, bcsb = att_state["bc"]
            ob = att_state["obank"]; att_state["obank"] = 3 if ob == 7 else 7
            n = len(kbs)

            def S(i):
                KT, kb_, V, vb_, eb, ebb = kbs[i]
                bank = 4 + (att_state["si"] % 3); att_state["si"] += 1
                P.op("pe", lambda e: e.matmul(ps_ap[:, bank, 0:nq], lhsT=KT, rhs=QT, start=True, stop=True), [kb_, qbuf], [psb[bank]])
                pt_, pb_ = ptl[att_state["pi"] % len(ptl)]; att_state["pi"] += 1
                P.op("act", lambda e: e.activation(out=pt_[:, 0:nq], in_=ps_ap[:, bank, 0:nq], func=AF.Exp, scale=scale), [psb[bank]], [pb_])
                if eb is not None:
                    P.op("dve", lambda e: e.tensor_tensor(out=pt_[:, 0:nq], in0=pt_[:, 0:nq], in1=eb, op=ALU.mult), [pb_, ebb], [pb_])
                return pt_, pb_

            def PV(i, pt_, pb_):
                KT, kb_, V, vb_, eb, ebb = kbs[i]
                P.op("pe", lambda e: e.matmul(ps_ap[0:65, ob, 0:nq], lhsT=V, rhs=pt_[:, 0:nq], start=(i == 0), stop=(i == n - 1)), [vb_, pb_], [psb[ob]])

            pend = []
            for i in range(n):
                pend.append((i,) + S(i))
                if len(pend) > 1:
                    PV(*pend.pop(0))
            while pend:
                PV(*pend.pop(0))
            P.op("dve", lambda e: e.reciprocal(out=rs[64:65, 0:nq], in_=ps_ap[64:65, ob, 0:nq]), [psb[ob]], [rsb])
            bb = 4 + (att_state["si"] % 3); att_state["si"] += 1
            P.op("pe", lambda e: e.matmul(ps_ap[0:64, bb, 0:nq], lhsT=ones[64:65, 0:64], rhs=rs[64:65, 0:nq], start=True, stop=True), [b_ones, rsb], [psb[bb]])
            P.op("act", lambda e: e.copy(out=bcs[0:64, 0:nq], in_=ps_ap[0:64, bb, 0:nq]), [psb[bb]], [bcsb])
            P.op("dve", lambda e: e.tensor_tensor(out=ytile, in0=ps_ap[0:64, ob, 0:nq], in1=bcs[0:64, 0:nq], op=ALU.mult), [psb[ob], bcsb], [ybuf])

        def phase1(l, xt, xbuf, nt, s, tok0, tidx, T1, store_x=True):
            lat = (s == 0)
            nsub = nt // 128
            A = T1["a"]; bA = T1["ab"]
            for k in range(8):
                P.op("dve", lambda e, k=k: e.tensor_scalar(out=A[:, k, 0:nt], in0=xt[:, k, 0:nt], scalar1=MOD(l, 8 + k, s, True), scalar2=MOD(l, k, s, False),
                                                           op0=ALU.mult, op1=ALU.add), [xbuf, b_mod_, b_mod1], [bA])
            xdst = xview(l % 2, lat, tok0 // NT)
            if store_x:
                P.dma("pool", xdst, xt[:, :, 0:nt], [xbuf], [db("XL%d" % (l % 2) if lat else "XC%d" % (l % 2))], own=xbuf)
            zt_, bz = T1["z"]; fab, bfab = T1["fab"]

            def cons_f(ci, ps, pb, rows):
                g = ci
                copy_op(evac_eng(), zt_[:, 0:nt], ps, [pb], [bz])
                for sub in range(nsub):
                    P.op("pe", lambda e, sub=sub: e.matmul(ps_ap[:, 6, 0:256], lhsT=zt_[:, sub * 128:(sub + 1) * 128], rhs=cs128, start=True, stop=True), [bz, b_cs], [psb[6]])
                    ov = fab[:, sub, g]; iv = ps_ap[:, 6, 0:256]
                    copy_op(evac_eng(), ov, iv, [psb[6]], [bfab])
            linear_fm([(l, "f")], A, bA, nt, cons_f)
            for sub in range(nsub):
                if lat:
                    dst = FAB.rearrange("(g t) c -> t g c", g=4)[tok0 + sub * 128: tok0 + (sub + 1) * 128]
                    P.dma("pool", dst, fab[:, sub], [bfab], [db("FAB")], own=bfab)
                else:
                    P.dma("pool", FABc[sub * 128:(sub + 1) * 128], fab[:, sub], [bfab], [db("FABc")], own=bfab)
            for (wkey, key) in (("nq", "naq"), ("nk", "nak")):
                tl, tb = T1[key]

                def cons(ci, ps, pb, rows, tl=tl, tb=tb):
                    copy_op(evac_eng(), tl[:, ci, 0:nt], ps, [pb], [tb])
                linear_fm([(l, wkey)], A, bA, nt, cons)
                if key == "naq":
                    dst = (NAQT[:, :, tok0:tok0 + nt] if lat else NAQTc[:, :, 0:nt]).rearrange("k p t -> p k t")
                    P.dma("pool", dst, tl[:, :, 0:nt], [tb], [db("NAQT" if lat else "NAQTc")], own=tb)
                else:
                    dst = (NAKW[:, :, 256 + tok0:256 + tok0 + nt] if lat else NAKTc[:, :, 0:nt]).rearrange("k p t -> p k t")
                    P.dma("pool", dst, tl[:, :, 0:nt], [tb], [db("NAKW" if lat else "NAKTc")], own=tb)
                    if lat and tidx == 0:
                        P.dma("pool", HK.rearrange("(k p) t -> p k t", p=128)[:, :, 0:256], tl[:, :, 0:256], [tb], [db("HK")], own=tb)
                    if lat and tidx == NTL - 1:
                        P.dma("pool", HK.rearrange("(k p) t -> p k t", p=128)[:, :, 256:512], tl[:, :, nt - 256:nt], [tb], [db("HK")], own=tb)
            wvv, bwv, _, _ = load_piece((l, "nv"))
            nav, bnav = T1["nav"]
            for sub in range(nsub):
                for k in range(8):
                    P.op("pe", lambda e, k=k, sub=sub: e.matmul(ps_ap[:, 7, :], lhsT=A[:, k, sub * 128:(sub + 1) * 128], rhs=wvv[:, k, :], start=(k == 0), stop=(k == 7)),
                         [bA, bwv], [psb[7]], inc=(k == 7))
                copy_op(evac_eng(), nav[:, :, sub, 0:64], ps_ap[:, 7, :].rearrange("p (h d) -> p h d", h=8), [psb[7]], [bnav])
            vview = lambda Vd: Vd.rearrange("(h p) (b c) -> p h b c", p=128, c=65)
            if lat:
                b0 = 2 + (tok0 // 128)
                P.dma("pool", vview(NAVW)[:, :, b0:b0 + nsub, :], nav[:, :, 0:nsub, :], [bnav], [db("NAVW")], own=bnav)
                if tidx == 0:
                    for s_ in range(2):
                        P.dma("pool", HV[s_ * 128:(s_ + 1) * 128].rearrange("p (h d) -> p h d", h=8), nav[:, :, s_, 0:64], [bnav], [db("HV")], own=bnav)
                if tidx == NTL - 1:
                    for s_ in range(2):
                        P.dma("pool", HV[256 + s_ * 128:256 + (s_ + 1) * 128].rearrange("p (h d) -> p h d", h=8), nav[:, :, nsub - 2 + s_, 0:64], [bnav], [db("HV")], own=bnav)
            else:
                P.dma("pool", vview(NAVc)[:, :, 0:nsub, :], nav[:, :, 0:nsub, :], [bnav], [db("NAVc")], own=bnav)
            cq, bcq = T1["cq"]; ckv, bckv = T1["ckv"]

            def cons_cq(ci, ps, pb, rows):
                copy_op(evac_eng(), cq[:, ci, 0:nt], ps, [pb], [bcq])
            linear_fm([(l, "cq")], A, bA, nt, cons_cq)

            def cons_ckv(ci, ps, pb, rows):
                copy_op(evac_eng(), ckv[:, 0, 0:nt], ps, [pb], [bckv])
            linear_fm([(l, "ckv")], A, bA, nt, cons_ckv)
            sq, bsq = T1["sq"]; st4, bst4 = T1["st4"]
            cqn, bcqn = T1["cqn"]; ckvn, bckvn = T1["ckvn"]
            rms_fm(cq, bcq, 2, nt, gq[:, l, :], cqn, bcqn, sq, bsq, st4, bst4)
            rms_fm(ckv, bckv, 1, nt, gkv[:, l, :], ckvn, bckvn, sq, bsq, st4, bst4)
            rope, brope = T1["rope"]
            if lat:
                P.dma("sp", rope[64:96, :, 0:nt], k_rope[:, :, tok0:tok0 + nt].rearrange("a p t -> p a t"), [], [brope])
            tmp, btmp = T1["tmp"]
            krr, bkrr = T1["krr"]
            for v in range(2 if lat else 1):
                for k in range(8):
                    P.op("pe", lambda e, k=k, v=v: e.matmul(ps_ap[0:96, 6 + v, 0:nt], lhsT=wkr[:, k, v, :], rhs=A[:, k, 0:nt], start=(k == 0), stop=(k == 7)),
                         [b_wkr, bA], [psb[6 + v]], inc=(k == 7))
            if lat:
                P.op("dve", lambda e: e.tensor_tensor(out=tmp[64:96, 0, 0:nt], in0=ps_ap[64:96, 6, 0:nt], in1=rope[64:96, 0, 0:nt], op=ALU.mult), [psb[6], brope], [btmp])
                P.op("dve", lambda e: e.tensor_tensor(out=tmp[64:96, 1, 0:nt], in0=ps_ap[64:96, 7, 0:nt], in1=rope[64:96, 1, 0:nt], op=ALU.mult), [psb[7], brope], [btmp])
                P.op("pool", lambda e: e.tensor_tensor(out=krr[64:96, 0:nt], in0=tmp[64:96, 0, 0:nt], in1=tmp[64:96, 1, 0:nt], op=ALU.add), [btmp], [bkrr])
            else:
                P.op("act", lambda e: e.copy(out=krr[64:96, 0:nt], in_=ps_ap[64:96, 6, 0:nt]), [psb[6]], [bkrr])
            for h in range(8):
                qt_, bq_ = T1["qt"][h % 2]
                for v in range(2 if lat else 1):
                    wsrc = wq if v == 0 else wqs
                    for k in range(2):
                        P.op("pe", lambda e, k=k, v=v, h=h, wsrc=wsrc: e.matmul(ps_ap[0:96, 6 + v, 0:nt], lhsT=wsrc[:, k, h, :], rhs=cqn[:, k, 0:nt], start=(k == 0), stop=(k == 1)),
                             [b_wq, b_wqs, bcqn], [psb[6 + v]], inc=(k == 1))
                if lat:
                    P.op("act", lambda e, qt_=qt_: e.copy(out=qt_[0:64, 0:nt], in_=ps_ap[0:64, 6, 0:nt]), [psb[6]], [bq_])
                    P.op("dve", lambda e: e.tensor_tensor(out=tmp[64:96, 0, 0:nt], in0=ps_ap[64:96, 6, 0:nt], in1=rope[64:96, 0, 0:nt], op=ALU.mult), [psb[6], brope], [btmp])
                    P.op("dve", lambda e: e.tensor_tensor(out=tmp[64:96, 1, 0:nt], in0=ps_ap[64:96, 7, 0:nt], in1=rope[64:96, 1, 0:nt], op=ALU.mult), [psb[7], brope], [btmp])
                    P.op("pool", lambda e, qt_=qt_: e.tensor_tensor(out=qt_[64:96, 0:nt], in0=tmp[64:96, 0, 0:nt], in1=tmp[64:96, 1, 0:nt], op=ALU.add), [btmp], [bq_])
                else:
                    P.op("act", lambda e, qt_=qt_: e.copy(out=qt_[0:96, 0:nt], in_=ps_ap[0:96, 6, 0:nt]), [psb[6]], [bq_])
                qd = MLAQT[h, :, tok0:tok0 + nt] if lat else MLAQTc[h, :, 0:nt]
                P.dma("pool", qd, qt_[0:96, 0:nt], [bq_], [db("MLAQT" if lat else "MLAQTc")], own=bq_)
                kt_, bk_ = T1["kt"][h % 2]
                bank = 4 + (h % 2)
                P.op("pe", lambda e, h=h, bank=bank: e.matmul(ps_ap[0:64, bank, 0:nt], lhsT=wk[:, h, :], rhs=ckvn[:, 0, 0:nt], start=True, stop=True), [b_wk, bckvn], [psb[bank]])
                P.op("dve", lambda e, kt_=kt_, bank=bank: e.tensor_copy(out=kt_[0:64, 0:nt], in_=ps_ap[0:64, bank, 0:nt]), [psb[bank]], [bk_])
                P.op("pool", lambda e, kt_=kt_: e.tensor_copy(out=kt_[64:96, 0:nt], in_=krr[64:96, 0:nt]), [bkrr], [bk_])
                kd = MLAKT[h * 96:(h + 1) * 96, tok0:tok0 + nt] if lat else MLAKTc[h, :, 0:nt]
                P.dma("pool", kd, kt_[0:96, 0:nt], [bk_], [db("MLAKT" if lat else "MLAKTc")], own=bk_)
            mv, bmv = T1["mlav"]
            for sub in range(nsub):
                P.op("pe", lambda e, sub=sub: e.matmul(ps_ap[:, 7, :], lhsT=ckvn[:, 0, sub * 128:(sub + 1) * 128], rhs=wuv, start=True, stop=True), [bckvn, b_wuv], [psb[7]])
                copy_op(evac_eng(), mv[:, :, sub, 0:64], ps_ap[:, 7, :].rearrange("p (h d) -> p h d", h=8), [psb[7]], [bmv])
            if lat:
                b0 = tok0 // 128
                P.dma("pool", vview(MLAV)[:, :, b0:b0 + nsub, :], mv[:, :, 0:nsub, :], [bmv], [db("MLAV")], own=bmv)
            else:
                P.dma("pool", vview(MLAVc)[:, :, 0:nsub, :], mv[:, :, 0:nsub, :], [bmv], [db("MLAVc")], own=bmv)

        def alloc_p1():
            T1 = {}
            T1["a"], T1["ab"] = pt("a", [8, NT], BF16)
            T1["z"] = pt("z", [NT], BF16)
            T1["fab"] = pt("fab", [4, 4, 256], BF16)
            T1["naq"] = pt("naq", [4, NT], BF16); T1["nak"] = pt("nak", [4, NT], BF16)
            T1["nav"] = pt("nav", [8, 4, 65], BF16)
            P.op("pool", lambda e: e.memset(T1["nav"][0], 1.0), [], [T1["nav"][1]])
            T1["cq"] = pt("cq", [2, NT], F32); T1["ckv"] = pt("ckv", [1, NT], F32)
            T1["cqn"] = pt("cqn", [2, NT], BF16); T1["ckvn"] = pt("ckvn", [1, NT], BF16)
            T1["rope"] = pt("rope", [2, NT], F32); T1["tmp"] = pt("tmp", [2, NT], F32)
            T1["krr"] = pt("krr", [NT], BF16)
            T1["qt"] = [pt("qt%d" % i, [NT], BF16) for i in range(2)]
            T1["kt"] = [pt("kt%d" % i, [NT], BF16) for i in range(2)]
            T1["mlav"] = pt("mlav", [8, 4, 65], BF16)
            P.op("pool", lambda e: e.memset(T1["mlav"][0], 1.0), [], [T1["mlav"][1]])
            T1["sq"] = pt("sq1", [2, NT], F32); T1["st4"] = pt("st4_1", [4, NT], F32)
            return T1

        def input_ln(src_rows, nblk, xt, xbuf, W):
            for blk in range(nblk):
                xin, bxin = W["xin"][blk % 2]
                P.dma("sp", xin, src_rows[blk * 128:(blk + 1) * 128, :], [], [bxin])
                s4, bs4 = W["s4"]
                P.op("dve", lambda e, xin=xin: e.reduce_sum(out=s4[:, 0:1], in_=xin, axis=AX.X), [bxin], [bs4])
                P.op("dve", lambda e: e.tensor_scalar(out=s4[:, 1:2], in0=s4[:, 0:1], scalar1=-1.0 / D, scalar2=None, op0=ALU.mult), [bs4], [bs4])
                P.op("act", lambda e, xin=xin: e.activation(out=xin, in_=xin, func=AF.Identity, bias=s4[:, 1:2], scale=1.0), [bxin, bs4], [bxin])
                xsq, bxsq = W["xsq"]
                P.op("pool", lambda e, xin=xin: e.tensor_tensor(out=xsq, in0=xin, in1=xin, op=ALU.mult), [bxin], [bxsq])
                P.op("dve", lambda e: e.reduce_sum(out=s4[:, 2:3], in_=xsq, axis=AX.X), [bxsq], [bs4])
                P.op("act", lambda e: e.activation(out=s4[:, 3:4], in_=s4[:, 2:3], func=AF.Ln, bias=epsb[:, 0:1], scale=1.0 / D), [bs4, b_eps], [bs4])
                P.op("act", lambda e: e.activation(out=s4[:, 3:4], in_=s4[:, 3:4], func=AF.Exp, scale=-0.5), [bs4], [bs4])
                P.op("act", lambda e, xin=xin: e.activation(out=xin, in_=xin, func=AF.Identity, scale=s4[:, 3:4]), [bxin, bs4], [bxin])
                for k in range(8):
                    bank = k % 4
                    P.op("pe", lambda e, xin=xin, k=k, bank=bank: e.transpose(ps_ap[:, bank, 0:128], xin[:, k * 128:(k + 1) * 128], ident), [bxin, b_ident], [psb[bank]])
                    P.op("act" if k % 2 else "dve",
                         (lambda e, k=k, bank=bank, blk=blk: e.activation(out=xt[:, k, blk * 128:(blk + 1) * 128], in_=ps_ap[:, bank, 0:128], func=AF.Identity,
                                                                       bias=lnin[:, 1, k:k + 1], scale=lnin[:, 0, k:k + 1])) if k % 2 else
                         (lambda e, k=k, bank=bank, blk=blk: e.tensor_scalar(out=xt[:, k, blk * 128:(blk + 1) * 128], in0=ps_ap[:, bank, 0:128], scalar1=lnin[:, 0, k:k + 1],
                                                                          scalar2=lnin[:, 1, k:k + 1], op0=ALU.mult, op1=ALU.add)),
                         [psb[bank], b_lnin], [xbuf])

        def exchange():
            P.barrier()
            cl = [(MLAKT[h * 96:(h + 1) * 96, :], MLAKTA[h], "MLAKT", "MLAKTA") for h in range(8)]
            cl += [(MLAV[h * 128:(h + 1) * 128, :], MLAVA[h], "MLAV", "MLAVA") for h in range(8)]
            cl += [(FAB[j * CH:(j + 1) * CH, :], FABA[j], "FAB", "FABA") for j in range(4 * HF)]
            cl += [(HK, HKA, "HK", "HKA"), (HV, HVA, "HV", "HVA")]
            for (src, dst, sn, dn) in cl:
                P.collective("AllGather", src.opt(), dst.opt(), GROUPS, [db(sn)], [db(dn)])
            P.barrier()
            ar.reset(PERSIST)
            hk = [pt("hk%d" % i, [4, 512], BF16) for i in range(2)]
            ho = [pt("ho%d" % i, [2, 256], BF16) for i in range(2)]
            hkv = HKA.rearrange("(r k p) t -> k p r t", r=4, k=4)
            for c in range(4):
                (hi, hb), (oi, ob) = hk[c % 2], ho[c % 2]
                P.dma("sp", hi, hkv[c], [db("HKA")], [hb])
                for half, (t0, so) in enumerate(((256, 0), (0, 4))):
                    P.op("dve", lambda e, hi=hi, oi=oi, half=half, t0=t0, so=so: e.tensor_scalar(out=oi[:, half, :], in0=hi[:, 0, t0:t0 + 256], scalar1=sel[:, so:so + 1], scalar2=None, op0=ALU.mult), [hb, b_sel], [ob])
                    for r in range(1, 4):
                        P.op("dve", lambda e, hi=hi, oi=oi, half=half, t0=t0, so=so, r=r: e.scalar_tensor_tensor(out=oi[:, half, :], in0=hi[:, r, t0:t0 + 256], scalar=sel[:, so + r:so + r + 1], in1=oi[:, half, :], op0=ALU.mult, op1=ALU.add), [hb, b_sel, ob], [ob])
                P.dma("pool", NAKW[c, :, 0:256], oi[:, 0, :], [ob], [db("NAKW")], own=ob)
                P.dma("pool", NAKW[c, :, 256 + T:TW], oi[:, 1, :], [ob], [db("NAKW")], own=ob)
            hv = [pt("hv%d" % i, [4, 512], BF16) for i in range(2)]
            hvo = [pt("hvo%d" % i, [8, 65], BF16) for i in range(2)]
            for (oi, ob) in hvo:
                P.op("pool", lambda e, oi=oi: e.memset(oi, 1.0), [], [ob])
            hvv = HVA.rearrange("(r s p) c -> s p r c", r=4, s=4)
            for sblk in range(4):
                (hi, hb), (oi, ob) = hv[sblk % 2], hvo[sblk % 2]
                srcb = sblk + 2 if sblk < 2 else sblk - 2
                so = 0 if sblk < 2 else 4
                P.dma("sp", hi, hvv[srcb], [db("HVA")], [hb])
                ov = oi[:, :, 0:64]
                hv3 = lambda r, hi=hi: hi[:, r, :].rearrange("p (h d) -> p h d", h=8)
                P.op("dve", lambda e, ov=ov, hv3=hv3, so=so: e.tensor_scalar(out=ov, in0=hv3(0), scalar1=sel[:, so:so + 1], scalar2=None, op0=ALU.mult), [hb, b_sel], [ob])
                for r in range(1, 4):
                    P.op("dve", lambda e, ov=ov, hv3=hv3, so=so, r=r: e.scalar_tensor_tensor(out=ov, in0=hv3(r), scalar=sel[:, so + r:so + r + 1], in1=ov, op0=ALU.mult, op1=ALU.add), [hb, b_sel, ob], [ob])
                wb = sblk if sblk < 2 else 2 + NBL + (sblk - 2)
                P.dma("pool", NAVW.rearrange("(h p) (b c) -> p h b c", p=128, c=65)[:, :, wb, :], oi, [ob], [db("NAVW")], own=ob)
            P.barrier()
            ar.reset(PERSIST)

        def alloc_att():
            att_state.clear()
            att_state.update(si=0, pi=0, obank=7)
            att_state["ptl"] = [pt("p%d" % i, [512], BF16) for i in range(3)]
            att_state["rs"] = pt("rs", [512], F32)
            att_state["bc"] = pt("bcs", [512], F32)

        def phase2_mla(l):
            ar.reset(PERSIST)
            alloc_att()
            NK = CTX + N
            NKB = NK // 128
            KT = [pt("KT%d" % i, [NK], BF16) for i in range(2)]
            VA = [pt("VA%d" % i, [NKB, 65], BF16) for i in range(2)]
            qts = [pt("q%d" % i, [512], BF16) for i in range(3)]
            ys = [pt("y%d" % i, [512], BF16) for i in range(2)]
            qi = 0
            for h in range(8):
                (kt, kb_), (va, vb) = KT[h % 2], VA[h % 2]
                P.dma("sp", kt[0:96, 0:CTX], MLAKTc[h], [db("MLAKTc")], [kb_])
                P.dma("sp", va[:, 0:2, :].rearrange("p a b -> p (a b)"), MLAVc[h * 128:(h + 1) * 128, :], [db("MLAVc")], [vb])
                for r in range(4):
                    P.dma("sp", kt[0:96, CTX + r * T:CTX + (r + 1) * T], MLAKTA[h][r * 96:(r + 1) * 96, :], [db("MLAKTA")], [kb_])
                    P.dma("sp", va[:, 2 + r * NBL:2 + (r + 1) * NBL, :].rearrange("p a b -> p (a b)"), MLAVA[h][r * 128:(r + 1) * 128, :], [db("MLAVA")], [vb])
                kbs_all = [(kt[0:96, j * 128:(j + 1) * 128], kb_, va[:, j, :], vb, None, None) for j in range(NKB)]
                c, po = h // 2, (h % 2) * 64
                for qb in range(NTL):
                    qt_, bq_ = qts[qi % 3]; y_, by_ = ys[qi % 2]; qi += 1
                    P.dma("sp", qt_[0:96, :], MLAQT[h, :, qb * 512:(qb + 1) * 512], [db("MLAQT")], [bq_])
                    attend(qt_[0:96, :], bq_, kbs_all, MLA_SCALE, 512, y_[0:64, :], by_)
                    P.dma("pool", YMLA[qb, po:po + 64, c * NT:(c + 1) * NT], y_[0:64, :], [by_], [db("YMLA")], own=by_)
                if l == 0:
                    qt_, bq_ = qts[qi % 3]; y_, by_ = ys[qi % 2]; qi += 1
                    P.dma("sp", qt_[0:96, 0:CTX], MLAQTc[h], [db("MLAQTc")], [bq_])
                    attend(qt_[0:96, 0:CTX], bq_, kbs_all[0:2], MLA_SCALE, CTX, y_[0:64, 0:CTX], by_)
                    P.dma("pool", YMLAc[c, po:po + 64, :], y_[0:64, 0:CTX], [by_], [db("YMLAc")], own=by_)
            P.barrier()

        def phase2_na(l):
            ar.reset(PERSIST)
            alloc_att()
            KW = [pt("KW%d" % i, [TW], BF16) for i in range(2)]
            QN = [pt("QN%d" % i, [T], BF16) for i in range(2)]
            KC = [pt("KC%d" % i, [CTX], BF16) for i in range(2)]
            QC = [pt("QC%d" % i, [CTX], BF16) for i in range(2)]
            VN = [pt("VN%d" % i, [NWB, 65], BF16) for i in range(2)]
            VC = [pt("VC%d" % i, [2, 65], BF16) for i in range(2)]
            EBF = pt("EBF", [8, 8, 64], BF16)
            EBY = [pt("EBY%d" % i, [8, 512], BF16) for i in range(3)]
            ys = [pt("y%d" % i, [512], BF16) for i in range(2)]
            yi = 0
            for h in range(8):
                c, po = h // 2, (h % 2) * 64
                (kw, bkw), (qn, bqn), (kc, bkc), (qc, bqc) = KW[c % 2], QN[c % 2], KC[c % 2], QC[c % 2]
                (vn, bvn), (vc, bvc) = VN[h % 2], VC[h % 2]
                if h % 2 == 0:
                    P.dma("sp", kw, NAKW[c], [db("NAKW")], [bkw])
                    P.dma("sp", qn, NAQT[c], [db("NAQT")], [bqn])
                    P.dma("sp", kc, NAKTc[c], [db("NAKTc")], [bkc])
                    if l == 0:
                        P.dma("sp", qc, NAQTc[c], [db("NAQTc")], [bqc])
                P.dma("sp", vn.rearrange("p a b -> p (a b)"), NAVW[h * 128:(h + 1) * 128, :], [db("NAVW")], [bvn])
                P.dma("sp", vc.rearrange("p a b -> p (a b)"), NAVc[h * 128:(h + 1) * 128, :], [db("NAVc")], [bvc])
                ebf, bebf = EBF
                for kb in range(8):
                    for krl in range(2):
                        e0 = 15 - 2 * kb - krl
                        P.dma("sp", ebf[krl * 64:(krl + 1) * 64, kb, :, :].rearrange("p a b -> p (a b)"), EBT[l][h][:, e0 * 64:(e0 + 8) * 64], [db("EBT%d" % l)], [bebf])
                for ty in range(3):
                    eby, beby = EBY[ty]
                    P.op("dve", lambda e, eby=eby, ty=ty: e.tensor_tensor(out=eby.rearrange("p a (b c) -> p a b c", b=8), in0=ebf,
                                                                        in1=rv[:, ty].unsqueeze(3).broadcast_to([128, 8, 8, 64]), op=ALU.mult), [bebf, b_rv], [beby])
                ctxk = [(kc[po:po + 64, j * 128:(j + 1) * 128], bkc, vc[:, j, :], bvc, None, None) for j in range(2)]
                for qb in range(NQB):
                    ty = 0 if qb == 0 else (2 if qb == NQB - 1 else 1)
                    eby, beby = EBY[ty]
                    kbs = [(kw[po:po + 64, (4 * qb + j) * 128:(4 * qb + j + 1) * 128], bkw, vn[:, 4 * qb + j, :], bvn, eby[:, j, :], beby) for j in range(8)] + ctxk
                    y_, by_ = ys[yi % 2]; yi += 1
                    attend(qn[po:po + 64, qb * 512:(qb + 1) * 512], bqn, kbs, NA_SCALE, 512, y_[0:64, :], by_)
                    P.dma("pool", YNA[qb, po:po + 64, c * NT:(c + 1) * NT], y_[0:64, :], [by_], [db("YNA")], own=by_)
                if l == 0:
                    y_, by_ = ys[yi % 2]; yi += 1
                    attend(qc[po:po + 64, :], bqc, ctxk, NA_SCALE, CTX, y_[0:64, 0:CTX], by_)
                    P.dma("pool", YNAc[c, po:po + 64, :], y_[0:64, 0:CTX], [by_], [db("YNAc")], own=by_)
            P.barrier()

        def phase2_fourier(l):
            ar.reset(PERSIST)
            AB = pt("AB", [128, 256], BF16)
            GRE = pt("GRE", [N1, 128], BF16); GIM = pt("GIM", [N1, 128], BF16)
            YFS = [pt("YFS%d" % i, [32, N1], BF16) for i in range(2)]
            for g in range(4):
                gre, bgre = GRE; gim, bgim = GIM
                ab, bab = AB
                npc = CH // 128
                for r in range(4):
                    for hf in range(HF):
                        p0 = r * NBL + hf * npc
                        P.dma("sp", ab[p0:p0 + npc].rearrange("p a b -> p (a b)"), FABA[g * HF + hf][r * CH:(r + 1) * CH, :].rearrange("(a n) c -> a (n c)", n=128), [db("FABA")], [bab])
                for j4 in range(32):
                    for half in range(2):
                        bank = half
                        for jj in range(4):
                            j = j4 * 4 + jj
                            Al = ab[0:N1, :, j]; Bl = ab[0:N1, :, 128 + j]
                            o = ps_ap[:, bank, jj * N1:(jj + 1) * N1]
                            if half == 0:
                                P.op("pe", lambda e, o=o, Al=Al: e.matmul(o, lhsT=Al, rhs=c1[0:N1], start=True, stop=False), [bab, b_c1], [psb[bank]], inc=False)
                                P.op("pe", lambda e, o=o, Bl=Bl: e.matmul(o, lhsT=Bl, rhs=ns1[0:N1], start=False, stop=True), [bab, b_ns1], [psb[bank]], inc=(jj == 3))
                            else:
                                P.op("pe", lambda e, o=o, Bl=Bl: e.matmul(o, lhsT=Bl, rhs=c1[0:N1], start=True, stop=False), [bab, b_c1], [psb[bank]], inc=False)
                                P.op("pe", lambda e, o=o, Al=Al: e.matmul(o, lhsT=Al, rhs=s1[0:N1], start=False, stop=True), [bab, b_s1], [psb[bank]], inc=(jj == 3))
                        l0 = j4 * 4
                        G, bG = (gre, bgre) if half == 0 else (gim, bgim)
                        copy_op("act" if half else "dve", G[:, :, l0:l0 + 4], ps_ap[:, bank, 0:4 * N1].rearrange("p (j k) -> p k j", j=4), [psb[bank]], [bG])
                yfs, byfs = YFS[g % 2]
                for k1b in range(N1 // 16):
                    bank = 2 + (k1b % 2)
                    for kk in range(16):
                        k1 = k1b * 16 + kk
                        o = ps_ap[:, bank, kk * 32:(kk + 1) * 32]
                        P.op("pe", lambda e, o=o, k1=k1: e.matmul(o, lhsT=gre[:, k1, :], rhs=mre[:, k1, :], start=True, stop=False), [bgre, b_mre], [psb[bank]], inc=False)
                        P.op("pe", lambda e, o=o, k1=k1: e.matmul(o, lhsT=gim[:, k1, :], rhs=mimn[:, k1, :], start=False, stop=True), [bgim, b_mim], [psb[bank]], inc=(kk == 15))
                    copy_op("act" if k1b % 2 else "dve", yfs[:, :, k1b * 16:(k1b + 1) * 16], ps_ap[:, bank, :].rearrange("p (a b) -> p b a", a=16), [psb[bank]], [byfs])
                P.dma("pool", YF.rearrange("t p (c n) -> p t c n", c=4)[:, :, g, :], yfs.rearrange("p a b -> p (a b)").rearrange("p (t n) -> p t n", n=NT), [byfs], [db("YF")], own=byfs)
            if l == 0:
                fc, bfc = pt("fc", [2, 4, 256], BF16)
                P.dma("sp", fc, FABc.rearrange("(s p) g c -> p s g c", p=128), [db("FABc")], [bfc])
                yc, byc = pt("yc", [4, 256], BF16)
                for g in range(4):
                    steps = [(fc[:, nb, g, 0:128], cc256[:, nb, :]) for nb in range(2)] + [(fc[:, nb, g, 128:256], sc256[:, nb, :]) for nb in range(2)]
                    for i, (lh, rh) in enumerate(steps):
                        P.op("pe", lambda e, lh=lh, rh=rh, i=i: e.matmul(ps_ap[:, 4, 0:256], lhsT=lh, rhs=rh, start=(i == 0), stop=(i == 3)), [bfc, b_cc, b_sc], [psb[4]], inc=(i == 3))
                    copy_op("dve", yc[:, g, :], ps_ap[:, 4, 0:256], [psb[4]], [byc])
                P.dma("pool", YFc.rearrange("g p t -> p g t"), yc, [byc], [db("YFc")], own=byc)
            P.barrier()

        def alloc_p3():
            T3 = {}
            T3["x"] = [pt("x3_%d" % i, [8, NT], F32) for i in range(1)]
            T3["mix"] = pt("mix", [8, NT], F32)
            T3["r"] = pt("r", [8, NT], F32)
            T3["a"] = pt("a3", [8, NT], BF16)
            T3["mixb"] = pt("mixb", [8, NT], BF16)
            T3["hid"] = pt("hid", [22, NT], BF16)
            T3["y"] = [pt("y3_%d" % i, [4, NT], BF16) for i in range(3)]
            T3["st4"] = pt("st4_3", [4, NT], F32)
            T3["sg"] = [pt("sg%d" % i, [NT], BF16) for i in range(2)]
            T3["tmpf"] = [pt("tmpf%d" % i, [NT], F32) for i in range(2)]
            T3["ot"] = [pt("ot%d" % i, [1024], F32) for i in range(2)]
            return T3

        def phase3(l, nt, s, tok0, T3):
            lat = (s == 0)
            xt, xbuf = T3["x"][0]
            xsrc = xview(l % 2, lat, tok0 // NT)
            P.dma("sp", xt[:, :, 0:nt], xsrc, [db("XL%d" % (l % 2) if lat else "XC%d" % (l % 2))], [xbuf])
            A, bA = T3["a"]
            for k in range(8):
                P.op("dve", lambda e, k=k: e.tensor_scalar(out=A[:, k, 0:nt], in0=xt[:, k, 0:nt], scalar1=MOD(l, 8 + k, s, True), scalar2=MOD(l, k, s, False),
                                                           op0=ALU.mult, op1=ALU.add), [xbuf, b_mod_, b_mod1], [bA])
            ysrc = ((YF, "YF"), (YNA, "YNA"), (YMLA, "YMLA")) if lat else ((YFc, "YFc"), (YNAc, "YNAc"), (YMLAc, "YMLAc"))
            for gi in range(3):
                yt, by = T3["y"][gi]
                yd = ysrc[gi][0][tok0 // NT].rearrange("p (k t) -> p k t", k=4) if lat else ysrc[gi][0][:, :, 0:nt].rearrange("k p t -> p k t")
                P.dma("sp", yt[:, :, 0:nt], yd, [db(ysrc[gi][1])], [by])
            mix, bmix = T3["mix"]
            for gi in range(3):
                yt, by = T3["y"][gi]
                for hb in range(2):
                    pg = load_piece((l, "g", gi, hb)); pb_ = load_piece((l, "br", gi, hb))
                    for j in range(4):
                        mc = hb * 4 + j
                        ps, pb = mm_chunk(pg, j * 128, 128, A, bA, nt)
                        sg, bsg = T3["sg"][mc % 2]
                        P.op("act", lambda e, sg=sg, ps=ps: e.activation(out=sg[:, 0:nt], in_=ps, func=AF.Sigmoid), [pb], [bsg])
                        ps2, pb2 = mm_chunk(pb_, j * 128, 128, yt, by, nt)
                        if gi == 0:
                            P.op("dve", lambda e, sg=sg, ps2=ps2, mc=mc: e.tensor_tensor(out=mix[:, mc, 0:nt], in0=ps2, in1=sg[:, 0:nt], op=ALU.mult), [pb2, bsg], [bmix])
                        else:
                            tf, btf = T3["tmpf"][mc % 2]
                            P.op("dve", lambda e, sg=sg, ps2=ps2, tf=tf: e.tensor_tensor(out=tf[:, 0:nt], in0=ps2, in1=sg[:, 0:nt], op=ALU.mult), [pb2, bsg], [btf])
                            P.op("pool", lambda e, tf=tf, mc=mc: e.tensor_tensor(out=mix[:, mc, 0:nt], in0=mix[:, mc, 0:nt], in1=tf[:, 0:nt], op=ALU.add), [btf, bmix], [bmix])
            mixb, bmixb = T3["mixb"]
            for k in range(8):
                copy_op("act" if k % 2 else "dve", mixb[:, k, 0:nt], mix[:, k, 0:nt], [bmix], [bmixb])
            r, rb = T3["r"]

            def cons_o(ci, ps, pb, rows):
                P.op("pool", lambda e: e.tensor_scalar(out=r[:, ci, 0:nt], in0=xt[:, ci, 0:nt], scalar1=ALPHA, scalar2=None, op0=ALU.mult), [xbuf], [rb])
                P.op("dve", lambda e: e.scalar_tensor_tensor(out=r[:, ci, 0:nt], in0=ps, scalar=MOD(l, 16 + ci, s, True), in1=r[:, ci, 0:nt], op0=ALU.mult, op1=ALU.add), [pb, rb, b_mod1], [rb])
            linear_fm([(l, "o", 0), (l, "o", 1)], mixb, bmixb, nt, cons_o)
            st4, bst4 = T3["st4"]
            ln_fm(r, rb, nt, lnp[:, l, 0, :], lnp[:, l, 1, :], mix, bmix, st4, bst4)
            for k in range(8):
                P.op("dve", lambda e, k=k: e.tensor_scalar(out=A[:, k, 0:nt], in0=r[:, k, 0:nt], scalar1=MOD(l, 32 + k, s, True), scalar2=MOD(l, 24 + k, s, False),
                                                           op0=ALU.mult, op1=ALU.add), [rb, b_mod_, b_mod1], [bA])
            hid, bhid = T3["hid"]
            for j6 in range(6):
                pfg = load_piece((l, "fg", j6)); pfu = load_piece((l, "fu", j6))
                for j in range(pfg[3] // 128):
                    mc = j6 * 4 + j
                    ps, pb = mm_chunk(pfg, j * 128, 128, A, bA, nt)
                    sg, bsg = T3["sg"][mc % 2]
                    P.op("act", lambda e, sg=sg, ps=ps: e.activation(out=sg[:, 0:nt], in_=ps, func=AF.Silu), [pb], [bsg])
                    ps2, pb2 = mm_chunk(pfu, j * 128, 128, A, bA, nt)
                    P.op("dve", lambda e, sg=sg, ps2=ps2, mc=mc: e.tensor_tensor(out=hid[:, mc, 0:nt], in0=ps2, in1=sg[:, 0:nt], op=ALU.mult), [pb2, bsg], [bhid])

            def cons_d(ci, ps, pb, rows):
                P.op("pool", lambda e: e.tensor_scalar(out=r[:, ci, 0:nt], in0=r[:, ci, 0:nt], scalar1=ALPHA, scalar2=None, op0=ALU.mult), [rb], [rb])
                P.op("dve", lambda e: e.scalar_tensor_tensor(out=r[:, ci, 0:nt], in0=ps, scalar=MOD(l, 40 + ci, s, True), in1=r[:, ci, 0:nt], op0=ALU.mult, op1=ALU.add), [pb, rb, b_mod1], [rb])
            linear_fm([(l, "fd", j) for j in range(4)], hid, bhid, nt, cons_d)
            ln_fm(r, rb, nt, lnp[:, l, 2, :], lnp[:, l, 3, :], mix, bmix, st4, bst4)
            return r, rb

        def write_out(r, rb, tok0, T3):
            for sub in range(4):
                ot, bot = T3["ot"][sub % 2]
                for k in range(8):
                    bank = 6 + (k // 4)
                    P.op("pe", lambda e, k=k, sub=sub, bank=bank: e.transpose(ps_ap[:, bank, (k % 4) * 128:(k % 4 + 1) * 128], r[:, k, sub * 128:(sub + 1) * 128], ident), [rb, b_ident], [psb[bank]])
                    if k % 4 == 3:
                        copy_op("act" if k == 3 else "dve", ot[:, (k // 4) * 512:(k // 4 + 1) * 512], ps_ap[:, bank, :], [psb[bank]], [bot])
                P.dma("pool", out_d[tok0 + sub * 128:tok0 + (sub + 1) * 128, :], ot, [bot], [db("out")], own=bot)

        def run_phase1_pass(l, from_input):
            ar.reset(PERSIST)
            load_layer_small(l)
            P.barrier()
            ar.reset(PERSIST)
            wslots[:] = [pt("wslot%d" % i, [6144], BF16) for i in range(3)]
            T1 = alloc_p1()
            xts = [pt("xt%d" % i, [8, NT], F32) for i in range(2)]
            if from_input:
                W = {"xin": [pt("xin%d" % i, [1024], F32) for i in range(2)], "s4": pt("s4", [4], F32), "xsq": pt("xsq", [1024], F32)}
            for ti in range(NTL + 1):
                xt, xbuf = xts[ti % 2]
                lat = ti < NTL
                nt = NT if lat else CTX
                if from_input:
                    input_ln(x_in[ti * NT:(ti + 1) * NT] if lat else ctx_in, nt // 128, xt, xbuf, W)
                else:
                    P.dma("sp", xt[:, :, 0:nt], xview(l % 2, lat, ti), [db("XL%d" % (l % 2) if lat else "XC%d" % (l % 2))], [xbuf])
                phase1(l, xt, xbuf, nt, 0 if lat else 1, ti * NT if lat else 0, ti if lat else -1, T1, store_x=from_input)

        class _Stop(Exception):
            pass

        def chk(stage):
            if cfg.stop <= stage:
                raise _Stop()

        try:
            chk(0)
            if 'ONLYP3' in cfg.debug:
                lsel = 1 if 'L1' in cfg.debug else 0
                wslots[:] = [pt("wslot%d" % i, [6144], BF16) for i in range(3)]
                T3 = alloc_p3()
                for ti in range(NTL):
                    r, rb = phase3(lsel, NT, 0, ti * NT, T3)
                    if 'NOOUT' not in cfg.debug:
                        write_out(r, rb, ti * NT, T3)
                raise _Stop()
            run_phase1_pass(0, True)
            chk(1)
            for l in range(L):
                exchange()
                chk(2 + 10 * l)
                phase2_mla(l)
                chk(3 + 10 * l)
                phase2_na(l)
                chk(4 + 10 * l)
                phase2_fourier(l)
                chk(5 + 10 * l)
                ar.reset(PERSIST)
                wslots[:] = [pt("wslot%d" % i, [6144], BF16) for i in range(3)]
                T3 = alloc_p3()
                last = (l + 1 == L)
                for ti in range(NTL + (0 if last else 1)):
                    lat = ti < NTL
                    nt = NT if lat else CTX
                    r, rb = phase3(l, nt, 0 if lat else 1, ti * NT if lat else 0, T3)
                    if last:
                        if 'NOOUT' not in cfg.debug:
                            write_out(r, rb, ti * NT, T3)
                    else:
                        P.dma("pool", xview((l + 1) % 2, lat, ti), r[:, :, 0:nt], [rb], [db("XL%d" % ((l + 1) % 2) if lat else "XC%d" % ((l + 1) % 2))], own=rb)
                chk(6 + 10 * l)
                if not last:
                    P.barrier()
                    run_phase1_pass(l + 1, False)
                chk(7 + 10 * l)
        except _Stop:
            pass
        P.barrier()
        P.emit()
    return nc, dbg


def _consts(cfg, q):
    bf = ml_dtypes.bfloat16
    R, T, N, N1 = cfg.R, cfg.T, cfg.N, cfg.N1
    out = {}
    out["k_ident"] = np.eye(128, dtype=np.float32)
    cl = np.outer(np.arange(128), np.arange(128)).astype(np.float64) * (2 * np.pi / 128)
    out["k_cs128"] = np.concatenate([np.cos(cl), np.sin(cl)], 1).astype(bf)
    a1 = np.outer(np.arange(N1), np.arange(N1)).astype(np.float64) * (2 * np.pi / N1)
    out["k_c1"] = np.cos(a1).astype(bf); out["k_s1"] = np.sin(a1).astype(bf); out["k_ns1"] = (-np.sin(a1)).astype(bf)
    sc = 1.0 / np.sqrt(N * 128.0)
    k1 = np.arange(N1)[None, :, None]; k2 = np.arange(32)[None, None, :]; n2 = np.arange(128)[:, None, None]
    k = T * q + k1 + N1 * k2
    ang = (n2 * k % N).astype(np.float64) * (2 * np.pi / N)
    out["k_mre"] = (np.cos(ang) * sc).reshape(128, N1 * 32).astype(bf)
    out["k_mimn"] = (-np.sin(ang) * sc).reshape(128, N1 * 32).astype(bf)
    ac = np.outer(np.arange(256), np.arange(256)).astype(np.float64) * (2 * np.pi / 256)
    scc = 1.0 / np.sqrt(256 * 128.0)
    out["k_cc256"] = (np.cos(ac) * scc).astype(bf); out["k_scn256"] = (-np.sin(ac) * scc).astype(bf)
    t = np.arange(T) + q * T
    rows = (t // GW).astype(np.float32); cols = (t % GW).astype(np.float32)
    inv = (np.float32(10000.0) ** (-np.arange(8, dtype=np.float32) / np.float32(8))).astype(np.float32)
    angr = np.concatenate([rows[:, None] * inv, cols[:, None] * inv], -1).astype(np.float32)
    co = np.cos(angr).T.astype(np.float32); si = np.sin(angr).T.astype(np.float32)
    out["k_rope"] = np.stack([np.concatenate([co, co], 0), np.concatenate([-si, si], 0)], 0).astype(np.float32)
    NR = 4 * R
    rvt = np.zeros((128, 3, 8, 8), np.float32)
    for ty, r0 in enumerate((R * q, R * q + 8, R * q + R - 8)):
        for p in range(128):
            for kb in range(8):
                krow = r0 - 4 + 2 * kb + p // 64
                for qi in range(8):
                    rs = min(max(r0 + qi - 4, 0), NR - 8)
                    rvt[p, ty, kb, qi] = 1.0 if rs <= krow < rs + 8 else 0.0
    out["k_rv"] = rvt.reshape(128, 192)
    sel = np.zeros((128, 8), np.float32)
    if q > 0:
        sel[:, q - 1] = 1.0
    if q < 3:
        sel[:, 4 + q + 1] = 1.0
    out["k_sel"] = sel
    return out


def _rpb_table(rpb):
    Ln, H = rpb.shape[0], rpb.shape[1]
    tab = np.full((Ln, H, 23, 64, 64), MASKVAL, np.float32)
    qc = np.arange(64)[None, :]; kc = np.arange(64)[:, None]
    cs = np.clip(qc - 8, 0, 48)
    valid = (kc >= cs) & (kc < cs + 16)
    dc = np.clip(kc - qc + 15, 0, 30)
    for e in range(23):
        dr = 11 - e
        if abs(dr) <= 7:
            g = rpb[:, :, dr + 7, :][:, :, dc]
            tab[:, :, e] = np.where(valid[None, None], g, np.float32(MASKVAL))
    return tab


def _pk(v, k):
    v = np.asarray(v, np.float32)
    sh = v.shape[:-1]
    return np.ascontiguousarray(np.swapaxes(v.reshape(sh + (k, 128)), -1, -2))


_CACHE = {}


def run_cfg(inputs, cfg):
    key = (cfg.R, cfg.debug, cfg.stop)
    if key not in _CACHE:
        _CACHE[key] = build_program(cfg)
    nc, dbg = _CACHE[key]
    T = cfg.T
    f32 = lambda a: np.ascontiguousarray(np.asarray(a, np.float32))
    shared = {
        "c_ctx": _pk(inputs["c_ctx"], 8), "ln_in_g": _pk(inputs["ln_in_g"], 8), "ln_in_b": _pk(inputs["ln_in_b"], 8),
        "w_mod": f32(inputs["w_mod"]), "b_mod": _pk(inputs["b_mod"], 48), "w_in": f32(inputs["w_in"]),
        "mla_q_norm_g": _pk(inputs["mla_q_norm_g"], 2), "mla_kv_norm_g": _pk(inputs["mla_kv_norm_g"], 1),
        "w_uq": f32(inputs["w_uq"]), "w_qr": f32(inputs["w_qr"]), "w_uk": f32(inputs["w_uk"]), "w_uv": f32(inputs["w_uv"]),
        "rpbT": _rpb_table(f32(inputs["na_rpb"])),
        "w_branch": f32(inputs["w_branch"]).reshape(L, 1536, D), "w_out": f32(inputs["w_out"]),
        "ln1_g": _pk(inputs["ln1_g"], 8), "ln1_b": _pk(inputs["ln1_b"], 8), "ln2_g": _pk(inputs["ln2_g"], 8), "ln2_b": _pk(inputs["ln2_b"], 8),
        "w_ffn_gate": f32(inputs["w_ffn_gate"]), "w_ffn_up": f32(inputs["w_ffn_up"]), "w_ffn_down": f32(inputs["w_ffn_down"]),
    }
    x = np.asarray(inputs["x"], np.float32); ctx = np.asarray(inputs["ctx"], np.float32); c = np.asarray(inputs["c"], np.float32)
    in_maps = []
    for i in range(8):
        b, q = i // 4, i % 4
        m = dict(shared)
        m["x"] = np.ascontiguousarray(x[b, q * T:(q + 1) * T]); m["ctx"] = np.ascontiguousarray(ctx[b]); m["c"] = _pk(c[b], 8)
        m.update(_consts(cfg, q))
        in_maps.append(m)
    res = run_bass_kernel_spmd(nc, in_maps, core_ids=list(range(8)))
    out = np.empty((2, 4 * T, D), np.float32)
    for i in range(8):
        b, q = i // 4, i % 4
        out[b, q * T:(q + 1) * T] = np.asarray(res.results[i]["out"], np.float32)
    return out, res


def kernel(**inputs):
    out, _ = run_cfg(inputs, Cfg(64))
    return out
```

```python
from contextlib import ExitStack
import ml_dtypes
from concourse.bass_utils import run_bass_kernel_spmd
import numpy as np
import concourse.bass as bass
import concourse.mybir as mybir

F32 = mybir.dt.float32
BF16 = mybir.dt.bfloat16
U8 = mybir.dt.uint8
ALU = mybir.AluOpType
AF = mybir.ActivationFunctionType
AX = mybir.AxisListType


class Buf:
    __slots__ = ("name", "w", "r", "dsem", "dcnt")

    def __init__(self, name):
        self.name = name
        self.w = None
        self.r = {}
        self.dsem = None
        self.dcnt = 0


class Prog:
    ENGS = ("pe", "act", "dve", "pool", "sp")

    def __init__(self, nc, stack):
        self.nc = nc
        self.stack = stack
        self.lists = {e: [] for e in self.ENGS}
        self.sems = {}
        self.cnt = {}
        self.waited = {e: {} for e in self.ENGS}
        for e in self.ENGS:
            self._newsem("eng_" + e)
        self._newsem("coll")
        self.pool_keys = []
        self.pool_idx = 0
        self.attached = []

    def _newsem(self, key):
        self.sems[key] = self.stack.enter_context(self.nc.semaphore(key))
        self.cnt[key] = 0
        return key

    def buf(self, name):
        return Buf(name)

    def _deps(self, reads, writes):
        deps = {}
        def add(t):
            if t is None:
                return
            k, v = t
            if deps.get(k, 0) < v:
                deps[k] = v
        for b in reads:
            add(b.w)
        for b in writes:
            add(b.w)
            for k, v in b.r.items():
                add((k, v))
        return deps

    def _waits(self, eng, deps):
        ws = []
        wd = self.waited[eng]
        for k, v in deps.items():
            if wd.get(k, 0) < v:
                wd[k] = v
                ws.append((k, v))
        return ws

    def _mark(self, tok, reads, writes):
        k, v = tok
        for b in reads:
            if b.r.get(k, 0) < v:
                b.r[k] = v
        for b in writes:
            b.w = tok
            b.r = {}

    def _dsem(self, b):
        if b.dsem is None:
            if self.pool_idx >= len(self.pool_keys):
                self.pool_keys.append(self._newsem("dq%d" % len(self.pool_keys)))
            b.dsem = self.pool_keys[self.pool_idx]
            self.pool_idx += 1
            self.attached.append(b)
        return b.dsem

    def op(self, eng, fn, reads=(), writes=(), inc=True):
        deps = self._deps(reads, writes)
        if eng == "pe":
            deps.pop("eng_pe", None)
        ws = self._waits(eng, deps)
        key = "eng_" + eng
        if inc:
            self.cnt[key] += 1
            tok = (key, self.cnt[key])
        else:
            tok = (key, self.cnt[key] + 1)
        self.lists[eng].append((ws, fn, (key, 1) if inc else None))
        self._mark(tok, reads, writes)
        if inc:
            self.waited[eng][key] = max(self.waited[eng].get(key, 0), 0)
        return tok

    def dma(self, q, out_ap, in_ap, reads, writes, own=None, **kw):
        if own is None:
            own = writes[0]
        key = self._dsem(own)
        ws = self._waits(q, self._deps(reads, writes))
        self.cnt[key] += 16
        tok = (key, self.cnt[key])
        fn = lambda e, o=out_ap, i=in_ap, kw=kw: e.dma_start(out=o, in_=i, **kw)
        self.lists[q].append((ws, fn, (key, 16)))
        self._mark(tok, reads, writes)
        return tok

    def collective(self, kind, in_ap, out_ap, groups, reads, writes):
        key = "coll"
        ws = self._waits("pool", self._deps(reads, writes))
        self.cnt[key] += 1
        tok = (key, self.cnt[key])
        fn = lambda e: e.collective_compute(kind, ALU.bypass, replica_groups=groups,
                                            ins=[in_ap], outs=[out_ap])
        self.lists["pool"].append((ws, fn, (key, 1)))
        self._mark(tok, reads, writes)
        return tok

    def barrier(self):
        for e in self.ENGS:
            ws = self._waits(e, {k: v for k, v in self.cnt.items() if v > 0})
            if ws:
                self.lists[e].append((ws, None, None))
        for b in self.attached:
            b.dsem = None
        self.attached = []
        self.pool_idx = 0

    def emit(self):
        nc = self.nc
        sems = self.sems
        lists = self.lists
        with nc.Block() as block:
            def run(eng_obj, lst):
                for ws, fn, inc in lst:
                    for k, v in ws:
                        eng_obj.wait_ge(sems[k], v)
                    if fn is not None:
                        ins = fn(eng_obj)
                        if inc is not None:
                            ins.then_inc(sems[inc[0]], inc[1])

            @block.tensor
            def _(e):
                run(e, lists["pe"])

            @block.scalar
            def _(e):
                run(e, lists["act"])

            @block.vector
            def _(e):
                run(e, lists["dve"])

            @block.gpsimd
            def _(e):
                run(e, lists["pool"])

            @block.sync
            def _(e):
                run(e, lists["sp"])


class Arena:
    def __init__(self, ap_u8, limit):
        self.ap = ap_u8
        self.limit = limit
        self.off = 0

    def reset(self, off=0):
        self.off = off

    def alloc(self, shape, dt, parts=128):
        esz = 4 if dt == F32 else 2
        n = int(np.prod(shape))
        nbytes = (n * esz + 63) // 64 * 64
        assert self.off + nbytes <= self.limit, ("arena overflow", self.off, nbytes, self.limit)
        a = self.ap[0:parts, self.off:self.off + n * esz].bitcast(dt)
        self.off += nbytes
        if len(shape) == 2:
            a = a.rearrange("p (a b) -> p a b", a=shape[0])
        elif len(shape) == 3:
            a = a.rearrange("p (a b c) -> p a b c", a=shape[0], b=shape[1])
        return a
D = 1024
CTX = 256
GW = 64
FH = 2816
L = 2
EPS = 1e-5
ALPHA = (2 * L) ** 0.25
NA_SCALE = 64 ** -0.5
MLA_SCALE = 96 ** -0.5
MASKVAL = -30000.0
O_F, O_NQ, O_NK, O_NV, O_CQ, O_CKV, O_KR, O_G = 0, 512, 1024, 1536, 2048, 2304, 2432, 2464
IN_DIM = 5536


class Cfg:
    def __init__(self, R=64, debug=(), stop=99):
        self.stop = stop
        self.R = R
        self.T = R * GW
        self.N = 4 * self.T
        self.N1 = self.N // 128
        self.NT = 512
        self.NTL = self.T // 512
        self.NQB = R // 8
        self.debug = tuple(debug)


def build_program(cfg):
    R, T, N, N1, NT, NTL, NQB = cfg.R, cfg.T, cfg.N, cfg.N1, cfg.NT, cfg.NTL, cfg.NQB
    TW = T + 512
    nc = bass.Bass("TRN2", target_bir_lowering=False)

    def din(name, shape, dt=F32):
        return nc.dram_tensor(name, list(shape), dt, kind="ExternalInput").ap()

    dbg = {}

    def dscr(name, shape, dt=BF16):
        if name in cfg.debug and name not in ('NOOUT', 'ONLYP3', 'L1'):
            t = nc.dram_tensor(name, list(shape), dt, kind="ExternalOutput")
            dbg[name] = t
        else:
            t = nc.dram_tensor(name, list(shape), dt)
        return t.ap()

    x_in = din("x", [T, D]); ctx_in = din("ctx", [CTX, D]); c_in = din("c", [128, 8]); cc_in = din("c_ctx", [128, 8])
    lng_in = din("ln_in_g", [128, 8]); lnb_in = din("ln_in_b", [128, 8])
    w_mod = din("w_mod", [L, D, 6 * D]); b_mod = din("b_mod", [L, 128, 48]); w_in = din("w_in", [L, D, IN_DIM])
    gq_in = din("mla_q_norm_g", [L, 128, 2]); gkv_in = din("mla_kv_norm_g", [L, 128, 1])
    w_uq = din("w_uq", [L, 256, 512]); w_qr = din("w_qr", [L, 256, 256]); w_uk = din("w_uk", [L, 128, 512]); w_uv = din("w_uv", [L, 128, 512])
    rpbT = din("rpbT", [L, 8, 23, 64, 64])
    w_br = din("w_branch", [L, 1536, D]); w_out = din("w_out", [L, D, D])
    ln1g = din("ln1_g", [L, 128, 8]); ln1b = din("ln1_b", [L, 128, 8]); ln2g = din("ln2_g", [L, 128, 8]); ln2b = din("ln2_b", [L, 128, 8])
    w_fg = din("w_ffn_gate", [L, D, FH]); w_fu = din("w_ffn_up", [L, D, FH]); w_fd = din("w_ffn_down", [L, FH, D])
    k_id = din("k_ident", [128, 128]); k_cs = din("k_cs128", [128, 256], BF16)
    k_c1 = din("k_c1", [N1, N1], BF16); k_s1 = din("k_s1", [N1, N1], BF16); k_ns1 = din("k_ns1", [N1, N1], BF16)
    k_mre = din("k_mre", [128, N1 * 32], BF16); k_mim = din("k_mimn", [128, N1 * 32], BF16)
    k_cc = din("k_cc256", [256, 256], BF16); k_sc = din("k_scn256", [256, 256], BF16)
    k_rope = din("k_rope", [2, 32, T]); k_rv = din("k_rv", [128, 3 * 64]); k_sel = din("k_sel", [128, 8])
    out_d = nc.dram_tensor("out", [T, D], F32, kind="ExternalOutput").ap()

    WP = {}
    CASTS = []

    def wpiece(key, src2d, r0, kch, c0, pc):
        WP[key] = (dscr("W_%s" % "_".join(str(x) for x in key), [128, kch * pc]), kch, pc)
        CASTS.append((key, src2d, r0, kch, c0, pc))

    for l in range(L):
        for nm, c0, pc in (("f", O_F, 512), ("nq", O_NQ, 512), ("nk", O_NK, 512), ("nv", O_NV, 512), ("cq", O_CQ, 256), ("ckv", O_CKV, 128)):
            wpiece((l, nm), w_in[l], 0, 8, c0, pc)
        for gi in range(3):
            for hb in range(2):
                wpiece((l, "g", gi, hb), w_in[l], 0, 8, O_G + gi * D + hb * 512, 512)
                wpiece((l, "br", gi, hb), w_br[l], gi * 512, 4, hb * 512, 512)
        for hb in range(2):
            wpiece((l, "o", hb), w_out[l], 0, 8, hb * 512, 512)
        for j in range(6):
            pc = 512 if j < 5 else 256
            wpiece((l, "fg", j), w_fg[l], 0, 8, j * 512, pc)
            wpiece((l, "fu", j), w_fu[l], 0, 8, j * 512, pc)
        for j in range(4):
            wpiece((l, "fd", j), w_fd[l], 0, 22, j * 256, 256)
    XL = [dscr("XL%d" % i, [NTL, 128, 8 * NT], F32) for i in range(2)]
    XC = [dscr("XC%d" % i, [128, 8 * CTX], F32) for i in range(2)]
    NAQT = dscr("NAQT", [4, 128, T]); NAKW = dscr("NAKW", [4, 128, TW]); NBL = T // 128; NWB = TW // 128
    NAVW = dscr("NAVW", [8 * 128, NWB * 65])
    HK = dscr("HK", [512, 512]); HKA = dscr("HKA", [4 * 512, 512]); HV = dscr("HV", [512, 512]); HVA = dscr("HVA", [4 * 512, 512])
    MLAQT = dscr("MLAQT", [8, 96, T]); MLAKT = dscr("MLAKT", [768, T]); MLAKTA = [dscr("MLAKTA%d" % h, [4 * 96, T]) for h in range(8)]
    MLAV = dscr("MLAV", [8 * 128, NBL * 65])
    MLAVA = [dscr("MLAVA%d" % h, [4 * 128, NBL * 65]) for h in range(8)]
    HF = 2 if T * 512 > (1 << 20) else 1; CH = T // HF
    FAB = dscr("FAB", [4 * T, 256]); FABA = [dscr("FABA%d" % j, [4 * CH, 256]) for j in range(4 * HF)]
    NAQTc = dscr("NAQTc", [4, 128, CTX]); NAKTc = dscr("NAKTc", [4, 128, CTX]); NAVc = dscr("NAVc", [8 * 128, 2 * 65])
    MLAQTc = dscr("MLAQTc", [8, 96, CTX]); MLAKTc = dscr("MLAKTc", [8, 96, CTX]); MLAVc = dscr("MLAVc", [8 * 128, 2 * 65])
    FABc = dscr("FABc", [CTX, 4, 256])
    YF = dscr("YF", [NTL, 128, 4 * NT]); YNA = dscr("YNA", [NTL, 128, 4 * NT]); YMLA = dscr("YMLA", [NTL, 128, 4 * NT])
    YFc = dscr("YFc", [4, 128, CTX]); YNAc = dscr("YNAc", [4, 128, CTX]); YMLAc = dscr("YMLAc", [4, 128, CTX])
    EBT = [dscr("EBT%d" % l, [8, 64, 23 * 64]) for l in range(L)]

    GROUPS = [[0, 1, 2, 3], [4, 5, 6, 7]]
    ARENA = 196 * 1024

    with ExitStack() as st:
        P = Prog(nc, st)
        ar_ap = st.enter_context(nc.sbuf_tensor("arena", [128, ARENA], U8))
        ps_ap = st.enter_context(nc.psum_tensor("ps", [128, 8, 512], F32))
        ar = Arena(ar_ap, ARENA)
        psb = [P.buf("ps%d" % i) for i in range(8)]
        PS = lambda i: ps_ap[:, i, :]
        dram = {}

        def db(name):
            if name not in dram:
                dram[name] = P.buf(name)
            return dram[name]

        rr = {"lin": 0, "ev": 0, "bank": 0}

        def xview(i, lat, ti):
            if lat:
                return XL[i][ti].rearrange("p (k t) -> p k t", k=8)
            return XC[i].rearrange("p (k t) -> p k t", k=8)

        def evac_eng():
            rr["ev"] ^= 1
            return "act" if rr["ev"] else "dve"

        def copy_op(eng, out, in_, rd, wr):
            if eng == "act":
                P.op("act", lambda e: e.copy(out=out, in_=in_), rd, wr)
            else:
                P.op(eng, lambda e: e.tensor_copy(out=out, in_=in_), rd, wr)

        def pt(name, shape, dt, parts=128):
            return ar.alloc(shape, dt, parts=parts), P.buf(name)

        ident, b_ident = pt("ident", [128], F32)
        ones, b_ones = pt("ones", [128], F32)
        epsb, b_eps = pt("eps", [1], F32)
        cs128, b_cs = pt("cs128", [256], BF16)
        c1, b_c1 = pt("c1", [N1], BF16); s1, b_s1 = pt("s1", [N1], BF16); ns1, b_ns1 = pt("ns1", [N1], BF16)
        mre, b_mre = pt("mre", [N1, 32], BF16); mimn, b_mim = pt("mimn", [N1, 32], BF16)
        cc256, b_cc = pt("cc256", [2, 256], BF16); sc256, b_sc = pt("sc256", [2, 256], BF16)
        rv, b_rv = pt("rv", [3, 8, 8], F32); sel, b_sel = pt("sel", [8], F32)
        lnin, b_lnin = pt("lnin", [2, 8], F32)
        modv, b_mod_ = pt("modv", [L, 48, 2], F32); mod1, b_mod1 = pt("mod1", [L, 48, 2], F32)
        lnp, b_lnp = pt("lnp", [L, 4, 8], F32)
        gq, b_gq = pt("gq", [L, 2], F32); gkv, b_gkv = pt("gkv", [L, 1], F32)
        wkr, b_wkr = pt("wkr", [8, 2, 96], BF16)
        wq, b_wq = pt("wq", [2, 8, 96], BF16); wqs, b_wqs = pt("wqs", [2, 8, 96], BF16)
        wk, b_wk = pt("wk", [8, 64], BF16); wuv, b_wuv = pt("wuv", [512], BF16)
        PERSIST = ar.off

        P.dma("sp", ident, k_id, [], [b_ident])
        P.op("pool", lambda e: e.memset(ones, 1.0), [], [b_ones])
        P.op("pool", lambda e: e.memset(epsb, EPS), [], [b_eps])
        P.dma("sp", cs128, k_cs, [], [b_cs])
        P.dma("sp", c1[0:N1], k_c1, [], [b_c1]); P.dma("sp", s1[0:N1], k_s1, [], [b_s1]); P.dma("sp", ns1[0:N1], k_ns1, [], [b_ns1])
        P.dma("sp", mre, k_mre.rearrange("p (a b) -> p a b", b=32), [], [b_mre])
        P.dma("sp", mimn, k_mim.rearrange("p (a b) -> p a b", b=32), [], [b_mim])
        P.dma("sp", cc256, k_cc.rearrange("(a p) k -> p a k", p=128), [], [b_cc])
        P.dma("sp", sc256, k_sc.rearrange("(a p) k -> p a k", p=128), [], [b_sc])
        P.dma("sp", rv, k_rv.rearrange("p (t a b) -> p t a b", t=3, a=8), [], [b_rv])
        P.dma("sp", sel, k_sel, [], [b_sel])
        P.dma("sp", lnin[:, 0, :], lng_in, [], [b_lnin])
        P.dma("sp", lnin[:, 1, :], lnb_in, [], [b_lnin])
        for l in range(L):
            for i, src in enumerate((ln1g, ln1b, ln2g, ln2b)):
                P.dma("sp", lnp[:, l, i, :], src[l], [], [b_lnp])
            P.dma("sp", gq[:, l, :], gq_in[l], [], [b_gq])
            P.dma("sp", gkv[:, l, :], gkv_in[l], [], [b_gkv])

        b_wcast = P.buf("wcast")
        b_wcast1 = P.buf("wcast1")
        late_casts = []
        for (key, src2d, r0, kch, c0, pc) in CASTS:
            dstp = WP[key][0]
            for k in range(kch):
                args = (dstp[:, k * pc:(k + 1) * pc], src2d[r0 + k * 128:r0 + (k + 1) * 128, c0:c0 + pc])
                if key[0] == 0:
                    P.dma("pool", args[0], args[1], [], [b_wcast])
                else:
                    late_casts.append(args)

        def issue_late_casts(n):
            for _ in range(min(n, len(late_casts))):
                o, i = late_casts.pop(0)
                P.dma("pool", o, i, [], [b_wcast1])

        ar.reset(PERSIST)
        cs_t, b_cst = pt("cs_t", [2, 8], F32)
        bm_t, b_bmt = pt("bm_t", [L, 48], F32)
        P.dma("sp", cs_t[:, 0, :], c_in, [], [b_cst])
        P.dma("sp", cs_t[:, 1, :], cc_in, [], [b_cst])
        P.op("act", lambda e: e.activation(out=cs_t, in_=cs_t, func=AF.Silu), [b_cst], [b_cst])
        for l in range(L):
            P.dma("sp", bm_t[:, l, :], b_mod[l], [], [b_bmt])
        wm = [pt("wm%d" % i, [8, 512], F32) for i in range(2)]
        for l in range(L):
            for j in range(12):
                wt_, wb_ = wm[j % 2]
                P.dma("sp", wt_, w_mod[l][:, j * 512:(j + 1) * 512].rearrange("(k p) m -> p k m", p=128), [], [wb_])
                for cc in range(4):
                    ch = j * 4 + cc
                    bank = ch % 4
                    for k in range(8):
                        P.op("pe", lambda e, wt_=wt_, k=k, cc=cc, bank=bank: e.matmul(
                            ps_ap[:, bank, 0:2], lhsT=wt_[:, k, cc * 128:(cc + 1) * 128], rhs=cs_t[:, :, k],
                            start=(k == 0), stop=(k == 7)), [wb_, b_cst], [psb[bank]], inc=(k == 7))
                    P.op("dve", lambda e, l=l, ch=ch, bank=bank: e.tensor_scalar(
                        out=modv[:, l, ch, :], in0=ps_ap[:, bank, 0:2], scalar1=bm_t[:, l, ch:ch + 1], scalar2=None,
                        op0=ALU.add), [psb[bank], b_bmt], [b_mod_])
        P.op("dve", lambda e: e.tensor_scalar(out=mod1, in0=modv, scalar1=1.0, scalar2=None, op0=ALU.add), [b_mod_], [b_mod1])

        def MOD(l, idx, s, plus1):
            src = mod1 if plus1 else modv
            return src[:, l, idx, s:s + 1]

        eb_t = [pt("ebt%d" % i, [23, 64], F32, parts=64) for i in range(2)]
        eb_o = [pt("ebo%d" % i, [23, 64], BF16, parts=64) for i in range(2)]
        for l in range(L):
            for h in range(8):
                (ti, tb), (oi, ob) = eb_t[h % 2], eb_o[h % 2]
                P.dma("sp", ti, rpbT[l, h].rearrange("e k q -> k e q"), [], [tb])
                P.op("act", lambda e, ti=ti, oi=oi: e.activation(out=oi, in_=ti, func=AF.Exp), [tb], [ob])
                P.dma("pool", EBT[l][h], oi.rearrange("p a b -> p (a b)"), [ob], [db("EBT%d" % l)], own=ob)
        P.barrier()
        ar.reset(PERSIST)

        def load_layer_small(l):
            st_uq, b1 = pt("st_uq", [2, 512], F32); st_qr, b2 = pt("st_qr", [2, 256], F32)
            P.dma("sp", st_uq, w_uq[l].rearrange("(k p) m -> p k m", p=128), [], [b1])
            P.dma("sp", st_qr, w_qr[l].rearrange("(k p) m -> p k m", p=128), [], [b2])
            P.op("pool", lambda e: e.memset(wqs, 0.0), [], [b_wqs])
            P.op("pool", lambda e: e.memset(wkr, 0.0), [], [b_wkr])
            uqv = st_uq.rearrange("p k (h d) -> p k h d", h=8); qrv = st_qr.rearrange("p k (h d) -> p k h d", h=8)
            P.op("dve", lambda e: e.tensor_copy(out=wq[:, :, :, 0:64], in_=uqv), [b1], [b_wq])
            P.op("dve", lambda e: e.tensor_copy(out=wq[:, :, :, 64:96], in_=qrv), [b2], [b_wq])
            P.op("dve", lambda e: e.tensor_copy(out=wqs[:, :, :, 64:80], in_=qrv[:, :, :, 16:32]), [b2], [b_wqs])
            P.op("dve", lambda e: e.tensor_copy(out=wqs[:, :, :, 80:96], in_=qrv[:, :, :, 0:16]), [b2], [b_wqs])
            st_kr, b3 = pt("st_kr", [8, 32], F32)
            P.dma("sp", st_kr, w_in[l][:, O_KR:O_KR + 32].rearrange("(k p) m -> p k m", p=128), [], [b3])
            P.op("dve", lambda e: e.tensor_copy(out=wkr[:, :, 0, 64:96], in_=st_kr), [b3], [b_wkr])
            P.op("dve", lambda e: e.tensor_copy(out=wkr[:, :, 1, 64:80], in_=st_kr[:, :, 16:32]), [b3], [b_wkr])
            P.op("dve", lambda e: e.tensor_copy(out=wkr[:, :, 1, 80:96], in_=st_kr[:, :, 0:16]), [b3], [b_wkr])
            st_uk, b4 = pt("st_uk", [512], F32); st_uv, b5 = pt("st_uv", [512], F32)
            P.dma("sp", st_uk, w_uk[l], [], [b4]); P.dma("sp", st_uv, w_uv[l], [], [b5])
            P.op("dve", lambda e: e.tensor_copy(out=wk, in_=st_uk.rearrange("p (h d) -> p h d", h=8)), [b4], [b_wk])
            P.op("dve", lambda e: e.tensor_copy(out=wuv, in_=st_uv), [b5], [b_wuv])

        wslots = []

        def load_piece(key):
            wd, kch, pc = WP[key]
            slot, sbuf_ = wslots[rr["lin"] % len(wslots)]
            rr["lin"] += 1
            P.dma("sp", slot[:, 0:kch * pc], wd, [b_wcast if key[0] == 0 else b_wcast1], [sbuf_])
            return slot[:, 0:kch * pc].rearrange("p (k m) -> p k m", k=kch), sbuf_, kch, pc

        def mm_chunk(piece, c0, rows, act, abuf, nt):
            sv, sbuf_, kch, pc = piece
            bank = rr["bank"] % 4; rr["bank"] += 1
            for k in range(kch):
                P.op("pe", lambda e, k=k: e.matmul(ps_ap[0:rows, bank, 0:nt], lhsT=sv[:, k, c0:c0 + rows], rhs=act[:, k, 0:nt],
                                                   start=(k == 0), stop=(k == kch - 1)), [sbuf_, abuf], [psb[bank]], inc=(k == kch - 1))
            return ps_ap[0:rows, bank, 0:nt], psb[bank]

        def linear_fm(keys, act, abuf, nt, consume):
            ci = 0
            for key in keys:
                piece = load_piece(key)
                for c0 in range(0, piece[3], 128):
                    rows = min(128, piece[3] - c0)
                    ps, pb = mm_chunk(piece, c0, rows, act, abuf, nt)
                    consume(ci, ps, pb, rows)
                    ci += 1

        def ln_fm(r, rbuf, nt, gap, bap, sq, sqbuf, st4, st4buf):
            for k in range(8):
                if k % 2 == 0:
                    P.op("act", lambda e, k=k: e.activation(out=sq[:, k, 0:nt], in_=r[:, k, 0:nt], func=AF.Square), [rbuf], [sqbuf])
                else:
                    P.op("pool", lambda e, k=k: e.tensor_tensor(out=sq[:, k, 0:nt], in0=r[:, k, 0:nt], in1=r[:, k, 0:nt], op=ALU.mult), [rbuf], [sqbuf])
            for k in range(8):
                P.op("pe", lambda e, k=k: e.matmul(ps_ap[:, 4, 0:nt], lhsT=ones, rhs=r[:, k, 0:nt], start=(k == 0), stop=(k == 7)),
                     [b_ones, rbuf], [psb[4]], inc=(k == 7))
            for k in range(8):
                P.op("pe", lambda e, k=k: e.matmul(ps_ap[:, 5, 0:nt], lhsT=ones, rhs=sq[:, k, 0:nt], start=(k == 0), stop=(k == 7)),
                     [b_ones, sqbuf], [psb[5]], inc=(k == 7))
            mean = st4[:, 0, 0:nt]; msq = st4[:, 1, 0:nt]; rstd = st4[:, 2, 0:nt]
            P.op("dve", lambda e: e.tensor_scalar(out=mean, in0=ps_ap[:, 4, 0:nt], scalar1=1.0 / D, scalar2=None, op0=ALU.mult), [psb[4]], [st4buf])
            P.op("dve", lambda e: e.tensor_tensor(out=msq, in0=mean, in1=mean, op=ALU.mult), [st4buf], [st4buf])
            P.op("dve", lambda e: e.scalar_tensor_tensor(out=rstd, in0=ps_ap[:, 5, 0:nt], scalar=1.0 / D, in1=msq, op0=ALU.mult, op1=ALU.subtract), [psb[5], st4buf], [st4buf])
            P.op("act", lambda e: e.activation(out=rstd, in_=rstd, func=AF.Ln, bias=epsb[:, 0:1], scale=1.0), [st4buf, b_eps], [st4buf])
            P.op("act", lambda e: e.activation(out=rstd, in_=rstd, func=AF.Exp, scale=-0.5), [st4buf], [st4buf])
            for k in range(8):
                rk = r[:, k, 0:nt]
                P.op("pool", lambda e, rk=rk: e.tensor_tensor(out=rk, in0=rk, in1=mean, op=ALU.subtract), [rbuf, st4buf], [rbuf])
                P.op("dve", lambda e, rk=rk: e.tensor_tensor(out=rk, in0=rk, in1=rstd, op=ALU.mult), [rbuf, st4buf], [rbuf])
                P.op("act", lambda e, rk=rk, k=k: e.activation(out=rk, in_=rk, func=AF.Identity, bias=bap[:, k:k + 1], scale=gap[:, k:k + 1]), [rbuf, b_lnp, b_lnin], [rbuf])

        def rms_fm(src, sbuf_, kch, nt, gsc, dst, dbuf, sq, sqbuf, st4, st4buf):
            for k in range(kch):
                P.op("pool", lambda e, k=k: e.tensor_tensor(out=sq[:, k, 0:nt], in0=src[:, k, 0:nt], in1=src[:, k, 0:nt], op=ALU.mult), [sbuf_], [sqbuf])
            for k in range(kch):
                P.op("pe", lambda e, k=k: e.matmul(ps_ap[:, 5, 0:nt], lhsT=ones, rhs=sq[:, k, 0:nt], start=(k == 0), stop=(k == kch - 1)),
                     [b_ones, sqbuf], [psb[5]], inc=(k == kch - 1))
            rstd = st4[:, 3, 0:nt]
            P.op("act", lambda e: e.activation(out=rstd, in_=ps_ap[:, 5, 0:nt], func=AF.Ln, bias=epsb[:, 0:1], scale=1.0 / (kch * 128)), [psb[5], b_eps], [st4buf])
            P.op("act", lambda e: e.activation(out=rstd, in_=rstd, func=AF.Exp, scale=-0.5), [st4buf], [st4buf])
            for k in range(kch):
                P.op("dve", lambda e, k=k: e.scalar_tensor_tensor(out=dst[:, k, 0:nt], in0=src[:, k, 0:nt], scalar=gsc[:, k:k + 1], in1=rstd, op0=ALU.mult, op1=ALU.mult),
                     [sbuf_, st4buf, b_gq, b_gkv], [dbuf])

        att_state = {}
        SBANKS = (0, 1, 2, 4, 5, 6)

        def attend(QT, qbuf, kbs, scale, nq, ytile, ybuf):
            ptl = att_state["ptl"]; rs, rsb = att_state["rs"]; bcs, bcsb = att_state["bc"]
            ob = att_state["obank"]; att_state["obank"] = 3 if ob == 7 else 7
            n = len(kbs)

            def S(i):
                KT, kb_, V, vb_, eb, ebb = kbs[i]
                bank = SBANKS[att_state["si"] % 6]; att_state["si"] += 1
                P.op("pe", lambda e: e.matmul(ps_ap[:, bank, 0:nq], lhsT=KT, rhs=QT, start=True, stop=True), [kb_, qbuf], [psb[bank]])
                pt_, pb_ = ptl[att_state["pi"] % len(ptl)]; att_state["pi"] += 1
                P.op("act", lambda e: e.activation(out=pt_[:, 0:nq], in_=ps_ap[:, bank, 0:nq], func=AF.Exp, scale=scale), [psb[bank]], [pb_])
                if eb is not None:
                    P.op("dve", lambda e: e.tensor_tensor(out=pt_[:, 0:nq], in0=pt_[:, 0:nq], in1=eb, op=ALU.mult), [pb_, ebb], [pb_])
                return pt_, pb_

            def PV(i, pt_, pb_):
                KT, kb_, V, vb_, eb, ebb = kbs[i]
                P.op("pe", lambda e: e.matmul(ps_ap[0:65, ob, 0:nq], lhsT=V, rhs=pt_[:, 0:nq], start=(i == 0), stop=(i == n - 1)), [vb_, pb_], [psb[ob]])

            pend = []
            for i in range(n):
                pend.append((i,) + S(i))
                if len(pend) > 3:
                    PV(*pend.pop(0))
            while pend:
                PV(*pend.pop(0))
            P.op("dve", lambda e: e.reciprocal(out=rs[64:65, 0:nq], in_=ps_ap[64:65, ob, 0:nq]), [psb[ob]], [rsb])
            bb = SBANKS[att_state["si"] % 6]; att_state["si"] += 1
            P.op("pe", lambda e: e.matmul(ps_ap[0:64, bb, 0:nq], lhsT=ones[64:65, 0:64], rhs=rs[64:65, 0:nq], start=True, stop=True), [b_ones, rsb], [psb[bb]])
            P.op("act", lambda e: e.copy(out=bcs[0:64, 0:nq], in_=ps_ap[0:64, bb, 0:nq]), [psb[bb]], [bcsb])
            P.op("dve", lambda e: e.tensor_tensor(out=ytile, in0=ps_ap[0:64, ob, 0:nq], in1=bcs[0:64, 0:nq], op=ALU.mult), [psb[ob], bcsb], [ybuf])

        def phase1(l, xt, xbuf, nt, s, tok0, tidx, T1, store_x=True):
            lat = (s == 0)
            nsub = nt // 128
            A = T1["a"]; bA = T1["ab"]
            for k in range(8):
                P.op("dve", lambda e, k=k: e.tensor_scalar(out=A[:, k, 0:nt], in0=xt[:, k, 0:nt], scalar1=MOD(l, 8 + k, s, True), scalar2=MOD(l, k, s, False),
                                                           op0=ALU.mult, op1=ALU.add), [xbuf, b_mod_, b_mod1], [bA])
            xdst = xview(l % 2, lat, tok0 // NT)
            if store_x:
                P.dma("pool", xdst, xt[:, :, 0:nt], [xbuf], [db("XL%d" % (l % 2) if lat else "XC%d" % (l % 2))], own=xbuf)
            zt_, bz = T1["z"]; fab, bfab = T1["fab"]

            def cons_f(ci, ps, pb, rows):
                g = ci
                copy_op(evac_eng(), zt_[:, 0:nt], ps, [pb], [bz])
                for sub in range(nsub):
                    P.op("pe", lambda e, sub=sub: e.matmul(ps_ap[:, 6, 0:256], lhsT=zt_[:, sub * 128:(sub + 1) * 128], rhs=cs128, start=True, stop=True), [bz, b_cs], [psb[6]])
                    ov = fab[:, sub, g]; iv = ps_ap[:, 6, 0:256]
                    copy_op(evac_eng(), ov, iv, [psb[6]], [bfab])
            linear_fm([(l, "f")], A, bA, nt, cons_f)
            for sub in range(nsub):
                if lat:
                    dst = FAB.rearrange("(g t) c -> t g c", g=4)[tok0 + sub * 128: tok0 + (sub + 1) * 128]
                    P.dma("pool", dst, fab[:, sub], [bfab], [db("FAB")], own=bfab)
                else:
                    P.dma("pool", FABc[sub * 128:(sub + 1) * 128], fab[:, sub], [bfab], [db("FABc")], own=bfab)
            for (wkey, key) in (("nq", "naq"), ("nk", "nak")):
                tl, tb = T1[key]

                def cons(ci, ps, pb, rows, tl=tl, tb=tb):
                    copy_op(evac_eng(), tl[:, ci, 0:nt], ps, [pb], [tb])
                linear_fm([(l, wkey)], A, bA, nt, cons)
                if key == "naq":
                    dst = (NAQT[:, :, tok0:tok0 + nt] if lat else NAQTc[:, :, 0:nt]).rearrange("k p t -> p k t")
                    P.dma("pool", dst, tl[:, :, 0:nt], [tb], [db("NAQT" if lat else "NAQTc")], own=tb)
                else:
                    dst = (NAKW[:, :, 256 + tok0:256 + tok0 + nt] if lat else NAKTc[:, :, 0:nt]).rearrange("k p t -> p k t")
                    P.dma("pool", dst, tl[:, :, 0:nt], [tb], [db("NAKW" if lat else "NAKTc")], own=tb)
                    if lat and tidx == 0:
                        P.dma("pool", HK.rearrange("(k p) t -> p k t", p=128)[:, :, 0:256], tl[:, :, 0:256], [tb], [db("HK")], own=tb)
                    if lat and tidx == NTL - 1:
                        P.dma("pool", HK.rearrange("(k p) t -> p k t", p=128)[:, :, 256:512], tl[:, :, nt - 256:nt], [tb], [db("HK")], own=tb)
            wvv, bwv, _, _ = load_piece((l, "nv"))
            nav, bnav = T1["nav"]
            for sub in range(nsub):
                for k in range(8):
                    P.op("pe", lambda e, k=k, sub=sub: e.matmul(ps_ap[:, 7, :], lhsT=A[:, k, sub * 128:(sub + 1) * 128], rhs=wvv[:, k, :], start=(k == 0), stop=(k == 7)),
                         [bA, bwv], [psb[7]], inc=(k == 7))
                copy_op(evac_eng(), nav[:, :, sub, 0:64], ps_ap[:, 7, :].rearrange("p (h d) -> p h d", h=8), [psb[7]], [bnav])
            vview = lambda Vd: Vd.rearrange("(h p) (b c) -> p h b c", p=128, c=65)
            if lat:
                b0 = 2 + (tok0 // 128)
                P.dma("pool", vview(NAVW)[:, :, b0:b0 + nsub, :], nav[:, :, 0:nsub, :], [bnav], [db("NAVW")], own=bnav)
                if tidx == 0:
                    for s_ in range(2):
                        P.dma("pool", HV[s_ * 128:(s_ + 1) * 128].rearrange("p (h d) -> p h d", h=8), nav[:, :, s_, 0:64], [bnav], [db("HV")], own=bnav)
                if tidx == NTL - 1:
                    for s_ in range(2):
                        P.dma("pool", HV[256 + s_ * 128:256 + (s_ + 1) * 128].rearrange("p (h d) -> p h d", h=8), nav[:, :, nsub - 2 + s_, 0:64], [bnav], [db("HV")], own=bnav)
            else:
                P.dma("pool", vview(NAVc)[:, :, 0:nsub, :], nav[:, :, 0:nsub, :], [bnav], [db("NAVc")], own=bnav)
            cq, bcq = T1["cq"]; ckv, bckv = T1["ckv"]

            def cons_cq(ci, ps, pb, rows):
                copy_op(evac_eng(), cq[:, ci, 0:nt], ps, [pb], [bcq])
            linear_fm([(l, "cq")], A, bA, nt, cons_cq)

            def cons_ckv(ci, ps, pb, rows):
                copy_op(evac_eng(), ckv[:, 0, 0:nt], ps, [pb], [bckv])
            linear_fm([(l, "ckv")], A, bA, nt, cons_ckv)
            sq, bsq = T1["sq"]; st4, bst4 = T1["st4"]
            cqn, bcqn = T1["cqn"]; ckvn, bckvn = T1["ckvn"]
            rms_fm(cq, bcq, 2, nt, gq[:, l, :], cqn, bcqn, sq, bsq, st4, bst4)
            rms_fm(ckv, bckv, 1, nt, gkv[:, l, :], ckvn, bckvn, sq, bsq, st4, bst4)
            rope, brope = T1["rope"]
            if lat:
                P.dma("sp", rope[64:96, :, 0:nt], k_rope[:, :, tok0:tok0 + nt].rearrange("a p t -> p a t"), [], [brope])
            tmp, btmp = T1["tmp"]
            krr, bkrr = T1["krr"]
            for v in range(2 if lat else 1):
                for k in range(8):
                    P.op("pe", lambda e, k=k, v=v: e.matmul(ps_ap[0:96, 6 + v, 0:nt], lhsT=wkr[:, k, v, :], rhs=A[:, k, 0:nt], start=(k == 0), stop=(k == 7)),
                         [b_wkr, bA], [psb[6 + v]], inc=(k == 7))
            if lat:
                P.op("dve", lambda e: e.tensor_tensor(out=tmp[64:96, 0, 0:nt], in0=ps_ap[64:96, 6, 0:nt], in1=rope[64:96, 0, 0:nt], op=ALU.mult), [psb[6], brope], [btmp])
                P.op("dve", lambda e: e.tensor_tensor(out=tmp[64:96, 1, 0:nt], in0=ps_ap[64:96, 7, 0:nt], in1=rope[64:96, 1, 0:nt], op=ALU.mult), [psb[7], brope], [btmp])
                P.op("pool", lambda e: e.tensor_tensor(out=krr[64:96, 0:nt], in0=tmp[64:96, 0, 0:nt], in1=tmp[64:96, 1, 0:nt], op=ALU.add), [btmp], [bkrr])
            else:
                P.op("act", lambda e: e.copy(out=krr[64:96, 0:nt], in_=ps_ap[64:96, 6, 0:nt]), [psb[6]], [bkrr])
            for h in range(8):
                qt_, bq_ = T1["qt"][h % 2]
                for v in range(2 if lat else 1):
                    wsrc = wq if v == 0 else wqs
                    for k in range(2):
                        P.op("pe", lambda e, k=k, v=v, h=h, wsrc=wsrc: e.matmul(ps_ap[0:96, 6 + v, 0:nt], lhsT=wsrc[:, k, h, :], rhs=cqn[:, k, 0:nt], start=(k == 0), stop=(k == 1)),
                             [b_wq, b_wqs, bcqn], [psb[6 + v]], inc=(k == 1))
                if lat:
                    P.op("act", lambda e, qt_=qt_: e.copy(out=qt_[0:64, 0:nt], in_=ps_ap[0:64, 6, 0:nt]), [psb[6]], [bq_])
                    P.op("dve", lambda e: e.tensor_tensor(out=tmp[64:96, 0, 0:nt], in0=ps_ap[64:96, 6, 0:nt], in1=rope[64:96, 0, 0:nt], op=ALU.mult), [psb[6], brope], [btmp])
                    P.op("dve", lambda e: e.tensor_tensor(out=tmp[64:96, 1, 0:nt], in0=ps_ap[64:96, 7, 0:nt], in1=rope[64:96, 1, 0:nt], op=ALU.mult), [psb[7], brope], [btmp])
                    P.op("pool", lambda e, qt_=qt_: e.tensor_tensor(out=qt_[64:96, 0:nt], in0=tmp[64:96, 0, 0:nt], in1=tmp[64:96, 1, 0:nt], op=ALU.add), [btmp], [bq_])
                else:
                    P.op("act", lambda e, qt_=qt_: e.copy(out=qt_[0:96, 0:nt], in_=ps_ap[0:96, 6, 0:nt]), [psb[6]], [bq_])
                qd = MLAQT[h, :, tok0:tok0 + nt] if lat else MLAQTc[h, :, 0:nt]
                P.dma("pool", qd, qt_[0:96, 0:nt], [bq_], [db("MLAQT" if lat else "MLAQTc")], own=bq_)
                kt_, bk_ = T1["kt"][h % 2]
                bank = 4 + (h % 2)
                P.op("pe", lambda e, h=h, bank=bank: e.matmul(ps_ap[0:64, bank, 0:nt], lhsT=wk[:, h, :], rhs=ckvn[:, 0, 0:nt], start=True, stop=True), [b_wk, bckvn], [psb[bank]])
                P.op("dve", lambda e, kt_=kt_, bank=bank: e.tensor_copy(out=kt_[0:64, 0:nt], in_=ps_ap[0:64, bank, 0:nt]), [psb[bank]], [bk_])
                P.op("pool", lambda e, kt_=kt_: e.tensor_copy(out=kt_[64:96, 0:nt], in_=krr[64:96, 0:nt]), [bkrr], [bk_])
                kd = MLAKT[h * 96:(h + 1) * 96, tok0:tok0 + nt] if lat else MLAKTc[h, :, 0:nt]
                P.dma("pool", kd, kt_[0:96, 0:nt], [bk_], [db("MLAKT" if lat else "MLAKTc")], own=bk_)
            mv, bmv = T1["mlav"]
            for sub in range(nsub):
                P.op("pe", lambda e, sub=sub: e.matmul(ps_ap[:, 7, :], lhsT=ckvn[:, 0, sub * 128:(sub + 1) * 128], rhs=wuv, start=True, stop=True), [bckvn, b_wuv], [psb[7]])
                copy_op(evac_eng(), mv[:, :, sub, 0:64], ps_ap[:, 7, :].rearrange("p (h d) -> p h d", h=8), [psb[7]], [bmv])
            if lat:
                b0 = tok0 // 128
                P.dma("pool", vview(MLAV)[:, :, b0:b0 + nsub, :], mv[:, :, 0:nsub, :], [bmv], [db("MLAV")], own=bmv)
            else:
                P.dma("pool", vview(MLAVc)[:, :, 0:nsub, :], mv[:, :, 0:nsub, :], [bmv], [db("MLAVc")], own=bmv)

        def alloc_p1():
            T1 = {}
            T1["a"], T1["ab"] = pt("a", [8, NT], BF16)
            T1["z"] = pt("z", [NT], BF16)
            T1["fab"] = pt("fab", [4, 4, 256], BF16)
            T1["naq"] = pt("naq", [4, NT], BF16); T1["nak"] = pt("nak", [4, NT], BF16)
            T1["nav"] = pt("nav", [8, 4, 65], BF16)
            P.op("pool", lambda e: e.memset(T1["nav"][0], 1.0), [], [T1["nav"][1]])
            T1["cq"] = pt("cq", [2, NT], F32); T1["ckv"] = pt("ckv", [1, NT], F32)
            T1["cqn"] = pt("cqn", [2, NT], BF16); T1["ckvn"] = pt("ckvn", [1, NT], BF16)
            T1["rope"] = pt("rope", [2, NT], F32); T1["tmp"] = pt("tmp", [2, NT], F32)
            T1["krr"] = pt("krr", [NT], BF16)
            T1["qt"] = [pt("qt%d" % i, [NT], BF16) for i in range(2)]
            T1["kt"] = [pt("kt%d" % i, [NT], BF16) for i in range(2)]
            T1["mlav"] = pt("mlav", [8, 4, 65], BF16)
            P.op("pool", lambda e: e.memset(T1["mlav"][0], 1.0), [], [T1["mlav"][1]])
            T1["sq"] = pt("sq1", [2, NT], F32); T1["st4"] = pt("st4_1", [4, NT], F32)
            return T1

        def input_ln(src_rows, nblk, xt, xbuf, W):
            for blk in range(nblk):
                xin, bxin = W["xin"][blk % 2]
                P.dma("sp", xin, src_rows[blk * 128:(blk + 1) * 128, :], [], [bxin])
                s4, bs4 = W["s4"]
                P.op("dve", lambda e, xin=xin: e.reduce_sum(out=s4[:, 0:1], in_=xin, axis=AX.X), [bxin], [bs4])
                P.op("dve", lambda e: e.tensor_scalar(out=s4[:, 1:2], in0=s4[:, 0:1], scalar1=-1.0 / D, scalar2=None, op0=ALU.mult), [bs4], [bs4])
                P.op("act", lambda e, xin=xin: e.activation(out=xin, in_=xin, func=AF.Identity, bias=s4[:, 1:2], scale=1.0), [bxin, bs4], [bxin])
                xsq, bxsq = W["xsq"]
                P.op("pool", lambda e, xin=xin: e.tensor_tensor(out=xsq, in0=xin, in1=xin, op=ALU.mult), [bxin], [bxsq])
                P.op("dve", lambda e: e.reduce_sum(out=s4[:, 2:3], in_=xsq, axis=AX.X), [bxsq], [bs4])
                P.op("act", lambda e: e.activation(out=s4[:, 3:4], in_=s4[:, 2:3], func=AF.Ln, bias=epsb[:, 0:1], scale=1.0 / D), [bs4, b_eps], [bs4])
                P.op("act", lambda e: e.activation(out=s4[:, 3:4], in_=s4[:, 3:4], func=AF.Exp, scale=-0.5), [bs4], [bs4])
                P.op("act", lambda e, xin=xin: e.activation(out=xin, in_=xin, func=AF.Identity, scale=s4[:, 3:4]), [bxin, bs4], [bxin])
                for k in range(8):
                    bank = k % 4
                    P.op("pe", lambda e, xin=xin, k=k, bank=bank: e.transpose(ps_ap[:, bank, 0:128], xin[:, k * 128:(k + 1) * 128], ident), [bxin, b_ident], [psb[bank]])
                    P.op("act" if k % 2 else "dve",
                         (lambda e, k=k, bank=bank, blk=blk: e.activation(out=xt[:, k, blk * 128:(blk + 1) * 128], in_=ps_ap[:, bank, 0:128], func=AF.Identity,
                                                                       bias=lnin[:, 1, k:k + 1], scale=lnin[:, 0, k:k + 1])) if k % 2 else
                         (lambda e, k=k, bank=bank, blk=blk: e.tensor_scalar(out=xt[:, k, blk * 128:(blk + 1) * 128], in0=ps_ap[:, bank, 0:128], scalar1=lnin[:, 0, k:k + 1],
                                                                          scalar2=lnin[:, 1, k:k + 1], op0=ALU.mult, op1=ALU.add)),
                         [psb[bank], b_lnin], [xbuf])

        def exchange():
            P.barrier()
            cl = [(MLAKT[h * 96:(h + 1) * 96, :], MLAKTA[h], "MLAKT", "MLAKTA") for h in range(8)]
            cl += [(MLAV[h * 128:(h + 1) * 128, :], MLAVA[h], "MLAV", "MLAVA") for h in range(8)]
            cl += [(FAB[j * CH:(j + 1) * CH, :], FABA[j], "FAB", "FABA") for j in range(4 * HF)]
            cl += [(HK, HKA, "HK", "HKA"), (HV, HVA, "HV", "HVA")]
            for (src, dst, sn, dn) in cl:
                P.collective("AllGather", src.opt(), dst.opt(), GROUPS, [db(sn)], [db(dn)])
            P.barrier()
            ar.reset(PERSIST)
            hk = [pt("hk%d" % i, [4, 512], BF16) for i in range(2)]
            ho = [pt("ho%d" % i, [2, 256], BF16) for i in range(2)]
            hkv = HKA.rearrange("(r k p) t -> k p r t", r=4, k=4)
            for c in range(4):
                (hi, hb), (oi, ob) = hk[c % 2], ho[c % 2]
                P.dma("sp", hi, hkv[c], [db("HKA")], [hb])
                for half, (t0, so) in enumerate(((256, 0), (0, 4))):
                    P.op("dve", lambda e, hi=hi, oi=oi, half=half, t0=t0, so=so: e.tensor_scalar(out=oi[:, half, :], in0=hi[:, 0, t0:t0 + 256], scalar1=sel[:, so:so + 1], scalar2=None, op0=ALU.mult), [hb, b_sel], [ob])
                    for r in range(1, 4):
                        P.op("dve", lambda e, hi=hi, oi=oi, half=half, t0=t0, so=so, r=r: e.scalar_tensor_tensor(out=oi[:, half, :], in0=hi[:, r, t0:t0 + 256], scalar=sel[:, so + r:so + r + 1], in1=oi[:, half, :], op0=ALU.mult, op1=ALU.add), [hb, b_sel, ob], [ob])
                P.dma("pool", NAKW[c, :, 0:256], oi[:, 0, :], [ob], [db("NAKW")], own=ob)
                P.dma("pool", NAKW[c, :, 256 + T:TW], oi[:, 1, :], [ob], [db("NAKW")], own=ob)
            hv = [pt("hv%d" % i, [4, 512], BF16) for i in range(2)]
            hvo = [pt("hvo%d" % i, [8, 65], BF16) for i in range(2)]
            for (oi, ob) in hvo:
                P.op("pool", lambda e, oi=oi: e.memset(oi, 1.0), [], [ob])
            hvv = HVA.rearrange("(r s p) c -> s p r c", r=4, s=4)
            for sblk in range(4):
                (hi, hb), (oi, ob) = hv[sblk % 2], hvo[sblk % 2]
                srcb = sblk + 2 if sblk < 2 else sblk - 2
                so = 0 if sblk < 2 else 4
                P.dma("sp", hi, hvv[srcb], [db("HVA")], [hb])
                ov = oi[:, :, 0:64]
                hv3 = lambda r, hi=hi: hi[:, r, :].rearrange("p (h d) -> p h d", h=8)
                P.op("dve", lambda e, ov=ov, hv3=hv3, so=so: e.tensor_scalar(out=ov, in0=hv3(0), scalar1=sel[:, so:so + 1], scalar2=None, op0=ALU.mult), [hb, b_sel], [ob])
                for r in range(1, 4):
                    P.op("dve", lambda e, ov=ov, hv3=hv3, so=so, r=r: e.scalar_tensor_tensor(out=ov, in0=hv3(r), scalar=sel[:, so + r:so + r + 1], in1=ov, op0=ALU.mult, op1=ALU.add), [hb, b_sel, ob], [ob])
                wb = sblk if sblk < 2 else 2 + NBL + (sblk - 2)
                P.dma("pool", NAVW.rearrange("(h p) (b c) -> p h b c", p=128, c=65)[:, :, wb, :], oi, [ob], [db("NAVW")], own=ob)
            P.barrier()
            ar.reset(PERSIST)

        def alloc_att():
            att_state.clear()
            att_state.update(si=0, pi=0, obank=7)
            att_state["ptl"] = [pt("p%d" % i, [512], BF16) for i in range(6)]
            att_state["rs"] = pt("rs", [512], F32)
            att_state["bc"] = pt("bcs", [512], F32)

        def phase2_mla(l):
            ar.reset(PERSIST)
            alloc_att()
            NK = CTX + N
            NKB = NK // 128
            KT = [pt("KT%d" % i, [NK], BF16) for i in range(2)]
            VA = [pt("VA%d" % i, [NKB, 65], BF16) for i in range(2)]
            qts = [pt("q%d" % i, [512], BF16) for i in range(3)]
            ys = [pt("y%d" % i, [512], BF16) for i in range(2)]
            qi = 0
            for h in range(8):
                (kt, kb_), (va, vb) = KT[h % 2], VA[h % 2]
                P.dma("sp", kt[0:96, 0:CTX], MLAKTc[h], [db("MLAKTc")], [kb_])
                P.dma("sp", va[:, 0:2, :].rearrange("p a b -> p (a b)"), MLAVc[h * 128:(h + 1) * 128, :], [db("MLAVc")], [vb])
                for r in range(4):
                    P.dma("sp", kt[0:96, CTX + r * T:CTX + (r + 1) * T], MLAKTA[h][r * 96:(r + 1) * 96, :], [db("MLAKTA")], [kb_])
                    P.dma("sp", va[:, 2 + r * NBL:2 + (r + 1) * NBL, :].rearrange("p a b -> p (a b)"), MLAVA[h][r * 128:(r + 1) * 128, :], [db("MLAVA")], [vb])
                kbs_all = [(kt[0:96, j * 128:(j + 1) * 128], kb_, va[:, j, :], vb, None, None) for j in range(NKB)]
                c, po = h // 2, (h % 2) * 64
                for qb in range(NTL):
                    qt_, bq_ = qts[qi % 3]; y_, by_ = ys[qi % 2]; qi += 1
                    P.dma("sp", qt_[0:96, :], MLAQT[h, :, qb * 512:(qb + 1) * 512], [db("MLAQT")], [bq_])
                    attend(qt_[0:96, :], bq_, kbs_all, MLA_SCALE, 512, y_[0:64, :], by_)
                    P.dma("pool", YMLA[qb, po:po + 64, c * NT:(c + 1) * NT], y_[0:64, :], [by_], [db("YMLA")], own=by_)
                    issue_late_casts(8)
                if l == 0:
                    qt_, bq_ = qts[qi % 3]; y_, by_ = ys[qi % 2]; qi += 1
                    P.dma("sp", qt_[0:96, 0:CTX], MLAQTc[h], [db("MLAQTc")], [bq_])
                    attend(qt_[0:96, 0:CTX], bq_, kbs_all[0:2], MLA_SCALE, CTX, y_[0:64, 0:CTX], by_)
                    P.dma("pool", YMLAc[c, po:po + 64, :], y_[0:64, 0:CTX], [by_], [db("YMLAc")], own=by_)
            issue_late_casts(10 ** 6)
            P.barrier()

        def phase2_na(l):
            ar.reset(PERSIST)
            alloc_att()
            KW = [pt("KW%d" % i, [TW], BF16) for i in range(2)]
            QN = [pt("QN%d" % i, [T], BF16) for i in range(2)]
            KC = [pt("KC%d" % i, [CTX], BF16) for i in range(2)]
            QC = [pt("QC%d" % i, [CTX], BF16) for i in range(2)]
            VN = [pt("VN%d" % i, [NWB, 65], BF16) for i in range(2)]
            VC = [pt("VC%d" % i, [2, 65], BF16) for i in range(2)]
            EBF = pt("EBF", [8, 8, 64], BF16)
            EBY = [pt("EBY%d" % i, [8, 512], BF16) for i in range(3)]
            ys = [pt("y%d" % i, [512], BF16) for i in range(2)]
            yi = 0
            for h in range(8):
                c, po = h // 2, (h % 2) * 64
                (kw, bkw), (qn, bqn), (kc, bkc), (qc, bqc) = KW[c % 2], QN[c % 2], KC[c % 2], QC[c % 2]
                (vn, bvn), (vc, bvc) = VN[h % 2], VC[h % 2]
                if h % 2 == 0:
                    P.dma("sp", kw, NAKW[c], [db("NAKW")], [bkw])
                    P.dma("sp", qn, NAQT[c], [db("NAQT")], [bqn])
                    P.dma("sp", kc, NAKTc[c], [db("NAKTc")], [bkc])
                    if l == 0:
                        P.dma("sp", qc, NAQTc[c], [db("NAQTc")], [bqc])
                P.dma("sp", vn.rearrange("p a b -> p (a b)"), NAVW[h * 128:(h + 1) * 128, :], [db("NAVW")], [bvn])
                P.dma("sp", vc.rearrange("p a b -> p (a b)"), NAVc[h * 128:(h + 1) * 128, :], [db("NAVc")], [bvc])
                ebf, bebf = EBF
                for kb in range(8):
                    for krl in range(2):
                        e0 = 15 - 2 * kb - krl
                        P.dma("sp", ebf[krl * 64:(krl + 1) * 64, kb, :, :].rearrange("p a b -> p (a b)"), EBT[l][h][:, e0 * 64:(e0 + 8) * 64], [db("EBT%d" % l)], [bebf])
                for ty in range(3):
                    eby, beby = EBY[ty]
                    P.op("dve", lambda e, eby=eby, ty=ty: e.tensor_tensor(out=eby.rearrange("p a (b c) -> p a b c", b=8), in0=ebf,
                                                                        in1=rv[:, ty].unsqueeze(3).broadcast_to([128, 8, 8, 64]), op=ALU.mult), [bebf, b_rv], [beby])
                ctxk = [(kc[po:po + 64, j * 128:(j + 1) * 128], bkc, vc[:, j, :], bvc, None, None) for j in range(2)]
                for qb in range(NQB):
                    ty = 0 if qb == 0 else (2 if qb == NQB - 1 else 1)
                    eby, beby = EBY[ty]
                    kbs = [(kw[po:po + 64, (4 * qb + j) * 128:(4 * qb + j + 1) * 128], bkw, vn[:, 4 * qb + j, :], bvn, eby[:, j, :], beby) for j in range(8)] + ctxk
                    y_, by_ = ys[yi % 2]; yi += 1
                    attend(qn[po:po + 64, qb * 512:(qb + 1) * 512], bqn, kbs, NA_SCALE, 512, y_[0:64, :], by_)
                    P.dma("pool", YNA[qb, po:po + 64, c * NT:(c + 1) * NT], y_[0:64, :], [by_], [db("YNA")], own=by_)
                if l == 0:
                    y_, by_ = ys[yi % 2]; yi += 1
                    attend(qc[po:po + 64, :], bqc, ctxk, NA_SCALE, CTX, y_[0:64, 0:CTX], by_)
                    P.dma("pool", YNAc[c, po:po + 64, :], y_[0:64, 0:CTX], [by_], [db("YNAc")], own=by_)
            P.barrier()

        def phase2_fourier(l):
            ar.reset(PERSIST)
            AB = pt("AB", [128, 256], BF16)
            GRE = pt("GRE", [N1, 128], BF16); GIM = pt("GIM", [N1, 128], BF16)
            YFS = [pt("YFS%d" % i, [32, N1], BF16) for i in range(2)]
            for g in range(4):
                gre, bgre = GRE; gim, bgim = GIM
                ab, bab = AB
                npc = CH // 128
                for r in range(4):
                    for hf in range(HF):
                        p0 = r * NBL + hf * npc
                        P.dma("sp", ab[p0:p0 + npc].rearrange("p a b -> p (a b)"), FABA[g * HF + hf][r * CH:(r + 1) * CH, :].rearrange("(a n) c -> a (n c)", n=128), [db("FABA")], [bab])
                for j4 in range(32):
                    for half in range(2):
                        bank = half
                        for jj in range(4):
                            j = j4 * 4 + jj
                            Al = ab[0:N1, :, j]; Bl = ab[0:N1, :, 128 + j]
                            o = ps_ap[:, bank, jj * N1:(jj + 1) * N1]
                            if half == 0:
                                P.op("pe", lambda e, o=o, Al=Al: e.matmul(o, lhsT=Al, rhs=c1[0:N1], start=True, stop=False), [bab, b_c1], [psb[bank]], inc=False)
                                P.op("pe", lambda e, o=o, Bl=Bl: e.matmul(o, lhsT=Bl, rhs=ns1[0:N1], start=False, stop=True), [bab, b_ns1], [psb[bank]], inc=(jj == 3))
                            else:
                                P.op("pe", lambda e, o=o, Bl=Bl: e.matmul(o, lhsT=Bl, rhs=c1[0:N1], start=True, stop=False), [bab, b_c1], [psb[bank]], inc=False)
                                P.op("pe", lambda e, o=o, Al=Al: e.matmul(o, lhsT=Al, rhs=s1[0:N1], start=False, stop=True), [bab, b_s1], [psb[bank]], inc=(jj == 3))
                        l0 = j4 * 4
                        G, bG = (gre, bgre) if half == 0 else (gim, bgim)
                        copy_op("act" if half else "dve", G[:, :, l0:l0 + 4], ps_ap[:, bank, 0:4 * N1].rearrange("p (j k) -> p k j", j=4), [psb[bank]], [bG])
                yfs, byfs = YFS[g % 2]
                for k1b in range(N1 // 16):
                    bank = 2 + (k1b % 2)
                    for kk in range(16):
                        k1 = k1b * 16 + kk
                        o = ps_ap[:, bank, kk * 32:(kk + 1) * 32]
                        P.op("pe", lambda e, o=o, k1=k1: e.matmul(o, lhsT=gre[:, k1, :], rhs=mre[:, k1, :], start=True, stop=False), [bgre, b_mre], [psb[bank]], inc=False)
                        P.op("pe", lambda e, o=o, k1=k1: e.matmul(o, lhsT=gim[:, k1, :], rhs=mimn[:, k1, :], start=False, stop=True), [bgim, b_mim], [psb[bank]], inc=(kk == 15))
                    copy_op("act" if k1b % 2 else "dve", yfs[:, :, k1b * 16:(k1b + 1) * 16], ps_ap[:, bank, :].rearrange("p (a b) -> p b a", a=16), [psb[bank]], [byfs])
                P.dma("pool", YF.rearrange("t p (c n) -> p t c n", c=4)[:, :, g, :], yfs.rearrange("p a b -> p (a b)").rearrange("p (t n) -> p t n", n=NT), [byfs], [db("YF")], own=byfs)
            if l == 0:
                fc, bfc = pt("fc", [2, 4, 256], BF16)
                P.dma("sp", fc, FABc.rearrange("(s p) g c -> p s g c", p=128), [db("FABc")], [bfc])
                yc, byc = pt("yc", [4, 256], BF16)
                for g in range(4):
                    steps = [(fc[:, nb, g, 0:128], cc256[:, nb, :]) for nb in range(2)] + [(fc[:, nb, g, 128:256], sc256[:, nb, :]) for nb in range(2)]
                    for i, (lh, rh) in enumerate(steps):
                        P.op("pe", lambda e, lh=lh, rh=rh, i=i: e.matmul(ps_ap[:, 4, 0:256], lhsT=lh, rhs=rh, start=(i == 0), stop=(i == 3)), [bfc, b_cc, b_sc], [psb[4]], inc=(i == 3))
                    copy_op("dve", yc[:, g, :], ps_ap[:, 4, 0:256], [psb[4]], [byc])
                P.dma("pool", YFc.rearrange("g p t -> p g t"), yc, [byc], [db("YFc")], own=byc)
            P.barrier()

        def alloc_p3():
            T3 = {}
            T3["x"] = [pt("x3_%d" % i, [8, NT], F32) for i in range(1)]
            T3["mix"] = pt("mix", [8, NT], F32)
            T3["r"] = pt("r", [8, NT], F32)
            T3["a"] = pt("a3", [8, NT], BF16)
            T3["mixb"] = pt("mixb", [8, NT], BF16)
            T3["hid"] = pt("hid", [22, NT], BF16)
            T3["y"] = [pt("y3_%d" % i, [4, NT], BF16) for i in range(3)]
            T3["st4"] = pt("st4_3", [4, NT], F32)
            T3["sg"] = [pt("sg%d" % i, [NT], BF16) for i in range(2)]
            T3["tmpf"] = [pt("tmpf%d" % i, [NT], F32) for i in range(2)]
            T3["ot"] = [pt("ot%d" % i, [1024], F32) for i in range(2)]
            return T3

        def phase3(l, nt, s, tok0, T3):
            lat = (s == 0)
            xt, xbuf = T3["x"][0]
            xsrc = xview(l % 2, lat, tok0 // NT)
            P.dma("sp", xt[:, :, 0:nt], xsrc, [db("XL%d" % (l % 2) if lat else "XC%d" % (l % 2))], [xbuf])
            A, bA = T3["a"]
            for k in range(8):
                P.op("dve", lambda e, k=k: e.tensor_scalar(out=A[:, k, 0:nt], in0=xt[:, k, 0:nt], scalar1=MOD(l, 8 + k, s, True), scalar2=MOD(l, k, s, False),
                                                           op0=ALU.mult, op1=ALU.add), [xbuf, b_mod_, b_mod1], [bA])
            ysrc = ((YF, "YF"), (YNA, "YNA"), (YMLA, "YMLA")) if lat else ((YFc, "YFc"), (YNAc, "YNAc"), (YMLAc, "YMLAc"))
            for gi in range(3):
                yt, by = T3["y"][gi]
                yd = ysrc[gi][0][tok0 // NT].rearrange("p (k t) -> p k t", k=4) if lat else ysrc[gi][0][:, :, 0:nt].rearrange("k p t -> p k t")
                P.dma("sp", yt[:, :, 0:nt], yd, [db(ysrc[gi][1])], [by])
            mix, bmix = T3["mix"]
            for gi in range(3):
                yt, by = T3["y"][gi]
                for hb in range(2):
                    pg = load_piece((l, "g", gi, hb)); pb_ = load_piece((l, "br", gi, hb))
                    for j in range(4):
                        mc = hb * 4 + j
                        ps, pb = mm_chunk(pg, j * 128, 128, A, bA, nt)
                        sg, bsg = T3["sg"][mc % 2]
                        P.op("act", lambda e, sg=sg, ps=ps: e.activation(out=sg[:, 0:nt], in_=ps, func=AF.Sigmoid), [pb], [bsg])
                        ps2, pb2 = mm_chunk(pb_, j * 128, 128, yt, by, nt)
                        if gi == 0:
                            P.op("dve", lambda e, sg=sg, ps2=ps2, mc=mc: e.tensor_tensor(out=mix[:, mc, 0:nt], in0=ps2, in1=sg[:, 0:nt], op=ALU.mult), [pb2, bsg], [bmix])
                        else:
                            tf, btf = T3["tmpf"][mc % 2]
                            P.op("dve", lambda e, sg=sg, ps2=ps2, tf=tf: e.tensor_tensor(out=tf[:, 0:nt], in0=ps2, in1=sg[:, 0:nt], op=ALU.mult), [pb2, bsg], [btf])
                            P.op("pool", lambda e, tf=tf, mc=mc: e.tensor_tensor(out=mix[:, mc, 0:nt], in0=mix[:, mc, 0:nt], in1=tf[:, 0:nt], op=ALU.add), [btf, bmix], [bmix])
            mixb, bmixb = T3["mixb"]
            for k in range(8):
                copy_op("act" if k % 2 else "dve", mixb[:, k, 0:nt], mix[:, k, 0:nt], [bmix], [bmixb])
            r, rb = T3["r"]

            def cons_o(ci, ps, pb, rows):
                P.op("pool", lambda e: e.tensor_scalar(out=r[:, ci, 0:nt], in0=xt[:, ci, 0:nt], scalar1=ALPHA, scalar2=None, op0=ALU.mult), [xbuf], [rb])
                P.op("dve", lambda e: e.scalar_tensor_tensor(out=r[:, ci, 0:nt], in0=ps, scalar=MOD(l, 16 + ci, s, True), in1=r[:, ci, 0:nt], op0=ALU.mult, op1=ALU.add), [pb, rb, b_mod1], [rb])
            linear_fm([(l, "o", 0), (l, "o", 1)], mixb, bmixb, nt, cons_o)
            st4, bst4 = T3["st4"]
            ln_fm(r, rb, nt, lnp[:, l, 0, :], lnp[:, l, 1, :], mix, bmix, st4, bst4)
            for k in range(8):
                P.op("dve", lambda e, k=k: e.tensor_scalar(out=A[:, k, 0:nt], in0=r[:, k, 0:nt], scalar1=MOD(l, 32 + k, s, True), scalar2=MOD(l, 24 + k, s, False),
                                                           op0=ALU.mult, op1=ALU.add), [rb, b_mod_, b_mod1], [bA])
            hid, bhid = T3["hid"]
            for j6 in range(6):
                pfg = load_piece((l, "fg", j6)); pfu = load_piece((l, "fu", j6))
                for j in range(pfg[3] // 128):
                    mc = j6 * 4 + j
                    ps, pb = mm_chunk(pfg, j * 128, 128, A, bA, nt)
                    sg, bsg = T3["sg"][mc % 2]
                    P.op("act", lambda e, sg=sg, ps=ps: e.activation(out=sg[:, 0:nt], in_=ps, func=AF.Silu), [pb], [bsg])
                    ps2, pb2 = mm_chunk(pfu, j * 128, 128, A, bA, nt)
                    P.op("dve", lambda e, sg=sg, ps2=ps2, mc=mc: e.tensor_tensor(out=hid[:, mc, 0:nt], in0=ps2, in1=sg[:, 0:nt], op=ALU.mult), [pb2, bsg], [bhid])

            def cons_d(ci, ps, pb, rows):
                P.op("pool", lambda e: e.tensor_scalar(out=r[:, ci, 0:nt], in0=r[:, ci, 0:nt], scalar1=ALPHA, scalar2=None, op0=ALU.mult), [rb], [rb])
                P.op("dve", lambda e: e.scalar_tensor_tensor(out=r[:, ci, 0:nt], in0=ps, scalar=MOD(l, 40 + ci, s, True), in1=r[:, ci, 0:nt], op0=ALU.mult, op1=ALU.add), [pb, rb, b_mod1], [rb])
            linear_fm([(l, "fd", j) for j in range(4)], hid, bhid, nt, cons_d)
            ln_fm(r, rb, nt, lnp[:, l, 2, :], lnp[:, l, 3, :], mix, bmix, st4, bst4)
            return r, rb

        def write_out(r, rb, tok0, T3):
            for sub in range(4):
                ot, bot = T3["ot"][sub % 2]
                for k in range(8):
                    bank = 6 + (k // 4)
                    P.op("pe", lambda e, k=k, sub=sub, bank=bank: e.transpose(ps_ap[:, bank, (k % 4) * 128:(k % 4 + 1) * 128], r[:, k, sub * 128:(sub + 1) * 128], ident), [rb, b_ident], [psb[bank]])
                    if k % 4 == 3:
                        copy_op("act" if k == 3 else "dve", ot[:, (k // 4) * 512:(k // 4 + 1) * 512], ps_ap[:, bank, :], [psb[bank]], [bot])
                P.dma("pool", out_d[tok0 + sub * 128:tok0 + (sub + 1) * 128, :], ot, [bot], [db("out")], own=bot)

        def run_phase1_pass(l, from_input):
            ar.reset(PERSIST)
            load_layer_small(l)
            P.barrier()
            ar.reset(PERSIST)
            wslots[:] = [pt("wslot%d" % i, [6144], BF16) for i in range(3)]
            T1 = alloc_p1()
            xts = [pt("xt%d" % i, [8, NT], F32) for i in range(2)]
            if from_input:
                W = {"xin": [pt("xin%d" % i, [1024], F32) for i in range(2)], "s4": pt("s4", [4], F32), "xsq": pt("xsq", [1024], F32)}
            for ti in range(NTL + 1):
                xt, xbuf = xts[ti % 2]
                lat = ti < NTL
                nt = NT if lat else CTX
                if from_input:
                    input_ln(x_in[ti * NT:(ti + 1) * NT] if lat else ctx_in, nt // 128, xt, xbuf, W)
                else:
                    P.dma("sp", xt[:, :, 0:nt], xview(l % 2, lat, ti), [db("XL%d" % (l % 2) if lat else "XC%d" % (l % 2))], [xbuf])
                phase1(l, xt, xbuf, nt, 0 if lat else 1, ti * NT if lat else 0, ti if lat else -1, T1, store_x=from_input)

        class _Stop(Exception):
            pass

        def chk(stage):
            if cfg.stop <= stage:
                raise _Stop()

        try:
            chk(0)
            if 'ONLYP3' in cfg.debug:
                lsel = 1 if 'L1' in cfg.debug else 0
                wslots[:] = [pt("wslot%d" % i, [6144], BF16) for i in range(3)]
                T3 = alloc_p3()
                for ti in range(NTL):
                    r, rb = phase3(lsel, NT, 0, ti * NT, T3)
                    if 'NOOUT' not in cfg.debug:
                        write_out(r, rb, ti * NT, T3)
                raise _Stop()
            run_phase1_pass(0, True)
            chk(1)
            for l in range(L):
                exchange()
                chk(2 + 10 * l)
                phase2_mla(l)
                chk(3 + 10 * l)
                phase2_na(l)
                chk(4 + 10 * l)
                phase2_fourier(l)
                chk(5 + 10 * l)
                ar.reset(PERSIST)
                wslots[:] = [pt("wslot%d" % i, [6144], BF16) for i in range(3)]
                T3 = alloc_p3()
                last = (l + 1 == L)
                for ti in range(NTL + (0 if last else 1)):
                    lat = ti < NTL
                    nt = NT if lat else CTX
                    r, rb = phase3(l, nt, 0 if lat else 1, ti * NT if lat else 0, T3)
                    if last:
                        if 'NOOUT' not in cfg.debug:
                            write_out(r, rb, ti * NT, T3)
                    else:
                        P.dma("pool", xview((l + 1) % 2, lat, ti), r[:, :, 0:nt], [rb], [db("XL%d" % ((l + 1) % 2) if lat else "XC%d" % ((l + 1) % 2))], own=rb)
                chk(6 + 10 * l)
                if not last:
                    P.barrier()
                    run_phase1_pass(l + 1, False)
                chk(7 + 10 * l)
        except _Stop:
            pass
        P.barrier()
        P.emit()
    return nc, dbg


def _consts(cfg, q):
    bf = ml_dtypes.bfloat16
    R, T, N, N1 = cfg.R, cfg.T, cfg.N, cfg.N1
    out = {}
    out["k_ident"] = np.eye(128, dtype=np.float32)
    cl = np.outer(np.arange(128), np.arange(128)).astype(np.float64) * (2 * np.pi / 128)
    out["k_cs128"] = np.concatenate([np.cos(cl), np.sin(cl)], 1).astype(bf)
    a1 = np.outer(np.arange(N1), np.arange(N1)).astype(np.float64) * (2 * np.pi / N1)
    out["k_c1"] = np.cos(a1).astype(bf); out["k_s1"] = np.sin(a1).astype(bf); out["k_ns1"] = (-np.sin(a1)).astype(bf)
    sc = 1.0 / np.sqrt(N * 128.0)
    k1 = np.arange(N1)[None, :, None]; k2 = np.arange(32)[None, None, :]; n2 = np.arange(128)[:, None, None]
    k = T * q + k1 + N1 * k2
    ang = (n2 * k % N).astype(np.float64) * (2 * np.pi / N)
    out["k_mre"] = (np.cos(ang) * sc).reshape(128, N1 * 32).astype(bf)
    out["k_mimn"] = (-np.sin(ang) * sc).reshape(128, N1 * 32).astype(bf)
    ac = np.outer(np.arange(256), np.arange(256)).astype(np.float64) * (2 * np.pi / 256)
    scc = 1.0 / np.sqrt(256 * 128.0)
    out["k_cc256"] = (np.cos(ac) * scc).astype(bf); out["k_scn256"] = (-np.sin(ac) * scc).astype(bf)
    t = np.arange(T) + q * T
    rows = (t // GW).astype(np.float32); cols = (t % GW).astype(np.float32)
    inv = (np.float32(10000.0) ** (-np.arange(8, dtype=np.float32) / np.float32(8))).astype(np.float32)
    angr = np.concatenate([rows[:, None] * inv, cols[:, None] * inv], -1).astype(np.float32)
    co = np.cos(angr).T.astype(np.float32); si = np.sin(angr).T.astype(np.float32)
    out["k_rope"] = np.stack([np.concatenate([co, co], 0), np.concatenate([-si, si], 0)], 0).astype(np.float32)
    NR = 4 * R
    rvt = np.zeros((128, 3, 8, 8), np.float32)
    for ty, r0 in enumerate((R * q, R * q + 8, R * q + R - 8)):
        for p in range(128):
            for kb in range(8):
                krow = r0 - 4 + 2 * kb + p // 64
                for qi in range(8):
                    rs = min(max(r0 + qi - 4, 0), NR - 8)
                    rvt[p, ty, kb, qi] = 1.0 if rs <= krow < rs + 8 else 0.0
    out["k_rv"] = rvt.reshape(128, 192)
    sel = np.zeros((128, 8), np.float32)
    if q > 0:
        sel[:, q - 1] = 1.0
    if q < 3:
        sel[:, 4 + q + 1] = 1.0
    out["k_sel"] = sel
    return out


def _rpb_table(rpb):
    Ln, H = rpb.shape[0], rpb.shape[1]
    tab = np.full((Ln, H, 23, 64, 64), MASKVAL, np.float32)
    qc = np.arange(64)[None, :]; kc = np.arange(64)[:, None]
    cs = np.clip(qc - 8, 0, 48)
    valid = (kc >= cs) & (kc < cs + 16)
    dc = np.clip(kc - qc + 15, 0, 30)
    for e in range(23):
        dr = 11 - e
        if abs(dr) <= 7:
            g = rpb[:, :, dr + 7, :][:, :, dc]
            tab[:, :, e] = np.where(valid[None, None], g, np.float32(MASKVAL))
    return tab


def _pk(v, k):
    v = np.asarray(v, np.float32)
    sh = v.shape[:-1]
    return np.ascontiguousarray(np.swapaxes(v.reshape(sh + (k, 128)), -1, -2))


_CACHE = {}


def run_cfg(inputs, cfg):
    key = (cfg.R, cfg.debug, cfg.stop)
    if key not in _CACHE:
        _CACHE[key] = build_program(cfg)
    nc, dbg = _CACHE[key]
    T = cfg.T
    f32 = lambda a: np.ascontiguousarray(np.asarray(a, np.float32))
    shared = {
        "c_ctx": _pk(inputs["c_ctx"], 8), "ln_in_g": _pk(inputs["ln_in_g"], 8), "ln_in_b": _pk(inputs["ln_in_b"], 8),
        "w_mod": f32(inputs["w_mod"]), "b_mod": _pk(inputs["b_mod"], 48), "w_in": f32(inputs["w_in"]),
        "mla_q_norm_g": _pk(inputs["mla_q_norm_g"], 2), "mla_kv_norm_g": _pk(inputs["mla_kv_norm_g"], 1),
        "w_uq": f32(inputs["w_uq"]), "w_qr": f32(inputs["w_qr"]), "w_uk": f32(inputs["w_uk"]), "w_uv": f32(inputs["w_uv"]),
        "rpbT": _rpb_table(f32(inputs["na_rpb"])),
        "w_branch": f32(inputs["w_branch"]).reshape(L, 1536, D), "w_out": f32(inputs["w_out"]),
        "ln1_g": _pk(inputs["ln1_g"], 8), "ln1_b": _pk(inputs["ln1_b"], 8), "ln2_g": _pk(inputs["ln2_g"], 8), "ln2_b": _pk(inputs["ln2_b"], 8),
        "w_ffn_gate": f32(inputs["w_ffn_gate"]), "w_ffn_up": f32(inputs["w_ffn_up"]), "w_ffn_down": f32(inputs["w_ffn_down"]),
    }
    x = np.asarray(inputs["x"], np.float32); ctx = np.asarray(inputs["ctx"], np.float32); c = np.asarray(inputs["c"], np.float32)
    in_maps = []
    for i in range(8):
        b, q = i // 4, i % 4
        m = dict(shared)
        m["x"] = np.ascontiguousarray(x[b, q * T:(q + 1) * T]); m["ctx"] = np.ascontiguousarray(ctx[b]); m["c"] = _pk(c[b], 8)
        m.update(_consts(cfg, q))
        in_maps.append(m)
    res = run_bass_kernel_spmd(nc, in_maps, core_ids=list(range(8)))
    out = np.empty((2, 4 * T, D), np.float32)
    for i in range(8):
        b, q = i // 4, i % 4
        out[b, q * T:(q + 1) * T] = np.asarray(res.results[i]["out"], np.float32)
    return out, res


def kernel(**inputs):
    out, _ = run_cfg(inputs, Cfg(64))
    return out
```

```python
from contextlib import ExitStack
import ml_dtypes
from concourse.bass_utils import run_bass_kernel_spmd
import numpy as np
import concourse.bass as bass
import concourse.mybir as mybir

F32 = mybir.dt.float32
BF16 = mybir.dt.bfloat16
U8 = mybir.dt.uint8
ALU = mybir.AluOpType
AF = mybir.ActivationFunctionType
AX = mybir.AxisListType


class Buf:
    __slots__ = ("name", "w", "r", "dsem", "dcnt")

    def __init__(self, name):
        self.name = name
        self.w = None
        self.r = {}
        self.dsem = None
        self.dcnt = 0


class Prog:
    ENGS = ("pe", "act", "dve", "pool", "sp")

    def __init__(self, nc, stack):
        self.nc = nc
        self.stack = stack
        self.lists = {e: [] for e in self.ENGS}
        self.sems = {}
        self.cnt = {}
        self.waited = {e: {} for e in self.ENGS}
        for e in self.ENGS:
            self._newsem("eng_" + e)
        self._newsem("coll")
        self.pool_keys = []
        self.pool_idx = 0
        self.attached = []

    def _newsem(self, key):
        self.sems[key] = self.stack.enter_context(self.nc.semaphore(key))
        self.cnt[key] = 0
        return key

    def buf(self, name):
        return Buf(name)

    def _deps(self, reads, writes):
        deps = {}
        def add(t):
            if t is None:
                return
            k, v = t
            if deps.get(k, 0) < v:
                deps[k] = v
        for b in reads:
            add(b.w)
        for b in writes:
            add(b.w)
            for k, v in b.r.items():
                add((k, v))
        return deps

    def _waits(self, eng, deps):
        ws = []
        wd = self.waited[eng]
        for k, v in deps.items():
            if wd.get(k, 0) < v:
                wd[k] = v
                ws.append((k, v))
        return ws

    def _mark(self, tok, reads, writes):
        k, v = tok
        for b in reads:
            if b.r.get(k, 0) < v:
                b.r[k] = v
        for b in writes:
            b.w = tok
            b.r = {}

    def _dsem(self, b):
        if b.dsem is None:
            if self.pool_idx >= len(self.pool_keys):
                self.pool_keys.append(self._newsem("dq%d" % len(self.pool_keys)))
            b.dsem = self.pool_keys[self.pool_idx]
            self.pool_idx += 1
            self.attached.append(b)
        return b.dsem

    def op(self, eng, fn, reads=(), writes=(), inc=True):
        deps = self._deps(reads, writes)
        if eng == "pe":
            deps.pop("eng_pe", None)
        ws = self._waits(eng, deps)
        key = "eng_" + eng
        if inc:
            self.cnt[key] += 1
            tok = (key, self.cnt[key])
        else:
            tok = (key, self.cnt[key] + 1)
        self.lists[eng].append((ws, fn, (key, 1) if inc else None))
        self._mark(tok, reads, writes)
        if inc:
            self.waited[eng][key] = max(self.waited[eng].get(key, 0), 0)
        return tok

    def dma(self, q, out_ap, in_ap, reads, writes, own=None, **kw):
        if own is None:
            own = writes[0]
        key = self._dsem(own)
        ws = self._waits(q, self._deps(reads, writes))
        self.cnt[key] += 16
        tok = (key, self.cnt[key])
        fn = lambda e, o=out_ap, i=in_ap, kw=kw: e.dma_start(out=o, in_=i, **kw)
        self.lists[q].append((ws, fn, (key, 16)))
        self._mark(tok, reads, writes)
        return tok

    def collective(self, kind, in_ap, out_ap, groups, reads, writes):
        key = "coll"
        ws = self._waits("pool", self._deps(reads, writes))
        self.cnt[key] += 1
        tok = (key, self.cnt[key])
        fn = lambda e: e.collective_compute(kind, ALU.bypass, replica_groups=groups,
                                            ins=[in_ap], outs=[out_ap])
        self.lists["pool"].append((ws, fn, (key, 1)))
        self._mark(tok, reads, writes)
        return tok

    def barrier(self):
        for e in self.ENGS:
            ws = self._waits(e, {k: v for k, v in self.cnt.items() if v > 0})
            if ws:
                self.lists[e].append((ws, None, None))
        for b in self.attached:
            b.dsem = None
        self.attached = []
        self.pool_idx = 0

    def emit(self):
        nc = self.nc
        sems = self.sems
        lists = self.lists
        with nc.Block() as block:
            def run(eng_obj, lst):
                for ws, fn, inc in lst:
                    for k, v in ws:
                        eng_obj.wait_ge(sems[k], v)
                    if fn is not None:
                        ins = fn(eng_obj)
                        if inc is not None:
                            ins.then_inc(sems[inc[0]], inc[1])

            @block.tensor
            def _(e):
                run(e, lists["pe"])

            @block.scalar
            def _(e):
                run(e, lists["act"])

            @block.vector
            def _(e):
                run(e, lists["dve"])

            @block.gpsimd
            def _(e):
                run(e, lists["pool"])

            @block.sync
            def _(e):
                run(e, lists["sp"])


class Arena:
    def __init__(self, ap_u8, limit):
        self.ap = ap_u8
        self.limit = limit
        self.off = 0

    def reset(self, off=0):
        self.off = off

    def alloc(self, shape, dt, parts=128):
        esz = 4 if dt == F32 else 2
        n = int(np.prod(shape))
        nbytes = (n * esz + 63) // 64 * 64
        assert self.off + nbytes <= self.limit, ("arena overflow", self.off, nbytes, self.limit)
        a = self.ap[0:parts, self.off:self.off + n * esz].bitcast(dt)
        self.off += nbytes
        if len(shape) == 2:
            a = a.rearrange("p (a b) -> p a b", a=shape[0])
        elif len(shape) == 3:
            a = a.rearrange("p (a b c) -> p a b c", a=shape[0], b=shape[1])
        return a
D = 1024
CTX = 256
GW = 64
FH = 2816
L = 2
EPS = 1e-5
ALPHA = (2 * L) ** 0.25
NA_SCALE = 64 ** -0.5
MLA_SCALE = 96 ** -0.5
MASKVAL = -30000.0
O_F, O_NQ, O_NK, O_NV, O_CQ, O_CKV, O_KR, O_G = 0, 512, 1024, 1536, 2048, 2304, 2432, 2464
IN_DIM = 5536


class Cfg:
    def __init__(self, R=64, debug=(), stop=99):
        self.stop = stop
        self.R = R
        self.T = R * GW
        self.N = 4 * self.T
        self.N1 = self.N // 128
        self.NT = 512
        self.NTL = self.T // 512
        self.NQB = R // 8
        self.debug = tuple(debug)


def build_program(cfg):
    R, T, N, N1, NT, NTL, NQB = cfg.R, cfg.T, cfg.N, cfg.N1, cfg.NT, cfg.NTL, cfg.NQB
    TW = T + 512
    nc = bass.Bass("TRN2", target_bir_lowering=False)

    def din(name, shape, dt=F32):
        return nc.dram_tensor(name, list(shape), dt, kind="ExternalInput").ap()

    dbg = {}

    def dscr(name, shape, dt=BF16):
        if name in cfg.debug and name not in ('NOOUT', 'ONLYP3', 'L1'):
            t = nc.dram_tensor(name, list(shape), dt, kind="ExternalOutput")
            dbg[name] = t
        else:
            t = nc.dram_tensor(name, list(shape), dt)
        return t.ap()

    x_in = din("x", [T, D]); ctx_in = din("ctx", [CTX, D]); c_in = din("c", [128, 8]); cc_in = din("c_ctx", [128, 8])
    lng_in = din("ln_in_g", [128, 8]); lnb_in = din("ln_in_b", [128, 8])
    w_mod = din("w_mod", [L, D, 6 * D]); b_mod = din("b_mod", [L, 128, 48]); w_in = din("w_in", [L, D, IN_DIM])
    gq_in = din("mla_q_norm_g", [L, 128, 2]); gkv_in = din("mla_kv_norm_g", [L, 128, 1])
    w_uq = din("w_uq", [L, 256, 512]); w_qr = din("w_qr", [L, 256, 256]); w_uk = din("w_uk", [L, 128, 512]); w_uv = din("w_uv", [L, 128, 512])
    rpbT = din("rpbT", [L, 8, 23, 64, 64])
    w_br = din("w_branch", [L, 1536, D]); w_out = din("w_out", [L, D, D])
    ln1g = din("ln1_g", [L, 128, 8]); ln1b = din("ln1_b", [L, 128, 8]); ln2g = din("ln2_g", [L, 128, 8]); ln2b = din("ln2_b", [L, 128, 8])
    w_fg = din("w_ffn_gate", [L, D, FH]); w_fu = din("w_ffn_up", [L, D, FH]); w_fd = din("w_ffn_down", [L, FH, D])
    k_id = din("k_ident", [128, 128]); k_cs = din("k_cs128", [128, 256], BF16)
    k_c1 = din("k_c1", [N1, N1], BF16); k_s1 = din("k_s1", [N1, N1], BF16); k_ns1 = din("k_ns1", [N1, N1], BF16)
    k_mre = din("k_mre", [128, N1 * 32], BF16); k_mim = din("k_mimn", [128, N1 * 32], BF16)
    k_cc = din("k_cc256", [256, 256], BF16); k_sc = din("k_scn256", [256, 256], BF16)
    k_rope = din("k_rope", [2, 32, T]); k_rv = din("k_rv", [128, 3 * 64]); k_sel = din("k_sel", [128, 8])
    out_d = nc.dram_tensor("out", [T, D], F32, kind="ExternalOutput").ap()

    WP = {}
    CASTS = []

    def wpiece(key, src2d, r0, kch, c0, pc):
        WP[key] = (dscr("W_%s" % "_".join(str(x) for x in key), [128, kch * pc]), kch, pc)
        CASTS.append((key, src2d, r0, kch, c0, pc))

    for l in range(L):
        for nm, c0, pc in (("f", O_F, 512), ("nq", O_NQ, 512), ("nk", O_NK, 512), ("nv", O_NV, 512), ("cq", O_CQ, 256), ("ckv", O_CKV, 128)):
            wpiece((l, nm), w_in[l], 0, 8, c0, pc)
        for gi in range(3):
            for hb in range(2):
                wpiece((l, "g", gi, hb), w_in[l], 0, 8, O_G + gi * D + hb * 512, 512)
                wpiece((l, "br", gi, hb), w_br[l], gi * 512, 4, hb * 512, 512)
        for hb in range(2):
            wpiece((l, "o", hb), w_out[l], 0, 8, hb * 512, 512)
        for j in range(6):
            pc = 512 if j < 5 else 256
            wpiece((l, "fg", j), w_fg[l], 0, 8, j * 512, pc)
            wpiece((l, "fu", j), w_fu[l], 0, 8, j * 512, pc)
        for j in range(4):
            wpiece((l, "fd", j), w_fd[l], 0, 22, j * 256, 256)
    XL = [dscr("XL%d" % i, [NTL, 128, 8 * NT], F32) for i in range(2)]
    XC = [dscr("XC%d" % i, [128, 8 * CTX], F32) for i in range(2)]
    NAQT = dscr("NAQT", [4, 128, T]); NAKW = dscr("NAKW", [4, 128, TW]); NBL = T // 128; NWB = TW // 128
    NAVW = dscr("NAVW", [8 * 128, NWB * 65])
    HK = dscr("HK", [512, 512]); HKA = dscr("HKA", [4 * 512, 512]); HV = dscr("HV", [512, 512]); HVA = dscr("HVA", [4 * 512, 512])
    MLAQT = dscr("MLAQT", [8, 96, T]); MLAKT = dscr("MLAKT", [768, T]); MLAKTA = [dscr("MLAKTA%d" % h, [4 * 96, T]) for h in range(8)]
    MLAV = dscr("MLAV", [8 * 128, NBL * 65])
    MLAVA = [dscr("MLAVA%d" % h, [4 * 128, NBL * 65]) for h in range(8)]
    HF = 2 if T * 512 > (1 << 20) else 1; CH = T // HF
    FAB = dscr("FAB", [4 * T, 256]); FABA = [dscr("FABA%d" % j, [4 * CH, 256]) for j in range(4 * HF)]
    NAQTc = dscr("NAQTc", [4, 128, CTX]); NAKTc = dscr("NAKTc", [4, 128, CTX]); NAVc = dscr("NAVc", [8 * 128, 2 * 65])
    MLAQTc = dscr("MLAQTc", [8, 96, CTX]); MLAKTc = dscr("MLAKTc", [8, 96, CTX]); MLAVc = dscr("MLAVc", [8 * 128, 2 * 65])
    FABc = dscr("FABc", [CTX, 4, 256])
    YF = dscr("YF", [NTL, 128, 4 * NT]); YNA = dscr("YNA", [NTL, 128, 4 * NT]); YMLA = dscr("YMLA", [NTL, 128, 4 * NT])
    YFc = dscr("YFc", [4, 128, CTX]); YNAc = dscr("YNAc", [4, 128, CTX]); YMLAc = dscr("YMLAc", [4, 128, CTX])
    EBT = [dscr("EBT%d" % l, [8, 64, 23 * 64]) for l in range(L)]

    GROUPS = [[0, 1, 2, 3], [4, 5, 6, 7]]
    ARENA = 196 * 1024

    with ExitStack() as st:
        P = Prog(nc, st)
        ar_ap = st.enter_context(nc.sbuf_tensor("arena", [128, ARENA], U8))
        ps_ap = st.enter_context(nc.psum_tensor("ps", [128, 8, 512], F32))
        ar = Arena(ar_ap, ARENA)
        psb = [P.buf("ps%d" % i) for i in range(8)]
        PS = lambda i: ps_ap[:, i, :]
        dram = {}

        def db(name):
            if name not in dram:
                dram[name] = P.buf(name)
            return dram[name]

        rr = {"lin": 0, "ev": 0, "bank": 0}

        def xview(i, lat, ti):
            if lat:
                return XL[i][ti].rearrange("p (k t) -> p k t", k=8)
            return XC[i].rearrange("p (k t) -> p k t", k=8)

        def evac_eng():
            rr["ev"] ^= 1
            return "act" if rr["ev"] else "dve"

        def copy_op(eng, out, in_, rd, wr):
            if eng == "act":
                P.op("act", lambda e: e.copy(out=out, in_=in_), rd, wr)
            else:
                P.op(eng, lambda e: e.tensor_copy(out=out, in_=in_), rd, wr)

        def pt(name, shape, dt, parts=128):
            return ar.alloc(shape, dt, parts=parts), P.buf(name)

        def ptk(name, shape, dt):
            return ar.alloc(shape, dt), [P.buf("%s_%d" % (name, k)) for k in range(shape[0])]

        def bk(b, k):
            return b[k] if isinstance(b, list) else b

        def ball(b):
            return list(b) if isinstance(b, list) else [b]

        ident, b_ident = pt("ident", [128], F32)
        ones, b_ones = pt("ones", [128], F32)
        epsb, b_eps = pt("eps", [1], F32)
        cs128, b_cs = pt("cs128", [256], BF16)
        c1, b_c1 = pt("c1", [N1], BF16); s1, b_s1 = pt("s1", [N1], BF16); ns1, b_ns1 = pt("ns1", [N1], BF16)
        mre, b_mre = pt("mre", [N1, 32], BF16); mimn, b_mim = pt("mimn", [N1, 32], BF16)
        cc256, b_cc = pt("cc256", [2, 256], BF16); sc256, b_sc = pt("sc256", [2, 256], BF16)
        rv, b_rv = pt("rv", [3, 8, 8], F32); sel, b_sel = pt("sel", [8], F32)
        lnin, b_lnin = pt("lnin", [2, 8], F32)
        modv, b_mod_ = pt("modv", [L, 48, 2], F32); mod1, b_mod1 = pt("mod1", [L, 48, 2], F32)
        lnp, b_lnp = pt("lnp", [L, 4, 8], F32)
        gq, b_gq = pt("gq", [L, 2], F32); gkv, b_gkv = pt("gkv", [L, 1], F32)
        wkr, b_wkr = pt("wkr", [8, 2, 96], BF16)
        wq, b_wq = pt("wq", [2, 8, 96], BF16); wqs, b_wqs = pt("wqs", [2, 8, 96], BF16)
        wk, b_wk = pt("wk", [8, 64], BF16); wuv, b_wuv = pt("wuv", [512], BF16)
        PERSIST = ar.off

        P.dma("sp", ident, k_id, [], [b_ident])
        P.op("pool", lambda e: e.memset(ones, 1.0), [], [b_ones])
        P.op("pool", lambda e: e.memset(epsb, EPS), [], [b_eps])
        P.dma("sp", cs128, k_cs, [], [b_cs])
        P.dma("sp", c1[0:N1], k_c1, [], [b_c1]); P.dma("sp", s1[0:N1], k_s1, [], [b_s1]); P.dma("sp", ns1[0:N1], k_ns1, [], [b_ns1])
        P.dma("sp", mre, k_mre.rearrange("p (a b) -> p a b", b=32), [], [b_mre])
        P.dma("sp", mimn, k_mim.rearrange("p (a b) -> p a b", b=32), [], [b_mim])
        P.dma("sp", cc256, k_cc.rearrange("(a p) k -> p a k", p=128), [], [b_cc])
        P.dma("sp", sc256, k_sc.rearrange("(a p) k -> p a k", p=128), [], [b_sc])
        P.dma("sp", rv, k_rv.rearrange("p (t a b) -> p t a b", t=3, a=8), [], [b_rv])
        P.dma("sp", sel, k_sel, [], [b_sel])
        P.dma("sp", lnin[:, 0, :], lng_in, [], [b_lnin])
        P.dma("sp", lnin[:, 1, :], lnb_in, [], [b_lnin])
        for l in range(L):
            for i, src in enumerate((ln1g, ln1b, ln2g, ln2b)):
                P.dma("sp", lnp[:, l, i, :], src[l], [], [b_lnp])
            P.dma("sp", gq[:, l, :], gq_in[l], [], [b_gq])
            P.dma("sp", gkv[:, l, :], gkv_in[l], [], [b_gkv])

        b_wcast = P.buf("wcast")
        b_wcast1 = P.buf("wcast1")
        late_casts = []
        for (key, src2d, r0, kch, c0, pc) in CASTS:
            dstp = WP[key][0]
            for k in range(kch):
                args = (dstp[:, k * pc:(k + 1) * pc], src2d[r0 + k * 128:r0 + (k + 1) * 128, c0:c0 + pc])
                if key[0] == 0:
                    P.dma("pool", args[0], args[1], [], [b_wcast])
                else:
                    late_casts.append(args)

        def issue_late_casts(n):
            for _ in range(min(n, len(late_casts))):
                o, i = late_casts.pop(0)
                P.dma("pool", o, i, [], [b_wcast1])

        ar.reset(PERSIST)
        cs_t, b_cst = pt("cs_t", [2, 8], F32)
        bm_t, b_bmt = pt("bm_t", [L, 48], F32)
        P.dma("sp", cs_t[:, 0, :], c_in, [], [b_cst])
        P.dma("sp", cs_t[:, 1, :], cc_in, [], [b_cst])
        P.op("act", lambda e: e.activation(out=cs_t, in_=cs_t, func=AF.Silu), [b_cst], [b_cst])
        for l in range(L):
            P.dma("sp", bm_t[:, l, :], b_mod[l], [], [b_bmt])
        wm = [pt("wm%d" % i, [8, 512], F32) for i in range(2)]
        for l in range(L):
            for j in range(12):
                wt_, wb_ = wm[j % 2]
                P.dma("sp", wt_, w_mod[l][:, j * 512:(j + 1) * 512].rearrange("(k p) m -> p k m", p=128), [], [wb_])
                for cc in range(4):
                    ch = j * 4 + cc
                    bank = ch % 4
                    for k in range(8):
                        P.op("pe", lambda e, wt_=wt_, k=k, cc=cc, bank=bank: e.matmul(
                            ps_ap[:, bank, 0:2], lhsT=wt_[:, k, cc * 128:(cc + 1) * 128], rhs=cs_t[:, :, k],
                            start=(k == 0), stop=(k == 7)), [wb_, b_cst], [psb[bank]], inc=(k == 7))
                    P.op("dve", lambda e, l=l, ch=ch, bank=bank: e.tensor_scalar(
                        out=modv[:, l, ch, :], in0=ps_ap[:, bank, 0:2], scalar1=bm_t[:, l, ch:ch + 1], scalar2=None,
                        op0=ALU.add), [psb[bank], b_bmt], [b_mod_])
        P.op("dve", lambda e: e.tensor_scalar(out=mod1, in0=modv, scalar1=1.0, scalar2=None, op0=ALU.add), [b_mod_], [b_mod1])

        def MOD(l, idx, s, plus1):
            src = mod1 if plus1 else modv
            return src[:, l, idx, s:s + 1]

        eb_t = [pt("ebt%d" % i, [23, 64], F32, parts=64) for i in range(2)]
        eb_o = [pt("ebo%d" % i, [23, 64], BF16, parts=64) for i in range(2)]
        for l in range(L):
            for h in range(8):
                (ti, tb), (oi, ob) = eb_t[h % 2], eb_o[h % 2]
                P.dma("sp", ti, rpbT[l, h].rearrange("e k q -> k e q"), [], [tb])
                P.op("act", lambda e, ti=ti, oi=oi: e.activation(out=oi, in_=ti, func=AF.Exp), [tb], [ob])
                P.dma("pool", EBT[l][h], oi.rearrange("p a b -> p (a b)"), [ob], [db("EBT%d" % l)], own=ob)
        P.barrier()
        ar.reset(PERSIST)

        def load_layer_small(l):
            st_uq, b1 = pt("st_uq", [2, 512], F32); st_qr, b2 = pt("st_qr", [2, 256], F32)
            P.dma("sp", st_uq, w_uq[l].rearrange("(k p) m -> p k m", p=128), [], [b1])
            P.dma("sp", st_qr, w_qr[l].rearrange("(k p) m -> p k m", p=128), [], [b2])
            P.op("pool", lambda e: e.memset(wqs, 0.0), [], [b_wqs])
            P.op("pool", lambda e: e.memset(wkr, 0.0), [], [b_wkr])
            uqv = st_uq.rearrange("p k (h d) -> p k h d", h=8); qrv = st_qr.rearrange("p k (h d) -> p k h d", h=8)
            P.op("dve", lambda e: e.tensor_copy(out=wq[:, :, :, 0:64], in_=uqv), [b1], [b_wq])
            P.op("dve", lambda e: e.tensor_copy(out=wq[:, :, :, 64:96], in_=qrv), [b2], [b_wq])
            P.op("dve", lambda e: e.tensor_copy(out=wqs[:, :, :, 64:80], in_=qrv[:, :, :, 16:32]), [b2], [b_wqs])
            P.op("dve", lambda e: e.tensor_copy(out=wqs[:, :, :, 80:96], in_=qrv[:, :, :, 0:16]), [b2], [b_wqs])
            st_kr, b3 = pt("st_kr", [8, 32], F32)
            P.dma("sp", st_kr, w_in[l][:, O_KR:O_KR + 32].rearrange("(k p) m -> p k m", p=128), [], [b3])
            P.op("dve", lambda e: e.tensor_copy(out=wkr[:, :, 0, 64:96], in_=st_kr), [b3], [b_wkr])
            P.op("dve", lambda e: e.tensor_copy(out=wkr[:, :, 1, 64:80], in_=st_kr[:, :, 16:32]), [b3], [b_wkr])
            P.op("dve", lambda e: e.tensor_copy(out=wkr[:, :, 1, 80:96], in_=st_kr[:, :, 0:16]), [b3], [b_wkr])
            st_uk, b4 = pt("st_uk", [512], F32); st_uv, b5 = pt("st_uv", [512], F32)
            P.dma("sp", st_uk, w_uk[l], [], [b4]); P.dma("sp", st_uv, w_uv[l], [], [b5])
            P.op("dve", lambda e: e.tensor_copy(out=wk, in_=st_uk.rearrange("p (h d) -> p h d", h=8)), [b4], [b_wk])
            P.op("dve", lambda e: e.tensor_copy(out=wuv, in_=st_uv), [b5], [b_wuv])

        wslots = []

        def load_piece(key):
            wd, kch, pc = WP[key]
            slot, sbuf_ = wslots[rr["lin"] % len(wslots)]
            rr["lin"] += 1
            P.dma("sp", slot[:, 0:kch * pc], wd, [b_wcast if key[0] == 0 else b_wcast1], [sbuf_])
            return slot[:, 0:kch * pc].rearrange("p (k m) -> p k m", k=kch), sbuf_, kch, pc

        def mm_chunk(piece, c0, rows, act, abuf, nt):
            sv, sbuf_, kch, pc = piece
            bank = rr["bank"] % 4; rr["bank"] += 1
            for k in range(kch):
                P.op("pe", lambda e, k=k: e.matmul(ps_ap[0:rows, bank, 0:nt], lhsT=sv[:, k, c0:c0 + rows], rhs=act[:, k, 0:nt],
                                                   start=(k == 0), stop=(k == kch - 1)), [sbuf_, bk(abuf, k)], [psb[bank]], inc=(k == kch - 1))
            return ps_ap[0:rows, bank, 0:nt], psb[bank]

        def linear_fm(keys, act, abuf, nt, consume):
            ci = 0
            for key in keys:
                piece = load_piece(key)
                for c0 in range(0, piece[3], 128):
                    rows = min(128, piece[3] - c0)
                    ps, pb = mm_chunk(piece, c0, rows, act, abuf, nt)
                    consume(ci, ps, pb, rows)
                    ci += 1

        def ln_fm(r, rbuf, nt, gap, bap, sq, sqbuf, st4, st4buf):
            for k in range(8):
                if k % 2 == 0:
                    P.op("act", lambda e, k=k: e.activation(out=sq[:, k, 0:nt], in_=r[:, k, 0:nt], func=AF.Square), [bk(rbuf, k)], [bk(sqbuf, k)])
                else:
                    P.op("pool", lambda e, k=k: e.tensor_tensor(out=sq[:, k, 0:nt], in0=r[:, k, 0:nt], in1=r[:, k, 0:nt], op=ALU.mult), [bk(rbuf, k)], [bk(sqbuf, k)])
            for k in range(8):
                P.op("pe", lambda e, k=k: e.matmul(ps_ap[:, 4, 0:nt], lhsT=ones, rhs=r[:, k, 0:nt], start=(k == 0), stop=(k == 7)),
                     [b_ones, bk(rbuf, k)], [psb[4]], inc=(k == 7))
            for k in range(8):
                P.op("pe", lambda e, k=k: e.matmul(ps_ap[:, 5, 0:nt], lhsT=ones, rhs=sq[:, k, 0:nt], start=(k == 0), stop=(k == 7)),
                     [b_ones, bk(sqbuf, k)], [psb[5]], inc=(k == 7))
            mean = st4[:, 0, 0:nt]; msq = st4[:, 1, 0:nt]; rstd = st4[:, 2, 0:nt]
            P.op("dve", lambda e: e.tensor_scalar(out=mean, in0=ps_ap[:, 4, 0:nt], scalar1=1.0 / D, scalar2=None, op0=ALU.mult), [psb[4]], [st4buf])
            P.op("dve", lambda e: e.tensor_tensor(out=msq, in0=mean, in1=mean, op=ALU.mult), [st4buf], [st4buf])
            P.op("dve", lambda e: e.scalar_tensor_tensor(out=rstd, in0=ps_ap[:, 5, 0:nt], scalar=1.0 / D, in1=msq, op0=ALU.mult, op1=ALU.subtract), [psb[5], st4buf], [st4buf])
            P.op("act", lambda e: e.activation(out=rstd, in_=rstd, func=AF.Ln, bias=epsb[:, 0:1], scale=1.0), [st4buf, b_eps], [st4buf])
            P.op("act", lambda e: e.activation(out=rstd, in_=rstd, func=AF.Exp, scale=-0.5), [st4buf], [st4buf])
            for k in range(8):
                rk = r[:, k, 0:nt]
                P.op("pool", lambda e, rk=rk: e.tensor_tensor(out=rk, in0=rk, in1=mean, op=ALU.subtract), [bk(rbuf, k), st4buf], [bk(rbuf, k)])
                P.op("dve", lambda e, rk=rk: e.tensor_tensor(out=rk, in0=rk, in1=rstd, op=ALU.mult), [bk(rbuf, k), st4buf], [bk(rbuf, k)])
                P.op("act", lambda e, rk=rk, k=k: e.activation(out=rk, in_=rk, func=AF.Identity, bias=bap[:, k:k + 1], scale=gap[:, k:k + 1]), [bk(rbuf, k), b_lnp, b_lnin], [bk(rbuf, k)])

        def rms_fm(src, sbuf_, kch, nt, gsc, dst, dbuf, sq, sqbuf, st4, st4buf):
            for k in range(kch):
                P.op("pool", lambda e, k=k: e.tensor_tensor(out=sq[:, k, 0:nt], in0=src[:, k, 0:nt], in1=src[:, k, 0:nt], op=ALU.mult), [sbuf_], [sqbuf])
            for k in range(kch):
                P.op("pe", lambda e, k=k: e.matmul(ps_ap[:, 5, 0:nt], lhsT=ones, rhs=sq[:, k, 0:nt], start=(k == 0), stop=(k == kch - 1)),
                     [b_ones, sqbuf], [psb[5]], inc=(k == kch - 1))
            rstd = st4[:, 3, 0:nt]
            P.op("act", lambda e: e.activation(out=rstd, in_=ps_ap[:, 5, 0:nt], func=AF.Ln, bias=epsb[:, 0:1], scale=1.0 / (kch * 128)), [psb[5], b_eps], [st4buf])
            P.op("act", lambda e: e.activation(out=rstd, in_=rstd, func=AF.Exp, scale=-0.5), [st4buf], [st4buf])
            for k in range(kch):
                P.op("dve", lambda e, k=k: e.scalar_tensor_tensor(out=dst[:, k, 0:nt], in0=src[:, k, 0:nt], scalar=gsc[:, k:k + 1], in1=rstd, op0=ALU.mult, op1=ALU.mult),
                     [sbuf_, st4buf, b_gq, b_gkv], [dbuf])

        att_state = {}
        SBANKS = (0, 1, 2, 4, 5, 6)

        def attend(QT, qbuf, kbs, scale, nq, ytile, ybuf):
            ptl = att_state["ptl"]; rs, rsb = att_state["rs"]; bcs, bcsb = att_state["bc"]
            ob = att_state["obank"]; att_state["obank"] = 3 if ob == 7 else 7
            n = len(kbs)

            def S(i):
                KT, kb_, V, vb_, eb, ebb = kbs[i]
                bank = SBANKS[att_state["si"] % 6]; att_state["si"] += 1
                P.op("pe", lambda e: e.matmul(ps_ap[:, bank, 0:nq], lhsT=KT, rhs=QT, start=True, stop=True), [kb_, qbuf], [psb[bank]])
                pt_, pb_ = ptl[att_state["pi"] % len(ptl)]; att_state["pi"] += 1
                P.op("act", lambda e: e.activation(out=pt_[:, 0:nq], in_=ps_ap[:, bank, 0:nq], func=AF.Exp, scale=scale), [psb[bank]], [pb_])
                if eb is not None:
                    P.op("dve", lambda e: e.tensor_tensor(out=pt_[:, 0:nq], in0=pt_[:, 0:nq], in1=eb, op=ALU.mult), [pb_, ebb], [pb_])
                return pt_, pb_

            def PV(i, pt_, pb_):
                KT, kb_, V, vb_, eb, ebb = kbs[i]
                P.op("pe", lambda e: e.matmul(ps_ap[0:65, ob, 0:nq], lhsT=V, rhs=pt_[:, 0:nq], start=(i == 0), stop=(i == n - 1)), [vb_, pb_], [psb[ob]])

            pend = []
            for i in range(n):
                pend.append((i,) + S(i))
                if len(pend) > 3:
                    PV(*pend.pop(0))
            while pend:
                PV(*pend.pop(0))
            P.op("dve", lambda e: e.reciprocal(out=rs[64:65, 0:nq], in_=ps_ap[64:65, ob, 0:nq]), [psb[ob]], [rsb])
            bb = SBANKS[att_state["si"] % 6]; att_state["si"] += 1
            P.op("pe", lambda e: e.matmul(ps_ap[0:64, bb, 0:nq], lhsT=ones[64:65, 0:64], rhs=rs[64:65, 0:nq], start=True, stop=True), [b_ones, rsb], [psb[bb]])
            P.op("act", lambda e: e.copy(out=bcs[0:64, 0:nq], in_=ps_ap[0:64, bb, 0:nq]), [psb[bb]], [bcsb])
            P.op("dve", lambda e: e.tensor_tensor(out=ytile, in0=ps_ap[0:64, ob, 0:nq], in1=bcs[0:64, 0:nq], op=ALU.mult), [psb[ob], bcsb], [ybuf])

        def phase1(l, xt, xbuf, nt, s, tok0, tidx, T1, store_x=True):
            lat = (s == 0)
            nsub = nt // 128
            A = T1["a"]; bA = T1["ab"]
            for k in range(8):
                P.op("dve", lambda e, k=k: e.tensor_scalar(out=A[:, k, 0:nt], in0=xt[:, k, 0:nt], scalar1=MOD(l, 8 + k, s, True), scalar2=MOD(l, k, s, False),
                                                           op0=ALU.mult, op1=ALU.add), [bk(xbuf, k), b_mod_, b_mod1], [bk(bA, k)])
            xdst = xview(l % 2, lat, tok0 // NT)
            if store_x:
                P.dma("pool", xdst, xt[:, :, 0:nt], ball(xbuf), [db("XL%d" % (l % 2) if lat else "XC%d" % (l % 2))], own=bk(xbuf, 0))
            zt_, bz = T1["z"]; fab, bfab = T1["fab"]

            def cons_f(ci, ps, pb, rows):
                g = ci
                copy_op(evac_eng(), zt_[:, 0:nt], ps, [pb], [bz])
                for sub in range(nsub):
                    P.op("pe", lambda e, sub=sub: e.matmul(ps_ap[:, 6, 0:256], lhsT=zt_[:, sub * 128:(sub + 1) * 128], rhs=cs128, start=True, stop=True), [bz, b_cs], [psb[6]])
                    ov = fab[:, sub, g]; iv = ps_ap[:, 6, 0:256]
                    copy_op(evac_eng(), ov, iv, [psb[6]], [bfab])
            linear_fm([(l, "f")], A, bA, nt, cons_f)
            for sub in range(nsub):
                if lat:
                    dst = FAB.rearrange("(g t) c -> t g c", g=4)[tok0 + sub * 128: tok0 + (sub + 1) * 128]
                    P.dma("pool", dst, fab[:, sub], [bfab], [db("FAB")], own=bfab)
                else:
                    P.dma("pool", FABc[sub * 128:(sub + 1) * 128], fab[:, sub], [bfab], [db("FABc")], own=bfab)
            for (wkey, key) in (("nq", "naq"), ("nk", "nak")):
                tl, tb = T1[key]

                def cons(ci, ps, pb, rows, tl=tl, tb=tb):
                    copy_op(evac_eng(), tl[:, ci, 0:nt], ps, [pb], [tb])
                linear_fm([(l, wkey)], A, bA, nt, cons)
                if key == "naq":
                    dst = (NAQT[:, :, tok0:tok0 + nt] if lat else NAQTc[:, :, 0:nt]).rearrange("k p t -> p k t")
                    P.dma("pool", dst, tl[:, :, 0:nt], [tb], [db("NAQT" if lat else "NAQTc")], own=tb)
                else:
                    dst = (NAKW[:, :, 256 + tok0:256 + tok0 + nt] if lat else NAKTc[:, :, 0:nt]).rearrange("k p t -> p k t")
                    P.dma("pool", dst, tl[:, :, 0:nt], [tb], [db("NAKW" if lat else "NAKTc")], own=tb)
                    if lat and tidx == 0:
                        P.dma("pool", HK.rearrange("(k p) t -> p k t", p=128)[:, :, 0:256], tl[:, :, 0:256], [tb], [db("HK")], own=tb)
                    if lat and tidx == NTL - 1:
                        P.dma("pool", HK.rearrange("(k p) t -> p k t", p=128)[:, :, 256:512], tl[:, :, nt - 256:nt], [tb], [db("HK")], own=tb)
            wvv, bwv, _, _ = load_piece((l, "nv"))
            nav, bnav = T1["nav"]
            for sub in range(nsub):
                for k in range(8):
                    P.op("pe", lambda e, k=k, sub=sub: e.matmul(ps_ap[:, 7, :], lhsT=A[:, k, sub * 128:(sub + 1) * 128], rhs=wvv[:, k, :], start=(k == 0), stop=(k == 7)),
                         [bk(bA, k), bwv], [psb[7]], inc=(k == 7))
                copy_op(evac_eng(), nav[:, :, sub, 0:64], ps_ap[:, 7, :].rearrange("p (h d) -> p h d", h=8), [psb[7]], [bnav])
            vview = lambda Vd: Vd.rearrange("(h p) (b c) -> p h b c", p=128, c=65)
            if lat:
                b0 = 2 + (tok0 // 128)
                P.dma("pool", vview(NAVW)[:, :, b0:b0 + nsub, :], nav[:, :, 0:nsub, :], [bnav], [db("NAVW")], own=bnav)
                if tidx == 0:
                    for s_ in range(2):
                        P.dma("pool", HV[s_ * 128:(s_ + 1) * 128].rearrange("p (h d) -> p h d", h=8), nav[:, :, s_, 0:64], [bnav], [db("HV")], own=bnav)
                if tidx == NTL - 1:
                    for s_ in range(2):
                        P.dma("pool", HV[256 + s_ * 128:256 + (s_ + 1) * 128].rearrange("p (h d) -> p h d", h=8), nav[:, :, nsub - 2 + s_, 0:64], [bnav], [db("HV")], own=bnav)
            else:
                P.dma("pool", vview(NAVc)[:, :, 0:nsub, :], nav[:, :, 0:nsub, :], [bnav], [db("NAVc")], own=bnav)
            cq, bcq = T1["cq"]; ckv, bckv = T1["ckv"]

            def cons_cq(ci, ps, pb, rows):
                copy_op(evac_eng(), cq[:, ci, 0:nt], ps, [pb], [bcq])
            linear_fm([(l, "cq")], A, bA, nt, cons_cq)

            def cons_ckv(ci, ps, pb, rows):
                copy_op(evac_eng(), ckv[:, 0, 0:nt], ps, [pb], [bckv])
            linear_fm([(l, "ckv")], A, bA, nt, cons_ckv)
            sq, bsq = T1["sq"]; st4, bst4 = T1["st4"]
            cqn, bcqn = T1["cqn"]; ckvn, bckvn = T1["ckvn"]
            rms_fm(cq, bcq, 2, nt, gq[:, l, :], cqn, bcqn, sq, bsq, st4, bst4)
            rms_fm(ckv, bckv, 1, nt, gkv[:, l, :], ckvn, bckvn, sq, bsq, st4, bst4)
            rope, brope = T1["rope"]
            if lat:
                P.dma("sp", rope[64:96, :, 0:nt], k_rope[:, :, tok0:tok0 + nt].rearrange("a p t -> p a t"), [], [brope])
            tmp, btmp = T1["tmp"]
            krr, bkrr = T1["krr"]
            for v in range(2 if lat else 1):
                for k in range(8):
                    P.op("pe", lambda e, k=k, v=v: e.matmul(ps_ap[0:96, 6 + v, 0:nt], lhsT=wkr[:, k, v, :], rhs=A[:, k, 0:nt], start=(k == 0), stop=(k == 7)),
                         [b_wkr, bk(bA, k)], [psb[6 + v]], inc=(k == 7))
            if lat:
                P.op("dve", lambda e: e.tensor_tensor(out=tmp[64:96, 0, 0:nt], in0=ps_ap[64:96, 6, 0:nt], in1=rope[64:96, 0, 0:nt], op=ALU.mult), [psb[6], brope], [btmp])
                P.op("dve", lambda e: e.tensor_tensor(out=tmp[64:96, 1, 0:nt], in0=ps_ap[64:96, 7, 0:nt], in1=rope[64:96, 1, 0:nt], op=ALU.mult), [psb[7], brope], [btmp])
                P.op("pool", lambda e: e.tensor_tensor(out=krr[64:96, 0:nt], in0=tmp[64:96, 0, 0:nt], in1=tmp[64:96, 1, 0:nt], op=ALU.add), [btmp], [bkrr])
            else:
                P.op("act", lambda e: e.copy(out=krr[64:96, 0:nt], in_=ps_ap[64:96, 6, 0:nt]), [psb[6]], [bkrr])
            for h in range(8):
                qt_, bq_ = T1["qt"][h % 2]
                for v in range(2 if lat else 1):
                    wsrc = wq if v == 0 else wqs
                    for k in range(2):
                        P.op("pe", lambda e, k=k, v=v, h=h, wsrc=wsrc: e.matmul(ps_ap[0:96, 6 + v, 0:nt], lhsT=wsrc[:, k, h, :], rhs=cqn[:, k, 0:nt], start=(k == 0), stop=(k == 1)),
                             [b_wq, b_wqs, bcqn], [psb[6 + v]], inc=(k == 1))
                if lat:
                    P.op("act", lambda e, qt_=qt_: e.copy(out=qt_[0:64, 0:nt], in_=ps_ap[0:64, 6, 0:nt]), [psb[6]], [bq_])
                    P.op("dve", lambda e: e.tensor_tensor(out=tmp[64:96, 0, 0:nt], in0=ps_ap[64:96, 6, 0:nt], in1=rope[64:96, 0, 0:nt], op=ALU.mult), [psb[6], brope], [btmp])
                    P.op("dve", lambda e: e.tensor_tensor(out=tmp[64:96, 1, 0:nt], in0=ps_ap[64:96, 7, 0:nt], in1=rope[64:96, 1, 0:nt], op=ALU.mult), [psb[7], brope], [btmp])
                    P.op("pool", lambda e, qt_=qt_: e.tensor_tensor(out=qt_[64:96, 0:nt], in0=tmp[64:96, 0, 0:nt], in1=tmp[64:96, 1, 0:nt], op=ALU.add), [btmp], [bq_])
                else:
                    P.op("act", lambda e, qt_=qt_: e.copy(out=qt_[0:96, 0:nt], in_=ps_ap[0:96, 6, 0:nt]), [psb[6]], [bq_])
                qd = MLAQT[h, :, tok0:tok0 + nt] if lat else MLAQTc[h, :, 0:nt]
                P.dma("pool", qd, qt_[0:96, 0:nt], [bq_], [db("MLAQT" if lat else "MLAQTc")], own=bq_)
                kt_, bk_ = T1["kt"][h % 2]
                bank = 4 + (h % 2)
                P.op("pe", lambda e, h=h, bank=bank: e.matmul(ps_ap[0:64, bank, 0:nt], lhsT=wk[:, h, :], rhs=ckvn[:, 0, 0:nt], start=True, stop=True), [b_wk, bckvn], [psb[bank]])
                P.op("dve", lambda e, kt_=kt_, bank=bank: e.tensor_copy(out=kt_[0:64, 0:nt], in_=ps_ap[0:64, bank, 0:nt]), [psb[bank]], [bk_])
                P.op("pool", lambda e, kt_=kt_: e.tensor_copy(out=kt_[64:96, 0:nt], in_=krr[64:96, 0:nt]), [bkrr], [bk_])
                kd = MLAKT[h * 96:(h + 1) * 96, tok0:tok0 + nt] if lat else MLAKTc[h, :, 0:nt]
                P.dma("pool", kd, kt_[0:96, 0:nt], [bk_], [db("MLAKT" if lat else "MLAKTc")], own=bk_)
            mv, bmv = T1["mlav"]
            for sub in range(nsub):
                P.op("pe", lambda e, sub=sub: e.matmul(ps_ap[:, 7, :], lhsT=ckvn[:, 0, sub * 128:(sub + 1) * 128], rhs=wuv, start=True, stop=True), [bckvn, b_wuv], [psb[7]])
                copy_op(evac_eng(), mv[:, :, sub, 0:64], ps_ap[:, 7, :].rearrange("p (h d) -> p h d", h=8), [psb[7]], [bmv])
            if lat:
                b0 = tok0 // 128
                P.dma("pool", vview(MLAV)[:, :, b0:b0 + nsub, :], mv[:, :, 0:nsub, :], [bmv], [db("MLAV")], own=bmv)
            else:
                P.dma("pool", vview(MLAVc)[:, :, 0:nsub, :], mv[:, :, 0:nsub, :], [bmv], [db("MLAVc")], own=bmv)

        def alloc_p1():
            T1 = {}
            T1["a"], T1["ab"] = ptk("a", [8, NT], BF16)
            T1["z"] = pt("z", [NT], BF16)
            T1["fab"] = pt("fab", [4, 4, 256], BF16)
            T1["naq"] = pt("naq", [4, NT], BF16); T1["nak"] = pt("nak", [4, NT], BF16)
            T1["nav"] = pt("nav", [8, 4, 65], BF16)
            P.op("pool", lambda e: e.memset(T1["nav"][0], 1.0), [], [T1["nav"][1]])
            T1["cq"] = pt("cq", [2, NT], F32); T1["ckv"] = pt("ckv", [1, NT], F32)
            T1["cqn"] = pt("cqn", [2, NT], BF16); T1["ckvn"] = pt("ckvn", [1, NT], BF16)
            T1["rope"] = pt("rope", [2, NT], F32); T1["tmp"] = pt("tmp", [2, NT], F32)
            T1["krr"] = pt("krr", [NT], BF16)
            T1["qt"] = [pt("qt%d" % i, [NT], BF16) for i in range(2)]
            T1["kt"] = [pt("kt%d" % i, [NT], BF16) for i in range(2)]
            T1["mlav"] = pt("mlav", [8, 4, 65], BF16)
            P.op("pool", lambda e: e.memset(T1["mlav"][0], 1.0), [], [T1["mlav"][1]])
            T1["sq"] = pt("sq1", [2, NT], F32); T1["st4"] = pt("st4_1", [4, NT], F32)
            return T1

        def input_ln(src_rows, nblk, xt, xbuf, W):
            for blk in range(nblk):
                xin, bxin = W["xin"][blk % 2]
                P.dma("sp", xin, src_rows[blk * 128:(blk + 1) * 128, :], [], [bxin])
                s4, bs4 = W["s4"]
                P.op("dve", lambda e, xin=xin: e.reduce_sum(out=s4[:, 0:1], in_=xin, axis=AX.X), [bxin], [bs4])
                P.op("dve", lambda e: e.tensor_scalar(out=s4[:, 1:2], in0=s4[:, 0:1], scalar1=-1.0 / D, scalar2=None, op0=ALU.mult), [bs4], [bs4])
                P.op("act", lambda e, xin=xin: e.activation(out=xin, in_=xin, func=AF.Identity, bias=s4[:, 1:2], scale=1.0), [bxin, bs4], [bxin])
                xsq, bxsq = W["xsq"]
                P.op("pool", lambda e, xin=xin: e.tensor_tensor(out=xsq, in0=xin, in1=xin, op=ALU.mult), [bxin], [bxsq])
                P.op("dve", lambda e: e.reduce_sum(out=s4[:, 2:3], in_=xsq, axis=AX.X), [bxsq], [bs4])
                P.op("act", lambda e: e.activation(out=s4[:, 3:4], in_=s4[:, 2:3], func=AF.Ln, bias=epsb[:, 0:1], scale=1.0 / D), [bs4, b_eps], [bs4])
                P.op("act", lambda e: e.activation(out=s4[:, 3:4], in_=s4[:, 3:4], func=AF.Exp, scale=-0.5), [bs4], [bs4])
                P.op("act", lambda e, xin=xin: e.activation(out=xin, in_=xin, func=AF.Identity, scale=s4[:, 3:4]), [bxin, bs4], [bxin])
                for k in range(8):
                    bank = k % 4
                    P.op("pe", lambda e, xin=xin, k=k, bank=bank: e.transpose(ps_ap[:, bank, 0:128], xin[:, k * 128:(k + 1) * 128], ident), [bxin, b_ident], [psb[bank]])
                    P.op("act" if k % 2 else "dve",
                         (lambda e, k=k, bank=bank, blk=blk: e.activation(out=xt[:, k, blk * 128:(blk + 1) * 128], in_=ps_ap[:, bank, 0:128], func=AF.Identity,
                                                                       bias=lnin[:, 1, k:k + 1], scale=lnin[:, 0, k:k + 1])) if k % 2 else
                         (lambda e, k=k, bank=bank, blk=blk: e.tensor_scalar(out=xt[:, k, blk * 128:(blk + 1) * 128], in0=ps_ap[:, bank, 0:128], scalar1=lnin[:, 0, k:k + 1],
                                                                          scalar2=lnin[:, 1, k:k + 1], op0=ALU.mult, op1=ALU.add)),
                         [psb[bank], b_lnin], [bk(xbuf, k)])

        def exchange():
            P.barrier()
            cl = [(MLAKT[h * 96:(h + 1) * 96, :], MLAKTA[h], "MLAKT", "MLAKTA") for h in range(8)]
            cl += [(MLAV[h * 128:(h + 1) * 128, :], MLAVA[h], "MLAV", "MLAVA") for h in range(8)]
            cl += [(FAB[j * CH:(j + 1) * CH, :], FABA[j], "FAB", "FABA") for j in range(4 * HF)]
            cl += [(HK, HKA, "HK", "HKA"), (HV, HVA, "HV", "HVA")]
            for (src, dst, sn, dn) in cl:
                P.collective("AllGather", src.opt(), dst.opt(), GROUPS, [db(sn)], [db(dn)])
            P.barrier()
            ar.reset(PERSIST)
            hk = [pt("hk%d" % i, [4, 512], BF16) for i in range(2)]
            ho = [pt("ho%d" % i, [2, 256], BF16) for i in range(2)]
            hkv = HKA.rearrange("(r k p) t -> k p r t", r=4, k=4)
            for c in range(4):
                (hi, hb), (oi, ob) = hk[c % 2], ho[c % 2]
                P.dma("sp", hi, hkv[c], [db("HKA")], [hb])
                for half, (t0, so) in enumerate(((256, 0), (0, 4))):
                    P.op("dve", lambda e, hi=hi, oi=oi, half=half, t0=t0, so=so: e.tensor_scalar(out=oi[:, half, :], in0=hi[:, 0, t0:t0 + 256], scalar1=sel[:, so:so + 1], scalar2=None, op0=ALU.mult), [hb, b_sel], [ob])
                    for r in range(1, 4):
                        P.op("dve", lambda e, hi=hi, oi=oi, half=half, t0=t0, so=so, r=r: e.scalar_tensor_tensor(out=oi[:, half, :], in0=hi[:, r, t0:t0 + 256], scalar=sel[:, so + r:so + r + 1], in1=oi[:, half, :], op0=ALU.mult, op1=ALU.add), [hb, b_sel, ob], [ob])
                P.dma("pool", NAKW[c, :, 0:256], oi[:, 0, :], [ob], [db("NAKW")], own=ob)
                P.dma("pool", NAKW[c, :, 256 + T:TW], oi[:, 1, :], [ob], [db("NAKW")], own=ob)
            hv = [pt("hv%d" % i, [4, 512], BF16) for i in range(2)]
            hvo = [pt("hvo%d" % i, [8, 65], BF16) for i in range(2)]
            for (oi, ob) in hvo:
                P.op("pool", lambda e, oi=oi: e.memset(oi, 1.0), [], [ob])
            hvv = HVA.rearrange("(r s p) c -> s p r c", r=4, s=4)
            for sblk in range(4):
                (hi, hb), (oi, ob) = hv[sblk % 2], hvo[sblk % 2]
                srcb = sblk + 2 if sblk < 2 else sblk - 2
                so = 0 if sblk < 2 else 4
                P.dma("sp", hi, hvv[srcb], [db("HVA")], [hb])
                ov = oi[:, :, 0:64]
                hv3 = lambda r, hi=hi: hi[:, r, :].rearrange("p (h d) -> p h d", h=8)
                P.op("dve", lambda e, ov=ov, hv3=hv3, so=so: e.tensor_scalar(out=ov, in0=hv3(0), scalar1=sel[:, so:so + 1], scalar2=None, op0=ALU.mult), [hb, b_sel], [ob])
                for r in range(1, 4):
                    P.op("dve", lambda e, ov=ov, hv3=hv3, so=so, r=r: e.scalar_tensor_tensor(out=ov, in0=hv3(r), scalar=sel[:, so + r:so + r + 1], in1=ov, op0=ALU.mult, op1=ALU.add), [hb, b_sel, ob], [ob])
                wb = sblk if sblk < 2 else 2 + NBL + (sblk - 2)
                P.dma("pool", NAVW.rearrange("(h p) (b c) -> p h b c", p=128, c=65)[:, :, wb, :], oi, [ob], [db("NAVW")], own=ob)
            P.barrier()
            ar.reset(PERSIST)

        def alloc_att():
            att_state.clear()
            att_state.update(si=0, pi=0, obank=7)
            att_state["ptl"] = [pt("p%d" % i, [512], BF16) for i in range(6)]
            att_state["rs"] = pt("rs", [512], F32)
            att_state["bc"] = pt("bcs", [512], F32)

        def phase2_mla(l):
            ar.reset(PERSIST)
            alloc_att()
            NK = CTX + N
            NKB = NK // 128
            KT = [pt("KT%d" % i, [NK], BF16) for i in range(2)]
            VA = [pt("VA%d" % i, [NKB, 65], BF16) for i in range(2)]
            qts = [pt("q%d" % i, [512], BF16) for i in range(3)]
            ys = [pt("y%d" % i, [512], BF16) for i in range(2)]
            qi = 0
            for h in range(8):
                (kt, kb_), (va, vb) = KT[h % 2], VA[h % 2]
                P.dma("sp", kt[0:96, 0:CTX], MLAKTc[h], [db("MLAKTc")], [kb_])
                P.dma("sp", va[:, 0:2, :].rearrange("p a b -> p (a b)"), MLAVc[h * 128:(h + 1) * 128, :], [db("MLAVc")], [vb])
                for r in range(4):
                    P.dma("sp", kt[0:96, CTX + r * T:CTX + (r + 1) * T], MLAKTA[h][r * 96:(r + 1) * 96, :], [db("MLAKTA")], [kb_])
                    P.dma("sp", va[:, 2 + r * NBL:2 + (r + 1) * NBL, :].rearrange("p a b -> p (a b)"), MLAVA[h][r * 128:(r + 1) * 128, :], [db("MLAVA")], [vb])
                kbs_all = [(kt[0:96, j * 128:(j + 1) * 128], kb_, va[:, j, :], vb, None, None) for j in range(NKB)]
                c, po = h // 2, (h % 2) * 64
                for qb in range(NTL):
                    qt_, bq_ = qts[qi % 3]; y_, by_ = ys[qi % 2]; qi += 1
                    P.dma("sp", qt_[0:96, :], MLAQT[h, :, qb * 512:(qb + 1) * 512], [db("MLAQT")], [bq_])
                    attend(qt_[0:96, :], bq_, kbs_all, MLA_SCALE, 512, y_[0:64, :], by_)
                    P.dma("pool", YMLA[qb, po:po + 64, c * NT:(c + 1) * NT], y_[0:64, :], [by_], [db("YMLA")], own=by_)
                    issue_late_casts(8)
                if l == 0:
                    qt_, bq_ = qts[qi % 3]; y_, by_ = ys[qi % 2]; qi += 1
                    P.dma("sp", qt_[0:96, 0:CTX], MLAQTc[h], [db("MLAQTc")], [bq_])
                    attend(qt_[0:96, 0:CTX], bq_, kbs_all[0:2], MLA_SCALE, CTX, y_[0:64, 0:CTX], by_)
                    P.dma("pool", YMLAc[c, po:po + 64, :], y_[0:64, 0:CTX], [by_], [db("YMLAc")], own=by_)
            issue_late_casts(10 ** 6)
            P.barrier()

        def phase2_na(l):
            ar.reset(PERSIST)
            alloc_att()
            KW = [pt("KW%d" % i, [TW], BF16) for i in range(2)]
            QN = [pt("QN%d" % i, [T], BF16) for i in range(2)]
            KC = [pt("KC%d" % i, [CTX], BF16) for i in range(2)]
            QC = [pt("QC%d" % i, [CTX], BF16) for i in range(2)]
            VN = [pt("VN%d" % i, [NWB, 65], BF16) for i in range(2)]
            VC = [pt("VC%d" % i, [2, 65], BF16) for i in range(2)]
            EBF = pt("EBF", [8, 8, 64], BF16)
            EBY = [pt("EBY%d" % i, [8, 512], BF16) for i in range(3)]
            ys = [pt("y%d" % i, [512], BF16) for i in range(2)]
            yi = 0
            for h in range(8):
                c, po = h // 2, (h % 2) * 64
                (kw, bkw), (qn, bqn), (kc, bkc), (qc, bqc) = KW[c % 2], QN[c % 2], KC[c % 2], QC[c % 2]
                (vn, bvn), (vc, bvc) = VN[h % 2], VC[h % 2]
                if h % 2 == 0:
                    P.dma("sp", kw, NAKW[c], [db("NAKW")], [bkw])
                    P.dma("sp", qn, NAQT[c], [db("NAQT")], [bqn])
                    P.dma("sp", kc, NAKTc[c], [db("NAKTc")], [bkc])
                    if l == 0:
                        P.dma("sp", qc, NAQTc[c], [db("NAQTc")], [bqc])
                P.dma("sp", vn.rearrange("p a b -> p (a b)"), NAVW[h * 128:(h + 1) * 128, :], [db("NAVW")], [bvn])
                P.dma("sp", vc.rearrange("p a b -> p (a b)"), NAVc[h * 128:(h + 1) * 128, :], [db("NAVc")], [bvc])
                ebf, bebf = EBF
                for kb in range(8):
                    for krl in range(2):
                        e0 = 15 - 2 * kb - krl
                        P.dma("sp", ebf[krl * 64:(krl + 1) * 64, kb, :, :].rearrange("p a b -> p (a b)"), EBT[l][h][:, e0 * 64:(e0 + 8) * 64], [db("EBT%d" % l)], [bebf])
                for ty in range(3):
                    eby, beby = EBY[ty]
                    P.op("dve", lambda e, eby=eby, ty=ty: e.tensor_tensor(out=eby.rearrange("p a (b c) -> p a b c", b=8), in0=ebf,
                                                                        in1=rv[:, ty].unsqueeze(3).broadcast_to([128, 8, 8, 64]), op=ALU.mult), [bebf, b_rv], [beby])
                ctxk = [(kc[po:po + 64, j * 128:(j + 1) * 128], bkc, vc[:, j, :], bvc, None, None) for j in range(2)]
                for qb in range(NQB):
                    ty = 0 if qb == 0 else (2 if qb == NQB - 1 else 1)
                    eby, beby = EBY[ty]
                    kbs = [(kw[po:po + 64, (4 * qb + j) * 128:(4 * qb + j + 1) * 128], bkw, vn[:, 4 * qb + j, :], bvn, eby[:, j, :], beby) for j in range(8)] + ctxk
                    y_, by_ = ys[yi % 2]; yi += 1
                    attend(qn[po:po + 64, qb * 512:(qb + 1) * 512], bqn, kbs, NA_SCALE, 512, y_[0:64, :], by_)
                    P.dma("pool", YNA[qb, po:po + 64, c * NT:(c + 1) * NT], y_[0:64, :], [by_], [db("YNA")], own=by_)
                if l == 0:
                    y_, by_ = ys[yi % 2]; yi += 1
                    attend(qc[po:po + 64, :], bqc, ctxk, NA_SCALE, CTX, y_[0:64, 0:CTX], by_)
                    P.dma("pool", YNAc[c, po:po + 64, :], y_[0:64, 0:CTX], [by_], [db("YNAc")], own=by_)
            P.barrier()

        def phase2_fourier(l):
            ar.reset(PERSIST)
            AB = pt("AB", [128, 256], BF16)
            GRE = pt("GRE", [N1, 128], BF16); GIM = pt("GIM", [N1, 128], BF16)
            YFS = [pt("YFS%d" % i, [32, N1], BF16) for i in range(2)]
            for g in range(4):
                gre, bgre = GRE; gim, bgim = GIM
                ab, bab = AB
                npc = CH // 128
                for r in range(4):
                    for hf in range(HF):
                        p0 = r * NBL + hf * npc
                        P.dma("sp", ab[p0:p0 + npc].rearrange("p a b -> p (a b)"), FABA[g * HF + hf][r * CH:(r + 1) * CH, :].rearrange("(a n) c -> a (n c)", n=128), [db("FABA")], [bab])
                for j4 in range(32):
                    for half in range(2):
                        bank = half
                        for jj in range(4):
                            j = j4 * 4 + jj
                            Al = ab[0:N1, :, j]; Bl = ab[0:N1, :, 128 + j]
                            o = ps_ap[:, bank, jj * N1:(jj + 1) * N1]
                            if half == 0:
                                P.op("pe", lambda e, o=o, Al=Al: e.matmul(o, lhsT=Al, rhs=c1[0:N1], start=True, stop=False), [bab, b_c1], [psb[bank]], inc=False)
                                P.op("pe", lambda e, o=o, Bl=Bl: e.matmul(o, lhsT=Bl, rhs=ns1[0:N1], start=False, stop=True), [bab, b_ns1], [psb[bank]], inc=(jj == 3))
                            else:
                                P.op("pe", lambda e, o=o, Bl=Bl: e.matmul(o, lhsT=Bl, rhs=c1[0:N1], start=True, stop=False), [bab, b_c1], [psb[bank]], inc=False)
                                P.op("pe", lambda e, o=o, Al=Al: e.matmul(o, lhsT=Al, rhs=s1[0:N1], start=False, stop=True), [bab, b_s1], [psb[bank]], inc=(jj == 3))
                        l0 = j4 * 4
                        G, bG = (gre, bgre) if half == 0 else (gim, bgim)
                        copy_op("act" if half else "dve", G[:, :, l0:l0 + 4], ps_ap[:, bank, 0:4 * N1].rearrange("p (j k) -> p k j", j=4), [psb[bank]], [bG])
                yfs, byfs = YFS[g % 2]
                for k1b in range(N1 // 16):
                    bank = 2 + (k1b % 2)
                    for kk in range(16):
                        k1 = k1b * 16 + kk
                        o = ps_ap[:, bank, kk * 32:(kk + 1) * 32]
                        P.op("pe", lambda e, o=o, k1=k1: e.matmul(o, lhsT=gre[:, k1, :], rhs=mre[:, k1, :], start=True, stop=False), [bgre, b_mre], [psb[bank]], inc=False)
                        P.op("pe", lambda e, o=o, k1=k1: e.matmul(o, lhsT=gim[:, k1, :], rhs=mimn[:, k1, :], start=False, stop=True), [bgim, b_mim], [psb[bank]], inc=(kk == 15))
                    copy_op("act" if k1b % 2 else "dve", yfs[:, :, k1b * 16:(k1b + 1) * 16], ps_ap[:, bank, :].rearrange("p (a b) -> p b a", a=16), [psb[bank]], [byfs])
                P.dma("pool", YF.rearrange("t p (c n) -> p t c n", c=4)[:, :, g, :], yfs.rearrange("p a b -> p (a b)").rearrange("p (t n) -> p t n", n=NT), [byfs], [db("YF")], own=byfs)
            if l == 0:
                fc, bfc = pt("fc", [2, 4, 256], BF16)
                P.dma("sp", fc, FABc.rearrange("(s p) g c -> p s g c", p=128), [db("FABc")], [bfc])
                yc, byc = pt("yc", [4, 256], BF16)
                for g in range(4):
                    steps = [(fc[:, nb, g, 0:128], cc256[:, nb, :]) for nb in range(2)] + [(fc[:, nb, g, 128:256], sc256[:, nb, :]) for nb in range(2)]
                    for i, (lh, rh) in enumerate(steps):
                        P.op("pe", lambda e, lh=lh, rh=rh, i=i: e.matmul(ps_ap[:, 4, 0:256], lhsT=lh, rhs=rh, start=(i == 0), stop=(i == 3)), [bfc, b_cc, b_sc], [psb[4]], inc=(i == 3))
                    copy_op("dve", yc[:, g, :], ps_ap[:, 4, 0:256], [psb[4]], [byc])
                P.dma("pool", YFc.rearrange("g p t -> p g t"), yc, [byc], [db("YFc")], own=byc)
            P.barrier()

        def alloc_p3():
            T3 = {}
            T3["x"] = [ptk("x3_%d" % i, [8, NT], F32) for i in range(1)]
            T3["mix"] = ptk("mix", [8, NT], F32)
            T3["r"] = ptk("r", [8, NT], F32)
            T3["a"] = ptk("a3", [8, NT], BF16)
            T3["mixb"] = ptk("mixb", [8, NT], BF16)
            T3["hid"] = ptk("hid", [22, NT], BF16)
            T3["y"] = [pt("y3_%d" % i, [4, NT], BF16) for i in range(3)]
            T3["st4"] = pt("st4_3", [4, NT], F32)
            T3["sg"] = [pt("sg%d" % i, [NT], BF16) for i in range(2)]
            T3["tmpf"] = [pt("tmpf%d" % i, [NT], F32) for i in range(2)]
            T3["ot"] = [pt("ot%d" % i, [1024], F32) for i in range(2)]
            return T3

        def phase3(l, nt, s, tok0, T3):
            lat = (s == 0)
            xt, xbuf = T3["x"][0]
            xsrc = xview(l % 2, lat, tok0 // NT)
            P.dma("sp", xt[:, :, 0:nt], xsrc, [db("XL%d" % (l % 2) if lat else "XC%d" % (l % 2))], xbuf)
            A, bA = T3["a"]
            for k in range(8):
                P.op("dve", lambda e, k=k: e.tensor_scalar(out=A[:, k, 0:nt], in0=xt[:, k, 0:nt], scalar1=MOD(l, 8 + k, s, True), scalar2=MOD(l, k, s, False),
                                                           op0=ALU.mult, op1=ALU.add), [xbuf[k], b_mod_, b_mod1], [bA[k]])
            ysrc = ((YF, "YF"), (YNA, "YNA"), (YMLA, "YMLA")) if lat else ((YFc, "YFc"), (YNAc, "YNAc"), (YMLAc, "YMLAc"))
            for gi in range(3):
                yt, by = T3["y"][gi]
                yd = ysrc[gi][0][tok0 // NT].rearrange("p (k t) -> p k t", k=4) if lat else ysrc[gi][0][:, :, 0:nt].rearrange("k p t -> p k t")
                P.dma("sp", yt[:, :, 0:nt], yd, [db(ysrc[gi][1])], [by])
            mix, bmix = T3["mix"]
            for gi in range(3):
                yt, by = T3["y"][gi]
                for hb in range(2):
                    pg = load_piece((l, "g", gi, hb)); pb_ = load_piece((l, "br", gi, hb))
                    for j in range(4):
                        mc = hb * 4 + j
                        ps, pb = mm_chunk(pg, j * 128, 128, A, bA, nt)
                        sg, bsg = T3["sg"][mc % 2]
                        P.op("act", lambda e, sg=sg, ps=ps: e.activation(out=sg[:, 0:nt], in_=ps, func=AF.Sigmoid), [pb], [bsg])
                        ps2, pb2 = mm_chunk(pb_, j * 128, 128, yt, by, nt)
                        if gi == 0:
                            P.op("dve", lambda e, sg=sg, ps2=ps2, mc=mc: e.tensor_tensor(out=mix[:, mc, 0:nt], in0=ps2, in1=sg[:, 0:nt], op=ALU.mult), [pb2, bsg], [bmix[mc]])
                        else:
                            tf, btf = T3["tmpf"][mc % 2]
                            P.op("dve", lambda e, sg=sg, ps2=ps2, tf=tf: e.tensor_tensor(out=tf[:, 0:nt], in0=ps2, in1=sg[:, 0:nt], op=ALU.mult), [pb2, bsg], [btf])
                            P.op("pool", lambda e, tf=tf, mc=mc: e.tensor_tensor(out=mix[:, mc, 0:nt], in0=mix[:, mc, 0:nt], in1=tf[:, 0:nt], op=ALU.add), [btf, bmix[mc]], [bmix[mc]])
            mixb, bmixb = T3["mixb"]
            for k in range(8):
                copy_op("act" if k % 2 else "dve", mixb[:, k, 0:nt], mix[:, k, 0:nt], [bmix[k]], [bmixb[k]])
            r, rb = T3["r"]

            def cons_o(ci, ps, pb, rows):
                P.op("pool", lambda e: e.tensor_scalar(out=r[:, ci, 0:nt], in0=xt[:, ci, 0:nt], scalar1=ALPHA, scalar2=None, op0=ALU.mult), [xbuf[ci]], [rb[ci]])
                P.op("dve", lambda e: e.scalar_tensor_tensor(out=r[:, ci, 0:nt], in0=ps, scalar=MOD(l, 16 + ci, s, True), in1=r[:, ci, 0:nt], op0=ALU.mult, op1=ALU.add), [pb, rb[ci], b_mod1], [rb[ci]])
            linear_fm([(l, "o", 0), (l, "o", 1)], mixb, bmixb, nt, cons_o)
            st4, bst4 = T3["st4"]
            ln_fm(r, rb, nt, lnp[:, l, 0, :], lnp[:, l, 1, :], mix, bmix, st4, bst4)
            for k in range(8):
                P.op("dve", lambda e, k=k: e.tensor_scalar(out=A[:, k, 0:nt], in0=r[:, k, 0:nt], scalar1=MOD(l, 32 + k, s, True), scalar2=MOD(l, 24 + k, s, False),
                                                           op0=ALU.mult, op1=ALU.add), [rb[k], b_mod_, b_mod1], [bA[k]])
            hid, bhid = T3["hid"]
            for j6 in range(6):
                pfg = load_piece((l, "fg", j6)); pfu = load_piece((l, "fu", j6))
                for j in range(pfg[3] // 128):
                    mc = j6 * 4 + j
                    ps, pb = mm_chunk(pfg, j * 128, 128, A, bA, nt)
                    sg, bsg = T3["sg"][mc % 2]
                    P.op("act", lambda e, sg=sg, ps=ps: e.activation(out=sg[:, 0:nt], in_=ps, func=AF.Silu), [pb], [bsg])
                    ps2, pb2 = mm_chunk(pfu, j * 128, 128, A, bA, nt)
                    P.op("dve", lambda e, sg=sg, ps2=ps2, mc=mc: e.tensor_tensor(out=hid[:, mc, 0:nt], in0=ps2, in1=sg[:, 0:nt], op=ALU.mult), [pb2, bsg], [bhid[mc]])

            def cons_d(ci, ps, pb, rows):
                P.op("pool", lambda e: e.tensor_scalar(out=r[:, ci, 0:nt], in0=r[:, ci, 0:nt], scalar1=ALPHA, scalar2=None, op0=ALU.mult), [rb[ci]], [rb[ci]])
                P.op("dve", lambda e: e.scalar_tensor_tensor(out=r[:, ci, 0:nt], in0=ps, scalar=MOD(l, 40 + ci, s, True), in1=r[:, ci, 0:nt], op0=ALU.mult, op1=ALU.add), [pb, rb[ci], b_mod1], [rb[ci]])
            linear_fm([(l, "fd", j) for j in range(4)], hid, bhid, nt, cons_d)
            ln_fm(r, rb, nt, lnp[:, l, 2, :], lnp[:, l, 3, :], mix, bmix, st4, bst4)
            return r, rb

        def write_out(r, rb, tok0, T3):
            for sub in range(4):
                ot, bot = T3["ot"][sub % 2]
                for k in range(8):
                    bank = 6 + (k // 4)
                    P.op("pe", lambda e, k=k, sub=sub, bank=bank: e.transpose(ps_ap[:, bank, (k % 4) * 128:(k % 4 + 1) * 128], r[:, k, sub * 128:(sub + 1) * 128], ident), [rb[k], b_ident], [psb[bank]])
                    if k % 4 == 3:
                        copy_op("act" if k == 3 else "dve", ot[:, (k // 4) * 512:(k // 4 + 1) * 512], ps_ap[:, bank, :], [psb[bank]], [bot])
                P.dma("pool", out_d[tok0 + sub * 128:tok0 + (sub + 1) * 128, :], ot, [bot], [db("out")], own=bot)

        def run_phase1_pass(l, from_input):
            ar.reset(PERSIST)
            load_layer_small(l)
            P.barrier()
            ar.reset(PERSIST)
            wslots[:] = [pt("wslot%d" % i, [6144], BF16) for i in range(3)]
            T1 = alloc_p1()
            xts = [ptk("xt%d" % i, [8, NT], F32) for i in range(2)]
            if from_input:
                W = {"xin": [pt("xin%d" % i, [1024], F32) for i in range(2)], "s4": pt("s4", [4], F32), "xsq": pt("xsq", [1024], F32)}
            for ti in range(NTL + 1):
                xt, xbuf = xts[ti % 2]
                lat = ti < NTL
                nt = NT if lat else CTX
                if from_input:
                    input_ln(x_in[ti * NT:(ti + 1) * NT] if lat else ctx_in, nt // 128, xt, xbuf, W)
                else:
                    P.dma("sp", xt[:, :, 0:nt], xview(l % 2, lat, ti), [db("XL%d" % (l % 2) if lat else "XC%d" % (l % 2))], ball(xbuf))
                phase1(l, xt, xbuf, nt, 0 if lat else 1, ti * NT if lat else 0, ti if lat else -1, T1, store_x=from_input)

        class _Stop(Exception):
            pass

        def chk(stage):
            if cfg.stop <= stage:
                raise _Stop()

        try:
            chk(0)
            if 'ONLYP3' in cfg.debug:
                lsel = 1 if 'L1' in cfg.debug else 0
                wslots[:] = [pt("wslot%d" % i, [6144], BF16) for i in range(3)]
                T3 = alloc_p3()
                for ti in range(NTL):
                    r, rb = phase3(lsel, NT, 0, ti * NT, T3)
                    if 'NOOUT' not in cfg.debug:
                        write_out(r, rb, ti * NT, T3)
                raise _Stop()
            run_phase1_pass(0, True)
            chk(1)
            for l in range(L):
                exchange()
                chk(2 + 10 * l)
                phase2_mla(l)
                chk(3 + 10 * l)
                phase2_na(l)
                chk(4 + 10 * l)
                phase2_fourier(l)
                chk(5 + 10 * l)
                ar.reset(PERSIST)
                wslots[:] = [pt("wslot%d" % i, [6144], BF16) for i in range(3)]
                T3 = alloc_p3()
                last = (l + 1 == L)
                for ti in range(NTL + (0 if last else 1)):
                    lat = ti < NTL
                    nt = NT if lat else CTX
                    r, rb = phase3(l, nt, 0 if lat else 1, ti * NT if lat else 0, T3)
                    if last:
                        if 'NOOUT' not in cfg.debug:
                            write_out(r, rb, ti * NT, T3)
                    else:
                        P.dma("pool", xview((l + 1) % 2, lat, ti), r[:, :, 0:nt], rb, [db("XL%d" % ((l + 1) % 2) if lat else "XC%d" % ((l + 1) % 2))], own=rb[0])
                chk(6 + 10 * l)
                if not last:
                    P.barrier()
                    run_phase1_pass(l + 1, False)
                chk(7 + 10 * l)
        except _Stop:
            pass
        P.barrier()
        P.emit()
    return nc, dbg


def _consts(cfg, q):
    bf = ml_dtypes.bfloat16
    R, T, N, N1 = cfg.R, cfg.T, cfg.N, cfg.N1
    out = {}
    out["k_ident"] = np.eye(128, dtype=np.float32)
    cl = np.outer(np.arange(128), np.arange(128)).astype(np.float64) * (2 * np.pi / 128)
    out["k_cs128"] = np.concatenate([np.cos(cl), np.sin(cl)], 1).astype(bf)
    a1 = np.outer(np.arange(N1), np.arange(N1)).astype(np.float64) * (2 * np.pi / N1)
    out["k_c1"] = np.cos(a1).astype(bf); out["k_s1"] = np.sin(a1).astype(bf); out["k_ns1"] = (-np.sin(a1)).astype(bf)
    sc = 1.0 / np.sqrt(N * 128.0)
    k1 = np.arange(N1)[None, :, None]; k2 = np.arange(32)[None, None, :]; n2 = np.arange(128)[:, None, None]
    k = T * q + k1 + N1 * k2
    ang = (n2 * k % N).astype(np.float64) * (2 * np.pi / N)
    out["k_mre"] = (np.cos(ang) * sc).reshape(128, N1 * 32).astype(bf)
    out["k_mimn"] = (-np.sin(ang) * sc).reshape(128, N1 * 32).astype(bf)
    ac = np.outer(np.arange(256), np.arange(256)).astype(np.float64) * (2 * np.pi / 256)
    scc = 1.0 / np.sqrt(256 * 128.0)
    out["k_cc256"] = (np.cos(ac) * scc).astype(bf); out["k_scn256"] = (-np.sin(ac) * scc).astype(bf)
    t = np.arange(T) + q * T
    rows = (t // GW).astype(np.float32); cols = (t % GW).astype(np.float32)
    inv = (np.float32(10000.0) ** (-np.arange(8, dtype=np.float32) / np.float32(8))).astype(np.float32)
    angr = np.concatenate([rows[:, None] * inv, cols[:, None] * inv], -1).astype(np.float32)
    co = np.cos(angr).T.astype(np.float32); si = np.sin(angr).T.astype(np.float32)
    out["k_rope"] = np.stack([np.concatenate([co, co], 0), np.concatenate([-si, si], 0)], 0).astype(np.float32)
    NR = 4 * R
    rvt = np.zeros((128, 3, 8, 8), np.float32)
    for ty, r0 in enumerate((R * q, R * q + 8, R * q + R - 8)):
        for p in range(128):
            for kb in range(8):
                krow = r0 - 4 + 2 * kb + p // 64
                for qi in range(8):
                    rs = min(max(r0 + qi - 4, 0), NR - 8)
                    rvt[p, ty, kb, qi] = 1.0 if rs <= krow < rs + 8 else 0.0
    out["k_rv"] = rvt.reshape(128, 192)
    sel = np.zeros((128, 8), np.float32)
    if q > 0:
        sel[:, q - 1] = 1.0
    if q < 3:
        sel[:, 4 + q + 1] = 1.0
    out["k_sel"] = sel
    return out


def _rpb_table(rpb):
    Ln, H = rpb.shape[0], rpb.shape[1]
    tab = np.full((Ln, H, 23, 64, 64), MASKVAL, np.float32)
    qc = np.arange(64)[None, :]; kc = np.arange(64)[:, None]
    cs = np.clip(qc - 8, 0, 48)
    valid = (kc >= cs) & (kc < cs + 16)
    dc = np.clip(kc - qc + 15, 0, 30)
    for e in range(23):
        dr = 11 - e
        if abs(dr) <= 7:
            g = rpb[:, :, dr + 7, :][:, :, dc]
            tab[:, :, e] = np.where(valid[None, None], g, np.float32(MASKVAL))
    return tab


def _pk(v, k):
    v = np.asarray(v, np.float32)
    sh = v.shape[:-1]
    return np.ascontiguousarray(np.swapaxes(v.reshape(sh + (k, 128)), -1, -2))


_CACHE = {}


def run_cfg(inputs, cfg):
    key = (cfg.R, cfg.debug, cfg.stop)
    if key not in _CACHE:
        _CACHE[key] = build_program(cfg)
    nc, dbg = _CACHE[key]
    T = cfg.T
    f32 = lambda a: np.ascontiguousarray(np.asarray(a, np.float32))
    shared = {
        "c_ctx": _pk(inputs["c_ctx"], 8), "ln_in_g": _pk(inputs["ln_in_g"], 8), "ln_in_b": _pk(inputs["ln_in_b"], 8),
        "w_mod": f32(inputs["w_mod"]), "b_mod": _pk(inputs["b_mod"], 48), "w_in": f32(inputs["w_in"]),
        "mla_q_norm_g": _pk(inputs["mla_q_norm_g"], 2), "mla_kv_norm_g": _pk(inputs["mla_kv_norm_g"], 1),
        "w_uq": f32(inputs["w_uq"]), "w_qr": f32(inputs["w_qr"]), "w_uk": f32(inputs["w_uk"]), "w_uv": f32(inputs["w_uv"]),
        "rpbT": _rpb_table(f32(inputs["na_rpb"])),
        "w_branch": f32(inputs["w_branch"]).reshape(L, 1536, D), "w_out": f32(inputs["w_out"]),
        "ln1_g": _pk(inputs["ln1_g"], 8), "ln1_b": _pk(inputs["ln1_b"], 8), "ln2_g": _pk(inputs["ln2_g"], 8), "ln2_b": _pk(inputs["ln2_b"], 8),
        "w_ffn_gate": f32(inputs["w_ffn_gate"]), "w_ffn_up": f32(inputs["w_ffn_up"]), "w_ffn_down": f32(inputs["w_ffn_down"]),
    }
    x = np.asarray(inputs["x"], np.float32); ctx = np.asarray(inputs["ctx"], np.float32); c = np.asarray(inputs["c"], np.float32)
    in_maps = []
    for i in range(8):
        b, q = i // 4, i % 4
        m = dict(shared)
        m["x"] = np.ascontiguousarray(x[b, q * T:(q + 1) * T]); m["ctx"] = np.ascontiguousarray(ctx[b]); m["c"] = _pk(c[b], 8)
        m.update(_consts(cfg, q))
        in_maps.append(m)
    res = run_bass_kernel_spmd(nc, in_maps, core_ids=list(range(8)))
    out = np.empty((2, 4 * T, D), np.float32)
    for i in range(8):
        b, q = i // 4, i % 4
        out[b, q * T:(q + 1) * T] = np.asarray(res.results[i]["out"], np.float32)
    return out, res


def kernel(**inputs):
    out, _ = run_cfg(inputs, Cfg(64))
    return out
```

```python
from contextlib import ExitStack
import ml_dtypes
from concourse.bass_utils import run_bass_kernel_spmd
import numpy as np
import concourse.bass as bass
import concourse.mybir as mybir

F32 = mybir.dt.float32
BF16 = mybir.dt.bfloat16
U8 = mybir.dt.uint8
ALU = mybir.AluOpType
AF = mybir.ActivationFunctionType
AX = mybir.AxisListType


class Buf:
    __slots__ = ("name", "w", "r", "dsem", "dcnt")

    def __init__(self, name):
        self.name = name
        self.w = None
        self.r = {}
        self.dsem = None
        self.dcnt = 0


class Prog:
    ENGS = ("pe", "act", "dve", "pool", "sp")

    def __init__(self, nc, stack):
        self.nc = nc
        self.stack = stack
        self.lists = {e: [] for e in self.ENGS}
        self.sems = {}
        self.cnt = {}
        self.waited = {e: {} for e in self.ENGS}
        for e in self.ENGS:
            self._newsem("eng_" + e)
        self._newsem("coll")
        self.pool_keys = []
        self.pool_idx = 0
        self.attached = []

    def _newsem(self, key):
        self.sems[key] = self.stack.enter_context(self.nc.semaphore(key))
        self.cnt[key] = 0
        return key

    def buf(self, name):
        return Buf(name)

    def _deps(self, reads, writes):
        deps = {}
        def add(t):
            if t is None:
                return
            k, v = t
            if deps.get(k, 0) < v:
                deps[k] = v
        for b in reads:
            add(b.w)
        for b in writes:
            add(b.w)
            for k, v in b.r.items():
                add((k, v))
        return deps

    def _waits(self, eng, deps):
        ws = []
        wd = self.waited[eng]
        for k, v in deps.items():
            if wd.get(k, 0) < v:
                wd[k] = v
                ws.append((k, v))
        return ws

    def _mark(self, tok, reads, writes):
        k, v = tok
        for b in reads:
            if b.r.get(k, 0) < v:
                b.r[k] = v
        for b in writes:
            b.w = tok
            b.r = {}

    def _dsem(self, b):
        if b.dsem is None:
            if self.pool_idx >= len(self.pool_keys):
                self.pool_keys.append(self._newsem("dq%d" % len(self.pool_keys)))
            b.dsem = self.pool_keys[self.pool_idx]
            self.pool_idx += 1
            self.attached.append(b)
        return b.dsem

    def op(self, eng, fn, reads=(), writes=(), inc=True):
        deps = self._deps(reads, writes)
        if eng == "pe":
            deps.pop("eng_pe", None)
        ws = self._waits(eng, deps)
        key = "eng_" + eng
        if inc:
            self.cnt[key] += 1
            tok = (key, self.cnt[key])
        else:
            tok = (key, self.cnt[key] + 1)
        self.lists[eng].append((ws, fn, (key, 1) if inc else None))
        self._mark(tok, reads, writes)
        if inc:
            self.waited[eng][key] = max(self.waited[eng].get(key, 0), 0)
        return tok

    def dma(self, q, out_ap, in_ap, reads, writes, own=None, **kw):
        if own is None:
            own = writes[0]
        key = self._dsem(own)
        ws = self._waits(q, self._deps(reads, writes))
        self.cnt[key] += 16
        tok = (key, self.cnt[key])
        fn = lambda e, o=out_ap, i=in_ap, kw=kw: e.dma_start(out=o, in_=i, **kw)
        self.lists[q].append((ws, fn, (key, 16)))
        self._mark(tok, reads, writes)
        return tok

    def collective(self, kind, in_ap, out_ap, groups, reads, writes):
        key = "coll"
        ws = self._waits("pool", self._deps(reads, writes))
        self.cnt[key] += 1
        tok = (key, self.cnt[key])
        fn = lambda e: e.collective_compute(kind, ALU.bypass, replica_groups=groups,
                                            ins=[in_ap], outs=[out_ap])
        self.lists["pool"].append((ws, fn, (key, 1)))
        self._mark(tok, reads, writes)
        return tok

    def barrier(self):
        for e in self.ENGS:
            ws = self._waits(e, {k: v for k, v in self.cnt.items() if v > 0})
            if ws:
                self.lists[e].append((ws, None, None))
        for b in self.attached:
            b.dsem = None
        self.attached = []
        self.pool_idx = 0

    def emit(self):
        nc = self.nc
        sems = self.sems
        lists = self.lists
        with nc.Block() as block:
            def run(eng_obj, lst):
                for ws, fn, inc in lst:
                    for k, v in ws:
                        eng_obj.wait_ge(sems[k], v)
                    if fn is not None:
                        ins = fn(eng_obj)
                        if inc is not None:
                            ins.then_inc(sems[inc[0]], inc[1])

            @block.tensor
            def _(e):
                run(e, lists["pe"])

            @block.scalar
            def _(e):
                run(e, lists["act"])

            @block.vector
            def _(e):
                run(e, lists["dve"])

            @block.gpsimd
            def _(e):
                run(e, lists["pool"])

            @block.sync
            def _(e):
                run(e, lists["sp"])


class Arena:
    def __init__(self, ap_u8, limit):
        self.ap = ap_u8
        self.limit = limit
        self.off = 0

    def reset(self, off=0):
        self.off = off

    def alloc(self, shape, dt, parts=128):
        esz = 4 if dt == F32 else 2
        n = int(np.prod(shape))
        nbytes = (n * esz + 63) // 64 * 64
        assert self.off + nbytes <= self.limit, ("arena overflow", self.off, nbytes, self.limit)
        a = self.ap[0:parts, self.off:self.off + n * esz].bitcast(dt)
        self.off += nbytes
        if len(shape) == 2:
            a = a.rearrange("p (a b) -> p a b", a=shape[0])
        elif len(shape) == 3:
            a = a.rearrange("p (a b c) -> p a b c", a=shape[0], b=shape[1])
        return a
D = 1024
CTX = 256
GW = 64
FH = 2816
L = 2
EPS = 1e-5
ALPHA = (2 * L) ** 0.25
NA_SCALE = 64 ** -0.5
MLA_SCALE = 96 ** -0.5
MASKVAL = -30000.0
O_F, O_NQ, O_NK, O_NV, O_CQ, O_CKV, O_KR, O_G = 0, 512, 1024, 1536, 2048, 2304, 2432, 2464
IN_DIM = 5536


class Cfg:
    def __init__(self, R=64, debug=(), stop=99):
        self.stop = stop
        self.R = R
        self.T = R * GW
        self.N = 4 * self.T
        self.N1 = self.N // 128
        self.NT = 512
        self.NTL = self.T // 512
        self.NQB = R // 8
        self.debug = tuple(debug)


def build_program(cfg):
    R, T, N, N1, NT, NTL, NQB = cfg.R, cfg.T, cfg.N, cfg.N1, cfg.NT, cfg.NTL, cfg.NQB
    TW = T + 512
    nc = bass.Bass("TRN2", target_bir_lowering=False)

    def din(name, shape, dt=F32):
        return nc.dram_tensor(name, list(shape), dt, kind="ExternalInput").ap()

    dbg = {}

    def dscr(name, shape, dt=BF16):
        if name in cfg.debug and name not in ('NOOUT', 'ONLYP3', 'L1'):
            t = nc.dram_tensor(name, list(shape), dt, kind="ExternalOutput")
            dbg[name] = t
        else:
            t = nc.dram_tensor(name, list(shape), dt)
        return t.ap()

    x_in = din("x", [T, D]); ctx_in = din("ctx", [CTX, D]); c_in = din("c", [128, 8]); cc_in = din("c_ctx", [128, 8])
    lng_in = din("ln_in_g", [128, 8]); lnb_in = din("ln_in_b", [128, 8])
    w_mod = din("w_mod", [L, D, 6 * D]); b_mod = din("b_mod", [L, 128, 48]); w_in = din("w_in", [L, D, IN_DIM])
    gq_in = din("mla_q_norm_g", [L, 128, 2]); gkv_in = din("mla_kv_norm_g", [L, 128, 1])
    w_uq = din("w_uq", [L, 256, 512]); w_qr = din("w_qr", [L, 256, 256]); w_uk = din("w_uk", [L, 128, 512]); w_uv = din("w_uv", [L, 128, 512])
    rpbT = din("rpbT", [L, 8, 23, 64, 64])
    w_br = din("w_branch", [L, 1536, D]); w_out = din("w_out", [L, D, D])
    ln1g = din("ln1_g", [L, 128, 8]); ln1b = din("ln1_b", [L, 128, 8]); ln2g = din("ln2_g", [L, 128, 8]); ln2b = din("ln2_b", [L, 128, 8])
    w_fg = din("w_ffn_gate", [L, D, FH]); w_fu = din("w_ffn_up", [L, D, FH]); w_fd = din("w_ffn_down", [L, FH, D])
    k_id = din("k_ident", [128, 128]); k_cs = din("k_cs128", [128, 256], BF16)
    k_c1 = din("k_c1", [N1, N1], BF16); k_s1 = din("k_s1", [N1, N1], BF16); k_ns1 = din("k_ns1", [N1, N1], BF16)
    k_mre = din("k_mre", [128, N1 * 32], BF16); k_mim = din("k_mimn", [128, N1 * 32], BF16)
    k_cc = din("k_cc256", [256, 256], BF16); k_sc = din("k_scn256", [256, 256], BF16)
    k_rope = din("k_rope", [2, 32, T]); k_rv = din("k_rv", [128, 3 * 64]); k_sel = din("k_sel", [128, 8])
    out_d = nc.dram_tensor("out", [T, D], F32, kind="ExternalOutput").ap()

    WP = {}
    CASTS = []

    def wpiece(key, src2d, r0, kch, c0, pc):
        WP[key] = (dscr("W_%s" % "_".join(str(x) for x in key), [128, kch * pc]), kch, pc)
        CASTS.append((key, src2d, r0, kch, c0, pc))

    for l in range(L):
        for nm, c0, pc in (("f", O_F, 512), ("nq", O_NQ, 512), ("nk", O_NK, 512), ("nv", O_NV, 512), ("cq", O_CQ, 256), ("ckv", O_CKV, 128)):
            wpiece((l, nm), w_in[l], 0, 8, c0, pc)
        for gi in range(3):
            for hb in range(2):
                wpiece((l, "g", gi, hb), w_in[l], 0, 8, O_G + gi * D + hb * 512, 512)
                wpiece((l, "br", gi, hb), w_br[l], gi * 512, 4, hb * 512, 512)
        for hb in range(2):
            wpiece((l, "o", hb), w_out[l], 0, 8, hb * 512, 512)
        for j in range(6):
            pc = 512 if j < 5 else 256
            wpiece((l, "fg", j), w_fg[l], 0, 8, j * 512, pc)
            wpiece((l, "fu", j), w_fu[l], 0, 8, j * 512, pc)
        for j in range(4):
            wpiece((l, "fd", j), w_fd[l], 0, 22, j * 256, 256)
    XL = [dscr("XL%d" % i, [NTL, 128, 8 * NT], F32) for i in range(2)]
    XC = [dscr("XC%d" % i, [128, 8 * CTX], F32) for i in range(2)]
    NAQT = dscr("NAQT", [4, 128, T]); NAKW = dscr("NAKW", [4, 128, TW]); NBL = T // 128; NWB = TW // 128
    NAVW = dscr("NAVW", [8 * 128, NWB * 65])
    HK = dscr("HK", [512, 512]); HKA = dscr("HKA", [4 * 512, 512]); HV = dscr("HV", [512, 512]); HVA = dscr("HVA", [4 * 512, 512])
    MLAQT = dscr("MLAQT", [8, 96, T]); MLAKT = dscr("MLAKT", [768, T]); MLAKTA = [dscr("MLAKTA%d" % h, [4 * 96, T]) for h in range(8)]
    MLAV = dscr("MLAV", [8 * 128, NBL * 65])
    MLAVA = [dscr("MLAVA%d" % h, [4 * 128, NBL * 65]) for h in range(8)]
    HF = 2 if T * 512 > (1 << 20) else 1; CH = T // HF
    FAB = dscr("FAB", [4 * T, 256]); FABA = [dscr("FABA%d" % j, [4 * CH, 256]) for j in range(4 * HF)]
    NAQTc = dscr("NAQTc", [4, 128, CTX]); NAKTc = dscr("NAKTc", [4, 128, CTX]); NAVc = dscr("NAVc", [8 * 128, 2 * 65])
    MLAQTc = dscr("MLAQTc", [8, 96, CTX]); MLAKTc = dscr("MLAKTc", [8, 96, CTX]); MLAVc = dscr("MLAVc", [8 * 128, 2 * 65])
    FABc = dscr("FABc", [CTX, 4, 256])
    YF = dscr("YF", [NTL, 128, 4 * NT]); YNA = dscr("YNA", [NTL, 128, 4 * NT]); YMLA = dscr("YMLA", [NTL, 128, 4 * NT])
    YFc = dscr("YFc", [4, 128, CTX]); YNAc = dscr("YNAc", [4, 128, CTX]); YMLAc = dscr("YMLAc", [4, 128, CTX])
    EBT = [dscr("EBT%d" % l, [8, 64, 23 * 64]) for l in range(L)]

    GROUPS = [[0, 1, 2, 3], [4, 5, 6, 7]]
    ARENA = 196 * 1024

    with ExitStack() as st:
        P = Prog(nc, st)
        ar_ap = st.enter_context(nc.sbuf_tensor("arena", [128, ARENA], U8))
        ps_ap = st.enter_context(nc.psum_tensor("ps", [128, 8, 512], F32))
        ar = Arena(ar_ap, ARENA)
        psb = [P.buf("ps%d" % i) for i in range(8)]
        PS = lambda i: ps_ap[:, i, :]
        dram = {}

        def db(name):
            if name not in dram:
                dram[name] = P.buf(name)
            return dram[name]

        rr = {"lin": 0, "ev": 0, "bank": 0}

        def xview(i, lat, ti):
            if lat:
                return XL[i][ti].rearrange("p (k t) -> p k t", k=8)
            return XC[i].rearrange("p (k t) -> p k t", k=8)

        def evac_eng():
            rr["ev"] ^= 1
            return "act" if rr["ev"] else "dve"

        def copy_op(eng, out, in_, rd, wr):
            if eng == "act":
                P.op("act", lambda e: e.copy(out=out, in_=in_), rd, wr)
            else:
                P.op(eng, lambda e: e.tensor_copy(out=out, in_=in_), rd, wr)

        def pt(name, shape, dt, parts=128):
            return ar.alloc(shape, dt, parts=parts), P.buf(name)

        def ptk(name, shape, dt):
            return ar.alloc(shape, dt), [P.buf("%s_%d" % (name, k)) for k in range(shape[0])]

        def bk(b, k):
            return b[k] if isinstance(b, list) else b

        def ball(b):
            return list(b) if isinstance(b, list) else [b]

        ident, b_ident = pt("ident", [128], F32)
        ones, b_ones = pt("ones", [128], F32)
        epsb, b_eps = pt("eps", [1], F32)
        cs128, b_cs = pt("cs128", [256], BF16)
        c1, b_c1 = pt("c1", [N1], BF16); s1, b_s1 = pt("s1", [N1], BF16); ns1, b_ns1 = pt("ns1", [N1], BF16)
        mre, b_mre = pt("mre", [N1, 32], BF16); mimn, b_mim = pt("mimn", [N1, 32], BF16)
        cc256, b_cc = pt("cc256", [2, 256], BF16); sc256, b_sc = pt("sc256", [2, 256], BF16)
        rv, b_rv = pt("rv", [3, 8, 8], F32); sel, b_sel = pt("sel", [8], F32)
        lnin, b_lnin = pt("lnin", [2, 8], F32)
        modv, b_mod_ = pt("modv", [L, 48, 2], F32); mod1, b_mod1 = pt("mod1", [L, 48, 2], F32)
        lnp, b_lnp = pt("lnp", [L, 4, 8], F32)
        gq, b_gq = pt("gq", [L, 2], F32); gkv, b_gkv = pt("gkv", [L, 1], F32)
        wkr, b_wkr = pt("wkr", [8, 2, 96], BF16)
        wq, b_wq = pt("wq", [2, 8, 96], BF16); wqs, b_wqs = pt("wqs", [2, 8, 96], BF16)
        wk, b_wk = pt("wk", [8, 64], BF16); wuv, b_wuv = pt("wuv", [512], BF16)
        PERSIST = ar.off

        P.dma("sp", ident, k_id, [], [b_ident])
        P.op("pool", lambda e: e.memset(ones, 1.0), [], [b_ones])
        P.op("pool", lambda e: e.memset(epsb, EPS), [], [b_eps])
        P.dma("sp", cs128, k_cs, [], [b_cs])
        P.dma("sp", c1[0:N1], k_c1, [], [b_c1]); P.dma("sp", s1[0:N1], k_s1, [], [b_s1]); P.dma("sp", ns1[0:N1], k_ns1, [], [b_ns1])
        P.dma("sp", mre, k_mre.rearrange("p (a b) -> p a b", b=32), [], [b_mre])
        P.dma("sp", mimn, k_mim.rearrange("p (a b) -> p a b", b=32), [], [b_mim])
        P.dma("sp", cc256, k_cc.rearrange("(a p) k -> p a k", p=128), [], [b_cc])
        P.dma("sp", sc256, k_sc.rearrange("(a p) k -> p a k", p=128), [], [b_sc])
        P.dma("sp", rv, k_rv.rearrange("p (t a b) -> p t a b", t=3, a=8), [], [b_rv])
        P.dma("sp", sel, k_sel, [], [b_sel])
        P.dma("sp", lnin[:, 0, :], lng_in, [], [b_lnin])
        P.dma("sp", lnin[:, 1, :], lnb_in, [], [b_lnin])
        for l in range(L):
            for i, src in enumerate((ln1g, ln1b, ln2g, ln2b)):
                P.dma("sp", lnp[:, l, i, :], src[l], [], [b_lnp])
            P.dma("sp", gq[:, l, :], gq_in[l], [], [b_gq])
            P.dma("sp", gkv[:, l, :], gkv_in[l], [], [b_gkv])

        b_wcast = P.buf("wcast")
        b_wcast1 = P.buf("wcast1"); b_wcast3 = P.buf("wcast3")
        late_casts = []
        mid_casts = []
        P1KEYS = ("f", "nq", "nk", "nv", "cq", "ckv")

        def wbuf_of(key):
            if key[0] == 1:
                return b_wcast1
            return b_wcast if key[1] in P1KEYS else b_wcast3

        for (key, src2d, r0, kch, c0, pc) in CASTS:
            dstp = WP[key][0]
            for k in range(kch):
                args = (dstp[:, k * pc:(k + 1) * pc], src2d[r0 + k * 128:r0 + (k + 1) * 128, c0:c0 + pc])
                if key[0] == 0 and key[1] in P1KEYS:
                    P.dma("pool", args[0], args[1], [], [b_wcast])
                elif key[0] == 0:
                    mid_casts.append(args)
                else:
                    late_casts.append(args)

        def issue_late_casts(n):
            for _ in range(min(n, len(late_casts))):
                o, i = late_casts.pop(0)
                P.dma("pool", o, i, [], [b_wcast1])

        def issue_mid_casts(n):
            for _ in range(min(n, len(mid_casts))):
                o, i = mid_casts.pop(0)
                P.dma("pool", o, i, [], [b_wcast3])

        ar.reset(PERSIST)
        cs_t, b_cst = pt("cs_t", [2, 8], F32)
        bm_t, b_bmt = pt("bm_t", [L, 48], F32)
        P.dma("sp", cs_t[:, 0, :], c_in, [], [b_cst])
        P.dma("sp", cs_t[:, 1, :], cc_in, [], [b_cst])
        P.op("act", lambda e: e.activation(out=cs_t, in_=cs_t, func=AF.Silu), [b_cst], [b_cst])
        for l in range(L):
            P.dma("sp", bm_t[:, l, :], b_mod[l], [], [b_bmt])
        wm = [pt("wm%d" % i, [8, 512], F32) for i in range(2)]
        for l in range(L):
            for j in range(12):
                wt_, wb_ = wm[j % 2]
                P.dma("sp", wt_, w_mod[l][:, j * 512:(j + 1) * 512].rearrange("(k p) m -> p k m", p=128), [], [wb_])
                for cc in range(4):
                    ch = j * 4 + cc
                    bank = ch % 4
                    for k in range(8):
                        P.op("pe", lambda e, wt_=wt_, k=k, cc=cc, bank=bank: e.matmul(
                            ps_ap[:, bank, 0:2], lhsT=wt_[:, k, cc * 128:(cc + 1) * 128], rhs=cs_t[:, :, k],
                            start=(k == 0), stop=(k == 7)), [wb_, b_cst], [psb[bank]], inc=(k == 7))
                    P.op("dve", lambda e, l=l, ch=ch, bank=bank: e.tensor_scalar(
                        out=modv[:, l, ch, :], in0=ps_ap[:, bank, 0:2], scalar1=bm_t[:, l, ch:ch + 1], scalar2=None,
                        op0=ALU.add), [psb[bank], b_bmt], [b_mod_])
        P.op("dve", lambda e: e.tensor_scalar(out=mod1, in0=modv, scalar1=1.0, scalar2=None, op0=ALU.add), [b_mod_], [b_mod1])

        def MOD(l, idx, s, plus1):
            src = mod1 if plus1 else modv
            return src[:, l, idx, s:s + 1]

        eb_t = [pt("ebt%d" % i, [23, 64], F32, parts=64) for i in range(2)]
        eb_o = [pt("ebo%d" % i, [23, 64], BF16, parts=64) for i in range(2)]
        for l in range(L):
            for h in range(8):
                (ti, tb), (oi, ob) = eb_t[h % 2], eb_o[h % 2]
                P.dma("sp", ti, rpbT[l, h].rearrange("e k q -> k e q"), [], [tb])
                P.op("act", lambda e, ti=ti, oi=oi: e.activation(out=oi, in_=ti, func=AF.Exp), [tb], [ob])
                P.dma("pool", EBT[l][h], oi.rearrange("p a b -> p (a b)"), [ob], [db("EBT%d" % l)], own=ob)
        P.barrier()
        ar.reset(PERSIST)

        def load_layer_small(l):
            st_uq, b1 = pt("st_uq", [2, 512], F32); st_qr, b2 = pt("st_qr", [2, 256], F32)
            P.dma("sp", st_uq, w_uq[l].rearrange("(k p) m -> p k m", p=128), [], [b1])
            P.dma("sp", st_qr, w_qr[l].rearrange("(k p) m -> p k m", p=128), [], [b2])
            P.op("pool", lambda e: e.memset(wqs, 0.0), [], [b_wqs])
            P.op("pool", lambda e: e.memset(wkr, 0.0), [], [b_wkr])
            uqv = st_uq.rearrange("p k (h d) -> p k h d", h=8); qrv = st_qr.rearrange("p k (h d) -> p k h d", h=8)
            P.op("dve", lambda e: e.tensor_copy(out=wq[:, :, :, 0:64], in_=uqv), [b1], [b_wq])
            P.op("dve", lambda e: e.tensor_copy(out=wq[:, :, :, 64:96], in_=qrv), [b2], [b_wq])
            P.op("dve", lambda e: e.tensor_copy(out=wqs[:, :, :, 64:80], in_=qrv[:, :, :, 16:32]), [b2], [b_wqs])
            P.op("dve", lambda e: e.tensor_copy(out=wqs[:, :, :, 80:96], in_=qrv[:, :, :, 0:16]), [b2], [b_wqs])
            st_kr, b3 = pt("st_kr", [8, 32], F32)
            P.dma("sp", st_kr, w_in[l][:, O_KR:O_KR + 32].rearrange("(k p) m -> p k m", p=128), [], [b3])
            P.op("dve", lambda e: e.tensor_copy(out=wkr[:, :, 0, 64:96], in_=st_kr), [b3], [b_wkr])
            P.op("dve", lambda e: e.tensor_copy(out=wkr[:, :, 1, 64:80], in_=st_kr[:, :, 16:32]), [b3], [b_wkr])
            P.op("dve", lambda e: e.tensor_copy(out=wkr[:, :, 1, 80:96], in_=st_kr[:, :, 0:16]), [b3], [b_wkr])
            st_uk, b4 = pt("st_uk", [512], F32); st_uv, b5 = pt("st_uv", [512], F32)
            P.dma("sp", st_uk, w_uk[l], [], [b4]); P.dma("sp", st_uv, w_uv[l], [], [b5])
            P.op("dve", lambda e: e.tensor_copy(out=wk, in_=st_uk.rearrange("p (h d) -> p h d", h=8)), [b4], [b_wk])
            P.op("dve", lambda e: e.tensor_copy(out=wuv, in_=st_uv), [b5], [b_wuv])

        wslots = []

        def load_piece(key):
            wd, kch, pc = WP[key]
            slot, sbuf_ = wslots[rr["lin"] % len(wslots)]
            rr["lin"] += 1
            P.dma("sp", slot[:, 0:kch * pc], wd, [wbuf_of(key)], [sbuf_])
            return slot[:, 0:kch * pc].rearrange("p (k m) -> p k m", k=kch), sbuf_, kch, pc

        def mm_chunk(piece, c0, rows, act, abuf, nt):
            sv, sbuf_, kch, pc = piece
            bank = rr["bank"] % 4; rr["bank"] += 1
            for k in range(kch):
                P.op("pe", lambda e, k=k: e.matmul(ps_ap[0:rows, bank, 0:nt], lhsT=sv[:, k, c0:c0 + rows], rhs=act[:, k, 0:nt],
                                                   start=(k == 0), stop=(k == kch - 1)), [sbuf_, bk(abuf, k)], [psb[bank]], inc=(k == kch - 1))
            return ps_ap[0:rows, bank, 0:nt], psb[bank]

        def linear_fm(keys, act, abuf, nt, consume):
            ci = 0
            for key in keys:
                piece = load_piece(key)
                for c0 in range(0, piece[3], 128):
                    rows = min(128, piece[3] - c0)
                    ps, pb = mm_chunk(piece, c0, rows, act, abuf, nt)
                    consume(ci, ps, pb, rows)
                    ci += 1

        def ln_fm(r, rbuf, nt, gap, bap, sq, sqbuf, st4, st4buf):
            for k in range(8):
                if k % 2 == 0:
                    P.op("act", lambda e, k=k: e.activation(out=sq[:, k, 0:nt], in_=r[:, k, 0:nt], func=AF.Square), [bk(rbuf, k)], [bk(sqbuf, k)])
                else:
                    P.op("pool", lambda e, k=k: e.tensor_tensor(out=sq[:, k, 0:nt], in0=r[:, k, 0:nt], in1=r[:, k, 0:nt], op=ALU.mult), [bk(rbuf, k)], [bk(sqbuf, k)])
            for k in range(8):
                P.op("pe", lambda e, k=k: e.matmul(ps_ap[:, 4, 0:nt], lhsT=ones, rhs=r[:, k, 0:nt], start=(k == 0), stop=(k == 7)),
                     [b_ones, bk(rbuf, k)], [psb[4]], inc=(k == 7))
            for k in range(8):
                P.op("pe", lambda e, k=k: e.matmul(ps_ap[:, 5, 0:nt], lhsT=ones, rhs=sq[:, k, 0:nt], start=(k == 0), stop=(k == 7)),
                     [b_ones, bk(sqbuf, k)], [psb[5]], inc=(k == 7))
            mean = st4[:, 0, 0:nt]; msq = st4[:, 1, 0:nt]; rstd = st4[:, 2, 0:nt]
            P.op("dve", lambda e: e.tensor_scalar(out=mean, in0=ps_ap[:, 4, 0:nt], scalar1=1.0 / D, scalar2=None, op0=ALU.mult), [psb[4]], [st4buf])
            P.op("dve", lambda e: e.tensor_tensor(out=msq, in0=mean, in1=mean, op=ALU.mult), [st4buf], [st4buf])
            P.op("dve", lambda e: e.scalar_tensor_tensor(out=rstd, in0=ps_ap[:, 5, 0:nt], scalar=1.0 / D, in1=msq, op0=ALU.mult, op1=ALU.subtract), [psb[5], st4buf], [st4buf])
            P.op("act", lambda e: e.activation(out=rstd, in_=rstd, func=AF.Ln, bias=epsb[:, 0:1], scale=1.0), [st4buf, b_eps], [st4buf])
            P.op("act", lambda e: e.activation(out=rstd, in_=rstd, func=AF.Exp, scale=-0.5), [st4buf], [st4buf])
            for k in range(8):
                rk = r[:, k, 0:nt]
                P.op("pool", lambda e, rk=rk: e.tensor_tensor(out=rk, in0=rk, in1=mean, op=ALU.subtract), [bk(rbuf, k), st4buf], [bk(rbuf, k)])
                P.op("dve", lambda e, rk=rk: e.tensor_tensor(out=rk, in0=rk, in1=rstd, op=ALU.mult), [bk(rbuf, k), st4buf], [bk(rbuf, k)])
                P.op("act", lambda e, rk=rk, k=k: e.activation(out=rk, in_=rk, func=AF.Identity, bias=bap[:, k:k + 1], scale=gap[:, k:k + 1]), [bk(rbuf, k), b_lnp, b_lnin], [bk(rbuf, k)])

        def rms_fm(src, sbuf_, kch, nt, gsc, dst, dbuf, sq, sqbuf, st4, st4buf):
            for k in range(kch):
                P.op("act", lambda e, k=k: e.activation(out=sq[:, k, 0:nt], in_=src[:, k, 0:nt], func=AF.Square), [sbuf_], [sqbuf])
            for k in range(kch):
                P.op("pe", lambda e, k=k: e.matmul(ps_ap[:, 5, 0:nt], lhsT=ones, rhs=sq[:, k, 0:nt], start=(k == 0), stop=(k == kch - 1)),
                     [b_ones, sqbuf], [psb[5]], inc=(k == kch - 1))
            rstd = st4[:, 3, 0:nt]
            P.op("act", lambda e: e.activation(out=rstd, in_=ps_ap[:, 5, 0:nt], func=AF.Ln, bias=epsb[:, 0:1], scale=1.0 / (kch * 128)), [psb[5], b_eps], [st4buf])
            P.op("act", lambda e: e.activation(out=rstd, in_=rstd, func=AF.Exp, scale=-0.5), [st4buf], [st4buf])
            for k in range(kch):
                P.op("dve", lambda e, k=k: e.scalar_tensor_tensor(out=dst[:, k, 0:nt], in0=src[:, k, 0:nt], scalar=gsc[:, k:k + 1], in1=rstd, op0=ALU.mult, op1=ALU.mult),
                     [sbuf_, st4buf, b_gq, b_gkv], [dbuf])

        att_state = {}
        SBANKS = (0, 1, 2, 4, 5, 6)

        def attend(QT, qbuf, kbs, scale, nq, ytile, ybuf):
            ptl = att_state["ptl"]; rs, rsb = att_state["rs"]; bcs, bcsb = att_state["bc"]
            ob = att_state["obank"]; att_state["obank"] = 3 if ob == 7 else 7
            n = len(kbs)

            def S(i):
                KT, kb_, V, vb_, eb, ebb = kbs[i]
                bank = SBANKS[att_state["si"] % 6]; att_state["si"] += 1
                P.op("pe", lambda e: e.matmul(ps_ap[:, bank, 0:nq], lhsT=KT, rhs=QT, start=True, stop=True), [kb_, qbuf], [psb[bank]])
                pt_, pb_ = ptl[att_state["pi"] % len(ptl)]; att_state["pi"] += 1
                P.op("act", lambda e: e.activation(out=pt_[:, 0:nq], in_=ps_ap[:, bank, 0:nq], func=AF.Exp, scale=scale), [psb[bank]], [pb_])
                if eb is not None:
                    P.op("dve", lambda e: e.tensor_tensor(out=pt_[:, 0:nq], in0=pt_[:, 0:nq], in1=eb, op=ALU.mult), [pb_, ebb], [pb_])
                return pt_, pb_

            def PV(i, pt_, pb_):
                KT, kb_, V, vb_, eb, ebb = kbs[i]
                P.op("pe", lambda e: e.matmul(ps_ap[0:65, ob, 0:nq], lhsT=V, rhs=pt_[:, 0:nq], start=(i == 0), stop=(i == n - 1)), [vb_, pb_], [psb[ob]])

            pend = []
            for i in range(n):
                pend.append((i,) + S(i))
                if len(pend) > 3:
                    PV(*pend.pop(0))
            while pend:
                PV(*pend.pop(0))
            P.op("dve", lambda e: e.reciprocal(out=rs[64:65, 0:nq], in_=ps_ap[64:65, ob, 0:nq]), [psb[ob]], [rsb])
            bb = SBANKS[att_state["si"] % 6]; att_state["si"] += 1
            P.op("pe", lambda e: e.matmul(ps_ap[0:64, bb, 0:nq], lhsT=ones[64:65, 0:64], rhs=rs[64:65, 0:nq], start=True, stop=True), [b_ones, rsb], [psb[bb]])
            P.op("act", lambda e: e.copy(out=bcs[0:64, 0:nq], in_=ps_ap[0:64, bb, 0:nq]), [psb[bb]], [bcsb])
            P.op("dve", lambda e: e.tensor_tensor(out=ytile, in0=ps_ap[0:64, ob, 0:nq], in1=bcs[0:64, 0:nq], op=ALU.mult), [psb[ob], bcsb], [ybuf])

        def phase1(l, xt, xbuf, nt, s, tok0, tidx, T1, store_x=True):
            lat = (s == 0)
            nsub = nt // 128
            A = T1["a"]; bA = T1["ab"]
            for k in range(8):
                P.op("dve", lambda e, k=k: e.tensor_scalar(out=A[:, k, 0:nt], in0=xt[:, k, 0:nt], scalar1=MOD(l, 8 + k, s, True), scalar2=MOD(l, k, s, False),
                                                           op0=ALU.mult, op1=ALU.add), [bk(xbuf, k), b_mod_, b_mod1], [bk(bA, k)])
            xdst = xview(l % 2, lat, tok0 // NT)
            if store_x:
                P.dma("pool", xdst, xt[:, :, 0:nt], ball(xbuf), [db("XL%d" % (l % 2) if lat else "XC%d" % (l % 2))], own=bk(xbuf, 0))
            zt_, bz = T1["z"]; fab, bfab = T1["fab"]

            def cons_f(ci, ps, pb, rows):
                g = ci
                copy_op(evac_eng(), zt_[:, 0:nt], ps, [pb], [bz])
                for sub in range(nsub):
                    P.op("pe", lambda e, sub=sub: e.matmul(ps_ap[:, 6, 0:256], lhsT=zt_[:, sub * 128:(sub + 1) * 128], rhs=cs128, start=True, stop=True), [bz, b_cs], [psb[6]])
                    ov = fab[:, sub, g]; iv = ps_ap[:, 6, 0:256]
                    copy_op(evac_eng(), ov, iv, [psb[6]], [bfab])
            linear_fm([(l, "f")], A, bA, nt, cons_f)
            for sub in range(nsub):
                if lat:
                    dst = FAB.rearrange("(g t) c -> t g c", g=4)[tok0 + sub * 128: tok0 + (sub + 1) * 128]
                    P.dma("pool", dst, fab[:, sub], [bfab], [db("FAB")], own=bfab)
                else:
                    P.dma("pool", FABc[sub * 128:(sub + 1) * 128], fab[:, sub], [bfab], [db("FABc")], own=bfab)
            for (wkey, key) in (("nq", "naq"), ("nk", "nak")):
                tl, tb = T1[key]

                def cons(ci, ps, pb, rows, tl=tl, tb=tb):
                    copy_op(evac_eng(), tl[:, ci, 0:nt], ps, [pb], [tb])
                linear_fm([(l, wkey)], A, bA, nt, cons)
                if key == "naq":
                    dst = (NAQT[:, :, tok0:tok0 + nt] if lat else NAQTc[:, :, 0:nt]).rearrange("k p t -> p k t")
                    P.dma("pool", dst, tl[:, :, 0:nt], [tb], [db("NAQT" if lat else "NAQTc")], own=tb)
                else:
                    dst = (NAKW[:, :, 256 + tok0:256 + tok0 + nt] if lat else NAKTc[:, :, 0:nt]).rearrange("k p t -> p k t")
                    P.dma("pool", dst, tl[:, :, 0:nt], [tb], [db("NAKW" if lat else "NAKTc")], own=tb)
                    if lat and tidx == 0:
                        P.dma("pool", HK.rearrange("(k p) t -> p k t", p=128)[:, :, 0:256], tl[:, :, 0:256], [tb], [db("HK")], own=tb)
                    if lat and tidx == NTL - 1:
                        P.dma("pool", HK.rearrange("(k p) t -> p k t", p=128)[:, :, 256:512], tl[:, :, nt - 256:nt], [tb], [db("HK")], own=tb)
            wvv, bwv, _, _ = load_piece((l, "nv"))
            nav, bnav = T1["nav"]
            for sub in range(nsub):
                for k in range(8):
                    P.op("pe", lambda e, k=k, sub=sub: e.matmul(ps_ap[:, 7, :], lhsT=A[:, k, sub * 128:(sub + 1) * 128], rhs=wvv[:, k, :], start=(k == 0), stop=(k == 7)),
                         [bk(bA, k), bwv], [psb[7]], inc=(k == 7))
                copy_op(evac_eng(), nav[:, :, sub, 0:64], ps_ap[:, 7, :].rearrange("p (h d) -> p h d", h=8), [psb[7]], [bnav])
            vview = lambda Vd: Vd.rearrange("(h p) (b c) -> p h b c", p=128, c=65)
            if lat:
                b0 = 2 + (tok0 // 128)
                P.dma("pool", vview(NAVW)[:, :, b0:b0 + nsub, :], nav[:, :, 0:nsub, :], [bnav], [db("NAVW")], own=bnav)
                if tidx == 0:
                    for s_ in range(2):
                        P.dma("pool", HV[s_ * 128:(s_ + 1) * 128].rearrange("p (h d) -> p h d", h=8), nav[:, :, s_, 0:64], [bnav], [db("HV")], own=bnav)
                if tidx == NTL - 1:
                    for s_ in range(2):
                        P.dma("pool", HV[256 + s_ * 128:256 + (s_ + 1) * 128].rearrange("p (h d) -> p h d", h=8), nav[:, :, nsub - 2 + s_, 0:64], [bnav], [db("HV")], own=bnav)
            else:
                P.dma("pool", vview(NAVc)[:, :, 0:nsub, :], nav[:, :, 0:nsub, :], [bnav], [db("NAVc")], own=bnav)
            cq, bcq = T1["cq"]; ckv, bckv = T1["ckv"]

            def cons_cq(ci, ps, pb, rows):
                copy_op(evac_eng(), cq[:, ci, 0:nt], ps, [pb], [bcq])
            linear_fm([(l, "cq")], A, bA, nt, cons_cq)

            def cons_ckv(ci, ps, pb, rows):
                copy_op(evac_eng(), ckv[:, 0, 0:nt], ps, [pb], [bckv])
            linear_fm([(l, "ckv")], A, bA, nt, cons_ckv)
            sq, bsq = T1["sq"]; st4, bst4 = T1["st4"]
            cqn, bcqn = T1["cqn"]; ckvn, bckvn = T1["ckvn"]
            rms_fm(cq, bcq, 2, nt, gq[:, l, :], cqn, bcqn, sq, bsq, st4, bst4)
            rms_fm(ckv, bckv, 1, nt, gkv[:, l, :], ckvn, bckvn, sq, bsq, st4, bst4)
            rope, brope = T1["rope"]
            if lat:
                P.dma("sp", rope[64:96, :, 0:nt], k_rope[:, :, tok0:tok0 + nt].rearrange("a p t -> p a t"), [], [brope])
            tmp, btmp = T1["tmp"]
            krr, bkrr = T1["krr"]
            for v in range(2 if lat else 1):
                for k in range(8):
                    P.op("pe", lambda e, k=k, v=v: e.matmul(ps_ap[0:96, 6 + v, 0:nt], lhsT=wkr[:, k, v, :], rhs=A[:, k, 0:nt], start=(k == 0), stop=(k == 7)),
                         [b_wkr, bk(bA, k)], [psb[6 + v]], inc=(k == 7))
            if lat:
                P.op("dve", lambda e: e.tensor_tensor(out=tmp[64:96, 0, 0:nt], in0=ps_ap[64:96, 6, 0:nt], in1=rope[64:96, 0, 0:nt], op=ALU.mult), [psb[6], brope], [btmp])
                P.op("dve", lambda e: e.tensor_tensor(out=tmp[64:96, 1, 0:nt], in0=ps_ap[64:96, 7, 0:nt], in1=rope[64:96, 1, 0:nt], op=ALU.mult), [psb[7], brope], [btmp])
                P.op("dve", lambda e: e.tensor_tensor(out=krr[64:96, 0:nt], in0=tmp[64:96, 0, 0:nt], in1=tmp[64:96, 1, 0:nt], op=ALU.add), [btmp], [bkrr])
            else:
                P.op("act", lambda e: e.copy(out=krr[64:96, 0:nt], in_=ps_ap[64:96, 6, 0:nt]), [psb[6]], [bkrr])
            for h in range(8):
                qt_, bq_ = T1["qt"][h % 2]
                for v in range(2 if lat else 1):
                    wsrc = wq if v == 0 else wqs
                    for k in range(2):
                        P.op("pe", lambda e, k=k, v=v, h=h, wsrc=wsrc: e.matmul(ps_ap[0:96, 6 + v, 0:nt], lhsT=wsrc[:, k, h, :], rhs=cqn[:, k, 0:nt], start=(k == 0), stop=(k == 1)),
                             [b_wq, b_wqs, bcqn], [psb[6 + v]], inc=(k == 1))
                if lat:
                    P.op("act", lambda e, qt_=qt_: e.copy(out=qt_[0:64, 0:nt], in_=ps_ap[0:64, 6, 0:nt]), [psb[6]], [bq_])
                    P.op("dve", lambda e: e.tensor_tensor(out=tmp[64:96, 0, 0:nt], in0=ps_ap[64:96, 6, 0:nt], in1=rope[64:96, 0, 0:nt], op=ALU.mult), [psb[6], brope], [btmp])
                    P.op("dve", lambda e: e.tensor_tensor(out=tmp[64:96, 1, 0:nt], in0=ps_ap[64:96, 7, 0:nt], in1=rope[64:96, 1, 0:nt], op=ALU.mult), [psb[7], brope], [btmp])
                    P.op("dve", lambda e, qt_=qt_: e.tensor_tensor(out=qt_[64:96, 0:nt], in0=tmp[64:96, 0, 0:nt], in1=tmp[64:96, 1, 0:nt], op=ALU.add), [btmp], [bq_])
                else:
                    P.op("act", lambda e, qt_=qt_: e.copy(out=qt_[0:96, 0:nt], in_=ps_ap[0:96, 6, 0:nt]), [psb[6]], [bq_])
                qd = MLAQT[h, :, tok0:tok0 + nt] if lat else MLAQTc[h, :, 0:nt]
                P.dma("pool", qd, qt_[0:96, 0:nt], [bq_], [db("MLAQT" if lat else "MLAQTc")], own=bq_)
                kt_, bk_ = T1["kt"][h % 2]
                bank = 4 + (h % 2)
                P.op("pe", lambda e, h=h, bank=bank: e.matmul(ps_ap[0:64, bank, 0:nt], lhsT=wk[:, h, :], rhs=ckvn[:, 0, 0:nt], start=True, stop=True), [b_wk, bckvn], [psb[bank]])
                P.op("dve", lambda e, kt_=kt_, bank=bank: e.tensor_copy(out=kt_[0:64, 0:nt], in_=ps_ap[0:64, bank, 0:nt]), [psb[bank]], [bk_])
                P.op("act", lambda e, kt_=kt_: e.copy(out=kt_[64:96, 0:nt], in_=krr[64:96, 0:nt]), [bkrr], [bk_])
                kd = MLAKT[h * 96:(h + 1) * 96, tok0:tok0 + nt] if lat else MLAKTc[h, :, 0:nt]
                P.dma("pool", kd, kt_[0:96, 0:nt], [bk_], [db("MLAKT" if lat else "MLAKTc")], own=bk_)
                if l == 0:
                    issue_mid_casts(6)
            mv, bmv = T1["mlav"]
            for sub in range(nsub):
                P.op("pe", lambda e, sub=sub: e.matmul(ps_ap[:, 7, :], lhsT=ckvn[:, 0, sub * 128:(sub + 1) * 128], rhs=wuv, start=True, stop=True), [bckvn, b_wuv], [psb[7]])
                copy_op(evac_eng(), mv[:, :, sub, 0:64], ps_ap[:, 7, :].rearrange("p (h d) -> p h d", h=8), [psb[7]], [bmv])
            if lat:
                b0 = tok0 // 128
                P.dma("pool", vview(MLAV)[:, :, b0:b0 + nsub, :], mv[:, :, 0:nsub, :], [bmv], [db("MLAV")], own=bmv)
            else:
                P.dma("pool", vview(MLAVc)[:, :, 0:nsub, :], mv[:, :, 0:nsub, :], [bmv], [db("MLAVc")], own=bmv)

        def alloc_p1():
            T1 = {}
            T1["a"], T1["ab"] = ptk("a", [8, NT], BF16)
            T1["z"] = pt("z", [NT], BF16)
            T1["fab"] = pt("fab", [4, 4, 256], BF16)
            T1["naq"] = pt("naq", [4, NT], BF16); T1["nak"] = pt("nak", [4, NT], BF16)
            T1["nav"] = pt("nav", [8, 4, 65], BF16)
            P.op("pool", lambda e: e.memset(T1["nav"][0], 1.0), [], [T1["nav"][1]])
            T1["cq"] = pt("cq", [2, NT], F32); T1["ckv"] = pt("ckv", [1, NT], F32)
            T1["cqn"] = pt("cqn", [2, NT], BF16); T1["ckvn"] = pt("ckvn", [1, NT], BF16)
            T1["rope"] = pt("rope", [2, NT], F32); T1["tmp"] = pt("tmp", [2, NT], F32)
            T1["krr"] = pt("krr", [NT], BF16)
            T1["qt"] = [pt("qt%d" % i, [NT], BF16) for i in range(2)]
            T1["kt"] = [pt("kt%d" % i, [NT], BF16) for i in range(2)]
            T1["mlav"] = pt("mlav", [8, 4, 65], BF16)
            P.op("pool", lambda e: e.memset(T1["mlav"][0], 1.0), [], [T1["mlav"][1]])
            T1["sq"] = pt("sq1", [2, NT], F32); T1["st4"] = pt("st4_1", [4, NT], F32)
            return T1

        def input_ln(src_rows, nblk, xt, xbuf, W):
            for blk in range(nblk):
                xin, bxin = W["xin"][blk % 2]
                P.dma("sp", xin, src_rows[blk * 128:(blk + 1) * 128, :], [], [bxin])
                s4, bs4 = W["s4"]
                P.op("dve", lambda e, xin=xin: e.reduce_sum(out=s4[:, 0:1], in_=xin, axis=AX.X), [bxin], [bs4])
                P.op("dve", lambda e: e.tensor_scalar(out=s4[:, 1:2], in0=s4[:, 0:1], scalar1=-1.0 / D, scalar2=None, op0=ALU.mult), [bs4], [bs4])
                P.op("act", lambda e, xin=xin: e.activation(out=xin, in_=xin, func=AF.Identity, bias=s4[:, 1:2], scale=1.0), [bxin, bs4], [bxin])
                xsq, bxsq = W["xsq"]
                P.op("pool", lambda e, xin=xin: e.tensor_tensor(out=xsq, in0=xin, in1=xin, op=ALU.mult), [bxin], [bxsq])
                P.op("dve", lambda e: e.reduce_sum(out=s4[:, 2:3], in_=xsq, axis=AX.X), [bxsq], [bs4])
                P.op("act", lambda e: e.activation(out=s4[:, 3:4], in_=s4[:, 2:3], func=AF.Ln, bias=epsb[:, 0:1], scale=1.0 / D), [bs4, b_eps], [bs4])
                P.op("act", lambda e: e.activation(out=s4[:, 3:4], in_=s4[:, 3:4], func=AF.Exp, scale=-0.5), [bs4], [bs4])
                P.op("act", lambda e, xin=xin: e.activation(out=xin, in_=xin, func=AF.Identity, scale=s4[:, 3:4]), [bxin, bs4], [bxin])
                for k in range(8):
                    bank = k % 4
                    P.op("pe", lambda e, xin=xin, k=k, bank=bank: e.transpose(ps_ap[:, bank, 0:128], xin[:, k * 128:(k + 1) * 128], ident), [bxin, b_ident], [psb[bank]])
                    P.op("act" if k % 2 else "dve",
                         (lambda e, k=k, bank=bank, blk=blk: e.activation(out=xt[:, k, blk * 128:(blk + 1) * 128], in_=ps_ap[:, bank, 0:128], func=AF.Identity,
                                                                       bias=lnin[:, 1, k:k + 1], scale=lnin[:, 0, k:k + 1])) if k % 2 else
                         (lambda e, k=k, bank=bank, blk=blk: e.tensor_scalar(out=xt[:, k, blk * 128:(blk + 1) * 128], in0=ps_ap[:, bank, 0:128], scalar1=lnin[:, 0, k:k + 1],
                                                                          scalar2=lnin[:, 1, k:k + 1], op0=ALU.mult, op1=ALU.add)),
                         [psb[bank], b_lnin], [bk(xbuf, k)])

        def exchange():
            P.barrier()
            cl = [(MLAKT[h * 96:(h + 1) * 96, :], MLAKTA[h], "MLAKT", "MLAKTA") for h in range(8)]
            cl += [(MLAV[h * 128:(h + 1) * 128, :], MLAVA[h], "MLAV", "MLAVA") for h in range(8)]
            cl += [(FAB[j * CH:(j + 1) * CH, :], FABA[j], "FAB", "FABA") for j in range(4 * HF)]
            cl += [(HK, HKA, "HK", "HKA"), (HV, HVA, "HV", "HVA")]
            for (src, dst, sn, dn) in cl:
                P.collective("AllGather", src.opt(), dst.opt(), GROUPS, [db(sn)], [db(dn)])
            P.barrier()
            ar.reset(PERSIST)
            hk = [pt("hk%d" % i, [4, 512], BF16) for i in range(2)]
            ho = [pt("ho%d" % i, [2, 256], BF16) for i in range(2)]
            hkv = HKA.rearrange("(r k p) t -> k p r t", r=4, k=4)
            for c in range(4):
                (hi, hb), (oi, ob) = hk[c % 2], ho[c % 2]
                P.dma("sp", hi, hkv[c], [db("HKA")], [hb])
                for half, (t0, so) in enumerate(((256, 0), (0, 4))):
                    P.op("dve", lambda e, hi=hi, oi=oi, half=half, t0=t0, so=so: e.tensor_scalar(out=oi[:, half, :], in0=hi[:, 0, t0:t0 + 256], scalar1=sel[:, so:so + 1], scalar2=None, op0=ALU.mult), [hb, b_sel], [ob])
                    for r in range(1, 4):
                        P.op("dve", lambda e, hi=hi, oi=oi, half=half, t0=t0, so=so, r=r: e.scalar_tensor_tensor(out=oi[:, half, :], in0=hi[:, r, t0:t0 + 256], scalar=sel[:, so + r:so + r + 1], in1=oi[:, half, :], op0=ALU.mult, op1=ALU.add), [hb, b_sel, ob], [ob])
                P.dma("pool", NAKW[c, :, 0:256], oi[:, 0, :], [ob], [db("NAKW")], own=ob)
                P.dma("pool", NAKW[c, :, 256 + T:TW], oi[:, 1, :], [ob], [db("NAKW")], own=ob)
            hv = [pt("hv%d" % i, [4, 512], BF16) for i in range(2)]
            hvo = [pt("hvo%d" % i, [8, 65], BF16) for i in range(2)]
            for (oi, ob) in hvo:
                P.op("pool", lambda e, oi=oi: e.memset(oi, 1.0), [], [ob])
            hvv = HVA.rearrange("(r s p) c -> s p r c", r=4, s=4)
            for sblk in range(4):
                (hi, hb), (oi, ob) = hv[sblk % 2], hvo[sblk % 2]
                srcb = sblk + 2 if sblk < 2 else sblk - 2
                so = 0 if sblk < 2 else 4
                P.dma("sp", hi, hvv[srcb], [db("HVA")], [hb])
                ov = oi[:, :, 0:64]
                hv3 = lambda r, hi=hi: hi[:, r, :].rearrange("p (h d) -> p h d", h=8)
                P.op("dve", lambda e, ov=ov, hv3=hv3, so=so: e.tensor_scalar(out=ov, in0=hv3(0), scalar1=sel[:, so:so + 1], scalar2=None, op0=ALU.mult), [hb, b_sel], [ob])
                for r in range(1, 4):
                    P.op("dve", lambda e, ov=ov, hv3=hv3, so=so, r=r: e.scalar_tensor_tensor(out=ov, in0=hv3(r), scalar=sel[:, so + r:so + r + 1], in1=ov, op0=ALU.mult, op1=ALU.add), [hb, b_sel, ob], [ob])
                wb = sblk if sblk < 2 else 2 + NBL + (sblk - 2)
                P.dma("pool", NAVW.rearrange("(h p) (b c) -> p h b c", p=128, c=65)[:, :, wb, :], oi, [ob], [db("NAVW")], own=ob)
            P.barrier()
            ar.reset(PERSIST)

        def alloc_att():
            att_state.clear()
            att_state.update(si=0, pi=0, obank=7)
            att_state["ptl"] = [pt("p%d" % i, [512], BF16) for i in range(6)]
            att_state["rs"] = pt("rs", [512], F32)
            att_state["bc"] = pt("bcs", [512], F32)

        def phase2_mla(l):
            ar.reset(PERSIST)
            alloc_att()
            NK = CTX + N
            NKB = NK // 128
            KT = [pt("KT%d" % i, [NK], BF16) for i in range(2)]
            VA = [pt("VA%d" % i, [NKB, 65], BF16) for i in range(2)]
            qts = [pt("q%d" % i, [512], BF16) for i in range(3)]
            ys = [pt("y%d" % i, [512], BF16) for i in range(2)]
            qi = 0
            for h in range(8):
                (kt, kb_), (va, vb) = KT[h % 2], VA[h % 2]
                P.dma("sp", kt[0:96, 0:CTX], MLAKTc[h], [db("MLAKTc")], [kb_])
                P.dma("sp", va[:, 0:2, :].rearrange("p a b -> p (a b)"), MLAVc[h * 128:(h + 1) * 128, :], [db("MLAVc")], [vb])
                for r in range(4):
                    P.dma("sp", kt[0:96, CTX + r * T:CTX + (r + 1) * T], MLAKTA[h][r * 96:(r + 1) * 96, :], [db("MLAKTA")], [kb_])
                    P.dma("sp", va[:, 2 + r * NBL:2 + (r + 1) * NBL, :].rearrange("p a b -> p (a b)"), MLAVA[h][r * 128:(r + 1) * 128, :], [db("MLAVA")], [vb])
                kbs_all = [(kt[0:96, j * 128:(j + 1) * 128], kb_, va[:, j, :], vb, None, None) for j in range(NKB)]
                c, po = h // 2, (h % 2) * 64
                for qb in range(NTL):
                    qt_, bq_ = qts[qi % 3]; y_, by_ = ys[qi % 2]; qi += 1
                    P.dma("sp", qt_[0:96, :], MLAQT[h, :, qb * 512:(qb + 1) * 512], [db("MLAQT")], [bq_])
                    attend(qt_[0:96, :], bq_, kbs_all, MLA_SCALE, 512, y_[0:64, :], by_)
                    P.dma("pool", YMLA[qb, po:po + 64, c * NT:(c + 1) * NT], y_[0:64, :], [by_], [db("YMLA")], own=by_)
                    issue_late_casts(8)
                if l == 0:
                    qt_, bq_ = qts[qi % 3]; y_, by_ = ys[qi % 2]; qi += 1
                    P.dma("sp", qt_[0:96, 0:CTX], MLAQTc[h], [db("MLAQTc")], [bq_])
                    attend(qt_[0:96, 0:CTX], bq_, kbs_all[0:2], MLA_SCALE, CTX, y_[0:64, 0:CTX], by_)
                    P.dma("pool", YMLAc[c, po:po + 64, :], y_[0:64, 0:CTX], [by_], [db("YMLAc")], own=by_)
            issue_late_casts(10 ** 6)
            P.barrier()

        def phase2_na(l):
            ar.reset(PERSIST)
            alloc_att()
            KW = [pt("KW%d" % i, [TW], BF16) for i in range(2)]
            QN = [pt("QN%d" % i, [T], BF16) for i in range(2)]
            KC = [pt("KC%d" % i, [CTX], BF16) for i in range(2)]
            QC = [pt("QC%d" % i, [CTX], BF16) for i in range(2)]
            VN = [pt("VN%d" % i, [NWB, 65], BF16) for i in range(2)]
            VC = [pt("VC%d" % i, [2, 65], BF16) for i in range(2)]
            EBF = pt("EBF", [8, 8, 64], BF16)
            EBY = [pt("EBY%d" % i, [8, 512], BF16) for i in range(3)]
            ys = [pt("y%d" % i, [512], BF16) for i in range(2)]
            yi = 0
            for h in range(8):
                c, po = h // 2, (h % 2) * 64
                (kw, bkw), (qn, bqn), (kc, bkc), (qc, bqc) = KW[c % 2], QN[c % 2], KC[c % 2], QC[c % 2]
                (vn, bvn), (vc, bvc) = VN[h % 2], VC[h % 2]
                if h % 2 == 0:
                    P.dma("sp", kw, NAKW[c], [db("NAKW")], [bkw])
                    P.dma("sp", qn, NAQT[c], [db("NAQT")], [bqn])
                    P.dma("sp", kc, NAKTc[c], [db("NAKTc")], [bkc])
                    if l == 0:
                        P.dma("sp", qc, NAQTc[c], [db("NAQTc")], [bqc])
                P.dma("sp", vn.rearrange("p a b -> p (a b)"), NAVW[h * 128:(h + 1) * 128, :], [db("NAVW")], [bvn])
                P.dma("sp", vc.rearrange("p a b -> p (a b)"), NAVc[h * 128:(h + 1) * 128, :], [db("NAVc")], [bvc])
                ebf, bebf = EBF
                for kb in range(8):
                    for krl in range(2):
                        e0 = 15 - 2 * kb - krl
                        P.dma("sp", ebf[krl * 64:(krl + 1) * 64, kb, :, :].rearrange("p a b -> p (a b)"), EBT[l][h][:, e0 * 64:(e0 + 8) * 64], [db("EBT%d" % l)], [bebf])
                for ty in range(3):
                    eby, beby = EBY[ty]
                    P.op("dve", lambda e, eby=eby, ty=ty: e.tensor_tensor(out=eby.rearrange("p a (b c) -> p a b c", b=8), in0=ebf,
                                                                        in1=rv[:, ty].unsqueeze(3).broadcast_to([128, 8, 8, 64]), op=ALU.mult), [bebf, b_rv], [beby])
                ctxk = [(kc[po:po + 64, j * 128:(j + 1) * 128], bkc, vc[:, j, :], bvc, None, None) for j in range(2)]
                for qb in range(NQB):
                    ty = 0 if qb == 0 else (2 if qb == NQB - 1 else 1)
                    eby, beby = EBY[ty]
                    kbs = [(kw[po:po + 64, (4 * qb + j) * 128:(4 * qb + j + 1) * 128], bkw, vn[:, 4 * qb + j, :], bvn, eby[:, j, :], beby) for j in range(8)] + ctxk
                    y_, by_ = ys[yi % 2]; yi += 1
                    attend(qn[po:po + 64, qb * 512:(qb + 1) * 512], bqn, kbs, NA_SCALE, 512, y_[0:64, :], by_)
                    P.dma("pool", YNA[qb, po:po + 64, c * NT:(c + 1) * NT], y_[0:64, :], [by_], [db("YNA")], own=by_)
                if l == 0:
                    y_, by_ = ys[yi % 2]; yi += 1
                    attend(qc[po:po + 64, :], bqc, ctxk, NA_SCALE, CTX, y_[0:64, 0:CTX], by_)
                    P.dma("pool", YNAc[c, po:po + 64, :], y_[0:64, 0:CTX], [by_], [db("YNAc")], own=by_)
            P.barrier()

        def phase2_fourier(l):
            ar.reset(PERSIST)
            AB = pt("AB", [128, 256], BF16)
            GRE = pt("GRE", [N1, 128], BF16); GIM = pt("GIM", [N1, 128], BF16)
            YFS = [pt("YFS%d" % i, [32, N1], BF16) for i in range(2)]
            for g in range(4):
                gre, bgre = GRE; gim, bgim = GIM
                ab, bab = AB
                npc = CH // 128
                for r in range(4):
                    for hf in range(HF):
                        p0 = r * NBL + hf * npc
                        P.dma("sp", ab[p0:p0 + npc].rearrange("p a b -> p (a b)"), FABA[g * HF + hf][r * CH:(r + 1) * CH, :].rearrange("(a n) c -> a (n c)", n=128), [db("FABA")], [bab])
                for j4 in range(32):
                    for half in range(2):
                        bank = half
                        for jj in range(4):
                            j = j4 * 4 + jj
                            Al = ab[0:N1, :, j]; Bl = ab[0:N1, :, 128 + j]
                            o = ps_ap[:, bank, jj * N1:(jj + 1) * N1]
                            if half == 0:
                                P.op("pe", lambda e, o=o, Al=Al: e.matmul(o, lhsT=Al, rhs=c1[0:N1], start=True, stop=False), [bab, b_c1], [psb[bank]], inc=False)
                                P.op("pe", lambda e, o=o, Bl=Bl: e.matmul(o, lhsT=Bl, rhs=ns1[0:N1], start=False, stop=True), [bab, b_ns1], [psb[bank]], inc=(jj == 3))
                            else:
                                P.op("pe", lambda e, o=o, Bl=Bl: e.matmul(o, lhsT=Bl, rhs=c1[0:N1], start=True, stop=False), [bab, b_c1], [psb[bank]], inc=False)
                                P.op("pe", lambda e, o=o, Al=Al: e.matmul(o, lhsT=Al, rhs=s1[0:N1], start=False, stop=True), [bab, b_s1], [psb[bank]], inc=(jj == 3))
                        l0 = j4 * 4
                        G, bG = (gre, bgre) if half == 0 else (gim, bgim)
                        copy_op("act" if half else "dve", G[:, :, l0:l0 + 4], ps_ap[:, bank, 0:4 * N1].rearrange("p (j k) -> p k j", j=4), [psb[bank]], [bG])
                yfs, byfs = YFS[g % 2]
                for k1b in range(N1 // 16):
                    bank = 2 + (k1b % 2)
                    for kk in range(16):
                        k1 = k1b * 16 + kk
                        o = ps_ap[:, bank, kk * 32:(kk + 1) * 32]
                        P.op("pe", lambda e, o=o, k1=k1: e.matmul(o, lhsT=gre[:, k1, :], rhs=mre[:, k1, :], start=True, stop=False), [bgre, b_mre], [psb[bank]], inc=False)
                        P.op("pe", lambda e, o=o, k1=k1: e.matmul(o, lhsT=gim[:, k1, :], rhs=mimn[:, k1, :], start=False, stop=True), [bgim, b_mim], [psb[bank]], inc=(kk == 15))
                    copy_op("act" if k1b % 2 else "dve", yfs[:, :, k1b * 16:(k1b + 1) * 16], ps_ap[:, bank, :].rearrange("p (a b) -> p b a", a=16), [psb[bank]], [byfs])
                P.dma("pool", YF.rearrange("t p (c n) -> p t c n", c=4)[:, :, g, :], yfs.rearrange("p a b -> p (a b)").rearrange("p (t n) -> p t n", n=NT), [byfs], [db("YF")], own=byfs)
            if l == 0:
                fc, bfc = pt("fc", [2, 4, 256], BF16)
                P.dma("sp", fc, FABc.rearrange("(s p) g c -> p s g c", p=128), [db("FABc")], [bfc])
                yc, byc = pt("yc", [4, 256], BF16)
                for g in range(4):
                    steps = [(fc[:, nb, g, 0:128], cc256[:, nb, :]) for nb in range(2)] + [(fc[:, nb, g, 128:256], sc256[:, nb, :]) for nb in range(2)]
                    for i, (lh, rh) in enumerate(steps):
                        P.op("pe", lambda e, lh=lh, rh=rh, i=i: e.matmul(ps_ap[:, 4, 0:256], lhsT=lh, rhs=rh, start=(i == 0), stop=(i == 3)), [bfc, b_cc, b_sc], [psb[4]], inc=(i == 3))
                    copy_op("dve", yc[:, g, :], ps_ap[:, 4, 0:256], [psb[4]], [byc])
                P.dma("pool", YFc.rearrange("g p t -> p g t"), yc, [byc], [db("YFc")], own=byc)
            P.barrier()

        def alloc_p3():
            T3 = {}
            T3["x"] = [ptk("x3_%d" % i, [8, NT], F32) for i in range(1)]
            T3["mix"] = ptk("mix", [8, NT], F32)
            T3["r"] = ptk("r", [8, NT], F32)
            T3["a"] = ptk("a3", [8, NT], BF16)
            T3["mixb"] = ptk("mixb", [8, NT], BF16)
            T3["hid"] = ptk("hid", [22, NT], BF16)
            T3["y"] = [pt("y3_%d" % i, [4, NT], BF16) for i in range(3)]
            T3["st4"] = pt("st4_3", [4, NT], F32)
            T3["sg"] = [pt("sg%d" % i, [NT], BF16) for i in range(2)]
            T3["tmpf"] = [pt("tmpf%d" % i, [NT], F32) for i in range(2)]
            T3["ot"] = [pt("ot%d" % i, [1024], F32) for i in range(2)]
            return T3

        def phase3(l, nt, s, tok0, T3):
            lat = (s == 0)
            xt, xbuf = T3["x"][0]
            xsrc = xview(l % 2, lat, tok0 // NT)
            P.dma("sp", xt[:, :, 0:nt], xsrc, [db("XL%d" % (l % 2) if lat else "XC%d" % (l % 2))], xbuf)
            A, bA = T3["a"]
            for k in range(8):
                P.op("dve", lambda e, k=k: e.tensor_scalar(out=A[:, k, 0:nt], in0=xt[:, k, 0:nt], scalar1=MOD(l, 8 + k, s, True), scalar2=MOD(l, k, s, False),
                                                           op0=ALU.mult, op1=ALU.add), [xbuf[k], b_mod_, b_mod1], [bA[k]])
            ysrc = ((YF, "YF"), (YNA, "YNA"), (YMLA, "YMLA")) if lat else ((YFc, "YFc"), (YNAc, "YNAc"), (YMLAc, "YMLAc"))
            for gi in range(3):
                yt, by = T3["y"][gi]
                yd = ysrc[gi][0][tok0 // NT].rearrange("p (k t) -> p k t", k=4) if lat else ysrc[gi][0][:, :, 0:nt].rearrange("k p t -> p k t")
                P.dma("sp", yt[:, :, 0:nt], yd, [db(ysrc[gi][1])], [by])
            mix, bmix = T3["mix"]
            for gi in range(3):
                yt, by = T3["y"][gi]
                for hb in range(2):
                    pg = load_piece((l, "g", gi, hb)); pb_ = load_piece((l, "br", gi, hb))
                    for j in range(4):
                        mc = hb * 4 + j
                        ps, pb = mm_chunk(pg, j * 128, 128, A, bA, nt)
                        sg, bsg = T3["sg"][mc % 2]
                        P.op("act", lambda e, sg=sg, ps=ps: e.activation(out=sg[:, 0:nt], in_=ps, func=AF.Sigmoid), [pb], [bsg])
                        ps2, pb2 = mm_chunk(pb_, j * 128, 128, yt, by, nt)
                        if gi == 0:
                            P.op("dve", lambda e, sg=sg, ps2=ps2, mc=mc: e.tensor_tensor(out=mix[:, mc, 0:nt], in0=ps2, in1=sg[:, 0:nt], op=ALU.mult), [pb2, bsg], [bmix[mc]])
                        else:
                            tf, btf = T3["tmpf"][mc % 2]
                            P.op("dve", lambda e, sg=sg, ps2=ps2, tf=tf: e.tensor_tensor(out=tf[:, 0:nt], in0=ps2, in1=sg[:, 0:nt], op=ALU.mult), [pb2, bsg], [btf])
                            P.op("pool", lambda e, tf=tf, mc=mc: e.tensor_tensor(out=mix[:, mc, 0:nt], in0=mix[:, mc, 0:nt], in1=tf[:, 0:nt], op=ALU.add), [btf, bmix[mc]], [bmix[mc]])
            mixb, bmixb = T3["mixb"]
            for k in range(8):
                copy_op("act" if k % 2 else "dve", mixb[:, k, 0:nt], mix[:, k, 0:nt], [bmix[k]], [bmixb[k]])
            r, rb = T3["r"]

            def cons_o(ci, ps, pb, rows):
                P.op("pool", lambda e: e.tensor_scalar(out=r[:, ci, 0:nt], in0=xt[:, ci, 0:nt], scalar1=ALPHA, scalar2=None, op0=ALU.mult), [xbuf[ci]], [rb[ci]])
                P.op("dve", lambda e: e.scalar_tensor_tensor(out=r[:, ci, 0:nt], in0=ps, scalar=MOD(l, 16 + ci, s, True), in1=r[:, ci, 0:nt], op0=ALU.mult, op1=ALU.add), [pb, rb[ci], b_mod1], [rb[ci]])
            linear_fm([(l, "o", 0), (l, "o", 1)], mixb, bmixb, nt, cons_o)
            st4, bst4 = T3["st4"]
            ln_fm(r, rb, nt, lnp[:, l, 0, :], lnp[:, l, 1, :], mix, bmix, st4, bst4)
            for k in range(8):
                P.op("dve", lambda e, k=k: e.tensor_scalar(out=A[:, k, 0:nt], in0=r[:, k, 0:nt], scalar1=MOD(l, 32 + k, s, True), scalar2=MOD(l, 24 + k, s, False),
                                                           op0=ALU.mult, op1=ALU.add), [rb[k], b_mod_, b_mod1], [bA[k]])
            hid, bhid = T3["hid"]
            for j6 in range(6):
                pfg = load_piece((l, "fg", j6)); pfu = load_piece((l, "fu", j6))
                for j in range(pfg[3] // 128):
                    mc = j6 * 4 + j
                    ps, pb = mm_chunk(pfg, j * 128, 128, A, bA, nt)
                    sg, bsg = T3["sg"][mc % 2]
                    P.op("act", lambda e, sg=sg, ps=ps: e.activation(out=sg[:, 0:nt], in_=ps, func=AF.Silu), [pb], [bsg])
                    ps2, pb2 = mm_chunk(pfu, j * 128, 128, A, bA, nt)
                    P.op("dve", lambda e, sg=sg, ps2=ps2, mc=mc: e.tensor_tensor(out=hid[:, mc, 0:nt], in0=ps2, in1=sg[:, 0:nt], op=ALU.mult), [pb2, bsg], [bhid[mc]])

            def cons_d(ci, ps, pb, rows):
                P.op("pool", lambda e: e.tensor_scalar(out=r[:, ci, 0:nt], in0=r[:, ci, 0:nt], scalar1=ALPHA, scalar2=None, op0=ALU.mult), [rb[ci]], [rb[ci]])
                P.op("dve", lambda e: e.scalar_tensor_tensor(out=r[:, ci, 0:nt], in0=ps, scalar=MOD(l, 40 + ci, s, True), in1=r[:, ci, 0:nt], op0=ALU.mult, op1=ALU.add), [pb, rb[ci], b_mod1], [rb[ci]])
            linear_fm([(l, "fd", j) for j in range(4)], hid, bhid, nt, cons_d)
            ln_fm(r, rb, nt, lnp[:, l, 2, :], lnp[:, l, 3, :], mix, bmix, st4, bst4)
            return r, rb

        def write_out(r, rb, tok0, T3):
            for sub in range(4):
                ot, bot = T3["ot"][sub % 2]
                for k in range(8):
                    bank = 6 + (k // 4)
                    P.op("pe", lambda e, k=k, sub=sub, bank=bank: e.transpose(ps_ap[:, bank, (k % 4) * 128:(k % 4 + 1) * 128], r[:, k, sub * 128:(sub + 1) * 128], ident), [rb[k], b_ident], [psb[bank]])
                    if k % 4 == 3:
                        copy_op("act" if k == 3 else "dve", ot[:, (k // 4) * 512:(k // 4 + 1) * 512], ps_ap[:, bank, :], [psb[bank]], [bot])
                P.dma("pool", out_d[tok0 + sub * 128:tok0 + (sub + 1) * 128, :], ot, [bot], [db("out")], own=bot)

        def run_phase1_pass(l, from_input):
            ar.reset(PERSIST)
            load_layer_small(l)
            P.barrier()
            ar.reset(PERSIST)
            wslots[:] = [pt("wslot%d" % i, [6144], BF16) for i in range(3)]
            T1 = alloc_p1()
            xts = [ptk("xt%d" % i, [8, NT], F32) for i in range(2)]
            if from_input:
                W = {"xin": [pt("xin%d" % i, [1024], F32) for i in range(2)], "s4": pt("s4", [4], F32), "xsq": pt("xsq", [1024], F32)}
            for ti in range(NTL + 1):
                xt, xbuf = xts[ti % 2]
                lat = ti < NTL
                nt = NT if lat else CTX
                if from_input:
                    input_ln(x_in[ti * NT:(ti + 1) * NT] if lat else ctx_in, nt // 128, xt, xbuf, W)
                else:
                    P.dma("sp", xt[:, :, 0:nt], xview(l % 2, lat, ti), [db("XL%d" % (l % 2) if lat else "XC%d" % (l % 2))], ball(xbuf))
                phase1(l, xt, xbuf, nt, 0 if lat else 1, ti * NT if lat else 0, ti if lat else -1, T1, store_x=from_input)

        class _Stop(Exception):
            pass

        def chk(stage):
            if cfg.stop <= stage:
                raise _Stop()

        try:
            chk(0)
            if 'ONLYP3' in cfg.debug:
                lsel = 1 if 'L1' in cfg.debug else 0
                wslots[:] = [pt("wslot%d" % i, [6144], BF16) for i in range(3)]
                T3 = alloc_p3()
                for ti in range(NTL):
                    r, rb = phase3(lsel, NT, 0, ti * NT, T3)
                    if 'NOOUT' not in cfg.debug:
                        write_out(r, rb, ti * NT, T3)
                raise _Stop()
            run_phase1_pass(0, True)
            issue_mid_casts(10 ** 6)
            chk(1)
            for l in range(L):
                exchange()
                chk(2 + 10 * l)
                phase2_mla(l)
                chk(3 + 10 * l)
                phase2_na(l)
                chk(4 + 10 * l)
                phase2_fourier(l)
                chk(5 + 10 * l)
                ar.reset(PERSIST)
                wslots[:] = [pt("wslot%d" % i, [6144], BF16) for i in range(3)]
                T3 = alloc_p3()
                last = (l + 1 == L)
                for ti in range(NTL + (0 if last else 1)):
                    lat = ti < NTL
                    nt = NT if lat else CTX
                    r, rb = phase3(l, nt, 0 if lat else 1, ti * NT if lat else 0, T3)
                    if last:
                        if 'NOOUT' not in cfg.debug:
                            write_out(r, rb, ti * NT, T3)
                    else:
                        P.dma("pool", xview((l + 1) % 2, lat, ti), r[:, :, 0:nt], rb, [db("XL%d" % ((l + 1) % 2) if lat else "XC%d" % ((l + 1) % 2))], own=rb[0])
                chk(6 + 10 * l)
                if not last:
                    P.barrier()
                    run_phase1_pass(l + 1, False)
                chk(7 + 10 * l)
        except _Stop:
            pass
        P.barrier()
        P.emit()
    return nc, dbg


def _consts(cfg, q):
    bf = ml_dtypes.bfloat16
    R, T, N, N1 = cfg.R, cfg.T, cfg.N, cfg.N1
    out = {}
    out["k_ident"] = np.eye(128, dtype=np.float32)
    cl = np.outer(np.arange(128), np.arange(128)).astype(np.float64) * (2 * np.pi / 128)
    out["k_cs128"] = np.concatenate([np.cos(cl), np.sin(cl)], 1).astype(bf)
    a1 = np.outer(np.arange(N1), np.arange(N1)).astype(np.float64) * (2 * np.pi / N1)
    out["k_c1"] = np.cos(a1).astype(bf); out["k_s1"] = np.sin(a1).astype(bf); out["k_ns1"] = (-np.sin(a1)).astype(bf)
    sc = 1.0 / np.sqrt(N * 128.0)
    k1 = np.arange(N1)[None, :, None]; k2 = np.arange(32)[None, None, :]; n2 = np.arange(128)[:, None, None]
    k = T * q + k1 + N1 * k2
    ang = (n2 * k % N).astype(np.float64) * (2 * np.pi / N)
    out["k_mre"] = (np.cos(ang) * sc).reshape(128, N1 * 32).astype(bf)
    out["k_mimn"] = (-np.sin(ang) * sc).reshape(128, N1 * 32).astype(bf)
    ac = np.outer(np.arange(256), np.arange(256)).astype(np.float64) * (2 * np.pi / 256)
    scc = 1.0 / np.sqrt(256 * 128.0)
    out["k_cc256"] = (np.cos(ac) * scc).astype(bf); out["k_scn256"] = (-np.sin(ac) * scc).astype(bf)
    t = np.arange(T) + q * T
    rows = (t // GW).astype(np.float32); cols = (t % GW).astype(np.float32)
    inv = (np.float32(10000.0) ** (-np.arange(8, dtype=np.float32) / np.float32(8))).astype(np.float32)
    angr = np.concatenate([rows[:, None] * inv, cols[:, None] * inv], -1).astype(np.float32)
    co = np.cos(angr).T.astype(np.float32); si = np.sin(angr).T.astype(np.float32)
    out["k_rope"] = np.stack([np.concatenate([co, co], 0), np.concatenate([-si, si], 0)], 0).astype(np.float32)
    NR = 4 * R
    rvt = np.zeros((128, 3, 8, 8), np.float32)
    for ty, r0 in enumerate((R * q, R * q + 8, R * q + R - 8)):
        for p in range(128):
            for kb in range(8):
                krow = r0 - 4 + 2 * kb + p // 64
                for qi in range(8):
                    rs = min(max(r0 + qi - 4, 0), NR - 8)
                    rvt[p, ty, kb, qi] = 1.0 if rs <= krow < rs + 8 else 0.0
    out["k_rv"] = rvt.reshape(128, 192)
    sel = np.zeros((128, 8), np.float32)
    if q > 0:
        sel[:, q - 1] = 1.0
    if q < 3:
        sel[:, 4 + q + 1] = 1.0
    out["k_sel"] = sel
    return out


def _rpb_table(rpb):
    Ln, H = rpb.shape[0], rpb.shape[1]
    tab = np.full((Ln, H, 23, 64, 64), MASKVAL, np.float32)
    qc = np.arange(64)[None, :]; kc = np.arange(64)[:, None]
    cs = np.clip(qc - 8, 0, 48)
    valid = (kc >= cs) & (kc < cs + 16)
    dc = np.clip(kc - qc + 15, 0, 30)
    for e in range(23):
        dr = 11 - e
        if abs(dr) <= 7:
            g = rpb[:, :, dr + 7, :][:, :, dc]
            tab[:, :, e] = np.where(valid[None, None], g, np.float32(MASKVAL))
    return tab


def _pk(v, k):
    v = np.asarray(v, np.float32)
    sh = v.shape[:-1]
    return np.ascontiguousarray(np.swapaxes(v.reshape(sh + (k, 128)), -1, -2))


_CACHE = {}


def run_cfg(inputs, cfg):
    key = (cfg.R, cfg.debug, cfg.stop)
    if key not in _CACHE:
        _CACHE[key] = build_program(cfg)
    nc, dbg = _CACHE[key]
    T = cfg.T
    f32 = lambda a: np.ascontiguousarray(np.asarray(a, np.float32))
    shared = {
        "c_ctx": _pk(inputs["c_ctx"], 8), "ln_in_g": _pk(inputs["ln_in_g"], 8), "ln_in_b": _pk(inputs["ln_in_b"], 8),
        "w_mod": f32(inputs["w_mod"]), "b_mod": _pk(inputs["b_mod"], 48), "w_in": f32(inputs["w_in"]),
        "mla_q_norm_g": _pk(inputs["mla_q_norm_g"], 2), "mla_kv_norm_g": _pk(inputs["mla_kv_norm_g"], 1),
        "w_uq": f32(inputs["w_uq"]), "w_qr": f32(inputs["w_qr"]), "w_uk": f32(inputs["w_uk"]), "w_uv": f32(inputs["w_uv"]),
        "rpbT": _rpb_table(f32(inputs["na_rpb"])),
        "w_branch": f32(inputs["w_branch"]).reshape(L, 1536, D), "w_out": f32(inputs["w_out"]),
        "ln1_g": _pk(inputs["ln1_g"], 8), "ln1_b": _pk(inputs["ln1_b"], 8), "ln2_g": _pk(inputs["ln2_g"], 8), "ln2_b": _pk(inputs["ln2_b"], 8),
        "w_ffn_gate": f32(inputs["w_ffn_gate"]), "w_ffn_up": f32(inputs["w_ffn_up"]), "w_ffn_down": f32(inputs["w_ffn_down"]),
    }
    x = np.asarray(inputs["x"], np.float32); ctx = np.asarray(inputs["ctx"], np.float32); c = np.asarray(inputs["c"], np.float32)
    in_maps = []
    for i in range(8):
        b, q = i // 4, i % 4
        m = dict(shared)
        m["x"] = np.ascontiguousarray(x[b, q * T:(q + 1) * T]); m["ctx"] = np.ascontiguousarray(ctx[b]); m["c"] = _pk(c[b], 8)
        m.update(_consts(cfg, q))
        in_maps.append(m)
    res = run_bass_kernel_spmd(nc, in_maps, core_ids=list(range(8)))
    out = np.empty((2, 4 * T, D), np.float32)
    for i in range(8):
        b, q = i // 4, i % 4
        out[b, q * T:(q + 1) * T] = np.asarray(res.results[i]["out"], np.float32)
    return out, res


def kernel(**inputs):
    out, _ = run_cfg(inputs, Cfg(64))
    return out
```

```python
from contextlib import ExitStack
import ml_dtypes
from concourse.bass_utils import run_bass_kernel_spmd
import numpy as np
import concourse.bass as bass
import concourse.mybir as mybir

F32 = mybir.dt.float32
BF16 = mybir.dt.bfloat16
U8 = mybir.dt.uint8
ALU = mybir.AluOpType
AF = mybir.ActivationFunctionType
AX = mybir.AxisListType


class Buf:
    __slots__ = ("name", "w", "r", "dsem", "dcnt")

    def __init__(self, name):
        self.name = name
        self.w = None
        self.r = {}
        self.dsem = None
        self.dcnt = 0


class Prog:
    ENGS = ("pe", "act", "dve", "pool", "sp")

    def __init__(self, nc, stack):
        self.nc = nc
        self.stack = stack
        self.lists = {e: [] for e in self.ENGS}
        self.sems = {}
        self.cnt = {}
        self.waited = {e: {} for e in self.ENGS}
        for e in self.ENGS:
            self._newsem("eng_" + e)
        self._newsem("coll")
        self.pool_keys = []
        self.pool_idx = 0
        self.attached = []

    def _newsem(self, key):
        self.sems[key] = self.stack.enter_context(self.nc.semaphore(key))
        self.cnt[key] = 0
        return key

    def buf(self, name):
        return Buf(name)

    def _deps(self, reads, writes):
        deps = {}
        def add(t):
            if t is None:
                return
            k, v = t
            if deps.get(k, 0) < v:
                deps[k] = v
        for b in reads:
            add(b.w)
        for b in writes:
            add(b.w)
            for k, v in b.r.items():
                add((k, v))
        return deps

    def _waits(self, eng, deps):
        ws = []
        wd = self.waited[eng]
        for k, v in deps.items():
            if wd.get(k, 0) < v:
                wd[k] = v
                ws.append((k, v))
        return ws

    def _mark(self, tok, reads, writes):
        k, v = tok
        for b in reads:
            if b.r.get(k, 0) < v:
                b.r[k] = v
        for b in writes:
            b.w = tok
            b.r = {}

    def _dsem(self, b):
        if b.dsem is None:
            if self.pool_idx >= len(self.pool_keys):
                self.pool_keys.append(self._newsem("dq%d" % len(self.pool_keys)))
            b.dsem = self.pool_keys[self.pool_idx]
            self.pool_idx += 1
            self.attached.append(b)
        return b.dsem

    def op(self, eng, fn, reads=(), writes=(), inc=True):
        deps = self._deps(reads, writes)
        if eng == "pe":
            deps.pop("eng_pe", None)
        ws = self._waits(eng, deps)
        key = "eng_" + eng
        if inc:
            self.cnt[key] += 1
            tok = (key, self.cnt[key])
        else:
            tok = (key, self.cnt[key] + 1)
        self.lists[eng].append((ws, fn, (key, 1) if inc else None))
        self._mark(tok, reads, writes)
        if inc:
            self.waited[eng][key] = max(self.waited[eng].get(key, 0), 0)
        return tok

    def dma(self, q, out_ap, in_ap, reads, writes, own=None, **kw):
        if own is None:
            own = writes[0]
        key = self._dsem(own)
        ws = self._waits(q, self._deps(reads, writes))
        self.cnt[key] += 16
        tok = (key, self.cnt[key])
        fn = lambda e, o=out_ap, i=in_ap, kw=kw: e.dma_start(out=o, in_=i, **kw)
        self.lists[q].append((ws, fn, (key, 16)))
        self._mark(tok, reads, writes)
        return tok

    def collective(self, kind, in_ap, out_ap, groups, reads, writes):
        key = "coll"
        ws = self._waits("pool", self._deps(reads, writes))
        self.cnt[key] += 1
        tok = (key, self.cnt[key])
        fn = lambda e: e.collective_compute(kind, ALU.bypass, replica_groups=groups,
                                            ins=[in_ap], outs=[out_ap])
        self.lists["pool"].append((ws, fn, (key, 1)))
        self._mark(tok, reads, writes)
        return tok

    def barrier(self):
        for e in self.ENGS:
            ws = self._waits(e, {k: v for k, v in self.cnt.items() if v > 0})
            if ws:
                self.lists[e].append((ws, None, None))
        for b in self.attached:
            b.dsem = None
        self.attached = []
        self.pool_idx = 0

    def emit(self):
        nc = self.nc
        sems = self.sems
        lists = self.lists
        with nc.Block() as block:
            def run(eng_obj, lst):
                for ws, fn, inc in lst:
                    for k, v in ws:
                        eng_obj.wait_ge(sems[k], v)
                    if fn is not None:
                        ins = fn(eng_obj)
                        if inc is not None:
                            ins.then_inc(sems[inc[0]], inc[1])

            @block.tensor
            def _(e):
                run(e, lists["pe"])

            @block.scalar
            def _(e):
                run(e, lists["act"])

            @block.vector
            def _(e):
                run(e, lists["dve"])

            @block.gpsimd
            def _(e):
                run(e, lists["pool"])

            @block.sync
            def _(e):
                run(e, lists["sp"])


class Arena:
    def __init__(self, ap_u8, limit):
        self.ap = ap_u8
        self.limit = limit
        self.off = 0

    def reset(self, off=0):
        self.off = off

    def alloc(self, shape, dt, parts=128):
        esz = 4 if dt == F32 else 2
        n = int(np.prod(shape))
        nbytes = (n * esz + 63) // 64 * 64
        assert self.off + nbytes <= self.limit, ("arena overflow", self.off, nbytes, self.limit)
        a = self.ap[0:parts, self.off:self.off + n * esz].bitcast(dt)
        self.off += nbytes
        if len(shape) == 2:
            a = a.rearrange("p (a b) -> p a b", a=shape[0])
        elif len(shape) == 3:
            a = a.rearrange("p (a b c) -> p a b c", a=shape[0], b=shape[1])
        return a
D = 1024
CTX = 256
GW = 64
FH = 2816
L = 2
EPS = 1e-5
ALPHA = (2 * L) ** 0.25
NA_SCALE = 64 ** -0.5
MLA_SCALE = 96 ** -0.5
MASKVAL = -30000.0
O_F, O_NQ, O_NK, O_NV, O_CQ, O_CKV, O_KR, O_G = 0, 512, 1024, 1536, 2048, 2304, 2432, 2464
IN_DIM = 5536


class Cfg:
    def __init__(self, R=64, debug=(), stop=99):
        self.stop = stop
        self.R = R
        self.T = R * GW
        self.N = 4 * self.T
        self.N1 = self.N // 128
        self.NT = 512
        self.NTL = self.T // 512
        self.NQB = R // 8
        self.debug = tuple(debug)


def build_program(cfg):
    R, T, N, N1, NT, NTL, NQB = cfg.R, cfg.T, cfg.N, cfg.N1, cfg.NT, cfg.NTL, cfg.NQB
    TW = T + 512
    nc = bass.Bass("TRN2", target_bir_lowering=False)

    def din(name, shape, dt=F32):
        return nc.dram_tensor(name, list(shape), dt, kind="ExternalInput").ap()

    dbg = {}

    def dscr(name, shape, dt=BF16):
        if name in cfg.debug and name not in ('NOOUT', 'ONLYP3', 'L1'):
            t = nc.dram_tensor(name, list(shape), dt, kind="ExternalOutput")
            dbg[name] = t
        else:
            t = nc.dram_tensor(name, list(shape), dt)
        return t.ap()

    x_in = din("x", [T, D]); ctx_in = din("ctx", [CTX, D]); c_in = din("c", [128, 8]); cc_in = din("c_ctx", [128, 8])
    lng_in = din("ln_in_g", [128, 8]); lnb_in = din("ln_in_b", [128, 8])
    w_mod = din("w_mod", [L, D, 6 * D]); b_mod = din("b_mod", [L, 128, 48]); w_in = din("w_in", [L, D, IN_DIM])
    gq_in = din("mla_q_norm_g", [L, 128, 2]); gkv_in = din("mla_kv_norm_g", [L, 128, 1])
    w_uq = din("w_uq", [L, 256, 512]); w_qr = din("w_qr", [L, 256, 256]); w_uk = din("w_uk", [L, 128, 512]); w_uv = din("w_uv", [L, 128, 512])
    rpbT = din("rpbT", [L, 8, 23, 64, 64])
    w_br = din("w_branch", [L, 1536, D]); w_out = din("w_out", [L, D, D])
    ln1g = din("ln1_g", [L, 128, 8]); ln1b = din("ln1_b", [L, 128, 8]); ln2g = din("ln2_g", [L, 128, 8]); ln2b = din("ln2_b", [L, 128, 8])
    w_fg = din("w_ffn_gate", [L, D, FH]); w_fu = din("w_ffn_up", [L, D, FH]); w_fd = din("w_ffn_down", [L, FH, D])
    k_id = din("k_ident", [128, 128]); k_cs = din("k_cs128", [128, 256], BF16)
    k_c1 = din("k_c1", [N1, N1], BF16); k_s1 = din("k_s1", [N1, N1], BF16); k_ns1 = din("k_ns1", [N1, N1], BF16)
    k_mre = din("k_mre", [128, N1 * 32], BF16); k_mim = din("k_mimn", [128, N1 * 32], BF16)
    k_cc = din("k_cc256", [256, 256], BF16); k_sc = din("k_scn256", [256, 256], BF16)
    k_rope = din("k_rope", [2, 32, T]); k_rv = din("k_rv", [128, 3 * 64]); k_sel = din("k_sel", [128, 8])
    out_d = nc.dram_tensor("out", [T, D], F32, kind="ExternalOutput").ap()

    WP = {}
    CASTS = []

    def wpiece(key, src2d, r0, kch, c0, pc):
        WP[key] = (dscr("W_%s" % "_".join(str(x) for x in key), [128, kch * pc]), kch, pc)
        CASTS.append((key, src2d, r0, kch, c0, pc))

    for l in range(L):
        for nm, c0, pc in (("f", O_F, 512), ("nq", O_NQ, 512), ("nk", O_NK, 512), ("nv", O_NV, 512), ("cq", O_CQ, 256), ("ckv", O_CKV, 128)):
            wpiece((l, nm), w_in[l], 0, 8, c0, pc)
        for gi in range(3):
            for hb in range(2):
                wpiece((l, "g", gi, hb), w_in[l], 0, 8, O_G + gi * D + hb * 512, 512)
                wpiece((l, "br", gi, hb), w_br[l], gi * 512, 4, hb * 512, 512)
        for hb in range(2):
            wpiece((l, "o", hb), w_out[l], 0, 8, hb * 512, 512)
        for j in range(6):
            pc = 512 if j < 5 else 256
            wpiece((l, "fg", j), w_fg[l], 0, 8, j * 512, pc)
            wpiece((l, "fu", j), w_fu[l], 0, 8, j * 512, pc)
        for j in range(4):
            wpiece((l, "fd", j), w_fd[l], 0, 22, j * 256, 256)
    XL = [dscr("XL%d" % i, [NTL, 128, 8 * NT], F32) for i in range(2)]
    XC = [dscr("XC%d" % i, [128, 8 * CTX], F32) for i in range(2)]
    NAQT = dscr("NAQT", [NTL, 128, 4 * NT]); NAKW = dscr("NAKW", [NTL, 128, 4 * NT]); NAKH = dscr("NAKH", [4, 128, 512]); NBL = T // 128; NWB = TW // 128
    NAVW = dscr("NAVW", [NTL * 128, 2080]); NAVH = dscr("NAVH", [4 * 128, 520])
    HK = dscr("HK", [512, 512]); HKA = dscr("HKA", [4 * 512, 512]); HV = dscr("HV", [512, 512]); HVA = dscr("HVA", [4 * 512, 512])
    MLAQT = dscr("MLAQT", [8, 96, T]); MLAKT = dscr("MLAKT", [768, T]); MLAKTA = [dscr("MLAKTA%d" % h, [4 * 96, T]) for h in range(8)]
    MLAV = dscr("MLAV", [NTL * 128, 2080])
    MLAVA = [dscr("MLAVA%d" % t, [4 * 128, 2080]) for t in range(NTL)]
    HF = 2 if T * 512 > (1 << 20) else 1; CH = T // HF
    FAB = dscr("FAB", [4 * T, 256]); FABA = [dscr("FABA%d" % j, [4 * CH, 256]) for j in range(4 * HF)]
    NAQTc = dscr("NAQTc", [4, 128, CTX]); NAKTc = dscr("NAKTc", [4, 128, CTX]); NAVc = dscr("NAVc", [128, 8 * 130])
    MLAQTc = dscr("MLAQTc", [8, 96, CTX]); MLAKTc = dscr("MLAKTc", [8, 96, CTX]); MLAVc = dscr("MLAVc", [128, 8 * 130])
    FABc = dscr("FABc", [CTX, 4, 256])
    YF = dscr("YF", [NTL, 128, 4 * NT]); YNA = dscr("YNA", [NTL, 128, 4 * NT]); YMLA = dscr("YMLA", [NTL, 128, 4 * NT])
    YFc = dscr("YFc", [4, 128, CTX]); YNAc = dscr("YNAc", [4, 128, CTX]); YMLAc = dscr("YMLAc", [4, 128, CTX])
    EBT = [dscr("EBT%d" % l, [8, 64, 23 * 64]) for l in range(L)]

    GROUPS = [[0, 1, 2, 3], [4, 5, 6, 7]]
    ARENA = 196 * 1024

    with ExitStack() as st:
        P = Prog(nc, st)
        ar_ap = st.enter_context(nc.sbuf_tensor("arena", [128, ARENA], U8))
        ps_ap = st.enter_context(nc.psum_tensor("ps", [128, 8, 512], F32))
        ar = Arena(ar_ap, ARENA)
        psb = [P.buf("ps%d" % i) for i in range(8)]
        PS = lambda i: ps_ap[:, i, :]
        dram = {}

        def db(name):
            if name not in dram:
                dram[name] = P.buf(name)
            return dram[name]

        rr = {"lin": 0, "ev": 0, "bank": 0}

        def xview(i, lat, ti):
            if lat:
                return XL[i][ti].rearrange("p (k t) -> p k t", k=8)
            return XC[i].rearrange("p (k t) -> p k t", k=8)

        def evac_eng():
            rr["ev"] ^= 1
            return "act" if rr["ev"] else "dve"

        def copy_op(eng, out, in_, rd, wr):
            if eng == "act":
                P.op("act", lambda e: e.copy(out=out, in_=in_), rd, wr)
            else:
                P.op(eng, lambda e: e.tensor_copy(out=out, in_=in_), rd, wr)

        def pt(name, shape, dt, parts=128):
            return ar.alloc(shape, dt, parts=parts), P.buf(name)

        def ptk(name, shape, dt):
            return ar.alloc(shape, dt), [P.buf("%s_%d" % (name, k)) for k in range(shape[0])]

        def bk(b, k):
            return b[k] if isinstance(b, list) else b

        def ball(b):
            return list(b) if isinstance(b, list) else [b]

        ident, b_ident = pt("ident", [128], F32)
        ones, b_ones = pt("ones", [128], F32)
        epsb, b_eps = pt("eps", [1], F32)
        cs128, b_cs = pt("cs128", [256], BF16)
        c1, b_c1 = pt("c1", [N1], BF16); s1, b_s1 = pt("s1", [N1], BF16); ns1, b_ns1 = pt("ns1", [N1], BF16)
        mre, b_mre = pt("mre", [N1, 32], BF16); mimn, b_mim = pt("mimn", [N1, 32], BF16)
        cc256, b_cc = pt("cc256", [2, 256], BF16); sc256, b_sc = pt("sc256", [2, 256], BF16)
        rv, b_rv = pt("rv", [3, 8, 8], F32); sel, b_sel = pt("sel", [8], F32)
        lnin, b_lnin = pt("lnin", [2, 8], F32)
        modv, b_mod_ = pt("modv", [L, 48, 2], F32); mod1, b_mod1 = pt("mod1", [L, 48, 2], F32)
        lnp, b_lnp = pt("lnp", [L, 4, 8], F32)
        gq, b_gq = pt("gq", [L, 2], F32); gkv, b_gkv = pt("gkv", [L, 1], F32)
        wkr, b_wkr = pt("wkr", [8, 2, 96], BF16)
        wq, b_wq = pt("wq", [2, 8, 96], BF16); wqs, b_wqs = pt("wqs", [2, 8, 96], BF16)
        wk, b_wk = pt("wk", [8, 64], BF16); wuv, b_wuv = pt("wuv", [512], BF16)
        PERSIST = ar.off

        P.dma("sp", ident, k_id, [], [b_ident])
        P.op("pool", lambda e: e.memset(ones, 1.0), [], [b_ones])
        P.op("pool", lambda e: e.memset(epsb, EPS), [], [b_eps])
        P.dma("sp", cs128, k_cs, [], [b_cs])
        P.dma("sp", c1[0:N1], k_c1, [], [b_c1]); P.dma("sp", s1[0:N1], k_s1, [], [b_s1]); P.dma("sp", ns1[0:N1], k_ns1, [], [b_ns1])
        P.dma("sp", mre, k_mre.rearrange("p (a b) -> p a b", b=32), [], [b_mre])
        P.dma("sp", mimn, k_mim.rearrange("p (a b) -> p a b", b=32), [], [b_mim])
        P.dma("sp", cc256, k_cc.rearrange("(a p) k -> p a k", p=128), [], [b_cc])
        P.dma("sp", sc256, k_sc.rearrange("(a p) k -> p a k", p=128), [], [b_sc])
        P.dma("sp", rv, k_rv.rearrange("p (t a b) -> p t a b", t=3, a=8), [], [b_rv])
        P.dma("sp", sel, k_sel, [], [b_sel])
        P.dma("sp", lnin[:, 0, :], lng_in, [], [b_lnin])
        P.dma("sp", lnin[:, 1, :], lnb_in, [], [b_lnin])
        for l in range(L):
            for i, src in enumerate((ln1g, ln1b, ln2g, ln2b)):
                P.dma("sp", lnp[:, l, i, :], src[l], [], [b_lnp])
            P.dma("sp", gq[:, l, :], gq_in[l], [], [b_gq])
            P.dma("sp", gkv[:, l, :], gkv_in[l], [], [b_gkv])

        b_wcast = P.buf("wcast")
        b_wcast1 = P.buf("wcast1"); b_wcast3 = P.buf("wcast3")
        late_casts = []
        mid_casts = []
        P1KEYS = ("f", "nq", "nk", "nv", "cq", "ckv")

        def wbuf_of(key):
            if key[0] == 1:
                return b_wcast1
            return b_wcast if key[1] in P1KEYS else b_wcast3

        for (key, src2d, r0, kch, c0, pc) in CASTS:
            dstp = WP[key][0]
            for k in range(kch):
                args = (dstp[:, k * pc:(k + 1) * pc], src2d[r0 + k * 128:r0 + (k + 1) * 128, c0:c0 + pc])
                if key[0] == 0 and key[1] in P1KEYS:
                    P.dma("pool", args[0], args[1], [], [b_wcast])
                elif key[0] == 0:
                    mid_casts.append(args)
                else:
                    late_casts.append(args)

        def issue_late_casts(n):
            for _ in range(min(n, len(late_casts))):
                o, i = late_casts.pop(0)
                P.dma("pool", o, i, [], [b_wcast1])

        def issue_mid_casts(n):
            for _ in range(min(n, len(mid_casts))):
                o, i = mid_casts.pop(0)
                P.dma("pool", o, i, [], [b_wcast3])

        ar.reset(PERSIST)
        cs_t, b_cst = pt("cs_t", [2, 8], F32)
        bm_t, b_bmt = pt("bm_t", [L, 48], F32)
        P.dma("sp", cs_t[:, 0, :], c_in, [], [b_cst])
        P.dma("sp", cs_t[:, 1, :], cc_in, [], [b_cst])
        P.op("act", lambda e: e.activation(out=cs_t, in_=cs_t, func=AF.Silu), [b_cst], [b_cst])
        for l in range(L):
            P.dma("sp", bm_t[:, l, :], b_mod[l], [], [b_bmt])
        wm = [pt("wm%d" % i, [8, 512], F32) for i in range(2)]
        for l in range(L):
            for j in range(12):
                wt_, wb_ = wm[j % 2]
                P.dma("sp", wt_, w_mod[l][:, j * 512:(j + 1) * 512].rearrange("(k p) m -> p k m", p=128), [], [wb_])
                for cc in range(4):
                    ch = j * 4 + cc
                    bank = ch % 4
                    for k in range(8):
                        P.op("pe", lambda e, wt_=wt_, k=k, cc=cc, bank=bank: e.matmul(
                            ps_ap[:, bank, 0:2], lhsT=wt_[:, k, cc * 128:(cc + 1) * 128], rhs=cs_t[:, :, k],
                            start=(k == 0), stop=(k == 7)), [wb_, b_cst], [psb[bank]], inc=(k == 7))
                    P.op("dve", lambda e, l=l, ch=ch, bank=bank: e.tensor_scalar(
                        out=modv[:, l, ch, :], in0=ps_ap[:, bank, 0:2], scalar1=bm_t[:, l, ch:ch + 1], scalar2=None,
                        op0=ALU.add), [psb[bank], b_bmt], [b_mod_])
        P.op("dve", lambda e: e.tensor_scalar(out=mod1, in0=modv, scalar1=1.0, scalar2=None, op0=ALU.add), [b_mod_], [b_mod1])

        def MOD(l, idx, s, plus1):
            src = mod1 if plus1 else modv
            return src[:, l, idx, s:s + 1]

        eb_t = [pt("ebt%d" % i, [23, 64], F32, parts=64) for i in range(2)]
        eb_o = [pt("ebo%d" % i, [23, 64], BF16, parts=64) for i in range(2)]
        for l in range(L):
            for h in range(8):
                (ti, tb), (oi, ob) = eb_t[h % 2], eb_o[h % 2]
                P.dma("sp", ti, rpbT[l, h].rearrange("e k q -> k e q"), [], [tb])
                P.op("act", lambda e, ti=ti, oi=oi: e.activation(out=oi, in_=ti, func=AF.Exp), [tb], [ob])
                P.dma("pool", EBT[l][h], oi.rearrange("p a b -> p (a b)"), [ob], [db("EBT%d" % l)], own=ob)
        P.barrier()
        ar.reset(PERSIST)

        def load_layer_small(l):
            st_uq, b1 = pt("st_uq", [2, 512], F32); st_qr, b2 = pt("st_qr", [2, 256], F32)
            P.dma("sp", st_uq, w_uq[l].rearrange("(k p) m -> p k m", p=128), [], [b1])
            P.dma("sp", st_qr, w_qr[l].rearrange("(k p) m -> p k m", p=128), [], [b2])
            P.op("pool", lambda e: e.memset(wqs, 0.0), [], [b_wqs])
            P.op("pool", lambda e: e.memset(wkr, 0.0), [], [b_wkr])
            uqv = st_uq.rearrange("p k (h d) -> p k h d", h=8); qrv = st_qr.rearrange("p k (h d) -> p k h d", h=8)
            P.op("dve", lambda e: e.tensor_copy(out=wq[:, :, :, 0:64], in_=uqv), [b1], [b_wq])
            P.op("dve", lambda e: e.tensor_copy(out=wq[:, :, :, 64:96], in_=qrv), [b2], [b_wq])
            P.op("dve", lambda e: e.tensor_copy(out=wqs[:, :, :, 64:80], in_=qrv[:, :, :, 16:32]), [b2], [b_wqs])
            P.op("dve", lambda e: e.tensor_copy(out=wqs[:, :, :, 80:96], in_=qrv[:, :, :, 0:16]), [b2], [b_wqs])
            st_kr, b3 = pt("st_kr", [8, 32], F32)
            P.dma("sp", st_kr, w_in[l][:, O_KR:O_KR + 32].rearrange("(k p) m -> p k m", p=128), [], [b3])
            P.op("dve", lambda e: e.tensor_copy(out=wkr[:, :, 0, 64:96], in_=st_kr), [b3], [b_wkr])
            P.op("dve", lambda e: e.tensor_copy(out=wkr[:, :, 1, 64:80], in_=st_kr[:, :, 16:32]), [b3], [b_wkr])
            P.op("dve", lambda e: e.tensor_copy(out=wkr[:, :, 1, 80:96], in_=st_kr[:, :, 0:16]), [b3], [b_wkr])
            st_uk, b4 = pt("st_uk", [512], F32); st_uv, b5 = pt("st_uv", [512], F32)
            P.dma("sp", st_uk, w_uk[l], [], [b4]); P.dma("sp", st_uv, w_uv[l], [], [b5])
            P.op("dve", lambda e: e.tensor_copy(out=wk, in_=st_uk.rearrange("p (h d) -> p h d", h=8)), [b4], [b_wk])
            P.op("dve", lambda e: e.tensor_copy(out=wuv, in_=st_uv), [b5], [b_wuv])

        wslots = []

        def load_piece(key):
            wd, kch, pc = WP[key]
            slot, sbuf_ = wslots[rr["lin"] % len(wslots)]
            rr["lin"] += 1
            P.dma("sp", slot[:, 0:kch * pc], wd, [wbuf_of(key)], [sbuf_])
            return slot[:, 0:kch * pc].rearrange("p (k m) -> p k m", k=kch), sbuf_, kch, pc

        def mm_chunk(piece, c0, rows, act, abuf, nt):
            sv, sbuf_, kch, pc = piece
            bank = rr["bank"] % 4; rr["bank"] += 1
            for k in range(kch):
                P.op("pe", lambda e, k=k: e.matmul(ps_ap[0:rows, bank, 0:nt], lhsT=sv[:, k, c0:c0 + rows], rhs=act[:, k, 0:nt],
                                                   start=(k == 0), stop=(k == kch - 1)), [sbuf_, bk(abuf, k)], [psb[bank]], inc=(k == kch - 1))
            return ps_ap[0:rows, bank, 0:nt], psb[bank]

        def linear_fm(keys, act, abuf, nt, consume):
            ci = 0
            for key in keys:
                piece = load_piece(key)
                for c0 in range(0, piece[3], 128):
                    rows = min(128, piece[3] - c0)
                    ps, pb = mm_chunk(piece, c0, rows, act, abuf, nt)
                    consume(ci, ps, pb, rows)
                    ci += 1

        def ln_fm(r, rbuf, nt, gap, bap, sq, sqbuf, st4, st4buf):
            for k in range(8):
                if k % 2 == 0:
                    P.op("act", lambda e, k=k: e.activation(out=sq[:, k, 0:nt], in_=r[:, k, 0:nt], func=AF.Square), [bk(rbuf, k)], [bk(sqbuf, k)])
                else:
                    P.op("pool", lambda e, k=k: e.tensor_tensor(out=sq[:, k, 0:nt], in0=r[:, k, 0:nt], in1=r[:, k, 0:nt], op=ALU.mult), [bk(rbuf, k)], [bk(sqbuf, k)])
            for k in range(8):
                P.op("pe", lambda e, k=k: e.matmul(ps_ap[:, 4, 0:nt], lhsT=ones, rhs=r[:, k, 0:nt], start=(k == 0), stop=(k == 7)),
                     [b_ones, bk(rbuf, k)], [psb[4]], inc=(k == 7))
            for k in range(8):
                P.op("pe", lambda e, k=k: e.matmul(ps_ap[:, 5, 0:nt], lhsT=ones, rhs=sq[:, k, 0:nt], start=(k == 0), stop=(k == 7)),
                     [b_ones, bk(sqbuf, k)], [psb[5]], inc=(k == 7))
            mean = st4[:, 0, 0:nt]; msq = st4[:, 1, 0:nt]; rstd = st4[:, 2, 0:nt]
            P.op("dve", lambda e: e.tensor_scalar(out=mean, in0=ps_ap[:, 4, 0:nt], scalar1=1.0 / D, scalar2=None, op0=ALU.mult), [psb[4]], [st4buf])
            P.op("dve", lambda e: e.tensor_tensor(out=msq, in0=mean, in1=mean, op=ALU.mult), [st4buf], [st4buf])
            P.op("dve", lambda e: e.scalar_tensor_tensor(out=rstd, in0=ps_ap[:, 5, 0:nt], scalar=1.0 / D, in1=msq, op0=ALU.mult, op1=ALU.subtract), [psb[5], st4buf], [st4buf])
            P.op("act", lambda e: e.activation(out=rstd, in_=rstd, func=AF.Ln, bias=epsb[:, 0:1], scale=1.0), [st4buf, b_eps], [st4buf])
            P.op("act", lambda e: e.activation(out=rstd, in_=rstd, func=AF.Exp, scale=-0.5), [st4buf], [st4buf])
            for k in range(8):
                rk = r[:, k, 0:nt]
                P.op("pool", lambda e, rk=rk: e.tensor_tensor(out=rk, in0=rk, in1=mean, op=ALU.subtract), [bk(rbuf, k), st4buf], [bk(rbuf, k)])
                P.op("dve", lambda e, rk=rk: e.tensor_tensor(out=rk, in0=rk, in1=rstd, op=ALU.mult), [bk(rbuf, k), st4buf], [bk(rbuf, k)])
                P.op("act", lambda e, rk=rk, k=k: e.activation(out=rk, in_=rk, func=AF.Identity, bias=bap[:, k:k + 1], scale=gap[:, k:k + 1]), [bk(rbuf, k), b_lnp, b_lnin], [bk(rbuf, k)])

        def rms_fm(src, sbuf_, kch, nt, gsc, dst, dbuf, sq, sqbuf, st4, st4buf):
            for k in range(kch):
                P.op("act", lambda e, k=k: e.activation(out=sq[:, k, 0:nt], in_=src[:, k, 0:nt], func=AF.Square), [sbuf_], [sqbuf])
            for k in range(kch):
                P.op("pe", lambda e, k=k: e.matmul(ps_ap[:, 5, 0:nt], lhsT=ones, rhs=sq[:, k, 0:nt], start=(k == 0), stop=(k == kch - 1)),
                     [b_ones, sqbuf], [psb[5]], inc=(k == kch - 1))
            rstd = st4[:, 3, 0:nt]
            P.op("act", lambda e: e.activation(out=rstd, in_=ps_ap[:, 5, 0:nt], func=AF.Ln, bias=epsb[:, 0:1], scale=1.0 / (kch * 128)), [psb[5], b_eps], [st4buf])
            P.op("act", lambda e: e.activation(out=rstd, in_=rstd, func=AF.Exp, scale=-0.5), [st4buf], [st4buf])
            for k in range(kch):
                P.op("dve", lambda e, k=k: e.scalar_tensor_tensor(out=dst[:, k, 0:nt], in0=src[:, k, 0:nt], scalar=gsc[:, k:k + 1], in1=rstd, op0=ALU.mult, op1=ALU.mult),
                     [sbuf_, st4buf, b_gq, b_gkv], [dbuf])

        att_state = {}
        SBANKS = (0, 1, 2, 4, 5, 6)

        def attend(QT, qbuf, kbs, scale, nq, ytile, ybuf):
            ptl = att_state["ptl"]; rs, rsb = att_state["rs"]; bcs, bcsb = att_state["bc"]
            ob = att_state["obank"]; att_state["obank"] = 3 if ob == 7 else 7
            n = len(kbs)

            def S(i):
                KT, kb_, V, vb_, eb, ebb = kbs[i]
                bank = SBANKS[att_state["si"] % 6]; att_state["si"] += 1
                P.op("pe", lambda e: e.matmul(ps_ap[:, bank, 0:nq], lhsT=KT, rhs=QT, start=True, stop=True), [kb_, qbuf], [psb[bank]])
                pt_, pb_ = ptl[att_state["pi"] % len(ptl)]; att_state["pi"] += 1
                P.op("act", lambda e: e.activation(out=pt_[:, 0:nq], in_=ps_ap[:, bank, 0:nq], func=AF.Exp, scale=scale), [psb[bank]], [pb_])
                if eb is not None:
                    P.op("dve", lambda e: e.tensor_tensor(out=pt_[:, 0:nq], in0=pt_[:, 0:nq], in1=eb, op=ALU.mult), [pb_, ebb], [pb_])
                return pt_, pb_

            def PV(i, pt_, pb_):
                KT, kb_, V, vb_, eb, ebb = kbs[i]
                P.op("pe", lambda e: e.matmul(ps_ap[0:65, ob, 0:nq], lhsT=V, rhs=pt_[:, 0:nq], start=(i == 0), stop=(i == n - 1)), [vb_, pb_], [psb[ob]])

            pend = []
            for i in range(n):
                pend.append((i,) + S(i))
                if len(pend) > 3:
                    PV(*pend.pop(0))
            while pend:
                PV(*pend.pop(0))
            P.op("dve", lambda e: e.reciprocal(out=rs[64:65, 0:nq], in_=ps_ap[64:65, ob, 0:nq]), [psb[ob]], [rsb])
            bb = SBANKS[att_state["si"] % 6]; att_state["si"] += 1
            P.op("pe", lambda e: e.matmul(ps_ap[0:64, bb, 0:nq], lhsT=ones[64:65, 0:64], rhs=rs[64:65, 0:nq], start=True, stop=True), [b_ones, rsb], [psb[bb]])
            P.op("act", lambda e: e.copy(out=bcs[0:64, 0:nq], in_=ps_ap[0:64, bb, 0:nq]), [psb[bb]], [bcsb])
            P.op("dve", lambda e: e.tensor_tensor(out=ytile, in0=ps_ap[0:64, ob, 0:nq], in1=bcs[0:64, 0:nq], op=ALU.mult), [psb[ob], bcsb], [ybuf])

        def phase1(l, xt, xbuf, nt, s, tok0, tidx, T1, store_x=True):
            lat = (s == 0)
            nsub = nt // 128
            A = T1["a"]; bA = T1["ab"]
            for k in range(8):
                P.op("dve", lambda e, k=k: e.tensor_scalar(out=A[:, k, 0:nt], in0=xt[:, k, 0:nt], scalar1=MOD(l, 8 + k, s, True), scalar2=MOD(l, k, s, False),
                                                           op0=ALU.mult, op1=ALU.add), [bk(xbuf, k), b_mod_, b_mod1], [bk(bA, k)])
            xdst = xview(l % 2, lat, tok0 // NT)
            if store_x:
                P.dma("pool", xdst, xt[:, :, 0:nt], ball(xbuf), [db("XL%d" % (l % 2) if lat else "XC%d" % (l % 2))], own=bk(xbuf, 0))
            zt_, bz = T1["z"]; fab, bfab = T1["fab"]

            def cons_f(ci, ps, pb, rows):
                g = ci
                copy_op(evac_eng(), zt_[:, 0:nt], ps, [pb], [bz])
                for sub in range(nsub):
                    P.op("pe", lambda e, sub=sub: e.matmul(ps_ap[:, 6, 0:256], lhsT=zt_[:, sub * 128:(sub + 1) * 128], rhs=cs128, start=True, stop=True), [bz, b_cs], [psb[6]])
                    ov = fab[:, sub, g]; iv = ps_ap[:, 6, 0:256]
                    copy_op(evac_eng(), ov, iv, [psb[6]], [bfab])
            linear_fm([(l, "f")], A, bA, nt, cons_f)
            for sub in range(nsub):
                if lat:
                    dst = FAB.rearrange("(g t) c -> t g c", g=4)[tok0 + sub * 128: tok0 + (sub + 1) * 128]
                    P.dma("pool", dst, fab[:, sub], [bfab], [db("FAB")], own=bfab)
                else:
                    P.dma("pool", FABc[sub * 128:(sub + 1) * 128], fab[:, sub], [bfab], [db("FABc")], own=bfab)
            for (wkey, key) in (("nq", "naq"), ("nk", "nak")):
                tl, tb = T1[key]

                def cons(ci, ps, pb, rows, tl=tl, tb=tb):
                    copy_op(evac_eng(), tl[:, ci, 0:nt], ps, [pb], [tb])
                linear_fm([(l, wkey)], A, bA, nt, cons)
                if key == "naq":
                    dst = NAQT[tok0 // NT].rearrange("p (k t) -> p k t", k=4) if lat else NAQTc[:, :, 0:nt].rearrange("k p t -> p k t")
                    P.dma("pool", dst, tl[:, :, 0:nt], [tb], [db("NAQT" if lat else "NAQTc")], own=tb)
                else:
                    dst = NAKW[tok0 // NT].rearrange("p (k t) -> p k t", k=4) if lat else NAKTc[:, :, 0:nt].rearrange("k p t -> p k t")
                    P.dma("pool", dst, tl[:, :, 0:nt], [tb], [db("NAKW" if lat else "NAKTc")], own=tb)
                    if lat and tidx == 0:
                        P.dma("pool", HK.rearrange("(k p) t -> p k t", p=128)[:, :, 0:256], tl[:, :, 0:256], [tb], [db("HK")], own=tb)
                    if lat and tidx == NTL - 1:
                        P.dma("pool", HK.rearrange("(k p) t -> p k t", p=128)[:, :, 256:512], tl[:, :, nt - 256:nt], [tb], [db("HK")], own=tb)
            wvv, bwv, _, _ = load_piece((l, "nv"))
            nav, bnav = T1["nav"]
            for sub in range(nsub):
                for k in range(8):
                    P.op("pe", lambda e, k=k, sub=sub: e.matmul(ps_ap[:, 7, :], lhsT=A[:, k, sub * 128:(sub + 1) * 128], rhs=wvv[:, k, :], start=(k == 0), stop=(k == 7)),
                         [bk(bA, k), bwv], [psb[7]], inc=(k == 7))
                copy_op(evac_eng(), nav[:, :, sub, 0:64], ps_ap[:, 7, :].rearrange("p (h d) -> p h d", h=8), [psb[7]], [bnav])
            cview = lambda Vd: Vd.rearrange("p (h b c) -> p h b c", h=8, c=65)
            if lat:
                P.dma("pool", NAVW[(tok0 // NT) * 128:(tok0 // NT + 1) * 128, :], nav.rearrange("p h b c -> p (h b c)"), [bnav], [db("NAVW")], own=bnav)
                if tidx == 0:
                    for s_ in range(2):
                        P.dma("pool", HV[s_ * 128:(s_ + 1) * 128].rearrange("p (h d) -> p h d", h=8), nav[:, :, s_, 0:64], [bnav], [db("HV")], own=bnav)
                if tidx == NTL - 1:
                    for s_ in range(2):
                        P.dma("pool", HV[256 + s_ * 128:256 + (s_ + 1) * 128].rearrange("p (h d) -> p h d", h=8), nav[:, :, nsub - 2 + s_, 0:64], [bnav], [db("HV")], own=bnav)
            else:
                P.dma("pool", cview(NAVc), nav[:, :, 0:nsub, :], [bnav], [db("NAVc")], own=bnav)
            cq, bcq = T1["cq"]; ckv, bckv = T1["ckv"]

            def cons_cq(ci, ps, pb, rows):
                copy_op(evac_eng(), cq[:, ci, 0:nt], ps, [pb], [bcq])
            linear_fm([(l, "cq")], A, bA, nt, cons_cq)

            def cons_ckv(ci, ps, pb, rows):
                copy_op(evac_eng(), ckv[:, 0, 0:nt], ps, [pb], [bckv])
            linear_fm([(l, "ckv")], A, bA, nt, cons_ckv)
            sq, bsq = T1["sq"]; st4, bst4 = T1["st4"]
            cqn, bcqn = T1["cqn"]; ckvn, bckvn = T1["ckvn"]
            rms_fm(cq, bcq, 2, nt, gq[:, l, :], cqn, bcqn, sq, bsq, st4, bst4)
            rms_fm(ckv, bckv, 1, nt, gkv[:, l, :], ckvn, bckvn, sq, bsq, st4, bst4)
            rope, brope = T1["rope"]
            if lat:
                P.dma("sp", rope[64:96, :, 0:nt], k_rope[:, :, tok0:tok0 + nt].rearrange("a p t -> p a t"), [], [brope])
            tmp, btmp = T1["tmp"]
            krr, bkrr = T1["krr"]
            for v in range(2 if lat else 1):
                for k in range(8):
                    P.op("pe", lambda e, k=k, v=v: e.matmul(ps_ap[0:96, 6 + v, 0:nt], lhsT=wkr[:, k, v, :], rhs=A[:, k, 0:nt], start=(k == 0), stop=(k == 7)),
                         [b_wkr, bk(bA, k)], [psb[6 + v]], inc=(k == 7))
            if lat:
                P.op("dve", lambda e: e.tensor_tensor(out=tmp[64:96, 0, 0:nt], in0=ps_ap[64:96, 6, 0:nt], in1=rope[64:96, 0, 0:nt], op=ALU.mult), [psb[6], brope], [btmp])
                P.op("dve", lambda e: e.tensor_tensor(out=tmp[64:96, 1, 0:nt], in0=ps_ap[64:96, 7, 0:nt], in1=rope[64:96, 1, 0:nt], op=ALU.mult), [psb[7], brope], [btmp])
                P.op("dve", lambda e: e.tensor_tensor(out=krr[64:96, 0:nt], in0=tmp[64:96, 0, 0:nt], in1=tmp[64:96, 1, 0:nt], op=ALU.add), [btmp], [bkrr])
            else:
                P.op("act", lambda e: e.copy(out=krr[64:96, 0:nt], in_=ps_ap[64:96, 6, 0:nt]), [psb[6]], [bkrr])
            for h in range(8):
                qt_, bq_ = T1["qt"][h % 2]
                for v in range(2 if lat else 1):
                    wsrc = wq if v == 0 else wqs
                    for k in range(2):
                        P.op("pe", lambda e, k=k, v=v, h=h, wsrc=wsrc: e.matmul(ps_ap[0:96, 6 + v, 0:nt], lhsT=wsrc[:, k, h, :], rhs=cqn[:, k, 0:nt], start=(k == 0), stop=(k == 1)),
                             [b_wq, b_wqs, bcqn], [psb[6 + v]], inc=(k == 1))
                if lat:
                    P.op("act", lambda e, qt_=qt_: e.copy(out=qt_[0:64, 0:nt], in_=ps_ap[0:64, 6, 0:nt]), [psb[6]], [bq_])
                    P.op("dve", lambda e: e.tensor_tensor(out=tmp[64:96, 0, 0:nt], in0=ps_ap[64:96, 6, 0:nt], in1=rope[64:96, 0, 0:nt], op=ALU.mult), [psb[6], brope], [btmp])
                    P.op("dve", lambda e: e.tensor_tensor(out=tmp[64:96, 1, 0:nt], in0=ps_ap[64:96, 7, 0:nt], in1=rope[64:96, 1, 0:nt], op=ALU.mult), [psb[7], brope], [btmp])
                    P.op("dve", lambda e, qt_=qt_: e.tensor_tensor(out=qt_[64:96, 0:nt], in0=tmp[64:96, 0, 0:nt], in1=tmp[64:96, 1, 0:nt], op=ALU.add), [btmp], [bq_])
                else:
                    P.op("act", lambda e, qt_=qt_: e.copy(out=qt_[0:96, 0:nt], in_=ps_ap[0:96, 6, 0:nt]), [psb[6]], [bq_])
                qd = MLAQT[h, :, tok0:tok0 + nt] if lat else MLAQTc[h, :, 0:nt]
                P.dma("pool", qd, qt_[0:96, 0:nt], [bq_], [db("MLAQT" if lat else "MLAQTc")], own=bq_)
                kt_, bk_ = T1["kt"][h % 2]
                bank = 4 + (h % 2)
                P.op("pe", lambda e, h=h, bank=bank: e.matmul(ps_ap[0:64, bank, 0:nt], lhsT=wk[:, h, :], rhs=ckvn[:, 0, 0:nt], start=True, stop=True), [b_wk, bckvn], [psb[bank]])
                P.op("dve", lambda e, kt_=kt_, bank=bank: e.tensor_copy(out=kt_[0:64, 0:nt], in_=ps_ap[0:64, bank, 0:nt]), [psb[bank]], [bk_])
                P.op("act", lambda e, kt_=kt_: e.copy(out=kt_[64:96, 0:nt], in_=krr[64:96, 0:nt]), [bkrr], [bk_])
                kd = MLAKT[h * 96:(h + 1) * 96, tok0:tok0 + nt] if lat else MLAKTc[h, :, 0:nt]
                P.dma("pool", kd, kt_[0:96, 0:nt], [bk_], [db("MLAKT" if lat else "MLAKTc")], own=bk_)
            mv, bmv = T1["mlav"]
            for sub in range(nsub):
                P.op("pe", lambda e, sub=sub: e.matmul(ps_ap[:, 7, :], lhsT=ckvn[:, 0, sub * 128:(sub + 1) * 128], rhs=wuv, start=True, stop=True), [bckvn, b_wuv], [psb[7]])
                copy_op(evac_eng(), mv[:, :, sub, 0:64], ps_ap[:, 7, :].rearrange("p (h d) -> p h d", h=8), [psb[7]], [bmv])
            if lat:
                P.dma("pool", MLAV[(tok0 // NT) * 128:(tok0 // NT + 1) * 128, :], mv.rearrange("p h b c -> p (h b c)"), [bmv], [db("MLAV")], own=bmv)
            else:
                P.dma("pool", cview(MLAVc), mv[:, :, 0:nsub, :], [bmv], [db("MLAVc")], own=bmv)

        def alloc_p1():
            T1 = {}
            T1["a"], T1["ab"] = ptk("a", [8, NT], BF16)
            T1["z"] = pt("z", [NT], BF16)
            T1["fab"] = pt("fab", [4, 4, 256], BF16)
            T1["naq"] = pt("naq", [4, NT], BF16); T1["nak"] = pt("nak", [4, NT], BF16)
            T1["nav"] = pt("nav", [8, 4, 65], BF16)
            P.op("pool", lambda e: e.memset(T1["nav"][0], 1.0), [], [T1["nav"][1]])
            T1["cq"] = pt("cq", [2, NT], F32); T1["ckv"] = pt("ckv", [1, NT], F32)
            T1["cqn"] = pt("cqn", [2, NT], BF16); T1["ckvn"] = pt("ckvn", [1, NT], BF16)
            T1["rope"] = pt("rope", [2, NT], F32); T1["tmp"] = pt("tmp", [2, NT], F32)
            T1["krr"] = pt("krr", [NT], BF16)
            T1["qt"] = [pt("qt%d" % i, [NT], BF16) for i in range(2)]
            T1["kt"] = [pt("kt%d" % i, [NT], BF16) for i in range(2)]
            T1["mlav"] = pt("mlav", [8, 4, 65], BF16)
            P.op("pool", lambda e: e.memset(T1["mlav"][0], 1.0), [], [T1["mlav"][1]])
            T1["sq"] = pt("sq1", [2, NT], F32); T1["st4"] = pt("st4_1", [4, NT], F32)
            return T1

        def input_ln(src_rows, nblk, xt, xbuf, W):
            for blk in range(nblk):
                xin, bxin = W["xin"][blk % 2]
                P.dma("sp", xin, src_rows[blk * 128:(blk + 1) * 128, :], [], [bxin])
                s4, bs4 = W["s4"]
                P.op("dve", lambda e, xin=xin: e.reduce_sum(out=s4[:, 0:1], in_=xin, axis=AX.X), [bxin], [bs4])
                P.op("dve", lambda e: e.tensor_scalar(out=s4[:, 1:2], in0=s4[:, 0:1], scalar1=-1.0 / D, scalar2=None, op0=ALU.mult), [bs4], [bs4])
                P.op("act", lambda e, xin=xin: e.activation(out=xin, in_=xin, func=AF.Identity, bias=s4[:, 1:2], scale=1.0), [bxin, bs4], [bxin])
                xsq, bxsq = W["xsq"]
                P.op("pool", lambda e, xin=xin: e.tensor_tensor(out=xsq, in0=xin, in1=xin, op=ALU.mult), [bxin], [bxsq])
                P.op("dve", lambda e: e.reduce_sum(out=s4[:, 2:3], in_=xsq, axis=AX.X), [bxsq], [bs4])
                P.op("act", lambda e: e.activation(out=s4[:, 3:4], in_=s4[:, 2:3], func=AF.Ln, bias=epsb[:, 0:1], scale=1.0 / D), [bs4, b_eps], [bs4])
                P.op("act", lambda e: e.activation(out=s4[:, 3:4], in_=s4[:, 3:4], func=AF.Exp, scale=-0.5), [bs4], [bs4])
                P.op("act", lambda e, xin=xin: e.activation(out=xin, in_=xin, func=AF.Identity, scale=s4[:, 3:4]), [bxin, bs4], [bxin])
                for k in range(8):
                    bank = k % 4
                    P.op("pe", lambda e, xin=xin, k=k, bank=bank: e.transpose(ps_ap[:, bank, 0:128], xin[:, k * 128:(k + 1) * 128], ident), [bxin, b_ident], [psb[bank]])
                    P.op("act" if k % 2 else "dve",
                         (lambda e, k=k, bank=bank, blk=blk: e.activation(out=xt[:, k, blk * 128:(blk + 1) * 128], in_=ps_ap[:, bank, 0:128], func=AF.Identity,
                                                                       bias=lnin[:, 1, k:k + 1], scale=lnin[:, 0, k:k + 1])) if k % 2 else
                         (lambda e, k=k, bank=bank, blk=blk: e.tensor_scalar(out=xt[:, k, blk * 128:(blk + 1) * 128], in0=ps_ap[:, bank, 0:128], scalar1=lnin[:, 0, k:k + 1],
                                                                          scalar2=lnin[:, 1, k:k + 1], op0=ALU.mult, op1=ALU.add)),
                         [psb[bank], b_lnin], [bk(xbuf, k)])

        def exchange():
            P.barrier()
            cl = [(MLAKT[h * 96:(h + 1) * 96, :], MLAKTA[h], "MLAKT", "MLAKTA") for h in range(8)]
            cl += [(MLAV[t * 128:(t + 1) * 128, :], MLAVA[t], "MLAV", "MLAVA") for t in range(NTL)]
            cl += [(FAB[j * CH:(j + 1) * CH, :], FABA[j], "FAB", "FABA") for j in range(4 * HF)]
            cl += [(HK, HKA, "HK", "HKA"), (HV, HVA, "HV", "HVA")]
            for (src, dst, sn, dn) in cl:
                P.collective("AllGather", src.opt(), dst.opt(), GROUPS, [db(sn)], [db(dn)])
            P.barrier()
            ar.reset(PERSIST)
            hk = [pt("hk%d" % i, [4, 512], BF16) for i in range(2)]
            ho = [pt("ho%d" % i, [2, 256], BF16) for i in range(2)]
            hkv = HKA.rearrange("(r k p) t -> k p r t", r=4, k=4)
            for c in range(4):
                (hi, hb), (oi, ob) = hk[c % 2], ho[c % 2]
                P.dma("sp", hi, hkv[c], [db("HKA")], [hb])
                for half, (t0, so) in enumerate(((256, 0), (0, 4))):
                    P.op("dve", lambda e, hi=hi, oi=oi, half=half, t0=t0, so=so: e.tensor_scalar(out=oi[:, half, :], in0=hi[:, 0, t0:t0 + 256], scalar1=sel[:, so:so + 1], scalar2=None, op0=ALU.mult), [hb, b_sel], [ob])
                    for r in range(1, 4):
                        P.op("dve", lambda e, hi=hi, oi=oi, half=half, t0=t0, so=so, r=r: e.scalar_tensor_tensor(out=oi[:, half, :], in0=hi[:, r, t0:t0 + 256], scalar=sel[:, so + r:so + r + 1], in1=oi[:, half, :], op0=ALU.mult, op1=ALU.add), [hb, b_sel, ob], [ob])
                P.dma("pool", NAKH[c], oi.rearrange("p a b -> p (a b)"), [ob], [db("NAKH")], own=ob)
            hv = [pt("hv%d" % i, [4, 512], BF16) for i in range(2)]
            hvo = [pt("hvo%d" % i, [8, 65], BF16) for i in range(2)]
            for (oi, ob) in hvo:
                P.op("pool", lambda e, oi=oi: e.memset(oi, 1.0), [], [ob])
            hvv = HVA.rearrange("(r s p) c -> s p r c", r=4, s=4)
            for sblk in range(4):
                (hi, hb), (oi, ob) = hv[sblk % 2], hvo[sblk % 2]
                srcb = sblk + 2 if sblk < 2 else sblk - 2
                so = 0 if sblk < 2 else 4
                P.dma("sp", hi, hvv[srcb], [db("HVA")], [hb])
                ov = oi[:, :, 0:64]
                hv3 = lambda r, hi=hi: hi[:, r, :].rearrange("p (h d) -> p h d", h=8)
                P.op("dve", lambda e, ov=ov, hv3=hv3, so=so: e.tensor_scalar(out=ov, in0=hv3(0), scalar1=sel[:, so:so + 1], scalar2=None, op0=ALU.mult), [hb, b_sel], [ob])
                for r in range(1, 4):
                    P.op("dve", lambda e, ov=ov, hv3=hv3, so=so, r=r: e.scalar_tensor_tensor(out=ov, in0=hv3(r), scalar=sel[:, so + r:so + r + 1], in1=ov, op0=ALU.mult, op1=ALU.add), [hb, b_sel, ob], [ob])
                P.dma("pool", NAVH[sblk * 128:(sblk + 1) * 128, :], oi.rearrange("p h c -> p (h c)"), [ob], [db("NAVH")], own=ob)
            P.barrier()
            ar.reset(PERSIST)

        def alloc_att():
            att_state.clear()
            att_state.update(si=0, pi=0, obank=7)
            att_state["ptl"] = [pt("p%d" % i, [512], BF16) for i in range(6)]
            att_state["rs"] = pt("rs", [512], F32)
            att_state["bc"] = pt("bcs", [512], F32)

        def phase2_mla(l):
            ar.reset(PERSIST)
            alloc_att()
            NK = CTX + N
            NKB = NK // 128
            KT = [pt("KT%d" % i, [NK], BF16) for i in range(2)]
            VA = [pt("VA%d" % i, [NKB, 65], BF16) for i in range(2)]
            qts = [pt("q%d" % i, [512], BF16) for i in range(3)]
            ys = [pt("y%d" % i, [512], BF16) for i in range(2)]
            qi = 0
            for h in range(8):
                (kt, kb_), (va, vb) = KT[h % 2], VA[h % 2]
                P.dma("sp", kt[0:96, 0:CTX], MLAKTc[h], [db("MLAKTc")], [kb_])
                P.dma("sp", va[:, 0:2, :].rearrange("p a b -> p (a b)"), MLAVc[:, h * 130:(h + 1) * 130], [db("MLAVc")], [vb])
                for r in range(4):
                    P.dma("sp", kt[0:96, CTX + r * T:CTX + (r + 1) * T], MLAKTA[h][r * 96:(r + 1) * 96, :], [db("MLAKTA")], [kb_])
                    for t in range(NTL):
                        b0 = 2 + r * NBL + 4 * t
                        P.dma("sp", va[:, b0:b0 + 4, :].rearrange("p a b -> p (a b)"), MLAVA[t][r * 128:(r + 1) * 128, h * 260:(h + 1) * 260], [db("MLAVA")], [vb])
                kbs_all = [(kt[0:96, j * 128:(j + 1) * 128], kb_, va[:, j, :], vb, None, None) for j in range(NKB)]
                c, po = h // 2, (h % 2) * 64
                for qb in range(NTL):
                    qt_, bq_ = qts[qi % 3]; y_, by_ = ys[qi % 2]; qi += 1
                    P.dma("sp", qt_[0:96, :], MLAQT[h, :, qb * 512:(qb + 1) * 512], [db("MLAQT")], [bq_])
                    attend(qt_[0:96, :], bq_, kbs_all, MLA_SCALE, 512, y_[0:64, :], by_)
                    P.dma("pool", YMLA[qb, po:po + 64, c * NT:(c + 1) * NT], y_[0:64, :], [by_], [db("YMLA")], own=by_)
                    issue_mid_casts(6)
                    issue_late_casts(6)
                if l == 0:
                    qt_, bq_ = qts[qi % 3]; y_, by_ = ys[qi % 2]; qi += 1
                    P.dma("sp", qt_[0:96, 0:CTX], MLAQTc[h], [db("MLAQTc")], [bq_])
                    attend(qt_[0:96, 0:CTX], bq_, kbs_all[0:2], MLA_SCALE, CTX, y_[0:64, 0:CTX], by_)
                    P.dma("pool", YMLAc[c, po:po + 64, :], y_[0:64, 0:CTX], [by_], [db("YMLAc")], own=by_)
            issue_mid_casts(10 ** 6)
            issue_late_casts(10 ** 6)
            P.barrier()

        def phase2_na(l):
            ar.reset(PERSIST)
            alloc_att()
            KW = [pt("KW%d" % i, [TW], BF16) for i in range(2)]
            QN = [pt("QN%d" % i, [T], BF16) for i in range(2)]
            KC = [pt("KC%d" % i, [CTX], BF16) for i in range(2)]
            QC = [pt("QC%d" % i, [CTX], BF16) for i in range(2)]
            VN = [pt("VN%d" % i, [NWB, 65], BF16) for i in range(2)]
            VC = [pt("VC%d" % i, [2, 65], BF16) for i in range(2)]
            EBF = pt("EBF", [8, 8, 64], BF16)
            EBY = [pt("EBY%d" % i, [8, 512], BF16) for i in range(3)]
            ys = [pt("y%d" % i, [512], BF16) for i in range(2)]
            yi = 0
            for h in range(8):
                c, po = h // 2, (h % 2) * 64
                (kw, bkw), (qn, bqn), (kc, bkc), (qc, bqc) = KW[c % 2], QN[c % 2], KC[c % 2], QC[c % 2]
                (vn, bvn), (vc, bvc) = VN[h % 2], VC[h % 2]
                if h % 2 == 0:
                    for t in range(NTL):
                        P.dma("sp", kw[:, 256 + t * NT:256 + (t + 1) * NT], NAKW[t][:, c * NT:(c + 1) * NT], [db("NAKW")], [bkw])
                        P.dma("sp", qn[:, t * NT:(t + 1) * NT], NAQT[t][:, c * NT:(c + 1) * NT], [db("NAQT")], [bqn])
                    P.dma("sp", kw[:, 0:256], NAKH[c][:, 0:256], [db("NAKH")], [bkw])
                    P.dma("sp", kw[:, 256 + T:TW], NAKH[c][:, 256:512], [db("NAKH")], [bkw])
                    P.dma("sp", kc, NAKTc[c], [db("NAKTc")], [bkc])
                    if l == 0:
                        P.dma("sp", qc, NAQTc[c], [db("NAQTc")], [bqc])
                for t in range(NTL):
                    P.dma("sp", vn[:, 2 + 4 * t:6 + 4 * t, :].rearrange("p a b -> p (a b)"), NAVW[t * 128:(t + 1) * 128, h * 260:(h + 1) * 260], [db("NAVW")], [bvn])
                for sblk in range(4):
                    wb = sblk if sblk < 2 else 2 + NBL + (sblk - 2)
                    P.dma("sp", vn[:, wb, :], NAVH[sblk * 128:(sblk + 1) * 128, h * 65:(h + 1) * 65], [db("NAVH")], [bvn])
                P.dma("sp", vc.rearrange("p a b -> p (a b)"), NAVc[:, h * 130:(h + 1) * 130], [db("NAVc")], [bvc])
                ebf, bebf = EBF
                for kb in range(8):
                    for krl in range(2):
                        e0 = 15 - 2 * kb - krl
                        P.dma("sp", ebf[krl * 64:(krl + 1) * 64, kb, :, :].rearrange("p a b -> p (a b)"), EBT[l][h][:, e0 * 64:(e0 + 8) * 64], [db("EBT%d" % l)], [bebf])
                for ty in range(3):
                    eby, beby = EBY[ty]
                    P.op("dve", lambda e, eby=eby, ty=ty: e.tensor_tensor(out=eby.rearrange("p a (b c) -> p a b c", b=8), in0=ebf,
                                                                        in1=rv[:, ty].unsqueeze(3).broadcast_to([128, 8, 8, 64]), op=ALU.mult), [bebf, b_rv], [beby])
                ctxk = [(kc[po:po + 64, j * 128:(j + 1) * 128], bkc, vc[:, j, :], bvc, None, None) for j in range(2)]
                for qb in range(NQB):
                    ty = 0 if qb == 0 else (2 if qb == NQB - 1 else 1)
                    eby, beby = EBY[ty]
                    kbs = [(kw[po:po + 64, (4 * qb + j) * 128:(4 * qb + j + 1) * 128], bkw, vn[:, 4 * qb + j, :], bvn, eby[:, j, :], beby) for j in range(8)] + ctxk
                    y_, by_ = ys[yi % 2]; yi += 1
                    attend(qn[po:po + 64, qb * 512:(qb + 1) * 512], bqn, kbs, NA_SCALE, 512, y_[0:64, :], by_)
                    P.dma("pool", YNA[qb, po:po + 64, c * NT:(c + 1) * NT], y_[0:64, :], [by_], [db("YNA")], own=by_)
                if l == 0:
                    y_, by_ = ys[yi % 2]; yi += 1
                    attend(qc[po:po + 64, :], bqc, ctxk, NA_SCALE, CTX, y_[0:64, 0:CTX], by_)
                    P.dma("pool", YNAc[c, po:po + 64, :], y_[0:64, 0:CTX], [by_], [db("YNAc")], own=by_)
            P.barrier()

        def phase2_fourier(l):
            ar.reset(PERSIST)
            AB = pt("AB", [128, 256], BF16)
            GRE = pt("GRE", [N1, 128], BF16); GIM = pt("GIM", [N1, 128], BF16)
            YFS = [pt("YFS%d" % i, [32, N1], BF16) for i in range(2)]
            for g in range(4):
                gre, bgre = GRE; gim, bgim = GIM
                ab, bab = AB
                npc = CH // 128
                for r in range(4):
                    for hf in range(HF):
                        p0 = r * NBL + hf * npc
                        P.dma("sp", ab[p0:p0 + npc].rearrange("p a b -> p (a b)"), FABA[g * HF + hf][r * CH:(r + 1) * CH, :].rearrange("(a n) c -> a (n c)", n=128), [db("FABA")], [bab])
                for j4 in range(32):
                    for half in range(2):
                        bank = half
                        for jj in range(4):
                            j = j4 * 4 + jj
                            Al = ab[0:N1, :, j]; Bl = ab[0:N1, :, 128 + j]
                            o = ps_ap[:, bank, jj * N1:(jj + 1) * N1]
                            if half == 0:
                                P.op("pe", lambda e, o=o, Al=Al: e.matmul(o, lhsT=Al, rhs=c1[0:N1], start=True, stop=False), [bab, b_c1], [psb[bank]], inc=False)
                                P.op("pe", lambda e, o=o, Bl=Bl: e.matmul(o, lhsT=Bl, rhs=ns1[0:N1], start=False, stop=True), [bab, b_ns1], [psb[bank]], inc=(jj == 3))
                            else:
                                P.op("pe", lambda e, o=o, Bl=Bl: e.matmul(o, lhsT=Bl, rhs=c1[0:N1], start=True, stop=False), [bab, b_c1], [psb[bank]], inc=False)
                                P.op("pe", lambda e, o=o, Al=Al: e.matmul(o, lhsT=Al, rhs=s1[0:N1], start=False, stop=True), [bab, b_s1], [psb[bank]], inc=(jj == 3))
                        l0 = j4 * 4
                        G, bG = (gre, bgre) if half == 0 else (gim, bgim)
                        copy_op("act" if half else "dve", G[:, :, l0:l0 + 4], ps_ap[:, bank, 0:4 * N1].rearrange("p (j k) -> p k j", j=4), [psb[bank]], [bG])
                yfs, byfs = YFS[g % 2]
                for k1b in range(N1 // 16):
                    bank = 2 + (k1b % 2)
                    for kk in range(16):
                        k1 = k1b * 16 + kk
                        o = ps_ap[:, bank, kk * 32:(kk + 1) * 32]
                        P.op("pe", lambda e, o=o, k1=k1: e.matmul(o, lhsT=gre[:, k1, :], rhs=mre[:, k1, :], start=True, stop=False), [bgre, b_mre], [psb[bank]], inc=False)
                        P.op("pe", lambda e, o=o, k1=k1: e.matmul(o, lhsT=gim[:, k1, :], rhs=mimn[:, k1, :], start=False, stop=True), [bgim, b_mim], [psb[bank]], inc=(kk == 15))
                    copy_op("act" if k1b % 2 else "dve", yfs[:, :, k1b * 16:(k1b + 1) * 16], ps_ap[:, bank, :].rearrange("p (a b) -> p b a", a=16), [psb[bank]], [byfs])
                P.dma("pool", YF.rearrange("t p (c n) -> p t c n", c=4)[:, :, g, :], yfs.rearrange("p a b -> p (a b)").rearrange("p (t n) -> p t n", n=NT), [byfs], [db("YF")], own=byfs)
            if l == 0:
                fc, bfc = pt("fc", [2, 4, 256], BF16)
                P.dma("sp", fc, FABc.rearrange("(s p) g c -> p s g c", p=128), [db("FABc")], [bfc])
                yc, byc = pt("yc", [4, 256], BF16)
                for g in range(4):
                    steps = [(fc[:, nb, g, 0:128], cc256[:, nb, :]) for nb in range(2)] + [(fc[:, nb, g, 128:256], sc256[:, nb, :]) for nb in range(2)]
                    for i, (lh, rh) in enumerate(steps):
                        P.op("pe", lambda e, lh=lh, rh=rh, i=i: e.matmul(ps_ap[:, 4, 0:256], lhsT=lh, rhs=rh, start=(i == 0), stop=(i == 3)), [bfc, b_cc, b_sc], [psb[4]], inc=(i == 3))
                    copy_op("dve", yc[:, g, :], ps_ap[:, 4, 0:256], [psb[4]], [byc])
                P.dma("pool", YFc.rearrange("g p t -> p g t"), yc, [byc], [db("YFc")], own=byc)
            P.barrier()

        def alloc_p3():
            T3 = {}
            T3["x"] = [ptk("x3_%d" % i, [8, NT], F32) for i in range(1)]
            T3["mix"] = ptk("mix", [8, NT], F32)
            T3["r"] = ptk("r", [8, NT], F32)
            T3["a"] = ptk("a3", [8, NT], BF16)
            T3["mixb"] = ptk("mixb", [8, NT], BF16)
            T3["hid"] = ptk("hid", [22, NT], BF16)
            T3["y"] = [pt("y3_%d" % i, [4, NT], BF16) for i in range(3)]
            T3["st4"] = pt("st4_3", [4, NT], F32)
            T3["sg"] = [pt("sg%d" % i, [NT], BF16) for i in range(2)]
            T3["tmpf"] = [pt("tmpf%d" % i, [NT], F32) for i in range(2)]
            T3["ot"] = [pt("ot%d" % i, [1024], F32) for i in range(2)]
            return T3

        def phase3(l, nt, s, tok0, T3):
            lat = (s == 0)
            xt, xbuf = T3["x"][0]
            xsrc = xview(l % 2, lat, tok0 // NT)
            P.dma("sp", xt[:, :, 0:nt], xsrc, [db("XL%d" % (l % 2) if lat else "XC%d" % (l % 2))], xbuf)
            A, bA = T3["a"]
            for k in range(8):
                P.op("dve", lambda e, k=k: e.tensor_scalar(out=A[:, k, 0:nt], in0=xt[:, k, 0:nt], scalar1=MOD(l, 8 + k, s, True), scalar2=MOD(l, k, s, False),
                                                           op0=ALU.mult, op1=ALU.add), [xbuf[k], b_mod_, b_mod1], [bA[k]])
            ysrc = ((YF, "YF"), (YNA, "YNA"), (YMLA, "YMLA")) if lat else ((YFc, "YFc"), (YNAc, "YNAc"), (YMLAc, "YMLAc"))
            for gi in range(3):
                yt, by = T3["y"][gi]
                yd = ysrc[gi][0][tok0 // NT].rearrange("p (k t) -> p k t", k=4) if lat else ysrc[gi][0][:, :, 0:nt].rearrange("k p t -> p k t")
                P.dma("sp", yt[:, :, 0:nt], yd, [db(ysrc[gi][1])], [by])
            mix, bmix = T3["mix"]
            for gi in range(3):
                yt, by = T3["y"][gi]
                for hb in range(2):
                    pg = load_piece((l, "g", gi, hb)); pb_ = load_piece((l, "br", gi, hb))
                    for j in range(4):
                        mc = hb * 4 + j
                        ps, pb = mm_chunk(pg, j * 128, 128, A, bA, nt)
                        sg, bsg = T3["sg"][mc % 2]
                        P.op("act", lambda e, sg=sg, ps=ps: e.activation(out=sg[:, 0:nt], in_=ps, func=AF.Sigmoid), [pb], [bsg])
                        ps2, pb2 = mm_chunk(pb_, j * 128, 128, yt, by, nt)
                        if gi == 0:
                            P.op("dve", lambda e, sg=sg, ps2=ps2, mc=mc: e.tensor_tensor(out=mix[:, mc, 0:nt], in0=ps2, in1=sg[:, 0:nt], op=ALU.mult), [pb2, bsg], [bmix[mc]])
                        else:
                            tf, btf = T3["tmpf"][mc % 2]
                            P.op("dve", lambda e, sg=sg, ps2=ps2, tf=tf: e.tensor_tensor(out=tf[:, 0:nt], in0=ps2, in1=sg[:, 0:nt], op=ALU.mult), [pb2, bsg], [btf])
                            P.op("pool", lambda e, tf=tf, mc=mc: e.tensor_tensor(out=mix[:, mc, 0:nt], in0=mix[:, mc, 0:nt], in1=tf[:, 0:nt], op=ALU.add), [btf, bmix[mc]], [bmix[mc]])
            mixb, bmixb = T3["mixb"]
            for k in range(8):
                copy_op("act" if k % 2 else "dve", mixb[:, k, 0:nt], mix[:, k, 0:nt], [bmix[k]], [bmixb[k]])
            r, rb = T3["r"]

            def cons_o(ci, ps, pb, rows):
                P.op("pool", lambda e: e.tensor_scalar(out=r[:, ci, 0:nt], in0=xt[:, ci, 0:nt], scalar1=ALPHA, scalar2=None, op0=ALU.mult), [xbuf[ci]], [rb[ci]])
                P.op("dve", lambda e: e.scalar_tensor_tensor(out=r[:, ci, 0:nt], in0=ps, scalar=MOD(l, 16 + ci, s, True), in1=r[:, ci, 0:nt], op0=ALU.mult, op1=ALU.add), [pb, rb[ci], b_mod1], [rb[ci]])
            linear_fm([(l, "o", 0), (l, "o", 1)], mixb, bmixb, nt, cons_o)
            st4, bst4 = T3["st4"]
            ln_fm(r, rb, nt, lnp[:, l, 0, :], lnp[:, l, 1, :], mix, bmix, st4, bst4)
            for k in range(8):
                P.op("dve", lambda e, k=k: e.tensor_scalar(out=A[:, k, 0:nt], in0=r[:, k, 0:nt], scalar1=MOD(l, 32 + k, s, True), scalar2=MOD(l, 24 + k, s, False),
                                                           op0=ALU.mult, op1=ALU.add), [rb[k], b_mod_, b_mod1], [bA[k]])
            hid, bhid = T3["hid"]
            for j6 in range(6):
                pfg = load_piece((l, "fg", j6)); pfu = load_piece((l, "fu", j6))
                for j in range(pfg[3] // 128):
                    mc = j6 * 4 + j
                    ps, pb = mm_chunk(pfg, j * 128, 128, A, bA, nt)
                    sg, bsg = T3["sg"][mc % 2]
                    P.op("act", lambda e, sg=sg, ps=ps: e.activation(out=sg[:, 0:nt], in_=ps, func=AF.Silu), [pb], [bsg])
                    ps2, pb2 = mm_chunk(pfu, j * 128, 128, A, bA, nt)
                    P.op("dve", lambda e, sg=sg, ps2=ps2, mc=mc: e.tensor_tensor(out=hid[:, mc, 0:nt], in0=ps2, in1=sg[:, 0:nt], op=ALU.mult), [pb2, bsg], [bhid[mc]])

            def cons_d(ci, ps, pb, rows):
                P.op("pool", lambda e: e.tensor_scalar(out=r[:, ci, 0:nt], in0=r[:, ci, 0:nt], scalar1=ALPHA, scalar2=None, op0=ALU.mult), [rb[ci]], [rb[ci]])
                P.op("dve", lambda e: e.scalar_tensor_tensor(out=r[:, ci, 0:nt], in0=ps, scalar=MOD(l, 40 + ci, s, True), in1=r[:, ci, 0:nt], op0=ALU.mult, op1=ALU.add), [pb, rb[ci], b_mod1], [rb[ci]])
            linear_fm([(l, "fd", j) for j in range(4)], hid, bhid, nt, cons_d)
            ln_fm(r, rb, nt, lnp[:, l, 2, :], lnp[:, l, 3, :], mix, bmix, st4, bst4)
            return r, rb

        def write_out(r, rb, tok0, T3):
            for sub in range(4):
                ot, bot = T3["ot"][sub % 2]
                for k in range(8):
                    bank = 6 + (k // 4)
                    P.op("pe", lambda e, k=k, sub=sub, bank=bank: e.transpose(ps_ap[:, bank, (k % 4) * 128:(k % 4 + 1) * 128], r[:, k, sub * 128:(sub + 1) * 128], ident), [rb[k], b_ident], [psb[bank]])
                    if k % 4 == 3:
                        copy_op("act" if k == 3 else "dve", ot[:, (k // 4) * 512:(k // 4 + 1) * 512], ps_ap[:, bank, :], [psb[bank]], [bot])
                P.dma("pool", out_d[tok0 + sub * 128:tok0 + (sub + 1) * 128, :], ot, [bot], [db("out")], own=bot)

        def run_phase1_pass(l, from_input):
            ar.reset(PERSIST)
            load_layer_small(l)
            P.barrier()
            ar.reset(PERSIST)
            wslots[:] = [pt("wslot%d" % i, [6144], BF16) for i in range(3)]
            T1 = alloc_p1()
            xts = [ptk("xt%d" % i, [8, NT], F32) for i in range(2)]
            if from_input:
                W = {"xin": [pt("xin%d" % i, [1024], F32) for i in range(2)], "s4": pt("s4", [4], F32), "xsq": pt("xsq", [1024], F32)}
            for ti in range(NTL + 1):
                xt, xbuf = xts[ti % 2]
                lat = ti < NTL
                nt = NT if lat else CTX
                if from_input:
                    input_ln(x_in[ti * NT:(ti + 1) * NT] if lat else ctx_in, nt // 128, xt, xbuf, W)
                else:
                    P.dma("sp", xt[:, :, 0:nt], xview(l % 2, lat, ti), [db("XL%d" % (l % 2) if lat else "XC%d" % (l % 2))], ball(xbuf))
                phase1(l, xt, xbuf, nt, 0 if lat else 1, ti * NT if lat else 0, ti if lat else -1, T1, store_x=from_input)

        class _Stop(Exception):
            pass

        def chk(stage):
            if cfg.stop <= stage:
                raise _Stop()

        try:
            chk(0)
            if 'ONLYP3' in cfg.debug:
                lsel = 1 if 'L1' in cfg.debug else 0
                wslots[:] = [pt("wslot%d" % i, [6144], BF16) for i in range(3)]
                T3 = alloc_p3()
                for ti in range(NTL):
                    r, rb = phase3(lsel, NT, 0, ti * NT, T3)
                    if 'NOOUT' not in cfg.debug:
                        write_out(r, rb, ti * NT, T3)
                raise _Stop()
            run_phase1_pass(0, True)
            chk(1)
            for l in range(L):
                exchange()
                chk(2 + 10 * l)
                phase2_mla(l)
                chk(3 + 10 * l)
                phase2_na(l)
                chk(4 + 10 * l)
                phase2_fourier(l)
                chk(5 + 10 * l)
                ar.reset(PERSIST)
                wslots[:] = [pt("wslot%d" % i, [6144], BF16) for i in range(3)]
                T3 = alloc_p3()
                last = (l + 1 == L)
                for ti in range(NTL + (0 if last else 1)):
                    lat = ti < NTL
                    nt = NT if lat else CTX
                    r, rb = phase3(l, nt, 0 if lat else 1, ti * NT if lat else 0, T3)
                    if last:
                        if 'NOOUT' not in cfg.debug:
                            write_out(r, rb, ti * NT, T3)
                    else:
                        P.dma("pool", xview((l + 1) % 2, lat, ti), r[:, :, 0:nt], rb, [db("XL%d" % ((l + 1) % 2) if lat else "XC%d" % ((l + 1) % 2))], own=rb[0])
                chk(6 + 10 * l)
                if not last:
                    P.barrier()
                    run_phase1_pass(l + 1, False)
                chk(7 + 10 * l)
        except _Stop:
            pass
        P.barrier()
        P.emit()
    return nc, dbg


def _consts(cfg, q):
    bf = ml_dtypes.bfloat16
    R, T, N, N1 = cfg.R, cfg.T, cfg.N, cfg.N1
    out = {}
    out["k_ident"] = np.eye(128, dtype=np.float32)
    cl = np.outer(np.arange(128), np.arange(128)).astype(np.float64) * (2 * np.pi / 128)
    out["k_cs128"] = np.concatenate([np.cos(cl), np.sin(cl)], 1).astype(bf)
    a1 = np.outer(np.arange(N1), np.arange(N1)).astype(np.float64) * (2 * np.pi / N1)
    out["k_c1"] = np.cos(a1).astype(bf); out["k_s1"] = np.sin(a1).astype(bf); out["k_ns1"] = (-np.sin(a1)).astype(bf)
    sc = 1.0 / np.sqrt(N * 128.0)
    k1 = np.arange(N1)[None, :, None]; k2 = np.arange(32)[None, None, :]; n2 = np.arange(128)[:, None, None]
    k = T * q + k1 + N1 * k2
    ang = (n2 * k % N).astype(np.float64) * (2 * np.pi / N)
    out["k_mre"] = (np.cos(ang) * sc).reshape(128, N1 * 32).astype(bf)
    out["k_mimn"] = (-np.sin(ang) * sc).reshape(128, N1 * 32).astype(bf)
    ac = np.outer(np.arange(256), np.arange(256)).astype(np.float64) * (2 * np.pi / 256)
    scc = 1.0 / np.sqrt(256 * 128.0)
    out["k_cc256"] = (np.cos(ac) * scc).astype(bf); out["k_scn256"] = (-np.sin(ac) * scc).astype(bf)
    t = np.arange(T) + q * T
    rows = (t // GW).astype(np.float32); cols = (t % GW).astype(np.float32)
    inv = (np.float32(10000.0) ** (-np.arange(8, dtype=np.float32) / np.float32(8))).astype(np.float32)
    angr = np.concatenate([rows[:, None] * inv, cols[:, None] * inv], -1).astype(np.float32)
    co = np.cos(angr).T.astype(np.float32); si = np.sin(angr).T.astype(np.float32)
    out["k_rope"] = np.stack([np.concatenate([co, co], 0), np.concatenate([-si, si], 0)], 0).astype(np.float32)
    NR = 4 * R
    rvt = np.zeros((128, 3, 8, 8), np.float32)
    for ty, r0 in enumerate((R * q, R * q + 8, R * q + R - 8)):
        for p in range(128):
            for kb in range(8):
                krow = r0 - 4 + 2 * kb + p // 64
                for qi in range(8):
                    rs = min(max(r0 + qi - 4, 0), NR - 8)
                    rvt[p, ty, kb, qi] = 1.0 if rs <= krow < rs + 8 else 0.0
    out["k_rv"] = rvt.reshape(128, 192)
    sel = np.zeros((128, 8), np.float32)
    if q > 0:
        sel[:, q - 1] = 1.0
    if q < 3:
        sel[:, 4 + q + 1] = 1.0
    out["k_sel"] = sel
    return out


def _rpb_table(rpb):
    Ln, H = rpb.shape[0], rpb.shape[1]
    tab = np.full((Ln, H, 23, 64, 64), MASKVAL, np.float32)
    qc = np.arange(64)[None, :]; kc = np.arange(64)[:, None]
    cs = np.clip(qc - 8, 0, 48)
    valid = (kc >= cs) & (kc < cs + 16)
    dc = np.clip(kc - qc + 15, 0, 30)
    for e in range(23):
        dr = 11 - e
        if abs(dr) <= 7:
            g = rpb[:, :, dr + 7, :][:, :, dc]
            tab[:, :, e] = np.where(valid[None, None], g, np.float32(MASKVAL))
    return tab


def _pk(v, k):
    v = np.asarray(v, np.float32)
    sh = v.shape[:-1]
    return np.ascontiguousarray(np.swapaxes(v.reshape(sh + (k, 128)), -1, -2))


_CACHE = {}


def run_cfg(inputs, cfg):
    key = (cfg.R, cfg.debug, cfg.stop)
    if key not in _CACHE:
        _CACHE[key] = build_program(cfg)
    nc, dbg = _CACHE[key]
    T = cfg.T
    f32 = lambda a: np.ascontiguousarray(np.asarray(a, np.float32))
    shared = {
        "c_ctx": _pk(inputs["c_ctx"], 8), "ln_in_g": _pk(inputs["ln_in_g"], 8), "ln_in_b": _pk(inputs["ln_in_b"], 8),
        "w_mod": f32(inputs["w_mod"]), "b_mod": _pk(inputs["b_mod"], 48), "w_in": f32(inputs["w_in"]),
        "mla_q_norm_g": _pk(inputs["mla_q_norm_g"], 2), "mla_kv_norm_g": _pk(inputs["mla_kv_norm_g"], 1),
        "w_uq": f32(inputs["w_uq"]), "w_qr": f32(inputs["w_qr"]), "w_uk": f32(inputs["w_uk"]), "w_uv": f32(inputs["w_uv"]),
        "rpbT": _rpb_table(f32(inputs["na_rpb"])),
        "w_branch": f32(inputs["w_branch"]).reshape(L, 1536, D), "w_out": f32(inputs["w_out"]),
        "ln1_g": _pk(inputs["ln1_g"], 8), "ln1_b": _pk(inputs["ln1_b"], 8), "ln2_g": _pk(inputs["ln2_g"], 8), "ln2_b": _pk(inputs["ln2_b"], 8),
        "w_ffn_gate": f32(inputs["w_ffn_gate"]), "w_ffn_up": f32(inputs["w_ffn_up"]), "w_ffn_down": f32(inputs["w_ffn_down"]),
    }
    x = np.asarray(inputs["x"], np.float32); ctx = np.asarray(inputs["ctx"], np.float32); c = np.asarray(inputs["c"], np.float32)
    in_maps = []
    for i in range(8):
        b, q = i // 4, i % 4
        m = dict(shared)
        m["x"] = np.ascontiguousarray(x[b, q * T:(q + 1) * T]); m["ctx"] = np.ascontiguousarray(ctx[b]); m["c"] = _pk(c[b], 8)
        m.update(_consts(cfg, q))
        in_maps.append(m)
    res = run_bass_kernel_spmd(nc, in_maps, core_ids=list(range(8)))
    out = np.empty((2, 4 * T, D), np.float32)
    for i in range(8):
        b, q = i // 4, i % 4
        out[b, q * T:(q + 1) * T] = np.asarray(res.results[i]["out"], np.float32)
    return out, res


def kernel(**inputs):
    out, _ = run_cfg(inputs, Cfg(64))
    return out
```

```python
from contextlib import ExitStack
import ml_dtypes
from concourse.bass_utils import run_bass_kernel_spmd
import numpy as np
import concourse.bass as bass
import concourse.mybir as mybir

F32 = mybir.dt.float32
BF16 = mybir.dt.bfloat16
U8 = mybir.dt.uint8
ALU = mybir.AluOpType
AF = mybir.ActivationFunctionType
AX = mybir.AxisListType


class Buf:
    __slots__ = ("name", "w", "r", "dsem", "dcnt")

    def __init__(self, name):
        self.name = name
        self.w = None
        self.r = {}
        self.dsem = None
        self.dcnt = 0


class Prog:
    ENGS = ("pe", "act", "dve", "pool", "sp")

    def __init__(self, nc, stack):
        self.nc = nc
        self.stack = stack
        self.lists = {e: [] for e in self.ENGS}
        self.sems = {}
        self.cnt = {}
        self.waited = {e: {} for e in self.ENGS}
        for e in self.ENGS:
            self._newsem("eng_" + e)
        self._newsem("coll")
        self.pool_keys = []
        self.pool_idx = 0
        self.attached = []

    def _newsem(self, key):
        self.sems[key] = self.stack.enter_context(self.nc.semaphore(key))
        self.cnt[key] = 0
        return key

    def buf(self, name):
        return Buf(name)

    def _deps(self, reads, writes):
        deps = {}
        def add(t):
            if t is None:
                return
            k, v = t
            if deps.get(k, 0) < v:
                deps[k] = v
        for b in reads:
            add(b.w)
        for b in writes:
            add(b.w)
            for k, v in b.r.items():
                add((k, v))
        return deps

    def _waits(self, eng, deps):
        ws = []
        wd = self.waited[eng]
        for k, v in deps.items():
            if wd.get(k, 0) < v:
                wd[k] = v
                ws.append((k, v))
        return ws

    def _mark(self, tok, reads, writes):
        k, v = tok
        for b in reads:
            if b.r.get(k, 0) < v:
                b.r[k] = v
        for b in writes:
            b.w = tok
            b.r = {}

    def _dsem(self, b):
        if b.dsem is None:
            if self.pool_idx >= len(self.pool_keys):
                self.pool_keys.append(self._newsem("dq%d" % len(self.pool_keys)))
            b.dsem = self.pool_keys[self.pool_idx]
            self.pool_idx += 1
            self.attached.append(b)
        return b.dsem

    def op(self, eng, fn, reads=(), writes=(), inc=True):
        deps = self._deps(reads, writes)
        if eng == "pe":
            deps.pop("eng_pe", None)
        ws = self._waits(eng, deps)
        key = "eng_" + eng
        if inc:
            self.cnt[key] += 1
            tok = (key, self.cnt[key])
        else:
            tok = (key, self.cnt[key] + 1)
        self.lists[eng].append((ws, fn, (key, 1) if inc else None))
        self._mark(tok, reads, writes)
        if inc:
            self.waited[eng][key] = max(self.waited[eng].get(key, 0), 0)
        return tok

    def dma(self, q, out_ap, in_ap, reads, writes, own=None, **kw):
        if own is None:
            own = writes[0]
        key = self._dsem(own)
        ws = self._waits(q, self._deps(reads, writes))
        self.cnt[key] += 16
        tok = (key, self.cnt[key])
        fn = lambda e, o=out_ap, i=in_ap, kw=kw: e.dma_start(out=o, in_=i, **kw)
        self.lists[q].append((ws, fn, (key, 16)))
        self._mark(tok, reads, writes)
        return tok

    def collective(self, kind, in_ap, out_ap, groups, reads, writes):
        key = "coll"
        ws = self._waits("pool", self._deps(reads, writes))
        self.cnt[key] += 1
        tok = (key, self.cnt[key])
        fn = lambda e: e.collective_compute(kind, ALU.bypass, replica_groups=groups,
                                            ins=[in_ap], outs=[out_ap])
        self.lists["pool"].append((ws, fn, (key, 1)))
        self._mark(tok, reads, writes)
        return tok

    def barrier(self):
        for e in self.ENGS:
            ws = self._waits(e, {k: v for k, v in self.cnt.items() if v > 0})
            if ws:
                self.lists[e].append((ws, None, None))
        for b in self.attached:
            b.dsem = None
        self.attached = []
        self.pool_idx = 0

    def emit(self):
        nc = self.nc
        sems = self.sems
        lists = self.lists
        with nc.Block() as block:
            def run(eng_obj, lst):
                for ws, fn, inc in lst:
                    for k, v in ws:
                        eng_obj.wait_ge(sems[k], v)
                    if fn is not None:
                        ins = fn(eng_obj)
                        if inc is not None:
                            ins.then_inc(sems[inc[0]], inc[1])

            @block.tensor
            def _(e):
                run(e, lists["pe"])

            @block.scalar
            def _(e):
                run(e, lists["act"])

            @block.vector
            def _(e):
                run(e, lists["dve"])

            @block.gpsimd
            def _(e):
                run(e, lists["pool"])

            @block.sync
            def _(e):
                run(e, lists["sp"])


class Arena:
    def __init__(self, ap_u8, limit):
        self.ap = ap_u8
        self.limit = limit
        self.off = 0

    def reset(self, off=0):
        self.off = off

    def alloc(self, shape, dt, parts=128):
        esz = 4 if dt == F32 else 2
        n = int(np.prod(shape))
        nbytes = (n * esz + 63) // 64 * 64
        assert self.off + nbytes <= self.limit, ("arena overflow", self.off, nbytes, self.limit)
        a = self.ap[0:parts, self.off:self.off + n * esz].bitcast(dt)
        self.off += nbytes
        if len(shape) == 2:
            a = a.rearrange("p (a b) -> p a b", a=shape[0])
        elif len(shape) == 3:
            a = a.rearrange("p (a b c) -> p a b c", a=shape[0], b=shape[1])
        return a
D = 1024
CTX = 256
GW = 64
FH = 2816
L = 2
EPS = 1e-5
ALPHA = (2 * L) ** 0.25
NA_SCALE = 64 ** -0.5
MLA_SCALE = 96 ** -0.5
MASKVAL = -30000.0
O_F, O_NQ, O_NK, O_NV, O_CQ, O_CKV, O_KR, O_G = 0, 512, 1024, 1536, 2048, 2304, 2432, 2464
IN_DIM = 5536


class Cfg:
    def __init__(self, R=64, debug=(), stop=99):
        self.stop = stop
        self.R = R
        self.T = R * GW
        self.N = 4 * self.T
        self.N1 = self.N // 128
        self.NT = 512
        self.NTL = self.T // 512
        self.NQB = R // 8
        self.debug = tuple(debug)


def build_program(cfg):
    R, T, N, N1, NT, NTL, NQB = cfg.R, cfg.T, cfg.N, cfg.N1, cfg.NT, cfg.NTL, cfg.NQB
    TW = T + 512
    nc = bass.Bass("TRN2", target_bir_lowering=False)

    def din(name, shape, dt=F32):
        return nc.dram_tensor(name, list(shape), dt, kind="ExternalInput").ap()

    dbg = {}

    def dscr(name, shape, dt=BF16):
        if name in cfg.debug and name not in ('NOOUT', 'ONLYP3', 'L1'):
            t = nc.dram_tensor(name, list(shape), dt, kind="ExternalOutput")
            dbg[name] = t
        else:
            t = nc.dram_tensor(name, list(shape), dt)
        return t.ap()

    x_in = din("x", [T, D]); ctx_in = din("ctx", [CTX, D]); c_in = din("c", [128, 8]); cc_in = din("c_ctx", [128, 8])
    lng_in = din("ln_in_g", [128, 8]); lnb_in = din("ln_in_b", [128, 8])
    w_mod = din("w_mod", [L, D, 6 * D]); b_mod = din("b_mod", [L, 128, 48]); w_in = din("w_in", [L, D, IN_DIM])
    gq_in = din("mla_q_norm_g", [L, 128, 2]); gkv_in = din("mla_kv_norm_g", [L, 128, 1])
    w_uq = din("w_uq", [L, 256, 512]); w_qr = din("w_qr", [L, 256, 256]); w_uk = din("w_uk", [L, 128, 512]); w_uv = din("w_uv", [L, 128, 512])
    rpbT = din("rpbT", [L, 8, 23, 64, 64])
    w_br = din("w_branch", [L, 1536, D]); w_out = din("w_out", [L, D, D])
    ln1g = din("ln1_g", [L, 128, 8]); ln1b = din("ln1_b", [L, 128, 8]); ln2g = din("ln2_g", [L, 128, 8]); ln2b = din("ln2_b", [L, 128, 8])
    w_fg = din("w_ffn_gate", [L, D, FH]); w_fu = din("w_ffn_up", [L, D, FH]); w_fd = din("w_ffn_down", [L, FH, D])
    k_id = din("k_ident", [128, 128]); k_cs = din("k_cs128", [128, 256], BF16)
    k_c1 = din("k_c1", [N1, N1], BF16); k_s1 = din("k_s1", [N1, N1], BF16); k_ns1 = din("k_ns1", [N1, N1], BF16)
    k_mre = din("k_mre", [128, N1 * 32], BF16); k_mim = din("k_mimn", [128, N1 * 32], BF16)
    k_cc = din("k_cc256", [256, 256], BF16); k_sc = din("k_scn256", [256, 256], BF16)
    k_rope = din("k_rope", [2, 32, T]); k_rv = din("k_rv", [128, 3 * 64]); k_sel = din("k_sel", [128, 8])
    out_d = nc.dram_tensor("out", [T, D], F32, kind="ExternalOutput").ap()

    WP = {}
    CASTS = []

    def wpiece(key, src2d, r0, kch, c0, pc):
        WP[key] = (dscr("W_%s" % "_".join(str(x) for x in key), [128, kch * pc]), kch, pc)
        CASTS.append((key, src2d, r0, kch, c0, pc))

    for l in range(L):
        for nm, c0, pc in (("f", O_F, 512), ("nq", O_NQ, 512), ("nk", O_NK, 512), ("nv", O_NV, 512), ("cq", O_CQ, 256), ("ckv", O_CKV, 128)):
            wpiece((l, nm), w_in[l], 0, 8, c0, pc)
        for gi in range(3):
            for hb in range(2):
                wpiece((l, "g", gi, hb), w_in[l], 0, 8, O_G + gi * D + hb * 512, 512)
                wpiece((l, "br", gi, hb), w_br[l], gi * 512, 4, hb * 512, 512)
        for hb in range(2):
            wpiece((l, "o", hb), w_out[l], 0, 8, hb * 512, 512)
        for j in range(6):
            pc = 512 if j < 5 else 256
            wpiece((l, "fg", j), w_fg[l], 0, 8, j * 512, pc)
            wpiece((l, "fu", j), w_fu[l], 0, 8, j * 512, pc)
        for j in range(4):
            wpiece((l, "fd", j), w_fd[l], 0, 22, j * 256, 256)
    XL = [dscr("XL%d" % i, [NTL, 128, 8 * NT], F32) for i in range(2)]
    XC = [dscr("XC%d" % i, [128, 8 * CTX], F32) for i in range(2)]
    NAQT = dscr("NAQT", [NTL, 128, 4 * NT]); NAKW = dscr("NAKW", [NTL, 128, 4 * NT]); NAKH = dscr("NAKH", [4, 128, 512]); NBL = T // 128; NWB = TW // 128
    NAVW = dscr("NAVW", [NTL * 128, 2080]); NAVH = dscr("NAVH", [4 * 128, 520])
    HK = dscr("HK", [512, 512]); HKA = dscr("HKA", [4 * 512, 512]); HV = dscr("HV", [512, 512]); HVA = dscr("HVA", [4 * 512, 512])
    MLAQT = dscr("MLAQT", [8, 96, T]); MLAKT = dscr("MLAKT", [768, T]); MLAKTA = [dscr("MLAKTA%d" % h, [4 * 96, T]) for h in range(8)]
    MLAV = dscr("MLAV", [NTL * 128, 2080])
    MLAVA = [dscr("MLAVA%d" % t, [4 * 128, 2080]) for t in range(NTL)]
    HF = 2 if T * 512 > (1 << 20) else 1; CH = T // HF
    FAB = dscr("FAB", [4 * T, 256]); FABA = [dscr("FABA%d" % j, [4 * CH, 256]) for j in range(4 * HF)]
    NAQTc = dscr("NAQTc", [4, 128, CTX]); NAKTc = dscr("NAKTc", [4, 128, CTX]); NAVc = dscr("NAVc", [128, 8 * 130])
    MLAQTc = dscr("MLAQTc", [8, 96, CTX]); MLAKTc = dscr("MLAKTc", [8, 96, CTX]); MLAVc = dscr("MLAVc", [128, 8 * 130])
    FABc = dscr("FABc", [CTX, 4, 256])
    YF = dscr("YF", [NTL, 128, 4 * NT]); YNA = dscr("YNA", [NTL, 128, 4 * NT]); YMLA = dscr("YMLA", [NTL, 128, 4 * NT])
    YFc = dscr("YFc", [4, 128, CTX]); YNAc = dscr("YNAc", [4, 128, CTX]); YMLAc = dscr("YMLAc", [4, 128, CTX])
    EBT = [dscr("EBT%d" % l, [8, 64, 23 * 64]) for l in range(L)]

    GROUPS = [[0, 1, 2, 3], [4, 5, 6, 7]]
    ARENA = 196 * 1024

    with ExitStack() as st:
        P = Prog(nc, st)
        ar_ap = st.enter_context(nc.sbuf_tensor("arena", [128, ARENA], U8))
        ps_ap = st.enter_context(nc.psum_tensor("ps", [128, 8, 512], F32))
        ar = Arena(ar_ap, ARENA)
        psb = [P.buf("ps%d" % i) for i in range(8)]
        PS = lambda i: ps_ap[:, i, :]
        dram = {}

        def db(name):
            if name not in dram:
                dram[name] = P.buf(name)
            return dram[name]

        rr = {"lin": 0, "ev": 0, "bank": 0}

        def xview(i, lat, ti):
            if lat:
                return XL[i][ti].rearrange("p (k t) -> p k t", k=8)
            return XC[i].rearrange("p (k t) -> p k t", k=8)

        def evac_eng():
            rr["ev"] ^= 1
            return "act" if rr["ev"] else "dve"

        def copy_op(eng, out, in_, rd, wr):
            if eng == "act":
                P.op("act", lambda e: e.copy(out=out, in_=in_), rd, wr)
            else:
                P.op(eng, lambda e: e.tensor_copy(out=out, in_=in_), rd, wr)

        def pt(name, shape, dt, parts=128):
            return ar.alloc(shape, dt, parts=parts), P.buf(name)

        def ptk(name, shape, dt):
            return ar.alloc(shape, dt), [P.buf("%s_%d" % (name, k)) for k in range(shape[0])]

        def bk(b, k):
            return b[k] if isinstance(b, list) else b

        def ball(b):
            return list(b) if isinstance(b, list) else [b]

        ident, b_ident = pt("ident", [128], F32)
        ones, b_ones = pt("ones", [128], F32)
        epsb, b_eps = pt("eps", [1], F32)
        cs128, b_cs = pt("cs128", [256], BF16)
        c1, b_c1 = pt("c1", [N1], BF16); s1, b_s1 = pt("s1", [N1], BF16); ns1, b_ns1 = pt("ns1", [N1], BF16)
        mre, b_mre = pt("mre", [N1, 32], BF16); mimn, b_mim = pt("mimn", [N1, 32], BF16)
        cc256, b_cc = pt("cc256", [2, 256], BF16); sc256, b_sc = pt("sc256", [2, 256], BF16)
        rv, b_rv = pt("rv", [3, 8, 8], F32); sel, b_sel = pt("sel", [8], F32)
        lnin, b_lnin = pt("lnin", [2, 8], F32)
        modv, b_mod_ = pt("modv", [L, 48, 2], F32); mod1, b_mod1 = pt("mod1", [L, 48, 2], F32)
        lnp, b_lnp = pt("lnp", [L, 4, 8], F32)
        gq, b_gq = pt("gq", [L, 2], F32); gkv, b_gkv = pt("gkv", [L, 1], F32)
        wkr, b_wkr = pt("wkr", [8, 2, 96], BF16)
        wq, b_wq = pt("wq", [2, 8, 96], BF16); wqs, b_wqs = pt("wqs", [2, 8, 96], BF16)
        wk, b_wk = pt("wk", [8, 64], BF16); wuv, b_wuv = pt("wuv", [512], BF16)
        PERSIST = ar.off

        P.dma("sp", ident, k_id, [], [b_ident])
        P.op("pool", lambda e: e.memset(ones, 1.0), [], [b_ones])
        P.op("pool", lambda e: e.memset(epsb, EPS), [], [b_eps])
        P.dma("sp", cs128, k_cs, [], [b_cs])
        P.dma("sp", c1[0:N1], k_c1, [], [b_c1]); P.dma("sp", s1[0:N1], k_s1, [], [b_s1]); P.dma("sp", ns1[0:N1], k_ns1, [], [b_ns1])
        P.dma("sp", mre, k_mre.rearrange("p (a b) -> p a b", b=32), [], [b_mre])
        P.dma("sp", mimn, k_mim.rearrange("p (a b) -> p a b", b=32), [], [b_mim])
        P.dma("sp", cc256, k_cc.rearrange("(a p) k -> p a k", p=128), [], [b_cc])
        P.dma("sp", sc256, k_sc.rearrange("(a p) k -> p a k", p=128), [], [b_sc])
        P.dma("sp", rv, k_rv.rearrange("p (t a b) -> p t a b", t=3, a=8), [], [b_rv])
        P.dma("sp", sel, k_sel, [], [b_sel])
        P.dma("sp", lnin[:, 0, :], lng_in, [], [b_lnin])
        P.dma("sp", lnin[:, 1, :], lnb_in, [], [b_lnin])
        for l in range(L):
            for i, src in enumerate((ln1g, ln1b, ln2g, ln2b)):
                P.dma("sp", lnp[:, l, i, :], src[l], [], [b_lnp])
            P.dma("sp", gq[:, l, :], gq_in[l], [], [b_gq])
            P.dma("sp", gkv[:, l, :], gkv_in[l], [], [b_gkv])

        b_wcast = P.buf("wcast")
        b_wcast1 = P.buf("wcast1"); b_wcast3 = P.buf("wcast3")
        late_casts = []
        mid_casts = []
        P1KEYS = ("f", "nq", "nk", "nv", "cq", "ckv")

        def wbuf_of(key):
            if key[0] == 1:
                return b_wcast1
            return b_wcast if key[1] in P1KEYS else b_wcast3

        for (key, src2d, r0, kch, c0, pc) in CASTS:
            dstp = WP[key][0]
            for k in range(kch):
                args = (dstp[:, k * pc:(k + 1) * pc], src2d[r0 + k * 128:r0 + (k + 1) * 128, c0:c0 + pc])
                if key[0] == 0 and key[1] in P1KEYS:
                    P.dma("pool", args[0], args[1], [], [b_wcast])
                elif key[0] == 0:
                    mid_casts.append(args)
                else:
                    late_casts.append(args)

        def issue_late_casts(n):
            for _ in range(min(n, len(late_casts))):
                o, i = late_casts.pop(0)
                P.dma("pool", o, i, [], [b_wcast1])

        def issue_mid_casts(n):
            for _ in range(min(n, len(mid_casts))):
                o, i = mid_casts.pop(0)
                P.dma("pool", o, i, [], [b_wcast3])

        ar.reset(PERSIST)
        cs_t, b_cst = pt("cs_t", [2, 8], F32)
        bm_t, b_bmt = pt("bm_t", [L, 48], F32)
        P.dma("sp", cs_t[:, 0, :], c_in, [], [b_cst])
        P.dma("sp", cs_t[:, 1, :], cc_in, [], [b_cst])
        P.op("act", lambda e: e.activation(out=cs_t, in_=cs_t, func=AF.Silu), [b_cst], [b_cst])
        for l in range(L):
            P.dma("sp", bm_t[:, l, :], b_mod[l], [], [b_bmt])
        wm = [pt("wm%d" % i, [8, 512], F32) for i in range(2)]
        for l in range(L):
            for j in range(12):
                wt_, wb_ = wm[j % 2]
                P.dma("sp", wt_, w_mod[l][:, j * 512:(j + 1) * 512].rearrange("(k p) m -> p k m", p=128), [], [wb_])
                for cc in range(4):
                    ch = j * 4 + cc
                    bank = ch % 4
                    for k in range(8):
                        P.op("pe", lambda e, wt_=wt_, k=k, cc=cc, bank=bank: e.matmul(
                            ps_ap[:, bank, 0:2], lhsT=wt_[:, k, cc * 128:(cc + 1) * 128], rhs=cs_t[:, :, k],
                            start=(k == 0), stop=(k == 7)), [wb_, b_cst], [psb[bank]], inc=(k == 7))
                    P.op("dve", lambda e, l=l, ch=ch, bank=bank: e.tensor_scalar(
                        out=modv[:, l, ch, :], in0=ps_ap[:, bank, 0:2], scalar1=bm_t[:, l, ch:ch + 1], scalar2=None,
                        op0=ALU.add), [psb[bank], b_bmt], [b_mod_])
        P.op("dve", lambda e: e.tensor_scalar(out=mod1, in0=modv, scalar1=1.0, scalar2=None, op0=ALU.add), [b_mod_], [b_mod1])

        def MOD(l, idx, s, plus1):
            src = mod1 if plus1 else modv
            return src[:, l, idx, s:s + 1]

        eb_t = [pt("ebt%d" % i, [23, 64], F32, parts=64) for i in range(2)]
        eb_o = [pt("ebo%d" % i, [23, 64], BF16, parts=64) for i in range(2)]
        for l in range(L):
            for h in range(8):
                (ti, tb), (oi, ob) = eb_t[h % 2], eb_o[h % 2]
                P.dma("sp", ti, rpbT[l, h].rearrange("e k q -> k e q"), [], [tb])
                P.op("act", lambda e, ti=ti, oi=oi: e.activation(out=oi, in_=ti, func=AF.Exp), [tb], [ob])
                P.dma("pool", EBT[l][h], oi.rearrange("p a b -> p (a b)"), [ob], [db("EBT%d" % l)], own=ob)
        P.barrier()
        ar.reset(PERSIST)

        def load_layer_small(l):
            st_uq, b1 = pt("st_uq", [2, 512], F32); st_qr, b2 = pt("st_qr", [2, 256], F32)
            P.dma("sp", st_uq, w_uq[l].rearrange("(k p) m -> p k m", p=128), [], [b1])
            P.dma("sp", st_qr, w_qr[l].rearrange("(k p) m -> p k m", p=128), [], [b2])
            P.op("pool", lambda e: e.memset(wqs, 0.0), [], [b_wqs])
            P.op("pool", lambda e: e.memset(wkr, 0.0), [], [b_wkr])
            uqv = st_uq.rearrange("p k (h d) -> p k h d", h=8); qrv = st_qr.rearrange("p k (h d) -> p k h d", h=8)
            P.op("dve", lambda e: e.tensor_copy(out=wq[:, :, :, 0:64], in_=uqv), [b1], [b_wq])
            P.op("dve", lambda e: e.tensor_copy(out=wq[:, :, :, 64:96], in_=qrv), [b2], [b_wq])
            P.op("dve", lambda e: e.tensor_copy(out=wqs[:, :, :, 64:80], in_=qrv[:, :, :, 16:32]), [b2], [b_wqs])
            P.op("dve", lambda e: e.tensor_copy(out=wqs[:, :, :, 80:96], in_=qrv[:, :, :, 0:16]), [b2], [b_wqs])
            st_kr, b3 = pt("st_kr", [8, 32], F32)
            P.dma("sp", st_kr, w_in[l][:, O_KR:O_KR + 32].rearrange("(k p) m -> p k m", p=128), [], [b3])
            P.op("dve", lambda e: e.tensor_copy(out=wkr[:, :, 0, 64:96], in_=st_kr), [b3], [b_wkr])
            P.op("dve", lambda e: e.tensor_copy(out=wkr[:, :, 1, 64:80], in_=st_kr[:, :, 16:32]), [b3], [b_wkr])
            P.op("dve", lambda e: e.tensor_copy(out=wkr[:, :, 1, 80:96], in_=st_kr[:, :, 0:16]), [b3], [b_wkr])
            st_uk, b4 = pt("st_uk", [512], F32); st_uv, b5 = pt("st_uv", [512], F32)
            P.dma("sp", st_uk, w_uk[l], [], [b4]); P.dma("sp", st_uv, w_uv[l], [], [b5])
            P.op("dve", lambda e: e.tensor_copy(out=wk, in_=st_uk.rearrange("p (h d) -> p h d", h=8)), [b4], [b_wk])
            P.op("dve", lambda e: e.tensor_copy(out=wuv, in_=st_uv), [b5], [b_wuv])

        wslots = []

        def load_piece(key):
            wd, kch, pc = WP[key]
            slot, sbuf_ = wslots[rr["lin"] % len(wslots)]
            rr["lin"] += 1
            P.dma("sp", slot[:, 0:kch * pc], wd, [wbuf_of(key)], [sbuf_])
            return slot[:, 0:kch * pc].rearrange("p (k m) -> p k m", k=kch), sbuf_, kch, pc

        def mm_chunk(piece, c0, rows, act, abuf, nt):
            sv, sbuf_, kch, pc = piece
            bank = rr["bank"] % 4; rr["bank"] += 1
            for k in range(kch):
                P.op("pe", lambda e, k=k: e.matmul(ps_ap[0:rows, bank, 0:nt], lhsT=sv[:, k, c0:c0 + rows], rhs=act[:, k, 0:nt],
                                                   start=(k == 0), stop=(k == kch - 1)), [sbuf_, bk(abuf, k)], [psb[bank]], inc=(k == kch - 1))
            return ps_ap[0:rows, bank, 0:nt], psb[bank]

        def linear_fm(keys, act, abuf, nt, consume):
            ci = 0
            for key in keys:
                piece = load_piece(key)
                for c0 in range(0, piece[3], 128):
                    rows = min(128, piece[3] - c0)
                    ps, pb = mm_chunk(piece, c0, rows, act, abuf, nt)
                    consume(ci, ps, pb, rows)
                    ci += 1

        def ln_fm(r, rbuf, nt, gap, bap, sq, sqbuf, st4, st4buf):
            for k in range(8):
                if k % 2 == 0:
                    P.op("act", lambda e, k=k: e.activation(out=sq[:, k, 0:nt], in_=r[:, k, 0:nt], func=AF.Square), [bk(rbuf, k)], [bk(sqbuf, k)])
                else:
                    P.op("dve", lambda e, k=k: e.tensor_tensor(out=sq[:, k, 0:nt], in0=r[:, k, 0:nt], in1=r[:, k, 0:nt], op=ALU.mult), [bk(rbuf, k)], [bk(sqbuf, k)])
            for k in range(8):
                P.op("pe", lambda e, k=k: e.matmul(ps_ap[:, 4, 0:nt], lhsT=ones, rhs=r[:, k, 0:nt], start=(k == 0), stop=(k == 7)),
                     [b_ones, bk(rbuf, k)], [psb[4]], inc=(k == 7))
            for k in range(8):
                P.op("pe", lambda e, k=k: e.matmul(ps_ap[:, 5, 0:nt], lhsT=ones, rhs=sq[:, k, 0:nt], start=(k == 0), stop=(k == 7)),
                     [b_ones, bk(sqbuf, k)], [psb[5]], inc=(k == 7))
            mean = st4[:, 0, 0:nt]; msq = st4[:, 1, 0:nt]; rstd = st4[:, 2, 0:nt]
            P.op("dve", lambda e: e.tensor_scalar(out=mean, in0=ps_ap[:, 4, 0:nt], scalar1=1.0 / D, scalar2=None, op0=ALU.mult), [psb[4]], [st4buf])
            P.op("dve", lambda e: e.tensor_tensor(out=msq, in0=mean, in1=mean, op=ALU.mult), [st4buf], [st4buf])
            P.op("dve", lambda e: e.scalar_tensor_tensor(out=rstd, in0=ps_ap[:, 5, 0:nt], scalar=1.0 / D, in1=msq, op0=ALU.mult, op1=ALU.subtract), [psb[5], st4buf], [st4buf])
            P.op("act", lambda e: e.activation(out=rstd, in_=rstd, func=AF.Ln, bias=epsb[:, 0:1], scale=1.0), [st4buf, b_eps], [st4buf])
            P.op("act", lambda e: e.activation(out=rstd, in_=rstd, func=AF.Exp, scale=-0.5), [st4buf], [st4buf])
            for k in range(8):
                rk = r[:, k, 0:nt]
                P.op("dve", lambda e, rk=rk: e.tensor_tensor(out=rk, in0=rk, in1=mean, op=ALU.subtract), [bk(rbuf, k), st4buf], [bk(rbuf, k)])
                P.op("dve", lambda e, rk=rk: e.tensor_tensor(out=rk, in0=rk, in1=rstd, op=ALU.mult), [bk(rbuf, k), st4buf], [bk(rbuf, k)])
                P.op("act", lambda e, rk=rk, k=k: e.activation(out=rk, in_=rk, func=AF.Identity, bias=bap[:, k:k + 1], scale=gap[:, k:k + 1]), [bk(rbuf, k), b_lnp, b_lnin], [bk(rbuf, k)])

        def rms_fm(src, sbuf_, kch, nt, gsc, dst, dbuf, sq, sqbuf, st4, st4buf):
            for k in range(kch):
                P.op("act", lambda e, k=k: e.activation(out=sq[:, k, 0:nt], in_=src[:, k, 0:nt], func=AF.Square), [sbuf_], [sqbuf])
            for k in range(kch):
                P.op("pe", lambda e, k=k: e.matmul(ps_ap[:, 5, 0:nt], lhsT=ones, rhs=sq[:, k, 0:nt], start=(k == 0), stop=(k == kch - 1)),
                     [b_ones, sqbuf], [psb[5]], inc=(k == kch - 1))
            rstd = st4[:, 3, 0:nt]
            P.op("act", lambda e: e.activation(out=rstd, in_=ps_ap[:, 5, 0:nt], func=AF.Ln, bias=epsb[:, 0:1], scale=1.0 / (kch * 128)), [psb[5], b_eps], [st4buf])
            P.op("act", lambda e: e.activation(out=rstd, in_=rstd, func=AF.Exp, scale=-0.5), [st4buf], [st4buf])
            for k in range(kch):
                P.op("dve", lambda e, k=k: e.scalar_tensor_tensor(out=dst[:, k, 0:nt], in0=src[:, k, 0:nt], scalar=gsc[:, k:k + 1], in1=rstd, op0=ALU.mult, op1=ALU.mult),
                     [sbuf_, st4buf, b_gq, b_gkv], [dbuf])

        att_state = {}
        SBANKS = (0, 1, 2, 4, 5, 6)

        def attend(QT, qbuf, kbs, scale, nq, ytile, ybuf):
            ptl = att_state["ptl"]; rs, rsb = att_state["rs"]; bcs, bcsb = att_state["bc"]
            ob = att_state["obank"]; att_state["obank"] = 3 if ob == 7 else 7
            n = len(kbs)

            def S(i):
                KT, kb_, V, vb_, eb, ebb = kbs[i]
                bank = SBANKS[att_state["si"] % 6]; att_state["si"] += 1
                P.op("pe", lambda e: e.matmul(ps_ap[:, bank, 0:nq], lhsT=KT, rhs=QT, start=True, stop=True), [kb_, qbuf], [psb[bank]])
                pt_, pb_ = ptl[att_state["pi"] % len(ptl)]; att_state["pi"] += 1
                P.op("act", lambda e: e.activation(out=pt_[:, 0:nq], in_=ps_ap[:, bank, 0:nq], func=AF.Exp, scale=scale), [psb[bank]], [pb_])
                if eb is not None:
                    P.op("dve", lambda e: e.tensor_tensor(out=pt_[:, 0:nq], in0=pt_[:, 0:nq], in1=eb, op=ALU.mult), [pb_, ebb], [pb_])
                return pt_, pb_

            def PV(i, pt_, pb_):
                KT, kb_, V, vb_, eb, ebb = kbs[i]
                P.op("pe", lambda e: e.matmul(ps_ap[0:65, ob, 0:nq], lhsT=V, rhs=pt_[:, 0:nq], start=(i == 0), stop=(i == n - 1)), [vb_, pb_], [psb[ob]])

            pend = []
            for i in range(n):
                pend.append((i,) + S(i))
                if len(pend) > 3:
                    PV(*pend.pop(0))
            while pend:
                PV(*pend.pop(0))
            P.op("dve", lambda e: e.reciprocal(out=rs[64:65, 0:nq], in_=ps_ap[64:65, ob, 0:nq]), [psb[ob]], [rsb])
            bb = SBANKS[att_state["si"] % 6]; att_state["si"] += 1
            P.op("pe", lambda e: e.matmul(ps_ap[0:64, bb, 0:nq], lhsT=ones[64:65, 0:64], rhs=rs[64:65, 0:nq], start=True, stop=True), [b_ones, rsb], [psb[bb]])
            P.op("act", lambda e: e.copy(out=bcs[0:64, 0:nq], in_=ps_ap[0:64, bb, 0:nq]), [psb[bb]], [bcsb])
            P.op("dve", lambda e: e.tensor_tensor(out=ytile, in0=ps_ap[0:64, ob, 0:nq], in1=bcs[0:64, 0:nq], op=ALU.mult), [psb[ob], bcsb], [ybuf])

        def phase1(l, xt, xbuf, nt, s, tok0, tidx, T1, store_x=True):
            lat = (s == 0)
            nsub = nt // 128
            A = T1["a"]; bA = T1["ab"]
            for k in range(8):
                P.op("dve", lambda e, k=k: e.tensor_scalar(out=A[:, k, 0:nt], in0=xt[:, k, 0:nt], scalar1=MOD(l, 8 + k, s, True), scalar2=MOD(l, k, s, False),
                                                           op0=ALU.mult, op1=ALU.add), [bk(xbuf, k), b_mod_, b_mod1], [bk(bA, k)])
            xdst = xview(l % 2, lat, tok0 // NT)
            if store_x:
                P.dma("pool", xdst, xt[:, :, 0:nt], ball(xbuf), [db("XL%d" % (l % 2) if lat else "XC%d" % (l % 2))], own=bk(xbuf, 0))
            zt_, bz = T1["z"]; fab, bfab = T1["fab"]

            def cons_f(ci, ps, pb, rows):
                g = ci
                copy_op(evac_eng(), zt_[:, 0:nt], ps, [pb], [bz])
                for sub in range(nsub):
                    P.op("pe", lambda e, sub=sub: e.matmul(ps_ap[:, 6, 0:256], lhsT=zt_[:, sub * 128:(sub + 1) * 128], rhs=cs128, start=True, stop=True), [bz, b_cs], [psb[6]])
                    ov = fab[:, sub, g]; iv = ps_ap[:, 6, 0:256]
                    copy_op(evac_eng(), ov, iv, [psb[6]], [bfab])
            linear_fm([(l, "f")], A, bA, nt, cons_f)
            for sub in range(nsub):
                if lat:
                    dst = FAB.rearrange("(g t) c -> t g c", g=4)[tok0 + sub * 128: tok0 + (sub + 1) * 128]
                    P.dma("pool", dst, fab[:, sub], [bfab], [db("FAB")], own=bfab)
                else:
                    P.dma("pool", FABc[sub * 128:(sub + 1) * 128], fab[:, sub], [bfab], [db("FABc")], own=bfab)
            for (wkey, key) in (("nq", "naq"), ("nk", "nak")):
                tl, tb = T1[key]

                def cons(ci, ps, pb, rows, tl=tl, tb=tb):
                    copy_op(evac_eng(), tl[:, ci, 0:nt], ps, [pb], [tb])
                linear_fm([(l, wkey)], A, bA, nt, cons)
                if key == "naq":
                    dst = NAQT[tok0 // NT].rearrange("p (k t) -> p k t", k=4) if lat else NAQTc[:, :, 0:nt].rearrange("k p t -> p k t")
                    P.dma("pool", dst, tl[:, :, 0:nt], [tb], [db("NAQT" if lat else "NAQTc")], own=tb)
                else:
                    dst = NAKW[tok0 // NT].rearrange("p (k t) -> p k t", k=4) if lat else NAKTc[:, :, 0:nt].rearrange("k p t -> p k t")
                    P.dma("pool", dst, tl[:, :, 0:nt], [tb], [db("NAKW" if lat else "NAKTc")], own=tb)
                    if lat and tidx == 0:
                        P.dma("pool", HK.rearrange("(k p) t -> p k t", p=128)[:, :, 0:256], tl[:, :, 0:256], [tb], [db("HK")], own=tb)
                    if lat and tidx == NTL - 1:
                        P.dma("pool", HK.rearrange("(k p) t -> p k t", p=128)[:, :, 256:512], tl[:, :, nt - 256:nt], [tb], [db("HK")], own=tb)
            wvv, bwv, _, _ = load_piece((l, "nv"))
            nav, bnav = T1["nav"]
            for sub in range(nsub):
                for k in range(8):
                    P.op("pe", lambda e, k=k, sub=sub: e.matmul(ps_ap[:, 7, :], lhsT=A[:, k, sub * 128:(sub + 1) * 128], rhs=wvv[:, k, :], start=(k == 0), stop=(k == 7)),
                         [bk(bA, k), bwv], [psb[7]], inc=(k == 7))
                copy_op(evac_eng(), nav[:, :, sub, 0:64], ps_ap[:, 7, :].rearrange("p (h d) -> p h d", h=8), [psb[7]], [bnav])
            cview = lambda Vd: Vd.rearrange("p (h b c) -> p h b c", h=8, c=65)
            if lat:
                P.dma("pool", NAVW[(tok0 // NT) * 128:(tok0 // NT + 1) * 128, :], nav.rearrange("p h b c -> p (h b c)"), [bnav], [db("NAVW")], own=bnav)
                if tidx == 0:
                    for s_ in range(2):
                        P.dma("pool", HV[s_ * 128:(s_ + 1) * 128].rearrange("p (h d) -> p h d", h=8), nav[:, :, s_, 0:64], [bnav], [db("HV")], own=bnav)
                if tidx == NTL - 1:
                    for s_ in range(2):
                        P.dma("pool", HV[256 + s_ * 128:256 + (s_ + 1) * 128].rearrange("p (h d) -> p h d", h=8), nav[:, :, nsub - 2 + s_, 0:64], [bnav], [db("HV")], own=bnav)
            else:
                P.dma("pool", cview(NAVc), nav[:, :, 0:nsub, :], [bnav], [db("NAVc")], own=bnav)
            cq, bcq = T1["cq"]; ckv, bckv = T1["ckv"]

            def cons_cq(ci, ps, pb, rows):
                copy_op(evac_eng(), cq[:, ci, 0:nt], ps, [pb], [bcq])
            linear_fm([(l, "cq")], A, bA, nt, cons_cq)

            def cons_ckv(ci, ps, pb, rows):
                copy_op(evac_eng(), ckv[:, 0, 0:nt], ps, [pb], [bckv])
            linear_fm([(l, "ckv")], A, bA, nt, cons_ckv)
            sq, bsq = T1["sq"]; st4, bst4 = T1["st4"]
            cqn, bcqn = T1["cqn"]; ckvn, bckvn = T1["ckvn"]
            rms_fm(cq, bcq, 2, nt, gq[:, l, :], cqn, bcqn, sq, bsq, st4, bst4)
            rms_fm(ckv, bckv, 1, nt, gkv[:, l, :], ckvn, bckvn, sq, bsq, st4, bst4)
            rope, brope = T1["rope"]
            if lat:
                P.dma("sp", rope[64:96, :, 0:nt], k_rope[:, :, tok0:tok0 + nt].rearrange("a p t -> p a t"), [], [brope])
            tmp, btmp = T1["tmp"]
            krr, bkrr = T1["krr"]
            for v in range(2 if lat else 1):
                for k in range(8):
                    P.op("pe", lambda e, k=k, v=v: e.matmul(ps_ap[0:96, 6 + v, 0:nt], lhsT=wkr[:, k, v, :], rhs=A[:, k, 0:nt], start=(k == 0), stop=(k == 7)),
                         [b_wkr, bk(bA, k)], [psb[6 + v]], inc=(k == 7))
            if lat:
                P.op("dve", lambda e: e.tensor_tensor(out=tmp[64:96, 0, 0:nt], in0=ps_ap[64:96, 6, 0:nt], in1=rope[64:96, 0, 0:nt], op=ALU.mult), [psb[6], brope], [btmp])
                P.op("dve", lambda e: e.tensor_tensor(out=tmp[64:96, 1, 0:nt], in0=ps_ap[64:96, 7, 0:nt], in1=rope[64:96, 1, 0:nt], op=ALU.mult), [psb[7], brope], [btmp])
                P.op("dve", lambda e: e.tensor_tensor(out=krr[64:96, 0:nt], in0=tmp[64:96, 0, 0:nt], in1=tmp[64:96, 1, 0:nt], op=ALU.add), [btmp], [bkrr])
            else:
                P.op("act", lambda e: e.copy(out=krr[64:96, 0:nt], in_=ps_ap[64:96, 6, 0:nt]), [psb[6]], [bkrr])
            for h in range(8):
                qt_, bq_ = T1["qt"][h % 2]
                for v in range(2 if lat else 1):
                    wsrc = wq if v == 0 else wqs
                    for k in range(2):
                        P.op("pe", lambda e, k=k, v=v, h=h, wsrc=wsrc: e.matmul(ps_ap[0:96, 6 + v, 0:nt], lhsT=wsrc[:, k, h, :], rhs=cqn[:, k, 0:nt], start=(k == 0), stop=(k == 1)),
                             [b_wq, b_wqs, bcqn], [psb[6 + v]], inc=(k == 1))
                if lat:
                    P.op("act", lambda e, qt_=qt_: e.copy(out=qt_[0:64, 0:nt], in_=ps_ap[0:64, 6, 0:nt]), [psb[6]], [bq_])
                    P.op("dve", lambda e: e.tensor_tensor(out=tmp[64:96, 0, 0:nt], in0=ps_ap[64:96, 6, 0:nt], in1=rope[64:96, 0, 0:nt], op=ALU.mult), [psb[6], brope], [btmp])
                    P.op("dve", lambda e: e.tensor_tensor(out=tmp[64:96, 1, 0:nt], in0=ps_ap[64:96, 7, 0:nt], in1=rope[64:96, 1, 0:nt], op=ALU.mult), [psb[7], brope], [btmp])
                    P.op("dve", lambda e, qt_=qt_: e.tensor_tensor(out=qt_[64:96, 0:nt], in0=tmp[64:96, 0, 0:nt], in1=tmp[64:96, 1, 0:nt], op=ALU.add), [btmp], [bq_])
                else:
                    P.op("act", lambda e, qt_=qt_: e.copy(out=qt_[0:96, 0:nt], in_=ps_ap[0:96, 6, 0:nt]), [psb[6]], [bq_])
                qd = MLAQT[h, :, tok0:tok0 + nt] if lat else MLAQTc[h, :, 0:nt]
                P.dma("pool", qd, qt_[0:96, 0:nt], [bq_], [db("MLAQT" if lat else "MLAQTc")], own=bq_)
                kt_, bk_ = T1["kt"][h % 2]
                bank = 4 + (h % 2)
                P.op("pe", lambda e, h=h, bank=bank: e.matmul(ps_ap[0:64, bank, 0:nt], lhsT=wk[:, h, :], rhs=ckvn[:, 0, 0:nt], start=True, stop=True), [b_wk, bckvn], [psb[bank]])
                P.op("dve", lambda e, kt_=kt_, bank=bank: e.tensor_copy(out=kt_[0:64, 0:nt], in_=ps_ap[0:64, bank, 0:nt]), [psb[bank]], [bk_])
                P.op("act", lambda e, kt_=kt_: e.copy(out=kt_[64:96, 0:nt], in_=krr[64:96, 0:nt]), [bkrr], [bk_])
                kd = MLAKT[h * 96:(h + 1) * 96, tok0:tok0 + nt] if lat else MLAKTc[h, :, 0:nt]
                P.dma("pool", kd, kt_[0:96, 0:nt], [bk_], [db("MLAKT" if lat else "MLAKTc")], own=bk_)
            mv, bmv = T1["mlav"]
            for sub in range(nsub):
                P.op("pe", lambda e, sub=sub: e.matmul(ps_ap[:, 7, :], lhsT=ckvn[:, 0, sub * 128:(sub + 1) * 128], rhs=wuv, start=True, stop=True), [bckvn, b_wuv], [psb[7]])
                copy_op(evac_eng(), mv[:, :, sub, 0:64], ps_ap[:, 7, :].rearrange("p (h d) -> p h d", h=8), [psb[7]], [bmv])
            if lat:
                P.dma("pool", MLAV[(tok0 // NT) * 128:(tok0 // NT + 1) * 128, :], mv.rearrange("p h b c -> p (h b c)"), [bmv], [db("MLAV")], own=bmv)
            else:
                P.dma("pool", cview(MLAVc), mv[:, :, 0:nsub, :], [bmv], [db("MLAVc")], own=bmv)

        def alloc_p1():
            T1 = {}
            T1["a"], T1["ab"] = ptk("a", [8, NT], BF16)
            T1["z"] = pt("z", [NT], BF16)
            T1["fab"] = pt("fab", [4, 4, 256], BF16)
            T1["naq"] = pt("naq", [4, NT], BF16); T1["nak"] = pt("nak", [4, NT], BF16)
            T1["nav"] = pt("nav", [8, 4, 65], BF16)
            P.op("pool", lambda e: e.memset(T1["nav"][0], 1.0), [], [T1["nav"][1]])
            T1["cq"] = pt("cq", [2, NT], F32); T1["ckv"] = pt("ckv", [1, NT], F32)
            T1["cqn"] = pt("cqn", [2, NT], BF16); T1["ckvn"] = pt("ckvn", [1, NT], BF16)
            T1["rope"] = pt("rope", [2, NT], F32); T1["tmp"] = pt("tmp", [2, NT], F32)
            T1["krr"] = pt("krr", [NT], BF16)
            T1["qt"] = [pt("qt%d" % i, [NT], BF16) for i in range(2)]
            T1["kt"] = [pt("kt%d" % i, [NT], BF16) for i in range(2)]
            T1["mlav"] = pt("mlav", [8, 4, 65], BF16)
            P.op("pool", lambda e: e.memset(T1["mlav"][0], 1.0), [], [T1["mlav"][1]])
            T1["sq"] = pt("sq1", [2, NT], F32); T1["st4"] = pt("st4_1", [4, NT], F32)
            return T1

        def input_ln(src_rows, nblk, xt, xbuf, W):
            for blk in range(nblk):
                xin, bxin = W["xin"][blk % 2]
                P.dma("sp", xin, src_rows[blk * 128:(blk + 1) * 128, :], [], [bxin])
                s4, bs4 = W["s4"]
                P.op("dve", lambda e, xin=xin: e.reduce_sum(out=s4[:, 0:1], in_=xin, axis=AX.X), [bxin], [bs4])
                P.op("dve", lambda e: e.tensor_scalar(out=s4[:, 1:2], in0=s4[:, 0:1], scalar1=-1.0 / D, scalar2=None, op0=ALU.mult), [bs4], [bs4])
                P.op("act", lambda e, xin=xin: e.activation(out=xin, in_=xin, func=AF.Identity, bias=s4[:, 1:2], scale=1.0), [bxin, bs4], [bxin])
                xsq, bxsq = W["xsq"]
                P.op("act", lambda e, xin=xin: e.activation(out=xsq, in_=xin, func=AF.Square), [bxin], [bxsq])
                P.op("dve", lambda e: e.reduce_sum(out=s4[:, 2:3], in_=xsq, axis=AX.X), [bxsq], [bs4])
                P.op("act", lambda e: e.activation(out=s4[:, 3:4], in_=s4[:, 2:3], func=AF.Ln, bias=epsb[:, 0:1], scale=1.0 / D), [bs4, b_eps], [bs4])
                P.op("act", lambda e: e.activation(out=s4[:, 3:4], in_=s4[:, 3:4], func=AF.Exp, scale=-0.5), [bs4], [bs4])
                P.op("act", lambda e, xin=xin: e.activation(out=xin, in_=xin, func=AF.Identity, scale=s4[:, 3:4]), [bxin, bs4], [bxin])
                for k in range(8):
                    bank = k % 4
                    P.op("pe", lambda e, xin=xin, k=k, bank=bank: e.transpose(ps_ap[:, bank, 0:128], xin[:, k * 128:(k + 1) * 128], ident), [bxin, b_ident], [psb[bank]])
                    P.op("act" if k % 2 else "dve",
                         (lambda e, k=k, bank=bank, blk=blk: e.activation(out=xt[:, k, blk * 128:(blk + 1) * 128], in_=ps_ap[:, bank, 0:128], func=AF.Identity,
                                                                       bias=lnin[:, 1, k:k + 1], scale=lnin[:, 0, k:k + 1])) if k % 2 else
                         (lambda e, k=k, bank=bank, blk=blk: e.tensor_scalar(out=xt[:, k, blk * 128:(blk + 1) * 128], in0=ps_ap[:, bank, 0:128], scalar1=lnin[:, 0, k:k + 1],
                                                                          scalar2=lnin[:, 1, k:k + 1], op0=ALU.mult, op1=ALU.add)),
                         [psb[bank], b_lnin], [bk(xbuf, k)])

        def exchange():
            P.barrier()
            cl = [(MLAKT[h * 96:(h + 1) * 96, :], MLAKTA[h], "MLAKT", "MLAKTA") for h in range(8)]
            cl += [(MLAV[t * 128:(t + 1) * 128, :], MLAVA[t], "MLAV", "MLAVA") for t in range(NTL)]
            cl += [(FAB[j * CH:(j + 1) * CH, :], FABA[j], "FAB", "FABA") for j in range(4 * HF)]
            cl += [(HK, HKA, "HK", "HKA"), (HV, HVA, "HV", "HVA")]
            for (src, dst, sn, dn) in cl:
                P.collective("AllGather", src.opt(), dst.opt(), GROUPS, [db(sn)], [db(dn)])
            P.barrier()
            ar.reset(PERSIST)
            hk = [pt("hk%d" % i, [4, 512], BF16) for i in range(2)]
            ho = [pt("ho%d" % i, [2, 256], BF16) for i in range(2)]
            hkv = HKA.rearrange("(r k p) t -> k p r t", r=4, k=4)
            for c in range(4):
                (hi, hb), (oi, ob) = hk[c % 2], ho[c % 2]
                P.dma("sp", hi, hkv[c], [db("HKA")], [hb])
                for half, (t0, so) in enumerate(((256, 0), (0, 4))):
                    P.op("dve", lambda e, hi=hi, oi=oi, half=half, t0=t0, so=so: e.tensor_scalar(out=oi[:, half, :], in0=hi[:, 0, t0:t0 + 256], scalar1=sel[:, so:so + 1], scalar2=None, op0=ALU.mult), [hb, b_sel], [ob])
                    for r in range(1, 4):
                        P.op("dve", lambda e, hi=hi, oi=oi, half=half, t0=t0, so=so, r=r: e.scalar_tensor_tensor(out=oi[:, half, :], in0=hi[:, r, t0:t0 + 256], scalar=sel[:, so + r:so + r + 1], in1=oi[:, half, :], op0=ALU.mult, op1=ALU.add), [hb, b_sel, ob], [ob])
                P.dma("pool", NAKH[c], oi.rearrange("p a b -> p (a b)"), [ob], [db("NAKH")], own=ob)
            hv = [pt("hv%d" % i, [4, 512], BF16) for i in range(2)]
            hvo = [pt("hvo%d" % i, [8, 65], BF16) for i in range(2)]
            for (oi, ob) in hvo:
                P.op("pool", lambda e, oi=oi: e.memset(oi, 1.0), [], [ob])
            hvv = HVA.rearrange("(r s p) c -> s p r c", r=4, s=4)
            for sblk in range(4):
                (hi, hb), (oi, ob) = hv[sblk % 2], hvo[sblk % 2]
                srcb = sblk + 2 if sblk < 2 else sblk - 2
                so = 0 if sblk < 2 else 4
                P.dma("sp", hi, hvv[srcb], [db("HVA")], [hb])
                ov = oi[:, :, 0:64]
                hv3 = lambda r, hi=hi: hi[:, r, :].rearrange("p (h d) -> p h d", h=8)
                P.op("dve", lambda e, ov=ov, hv3=hv3, so=so: e.tensor_scalar(out=ov, in0=hv3(0), scalar1=sel[:, so:so + 1], scalar2=None, op0=ALU.mult), [hb, b_sel], [ob])
                for r in range(1, 4):
                    P.op("dve", lambda e, ov=ov, hv3=hv3, so=so, r=r: e.scalar_tensor_tensor(out=ov, in0=hv3(r), scalar=sel[:, so + r:so + r + 1], in1=ov, op0=ALU.mult, op1=ALU.add), [hb, b_sel, ob], [ob])
                P.dma("pool", NAVH[sblk * 128:(sblk + 1) * 128, :], oi.rearrange("p h c -> p (h c)"), [ob], [db("NAVH")], own=ob)
            P.barrier()
            ar.reset(PERSIST)

        def alloc_att():
            att_state.clear()
            att_state.update(si=0, pi=0, obank=7)
            att_state["ptl"] = [pt("p%d" % i, [512], BF16) for i in range(6)]
            att_state["rs"] = pt("rs", [512], F32)
            att_state["bc"] = pt("bcs", [512], F32)

        def phase2_mla(l):
            ar.reset(PERSIST)
            alloc_att()
            NK = CTX + N
            NKB = NK // 128
            KT = [pt("KT%d" % i, [NK], BF16) for i in range(2)]
            VA = [pt("VA%d" % i, [NKB, 65], BF16) for i in range(2)]
            qts = [pt("q%d" % i, [512], BF16) for i in range(3)]
            ys = [pt("y%d" % i, [512], BF16) for i in range(2)]
            qi = 0
            for h in range(8):
                (kt, kb_), (va, vb) = KT[h % 2], VA[h % 2]
                P.dma("sp", kt[0:96, 0:CTX], MLAKTc[h], [db("MLAKTc")], [kb_])
                P.dma("sp", va[:, 0:2, :].rearrange("p a b -> p (a b)"), MLAVc[:, h * 130:(h + 1) * 130], [db("MLAVc")], [vb])
                for r in range(4):
                    P.dma("sp", kt[0:96, CTX + r * T:CTX + (r + 1) * T], MLAKTA[h][r * 96:(r + 1) * 96, :], [db("MLAKTA")], [kb_])
                    for t in range(NTL):
                        b0 = 2 + r * NBL + 4 * t
                        P.dma("sp", va[:, b0:b0 + 4, :].rearrange("p a b -> p (a b)"), MLAVA[t][r * 128:(r + 1) * 128, h * 260:(h + 1) * 260], [db("MLAVA")], [vb])
                kbs_all = [(kt[0:96, j * 128:(j + 1) * 128], kb_, va[:, j, :], vb, None, None) for j in range(NKB)]
                c, po = h // 2, (h % 2) * 64
                for qb in range(NTL):
                    qt_, bq_ = qts[qi % 3]; y_, by_ = ys[qi % 2]; qi += 1
                    P.dma("sp", qt_[0:96, :], MLAQT[h, :, qb * 512:(qb + 1) * 512], [db("MLAQT")], [bq_])
                    attend(qt_[0:96, :], bq_, kbs_all, MLA_SCALE, 512, y_[0:64, :], by_)
                    P.dma("pool", YMLA[qb, po:po + 64, c * NT:(c + 1) * NT], y_[0:64, :], [by_], [db("YMLA")], own=by_)
                    issue_mid_casts(6)
                    issue_late_casts(6)
                if l == 0:
                    qt_, bq_ = qts[qi % 3]; y_, by_ = ys[qi % 2]; qi += 1
                    P.dma("sp", qt_[0:96, 0:CTX], MLAQTc[h], [db("MLAQTc")], [bq_])
                    attend(qt_[0:96, 0:CTX], bq_, kbs_all[0:2], MLA_SCALE, CTX, y_[0:64, 0:CTX], by_)
                    P.dma("pool", YMLAc[c, po:po + 64, :], y_[0:64, 0:CTX], [by_], [db("YMLAc")], own=by_)
            issue_mid_casts(10 ** 6)
            issue_late_casts(10 ** 6)
            P.barrier()

        def phase2_na(l):
            ar.reset(PERSIST)
            alloc_att()
            KW = [pt("KW%d" % i, [TW], BF16) for i in range(2)]
            QN = [pt("QN%d" % i, [T], BF16) for i in range(2)]
            KC = [pt("KC%d" % i, [CTX], BF16) for i in range(2)]
            QC = [pt("QC%d" % i, [CTX], BF16) for i in range(2)]
            VN = [pt("VN%d" % i, [NWB, 65], BF16) for i in range(2)]
            VC = [pt("VC%d" % i, [2, 65], BF16) for i in range(2)]
            EBF = pt("EBF", [8, 8, 64], BF16)
            EBY = [pt("EBY%d" % i, [8, 512], BF16) for i in range(3)]
            ys = [pt("y%d" % i, [512], BF16) for i in range(2)]
            yi = 0
            for h in range(8):
                c, po = h // 2, (h % 2) * 64
                (kw, bkw), (qn, bqn), (kc, bkc), (qc, bqc) = KW[c % 2], QN[c % 2], KC[c % 2], QC[c % 2]
                (vn, bvn), (vc, bvc) = VN[h % 2], VC[h % 2]
                if h % 2 == 0:
                    for t in range(NTL):
                        P.dma("sp", kw[:, 256 + t * NT:256 + (t + 1) * NT], NAKW[t][:, c * NT:(c + 1) * NT], [db("NAKW")], [bkw])
                        P.dma("sp", qn[:, t * NT:(t + 1) * NT], NAQT[t][:, c * NT:(c + 1) * NT], [db("NAQT")], [bqn])
                    P.dma("sp", kw[:, 0:256], NAKH[c][:, 0:256], [db("NAKH")], [bkw])
                    P.dma("sp", kw[:, 256 + T:TW], NAKH[c][:, 256:512], [db("NAKH")], [bkw])
                    P.dma("sp", kc, NAKTc[c], [db("NAKTc")], [bkc])
                    if l == 0:
                        P.dma("sp", qc, NAQTc[c], [db("NAQTc")], [bqc])
                for t in range(NTL):
                    P.dma("sp", vn[:, 2 + 4 * t:6 + 4 * t, :].rearrange("p a b -> p (a b)"), NAVW[t * 128:(t + 1) * 128, h * 260:(h + 1) * 260], [db("NAVW")], [bvn])
                for sblk in range(4):
                    wb = sblk if sblk < 2 else 2 + NBL + (sblk - 2)
                    P.dma("sp", vn[:, wb, :], NAVH[sblk * 128:(sblk + 1) * 128, h * 65:(h + 1) * 65], [db("NAVH")], [bvn])
                P.dma("sp", vc.rearrange("p a b -> p (a b)"), NAVc[:, h * 130:(h + 1) * 130], [db("NAVc")], [bvc])
                ebf, bebf = EBF
                for kb in range(8):
                    for krl in range(2):
                        e0 = 15 - 2 * kb - krl
                        P.dma("sp", ebf[krl * 64:(krl + 1) * 64, kb, :, :].rearrange("p a b -> p (a b)"), EBT[l][h][:, e0 * 64:(e0 + 8) * 64], [db("EBT%d" % l)], [bebf])
                for ty in range(3):
                    eby, beby = EBY[ty]
                    P.op("dve", lambda e, eby=eby, ty=ty: e.tensor_tensor(out=eby.rearrange("p a (b c) -> p a b c", b=8), in0=ebf,
                                                                        in1=rv[:, ty].unsqueeze(3).broadcast_to([128, 8, 8, 64]), op=ALU.mult), [bebf, b_rv], [beby])
                ctxk = [(kc[po:po + 64, j * 128:(j + 1) * 128], bkc, vc[:, j, :], bvc, None, None) for j in range(2)]
                for qb in range(NQB):
                    ty = 0 if qb == 0 else (2 if qb == NQB - 1 else 1)
                    eby, beby = EBY[ty]
                    kbs = [(kw[po:po + 64, (4 * qb + j) * 128:(4 * qb + j + 1) * 128], bkw, vn[:, 4 * qb + j, :], bvn, eby[:, j, :], beby) for j in range(8)] + ctxk
                    y_, by_ = ys[yi % 2]; yi += 1
                    attend(qn[po:po + 64, qb * 512:(qb + 1) * 512], bqn, kbs, NA_SCALE, 512, y_[0:64, :], by_)
                    P.dma("pool", YNA[qb, po:po + 64, c * NT:(c + 1) * NT], y_[0:64, :], [by_], [db("YNA")], own=by_)
                if l == 0:
                    y_, by_ = ys[yi % 2]; yi += 1
                    attend(qc[po:po + 64, :], bqc, ctxk, NA_SCALE, CTX, y_[0:64, 0:CTX], by_)
                    P.dma("pool", YNAc[c, po:po + 64, :], y_[0:64, 0:CTX], [by_], [db("YNAc")], own=by_)
            P.barrier()

        def phase2_fourier(l):
            ar.reset(PERSIST)
            AB = pt("AB", [128, 256], BF16)
            GRE = pt("GRE", [N1, 128], BF16); GIM = pt("GIM", [N1, 128], BF16)
            YFS = [pt("YFS%d" % i, [32, N1], BF16) for i in range(2)]
            for g in range(4):
                gre, bgre = GRE; gim, bgim = GIM
                ab, bab = AB
                npc = CH // 128
                for r in range(4):
                    for hf in range(HF):
                        p0 = r * NBL + hf * npc
                        P.dma("sp", ab[p0:p0 + npc].rearrange("p a b -> p (a b)"), FABA[g * HF + hf][r * CH:(r + 1) * CH, :].rearrange("(a n) c -> a (n c)", n=128), [db("FABA")], [bab])
                for j4 in range(32):
                    for half in range(2):
                        bank = half
                        for jj in range(4):
                            j = j4 * 4 + jj
                            Al = ab[0:N1, :, j]; Bl = ab[0:N1, :, 128 + j]
                            o = ps_ap[:, bank, jj * N1:(jj + 1) * N1]
                            if half == 0:
                                P.op("pe", lambda e, o=o, Al=Al: e.matmul(o, lhsT=Al, rhs=c1[0:N1], start=True, stop=False), [bab, b_c1], [psb[bank]], inc=False)
                                P.op("pe", lambda e, o=o, Bl=Bl: e.matmul(o, lhsT=Bl, rhs=ns1[0:N1], start=False, stop=True), [bab, b_ns1], [psb[bank]], inc=(jj == 3))
                            else:
                                P.op("pe", lambda e, o=o, Bl=Bl: e.matmul(o, lhsT=Bl, rhs=c1[0:N1], start=True, stop=False), [bab, b_c1], [psb[bank]], inc=False)
                                P.op("pe", lambda e, o=o, Al=Al: e.matmul(o, lhsT=Al, rhs=s1[0:N1], start=False, stop=True), [bab, b_s1], [psb[bank]], inc=(jj == 3))
                        l0 = j4 * 4
                        G, bG = (gre, bgre) if half == 0 else (gim, bgim)
                        copy_op("act" if half else "dve", G[:, :, l0:l0 + 4], ps_ap[:, bank, 0:4 * N1].rearrange("p (j k) -> p k j", j=4), [psb[bank]], [bG])
                yfs, byfs = YFS[g % 2]
                for k1b in range(N1 // 16):
                    bank = 2 + (k1b % 2)
                    for kk in range(16):
                        k1 = k1b * 16 + kk
                        o = ps_ap[:, bank, kk * 32:(kk + 1) * 32]
                        P.op("pe", lambda e, o=o, k1=k1: e.matmul(o, lhsT=gre[:, k1, :], rhs=mre[:, k1, :], start=True, stop=False), [bgre, b_mre], [psb[bank]], inc=False)
                        P.op("pe", lambda e, o=o, k1=k1: e.matmul(o, lhsT=gim[:, k1, :], rhs=mimn[:, k1, :], start=False, stop=True), [bgim, b_mim], [psb[bank]], inc=(kk == 15))
                    copy_op("act" if k1b % 2 else "dve", yfs[:, :, k1b * 16:(k1b + 1) * 16], ps_ap[:, bank, :].rearrange("p (a b) -> p b a", a=16), [psb[bank]], [byfs])
                P.dma("pool", YF.rearrange("t p (c n) -> p t c n", c=4)[:, :, g, :], yfs.rearrange("p a b -> p (a b)").rearrange("p (t n) -> p t n", n=NT), [byfs], [db("YF")], own=byfs)
            if l == 0:
                fc, bfc = pt("fc", [2, 4, 256], BF16)
                P.dma("sp", fc, FABc.rearrange("(s p) g c -> p s g c", p=128), [db("FABc")], [bfc])
                yc, byc = pt("yc", [4, 256], BF16)
                for g in range(4):
                    steps = [(fc[:, nb, g, 0:128], cc256[:, nb, :]) for nb in range(2)] + [(fc[:, nb, g, 128:256], sc256[:, nb, :]) for nb in range(2)]
                    for i, (lh, rh) in enumerate(steps):
                        P.op("pe", lambda e, lh=lh, rh=rh, i=i: e.matmul(ps_ap[:, 4, 0:256], lhsT=lh, rhs=rh, start=(i == 0), stop=(i == 3)), [bfc, b_cc, b_sc], [psb[4]], inc=(i == 3))
                    copy_op("dve", yc[:, g, :], ps_ap[:, 4, 0:256], [psb[4]], [byc])
                P.dma("pool", YFc.rearrange("g p t -> p g t"), yc, [byc], [db("YFc")], own=byc)
            P.barrier()

        def alloc_p3():
            T3 = {}
            T3["x"] = [ptk("x3_%d" % i, [8, NT], F32) for i in range(1)]
            T3["mix"] = ptk("mix", [8, NT], F32)
            T3["r"] = ptk("r", [8, NT], F32)
            T3["a"] = ptk("a3", [8, NT], BF16)
            T3["mixb"] = ptk("mixb", [8, NT], BF16)
            T3["hid"] = ptk("hid", [22, NT], BF16)
            T3["y"] = [pt("y3_%d" % i, [4, NT], BF16) for i in range(3)]
            T3["st4"] = pt("st4_3", [4, NT], F32)
            T3["sg"] = [pt("sg%d" % i, [NT], BF16) for i in range(2)]
            T3["tmpf"] = [pt("tmpf%d" % i, [NT], F32) for i in range(2)]
            T3["ot"] = [pt("ot%d" % i, [1024], F32) for i in range(2)]
            return T3

        def phase3(l, nt, s, tok0, T3):
            lat = (s == 0)
            xt, xbuf = T3["x"][0]
            xsrc = xview(l % 2, lat, tok0 // NT)
            P.dma("sp", xt[:, :, 0:nt], xsrc, [db("XL%d" % (l % 2) if lat else "XC%d" % (l % 2))], xbuf)
            A, bA = T3["a"]
            for k in range(8):
                P.op("dve", lambda e, k=k: e.tensor_scalar(out=A[:, k, 0:nt], in0=xt[:, k, 0:nt], scalar1=MOD(l, 8 + k, s, True), scalar2=MOD(l, k, s, False),
                                                           op0=ALU.mult, op1=ALU.add), [xbuf[k], b_mod_, b_mod1], [bA[k]])
            ysrc = ((YF, "YF"), (YNA, "YNA"), (YMLA, "YMLA")) if lat else ((YFc, "YFc"), (YNAc, "YNAc"), (YMLAc, "YMLAc"))
            for gi in range(3):
                yt, by = T3["y"][gi]
                yd = ysrc[gi][0][tok0 // NT].rearrange("p (k t) -> p k t", k=4) if lat else ysrc[gi][0][:, :, 0:nt].rearrange("k p t -> p k t")
                P.dma("sp", yt[:, :, 0:nt], yd, [db(ysrc[gi][1])], [by])
            mix, bmix = T3["mix"]
            for gi in range(3):
                yt, by = T3["y"][gi]
                for hb in range(2):
                    pg = load_piece((l, "g", gi, hb)); pb_ = load_piece((l, "br", gi, hb))
                    for j in range(4):
                        mc = hb * 4 + j
                        ps, pb = mm_chunk(pg, j * 128, 128, A, bA, nt)
                        sg, bsg = T3["sg"][mc % 2]
                        P.op("act", lambda e, sg=sg, ps=ps: e.activation(out=sg[:, 0:nt], in_=ps, func=AF.Sigmoid), [pb], [bsg])
                        ps2, pb2 = mm_chunk(pb_, j * 128, 128, yt, by, nt)
                        if gi == 0:
                            P.op("dve", lambda e, sg=sg, ps2=ps2, mc=mc: e.tensor_tensor(out=mix[:, mc, 0:nt], in0=ps2, in1=sg[:, 0:nt], op=ALU.mult), [pb2, bsg], [bmix[mc]])
                        else:
                            tf, btf = T3["tmpf"][mc % 2]
                            P.op("dve", lambda e, sg=sg, ps2=ps2, tf=tf: e.tensor_tensor(out=tf[:, 0:nt], in0=ps2, in1=sg[:, 0:nt], op=ALU.mult), [pb2, bsg], [btf])
                            P.op("dve", lambda e, tf=tf, mc=mc: e.tensor_tensor(out=mix[:, mc, 0:nt], in0=mix[:, mc, 0:nt], in1=tf[:, 0:nt], op=ALU.add), [btf, bmix[mc]], [bmix[mc]])
            mixb, bmixb = T3["mixb"]
            for k in range(8):
                copy_op("act" if k % 2 else "dve", mixb[:, k, 0:nt], mix[:, k, 0:nt], [bmix[k]], [bmixb[k]])
            r, rb = T3["r"]

            def cons_o(ci, ps, pb, rows):
                P.op("act", lambda e: e.activation(out=r[:, ci, 0:nt], in_=xt[:, ci, 0:nt], func=AF.Identity, scale=ALPHA), [xbuf[ci]], [rb[ci]])
                P.op("dve", lambda e: e.scalar_tensor_tensor(out=r[:, ci, 0:nt], in0=ps, scalar=MOD(l, 16 + ci, s, True), in1=r[:, ci, 0:nt], op0=ALU.mult, op1=ALU.add), [pb, rb[ci], b_mod1], [rb[ci]])
            linear_fm([(l, "o", 0), (l, "o", 1)], mixb, bmixb, nt, cons_o)
            st4, bst4 = T3["st4"]
            ln_fm(r, rb, nt, lnp[:, l, 0, :], lnp[:, l, 1, :], mix, bmix, st4, bst4)
            for k in range(8):
                P.op("dve", lambda e, k=k: e.tensor_scalar(out=A[:, k, 0:nt], in0=r[:, k, 0:nt], scalar1=MOD(l, 32 + k, s, True), scalar2=MOD(l, 24 + k, s, False),
                                                           op0=ALU.mult, op1=ALU.add), [rb[k], b_mod_, b_mod1], [bA[k]])
            hid, bhid = T3["hid"]
            for j6 in range(6):
                pfg = load_piece((l, "fg", j6)); pfu = load_piece((l, "fu", j6))
                for j in range(pfg[3] // 128):
                    mc = j6 * 4 + j
                    ps, pb = mm_chunk(pfg, j * 128, 128, A, bA, nt)
                    sg, bsg = T3["sg"][mc % 2]
                    P.op("act", lambda e, sg=sg, ps=ps: e.activation(out=sg[:, 0:nt], in_=ps, func=AF.Silu), [pb], [bsg])
                    ps2, pb2 = mm_chunk(pfu, j * 128, 128, A, bA, nt)
                    P.op("dve", lambda e, sg=sg, ps2=ps2, mc=mc: e.tensor_tensor(out=hid[:, mc, 0:nt], in0=ps2, in1=sg[:, 0:nt], op=ALU.mult), [pb2, bsg], [bhid[mc]])

            def cons_d(ci, ps, pb, rows):
                P.op("act", lambda e: e.activation(out=r[:, ci, 0:nt], in_=r[:, ci, 0:nt], func=AF.Identity, scale=ALPHA), [rb[ci]], [rb[ci]])
                P.op("dve", lambda e: e.scalar_tensor_tensor(out=r[:, ci, 0:nt], in0=ps, scalar=MOD(l, 40 + ci, s, True), in1=r[:, ci, 0:nt], op0=ALU.mult, op1=ALU.add), [pb, rb[ci], b_mod1], [rb[ci]])
            linear_fm([(l, "fd", j) for j in range(4)], hid, bhid, nt, cons_d)
            ln_fm(r, rb, nt, lnp[:, l, 2, :], lnp[:, l, 3, :], mix, bmix, st4, bst4)
            return r, rb

        def write_out(r, rb, tok0, T3):
            for sub in range(4):
                ot, bot = T3["ot"][sub % 2]
                for k in range(8):
                    bank = 6 + (k // 4)
                    P.op("pe", lambda e, k=k, sub=sub, bank=bank: e.transpose(ps_ap[:, bank, (k % 4) * 128:(k % 4 + 1) * 128], r[:, k, sub * 128:(sub + 1) * 128], ident), [rb[k], b_ident], [psb[bank]])
                    if k % 4 == 3:
                        copy_op("act" if k == 3 else "dve", ot[:, (k // 4) * 512:(k // 4 + 1) * 512], ps_ap[:, bank, :], [psb[bank]], [bot])
                P.dma("pool", out_d[tok0 + sub * 128:tok0 + (sub + 1) * 128, :], ot, [bot], [db("out")], own=bot)

        def run_phase1_pass(l, from_input):
            ar.reset(PERSIST)
            load_layer_small(l)
            P.barrier()
            ar.reset(PERSIST)
            wslots[:] = [pt("wslot%d" % i, [6144], BF16) for i in range(3)]
            T1 = alloc_p1()
            xts = [ptk("xt%d" % i, [8, NT], F32) for i in range(2)]
            if from_input:
                W = {"xin": [pt("xin%d" % i, [1024], F32) for i in range(2)], "s4": pt("s4", [4], F32), "xsq": pt("xsq", [1024], F32)}
            for ti in range(NTL + 1):
                xt, xbuf = xts[ti % 2]
                lat = ti < NTL
                nt = NT if lat else CTX
                if from_input:
                    input_ln(x_in[ti * NT:(ti + 1) * NT] if lat else ctx_in, nt // 128, xt, xbuf, W)
                else:
                    P.dma("sp", xt[:, :, 0:nt], xview(l % 2, lat, ti), [db("XL%d" % (l % 2) if lat else "XC%d" % (l % 2))], ball(xbuf))
                phase1(l, xt, xbuf, nt, 0 if lat else 1, ti * NT if lat else 0, ti if lat else -1, T1, store_x=from_input)

        class _Stop(Exception):
            pass

        def chk(stage):
            if cfg.stop <= stage:
                raise _Stop()

        try:
            chk(0)
            if 'ONLYP3' in cfg.debug:
                lsel = 1 if 'L1' in cfg.debug else 0
                wslots[:] = [pt("wslot%d" % i, [6144], BF16) for i in range(3)]
                T3 = alloc_p3()
                for ti in range(NTL):
                    r, rb = phase3(lsel, NT, 0, ti * NT, T3)
                    if 'NOOUT' not in cfg.debug:
                        write_out(r, rb, ti * NT, T3)
                raise _Stop()
            run_phase1_pass(0, True)
            chk(1)
            for l in range(L):
                exchange()
                chk(2 + 10 * l)
                phase2_mla(l)
                chk(3 + 10 * l)
                phase2_na(l)
                chk(4 + 10 * l)
                phase2_fourier(l)
                chk(5 + 10 * l)
                ar.reset(PERSIST)
                wslots[:] = [pt("wslot%d" % i, [6144], BF16) for i in range(3)]
                T3 = alloc_p3()
                last = (l + 1 == L)
                for ti in range(NTL + (0 if last else 1)):
                    lat = ti < NTL
                    nt = NT if lat else CTX
                    r, rb = phase3(l, nt, 0 if lat else 1, ti * NT if lat else 0, T3)
                    if last:
                        if 'NOOUT' not in cfg.debug:
                            write_out(r, rb, ti * NT, T3)
                    else:
                        P.dma("pool", xview((l + 1) % 2, lat, ti), r[:, :, 0:nt], rb, [db("XL%d" % ((l + 1) % 2) if lat else "XC%d" % ((l + 1) % 2))], own=rb[0])
                chk(6 + 10 * l)
                if not last:
                    P.barrier()
                    run_phase1_pass(l + 1, False)
                chk(7 + 10 * l)
        except _Stop:
            pass
        P.barrier()
        P.emit()
    return nc, dbg


def _consts(cfg, q):
    bf = ml_dtypes.bfloat16
    R, T, N, N1 = cfg.R, cfg.T, cfg.N, cfg.N1
    out = {}
    out["k_ident"] = np.eye(128, dtype=np.float32)
    cl = np.outer(np.arange(128), np.arange(128)).astype(np.float64) * (2 * np.pi / 128)
    out["k_cs128"] = np.concatenate([np.cos(cl), np.sin(cl)], 1).astype(bf)
    a1 = np.outer(np.arange(N1), np.arange(N1)).astype(np.float64) * (2 * np.pi / N1)
    out["k_c1"] = np.cos(a1).astype(bf); out["k_s1"] = np.sin(a1).astype(bf); out["k_ns1"] = (-np.sin(a1)).astype(bf)
    sc = 1.0 / np.sqrt(N * 128.0)
    k1 = np.arange(N1)[None, :, None]; k2 = np.arange(32)[None, None, :]; n2 = np.arange(128)[:, None, None]
    k = T * q + k1 + N1 * k2
    ang = (n2 * k % N).astype(np.float64) * (2 * np.pi / N)
    out["k_mre"] = (np.cos(ang) * sc).reshape(128, N1 * 32).astype(bf)
    out["k_mimn"] = (-np.sin(ang) * sc).reshape(128, N1 * 32).astype(bf)
    ac = np.outer(np.arange(256), np.arange(256)).astype(np.float64) * (2 * np.pi / 256)
    scc = 1.0 / np.sqrt(256 * 128.0)
    out["k_cc256"] = (np.cos(ac) * scc).astype(bf); out["k_scn256"] = (-np.sin(ac) * scc).astype(bf)
    t = np.arange(T) + q * T
    rows = (t // GW).astype(np.float32); cols = (t % GW).astype(np.float32)
    inv = (np.float32(10000.0) ** (-np.arange(8, dtype=np.float32) / np.float32(8))).astype(np.float32)
    angr = np.concatenate([rows[:, None] * inv, cols[:, None] * inv], -1).astype(np.float32)
    co = np.cos(angr).T.astype(np.float32); si = np.sin(angr).T.astype(np.float32)
    out["k_rope"] = np.stack([np.concatenate([co, co], 0), np.concatenate([-si, si], 0)], 0).astype(np.float32)
    NR = 4 * R
    rvt = np.zeros((128, 3, 8, 8), np.float32)
    for ty, r0 in enumerate((R * q, R * q + 8, R * q + R - 8)):
        for p in range(128):
            for kb in range(8):
                krow = r0 - 4 + 2 * kb + p // 64
                for qi in range(8):
                    rs = min(max(r0 + qi - 4, 0), NR - 8)
                    rvt[p, ty, kb, qi] = 1.0 if rs <= krow < rs + 8 else 0.0
    out["k_rv"] = rvt.reshape(128, 192)
    sel = np.zeros((128, 8), np.float32)
    if q > 0:
        sel[:, q - 1] = 1.0
    if q < 3:
        sel[:, 4 + q + 1] = 1.0
    out["k_sel"] = sel
    return out


def _rpb_table(rpb):
    Ln, H = rpb.shape[0], rpb.shape[1]
    tab = np.full((Ln, H, 23, 64, 64), MASKVAL, np.float32)
    qc = np.arange(64)[None, :]; kc = np.arange(64)[:, None]
    cs = np.clip(qc - 8, 0, 48)
    valid = (kc >= cs) & (kc < cs + 16)
    dc = np.clip(kc - qc + 15, 0, 30)
    for e in range(23):
        dr = 11 - e
        if abs(dr) <= 7:
            g = rpb[:, :, dr + 7, :][:, :, dc]
            tab[:, :, e] = np.where(valid[None, None], g, np.float32(MASKVAL))
    return tab


def _pk(v, k):
    v = np.asarray(v, np.float32)
    sh = v.shape[:-1]
    return np.ascontiguousarray(np.swapaxes(v.reshape(sh + (k, 128)), -1, -2))


_CACHE = {}


def run_cfg(inputs, cfg):
    key = (cfg.R, cfg.debug, cfg.stop)
    if key not in _CACHE:
        _CACHE[key] = build_program(cfg)
    nc, dbg = _CACHE[key]
    T = cfg.T
    f32 = lambda a: np.ascontiguousarray(np.asarray(a, np.float32))
    shared = {
        "c_ctx": _pk(inputs["c_ctx"], 8), "ln_in_g": _pk(inputs["ln_in_g"], 8), "ln_in_b": _pk(inputs["ln_in_b"], 8),
        "w_mod": f32(inputs["w_mod"]), "b_mod": _pk(inputs["b_mod"], 48), "w_in": f32(inputs["w_in"]),
        "mla_q_norm_g": _pk(inputs["mla_q_norm_g"], 2), "mla_kv_norm_g": _pk(inputs["mla_kv_norm_g"], 1),
        "w_uq": f32(inputs["w_uq"]), "w_qr": f32(inputs["w_qr"]), "w_uk": f32(inputs["w_uk"]), "w_uv": f32(inputs["w_uv"]),
        "rpbT": _rpb_table(f32(inputs["na_rpb"])),
        "w_branch": f32(inputs["w_branch"]).reshape(L, 1536, D), "w_out": f32(inputs["w_out"]),
        "ln1_g": _pk(inputs["ln1_g"], 8), "ln1_b": _pk(inputs["ln1_b"], 8), "ln2_g": _pk(inputs["ln2_g"], 8), "ln2_b": _pk(inputs["ln2_b"], 8),
        "w_ffn_gate": f32(inputs["w_ffn_gate"]), "w_ffn_up": f32(inputs["w_ffn_up"]), "w_ffn_down": f32(inputs["w_ffn_down"]),
    }
    x = np.asarray(inputs["x"], np.float32); ctx = np.asarray(inputs["ctx"], np.float32); c = np.asarray(inputs["c"], np.float32)
    in_maps = []
    for i in range(8):
        b, q = i // 4, i % 4
        m = dict(shared)
        m["x"] = np.ascontiguousarray(x[b, q * T:(q + 1) * T]); m["ctx"] = np.ascontiguousarray(ctx[b]); m["c"] = _pk(c[b], 8)
        m.update(_consts(cfg, q))
        in_maps.append(m)
    res = run_bass_kernel_spmd(nc, in_maps, core_ids=list(range(8)))
    out = np.empty((2, 4 * T, D), np.float32)
    for i in range(8):
        b, q = i // 4, i % 4
        out[b, q * T:(q + 1) * T] = np.asarray(res.results[i]["out"], np.float32)
    return out, res


def kernel(**inputs):
    out, _ = run_cfg(inputs, Cfg(64))
    return out
```

```python
from contextlib import ExitStack
import ml_dtypes
from concourse.bass_utils import run_bass_kernel_spmd
import numpy as np
import concourse.bass as bass
import concourse.mybir as mybir

F32 = mybir.dt.float32
BF16 = mybir.dt.bfloat16
U8 = mybir.dt.uint8
ALU = mybir.AluOpType
AF = mybir.ActivationFunctionType
AX = mybir.AxisListType


class Buf:
    __slots__ = ("name", "w", "r", "dsem", "dcnt")

    def __init__(self, name):
        self.name = name
        self.w = None
        self.r = {}
        self.dsem = None
        self.dcnt = 0


class Prog:
    ENGS = ("pe", "act", "dve", "pool", "sp")

    def __init__(self, nc, stack):
        self.nc = nc
        self.stack = stack
        self.lists = {e: [] for e in self.ENGS}
        self.sems = {}
        self.cnt = {}
        self.waited = {e: {} for e in self.ENGS}
        for e in self.ENGS:
            self._newsem("eng_" + e)
        self._newsem("coll")
        self.pool_keys = []
        self.pool_idx = 0
        self.attached = []

    def _newsem(self, key):
        self.sems[key] = self.stack.enter_context(self.nc.semaphore(key))
        self.cnt[key] = 0
        return key

    def buf(self, name):
        return Buf(name)

    def _deps(self, reads, writes):
        deps = {}
        def add(t):
            if t is None:
                return
            k, v = t
            if deps.get(k, 0) < v:
                deps[k] = v
        for b in reads:
            add(b.w)
        for b in writes:
            add(b.w)
            for k, v in b.r.items():
                add((k, v))
        return deps

    def _waits(self, eng, deps):
        ws = []
        wd = self.waited[eng]
        for k, v in deps.items():
            if wd.get(k, 0) < v:
                wd[k] = v
                ws.append((k, v))
        return ws

    def _mark(self, tok, reads, writes):
        k, v = tok
        for b in reads:
            if b.r.get(k, 0) < v:
                b.r[k] = v
        for b in writes:
            b.w = tok
            b.r = {}

    def _dsem(self, b):
        if b.dsem is None:
            if self.pool_idx >= len(self.pool_keys):
                self.pool_keys.append(self._newsem("dq%d" % len(self.pool_keys)))
            b.dsem = self.pool_keys[self.pool_idx]
            self.pool_idx += 1
            self.attached.append(b)
        return b.dsem

    def op(self, eng, fn, reads=(), writes=(), inc=True):
        deps = self._deps(reads, writes)
        if eng == "pe":
            deps.pop("eng_pe", None)
        ws = self._waits(eng, deps)
        key = "eng_" + eng
        if inc:
            self.cnt[key] += 1
            tok = (key, self.cnt[key])
        else:
            tok = (key, self.cnt[key] + 1)
        self.lists[eng].append((ws, fn, (key, 1) if inc else None))
        self._mark(tok, reads, writes)
        if inc:
            self.waited[eng][key] = max(self.waited[eng].get(key, 0), 0)
        return tok

    def dma(self, q, out_ap, in_ap, reads, writes, own=None, **kw):
        if own is None:
            own = writes[0]
        key = self._dsem(own)
        ws = self._waits(q, self._deps(reads, writes))
        self.cnt[key] += 16
        tok = (key, self.cnt[key])
        fn = lambda e, o=out_ap, i=in_ap, kw=kw: e.dma_start(out=o, in_=i, **kw)
        self.lists[q].append((ws, fn, (key, 16)))
        self._mark(tok, reads, writes)
        return tok

    def collective(self, kind, in_ap, out_ap, groups, reads, writes):
        key = "coll"
        ws = self._waits("pool", self._deps(reads, writes))
        self.cnt[key] += 1
        tok = (key, self.cnt[key])
        fn = lambda e: e.collective_compute(kind, ALU.bypass, replica_groups=groups,
                                            ins=[in_ap], outs=[out_ap])
        self.lists["pool"].append((ws, fn, (key, 1)))
        self._mark(tok, reads, writes)
        return tok

    def barrier(self):
        for e in self.ENGS:
            ws = self._waits(e, {k: v for k, v in self.cnt.items() if v > 0})
            if ws:
                self.lists[e].append((ws, None, None))
        for b in self.attached:
            b.dsem = None
        self.attached = []
        self.pool_idx = 0

    def emit(self):
        nc = self.nc
        sems = self.sems
        lists = self.lists
        with nc.Block() as block:
            def run(eng_obj, lst):
                for ws, fn, inc in lst:
                    for k, v in ws:
                        eng_obj.wait_ge(sems[k], v)
                    if fn is not None:
                        ins = fn(eng_obj)
                        if inc is not None:
                            ins.then_inc(sems[inc[0]], inc[1])

            @block.tensor
            def _(e):
                run(e, lists["pe"])

            @block.scalar
            def _(e):
                run(e, lists["act"])

            @block.vector
            def _(e):
                run(e, lists["dve"])

            @block.gpsimd
            def _(e):
                run(e, lists["pool"])

            @block.sync
            def _(e):
                run(e, lists["sp"])


class Arena:
    def __init__(self, ap_u8, limit):
        self.ap = ap_u8
        self.limit = limit
        self.off = 0

    def reset(self, off=0):
        self.off = off

    def alloc(self, shape, dt, parts=128):
        esz = 4 if dt == F32 else 2
        n = int(np.prod(shape))
        nbytes = (n * esz + 63) // 64 * 64
        assert self.off + nbytes <= self.limit, ("arena overflow", self.off, nbytes, self.limit)
        a = self.ap[0:parts, self.off:self.off + n * esz].bitcast(dt)
        self.off += nbytes
        if len(shape) == 2:
            a = a.rearrange("p (a b) -> p a b", a=shape[0])
        elif len(shape) == 3:
            a = a.rearrange("p (a b c) -> p a b c", a=shape[0], b=shape[1])
        return a
D = 1024
CTX = 256
GW = 64
FH = 2816
L = 2
EPS = 1e-5
ALPHA = (2 * L) ** 0.25
NA_SCALE = 64 ** -0.5
MLA_SCALE = 96 ** -0.5
MASKVAL = -30000.0
O_F, O_NQ, O_NK, O_NV, O_CQ, O_CKV, O_KR, O_G = 0, 512, 1024, 1536, 2048, 2304, 2432, 2464
IN_DIM = 5536


class Cfg:
    def __init__(self, R=64, debug=(), stop=99):
        self.stop = stop
        self.R = R
        self.T = R * GW
        self.N = 4 * self.T
        self.N1 = self.N // 128
        self.NT = 512
        self.NTL = self.T // 512
        self.NQB = R // 8
        self.debug = tuple(debug)


def build_program(cfg):
    R, T, N, N1, NT, NTL, NQB = cfg.R, cfg.T, cfg.N, cfg.N1, cfg.NT, cfg.NTL, cfg.NQB
    TW = T + 512
    nc = bass.Bass("TRN2", target_bir_lowering=False)

    def din(name, shape, dt=F32):
        return nc.dram_tensor(name, list(shape), dt, kind="ExternalInput").ap()

    dbg = {}

    def dscr(name, shape, dt=BF16):
        if name in cfg.debug and name not in ('NOOUT', 'ONLYP3', 'L1'):
            t = nc.dram_tensor(name, list(shape), dt, kind="ExternalOutput")
            dbg[name] = t
        else:
            t = nc.dram_tensor(name, list(shape), dt)
        return t.ap()

    x_in = din("x", [T, D]); ctx_in = din("ctx", [CTX, D]); c_in = din("c", [128, 8]); cc_in = din("c_ctx", [128, 8])
    lng_in = din("ln_in_g", [128, 8]); lnb_in = din("ln_in_b", [128, 8])
    w_mod = din("w_mod", [L, D, 6 * D]); b_mod = din("b_mod", [L, 128, 48]); w_in = din("w_in", [L, D, IN_DIM])
    gq_in = din("mla_q_norm_g", [L, 128, 2]); gkv_in = din("mla_kv_norm_g", [L, 128, 1])
    w_uq = din("w_uq", [L, 256, 512]); w_qr = din("w_qr", [L, 256, 256]); w_uk = din("w_uk", [L, 128, 512]); w_uv = din("w_uv", [L, 128, 512])
    rpbT = din("rpbT", [L, 8, 23, 64, 64])
    w_br = din("w_branch", [L, 1536, D]); w_out = din("w_out", [L, D, D])
    ln1g = din("ln1_g", [L, 128, 8]); ln1b = din("ln1_b", [L, 128, 8]); ln2g = din("ln2_g", [L, 128, 8]); ln2b = din("ln2_b", [L, 128, 8])
    w_fg = din("w_ffn_gate", [L, D, FH]); w_fu = din("w_ffn_up", [L, D, FH]); w_fd = din("w_ffn_down", [L, FH, D])
    k_id = din("k_ident", [128, 128]); k_cs = din("k_cs128", [128, 256], BF16)
    k_c1 = din("k_c1", [N1, N1], BF16); k_s1 = din("k_s1", [N1, N1], BF16); k_ns1 = din("k_ns1", [N1, N1], BF16)
    k_mre = din("k_mre", [128, N1 * 32], BF16); k_mim = din("k_mimn", [128, N1 * 32], BF16)
    k_cc = din("k_cc256", [256, 256], BF16); k_sc = din("k_scn256", [256, 256], BF16)
    k_rope = din("k_rope", [2, 32, T]); k_rv = din("k_rv", [128, 3 * 64]); k_sel = din("k_sel", [128, 8])
    out_d = nc.dram_tensor("out", [T, D], F32, kind="ExternalOutput").ap()

    WP = {}
    CASTS = []

    def wpiece(key, src2d, r0, kch, c0, pc):
        WP[key] = (dscr("W_%s" % "_".join(str(x) for x in key), [128, kch * pc]), kch, pc)
        CASTS.append((key, src2d, r0, kch, c0, pc))

    for l in range(L):
        for nm, c0, pc in (("f", O_F, 512), ("nq", O_NQ, 512), ("nk", O_NK, 512), ("nv", O_NV, 512), ("cq", O_CQ, 256), ("ckv", O_CKV, 128)):
            wpiece((l, nm), w_in[l], 0, 8, c0, pc)
        for gi in range(3):
            for hb in range(2):
                wpiece((l, "g", gi, hb), w_in[l], 0, 8, O_G + gi * D + hb * 512, 512)
                wpiece((l, "br", gi, hb), w_br[l], gi * 512, 4, hb * 512, 512)
        for hb in range(2):
            wpiece((l, "o", hb), w_out[l], 0, 8, hb * 512, 512)
        for j in range(6):
            pc = 512 if j < 5 else 256
            wpiece((l, "fg", j), w_fg[l], 0, 8, j * 512, pc)
            wpiece((l, "fu", j), w_fu[l], 0, 8, j * 512, pc)
        for j in range(4):
            wpiece((l, "fd", j), w_fd[l], 0, 22, j * 256, 256)
    XL = [dscr("XL%d" % i, [NTL, 128, 8 * NT], F32) for i in range(2)]
    XC = [dscr("XC%d" % i, [128, 8 * CTX], F32) for i in range(2)]
    NAQT = dscr("NAQT", [NTL, 128, 4 * NT]); NAKW = dscr("NAKW", [NTL, 128, 4 * NT]); NAKH = dscr("NAKH", [4, 128, 512]); NBL = T // 128; NWB = TW // 128
    NAVW = dscr("NAVW", [NTL * 128, 2080]); NAVH = dscr("NAVH", [4 * 128, 520])
    HK = dscr("HK", [512, 512]); HKA = dscr("HKA", [4 * 512, 512]); HV = dscr("HV", [512, 512]); HVA = dscr("HVA", [4 * 512, 512])
    MLAQT = dscr("MLAQT", [8, 96, T]); MLAKT = dscr("MLAKT", [768, T]); MLAKTA = [dscr("MLAKTA%d" % h, [4 * 96, T]) for h in range(8)]
    MLAV = dscr("MLAV", [NTL * 128, 2080])
    MLAVA = [dscr("MLAVA%d" % t, [4 * 128, 2080]) for t in range(NTL)]
    HF = 2 if T * 512 > (1 << 20) else 1; CH = T // HF
    FAB = dscr("FAB", [4 * T, 256]); FABA = [dscr("FABA%d" % j, [4 * CH, 256]) for j in range(4 * HF)]
    NAQTc = dscr("NAQTc", [4, 128, CTX]); NAKTc = dscr("NAKTc", [4, 128, CTX]); NAVc = dscr("NAVc", [128, 8 * 130])
    MLAQTc = dscr("MLAQTc", [8, 96, CTX]); MLAKTc = dscr("MLAKTc", [8, 96, CTX]); MLAVc = dscr("MLAVc", [128, 8 * 130])
    FABc = dscr("FABc", [CTX, 4, 256])
    YF = dscr("YF", [NTL, 128, 4 * NT]); YNA = dscr("YNA", [NTL, 128, 4 * NT]); YMLA = dscr("YMLA", [NTL, 128, 4 * NT])
    YFc = dscr("YFc", [4, 128, CTX]); YNAc = dscr("YNAc", [4, 128, CTX]); YMLAc = dscr("YMLAc", [4, 128, CTX])
    EBT = [dscr("EBT%d" % l, [8, 64, 23 * 64]) for l in range(L)]

    GROUPS = [[0, 1, 2, 3], [4, 5, 6, 7]]
    ARENA = 196 * 1024

    with ExitStack() as st:
        P = Prog(nc, st)
        ar_ap = st.enter_context(nc.sbuf_tensor("arena", [128, ARENA], U8))
        ps_ap = st.enter_context(nc.psum_tensor("ps", [128, 8, 512], F32))
        ar = Arena(ar_ap, ARENA)
        psb = [P.buf("ps%d" % i) for i in range(8)]
        PS = lambda i: ps_ap[:, i, :]
        dram = {}

        def db(name):
            if name not in dram:
                dram[name] = P.buf(name)
            return dram[name]

        rr = {"lin": 0, "ev": 0, "bank": 0}

        def xview(i, lat, ti):
            if lat:
                return XL[i][ti].rearrange("p (k t) -> p k t", k=8)
            return XC[i].rearrange("p (k t) -> p k t", k=8)

        def evac_eng():
            rr["ev"] ^= 1
            return "act" if rr["ev"] else "dve"

        def copy_op(eng, out, in_, rd, wr):
            if eng == "act":
                P.op("act", lambda e: e.copy(out=out, in_=in_), rd, wr)
            else:
                P.op(eng, lambda e: e.tensor_copy(out=out, in_=in_), rd, wr)

        def pt(name, shape, dt, parts=128):
            return ar.alloc(shape, dt, parts=parts), P.buf(name)

        def ptk(name, shape, dt):
            return ar.alloc(shape, dt), [P.buf("%s_%d" % (name, k)) for k in range(shape[0])]

        def bk(b, k):
            return b[k] if isinstance(b, list) else b

        def ball(b):
            return list(b) if isinstance(b, list) else [b]

        ident, b_ident = pt("ident", [128], F32)
        ones, b_ones = pt("ones", [128], F32)
        epsb, b_eps = pt("eps", [1], F32)
        cs128, b_cs = pt("cs128", [256], BF16)
        c1, b_c1 = pt("c1", [N1], BF16); s1, b_s1 = pt("s1", [N1], BF16); ns1, b_ns1 = pt("ns1", [N1], BF16)
        mre, b_mre = pt("mre", [N1, 32], BF16); mimn, b_mim = pt("mimn", [N1, 32], BF16)
        cc256, b_cc = pt("cc256", [2, 256], BF16); sc256, b_sc = pt("sc256", [2, 256], BF16)
        rv, b_rv = pt("rv", [3, 8, 8], F32); sel, b_sel = pt("sel", [8], F32)
        lnin, b_lnin = pt("lnin", [2, 8], F32)
        modv, b_mod_ = pt("modv", [L, 48, 2], F32); mod1, b_mod1 = pt("mod1", [L, 48, 2], F32)
        lnp, b_lnp = pt("lnp", [L, 4, 8], F32)
        gq, b_gq = pt("gq", [L, 2], F32); gkv, b_gkv = pt("gkv", [L, 1], F32)
        wkr, b_wkr = pt("wkr", [8, 2, 96], BF16)
        wq, b_wq = pt("wq", [2, 8, 96], BF16); wqs, b_wqs = pt("wqs", [2, 8, 96], BF16)
        wk, b_wk = pt("wk", [8, 64], BF16); wuv, b_wuv = pt("wuv", [512], BF16)
        PERSIST = ar.off

        P.dma("sp", ident, k_id, [], [b_ident])
        P.op("pool", lambda e: e.memset(ones, 1.0), [], [b_ones])
        P.op("pool", lambda e: e.memset(epsb, EPS), [], [b_eps])
        P.dma("sp", cs128, k_cs, [], [b_cs])
        P.dma("sp", c1[0:N1], k_c1, [], [b_c1]); P.dma("sp", s1[0:N1], k_s1, [], [b_s1]); P.dma("sp", ns1[0:N1], k_ns1, [], [b_ns1])
        P.dma("sp", mre, k_mre.rearrange("p (a b) -> p a b", b=32), [], [b_mre])
        P.dma("sp", mimn, k_mim.rearrange("p (a b) -> p a b", b=32), [], [b_mim])
        P.dma("sp", cc256, k_cc.rearrange("(a p) k -> p a k", p=128), [], [b_cc])
        P.dma("sp", sc256, k_sc.rearrange("(a p) k -> p a k", p=128), [], [b_sc])
        P.dma("sp", rv, k_rv.rearrange("p (t a b) -> p t a b", t=3, a=8), [], [b_rv])
        P.dma("sp", sel, k_sel, [], [b_sel])
        P.dma("sp", lnin[:, 0, :], lng_in, [], [b_lnin])
        P.dma("sp", lnin[:, 1, :], lnb_in, [], [b_lnin])
        for l in range(L):
            for i, src in enumerate((ln1g, ln1b, ln2g, ln2b)):
                P.dma("sp", lnp[:, l, i, :], src[l], [], [b_lnp])
            P.dma("sp", gq[:, l, :], gq_in[l], [], [b_gq])
            P.dma("sp", gkv[:, l, :], gkv_in[l], [], [b_gkv])

        b_wcast = P.buf("wcast")
        b_wcast1 = P.buf("wcast1"); b_wcast3 = P.buf("wcast3")
        late_casts = []
        mid_casts = []
        P1KEYS = ("f", "nq", "nk", "nv", "cq", "ckv")

        def wbuf_of(key):
            if key[0] == 1:
                return b_wcast1
            return b_wcast if key[1] in P1KEYS else b_wcast3

        for (key, src2d, r0, kch, c0, pc) in CASTS:
            dstp = WP[key][0]
            for k in range(kch):
                args = (dstp[:, k * pc:(k + 1) * pc], src2d[r0 + k * 128:r0 + (k + 1) * 128, c0:c0 + pc])
                if key[0] == 0 and key[1] in P1KEYS:
                    P.dma("pool", args[0], args[1], [], [b_wcast])
                elif key[0] == 0:
                    mid_casts.append(args)
                else:
                    late_casts.append(args)

        def issue_late_casts(n):
            for _ in range(min(n, len(late_casts))):
                o, i = late_casts.pop(0)
                P.dma("pool", o, i, [], [b_wcast1])

        def issue_mid_casts(n):
            for _ in range(min(n, len(mid_casts))):
                o, i = mid_casts.pop(0)
                P.dma("pool", o, i, [], [b_wcast3])

        ar.reset(PERSIST)
        cs_t, b_cst = pt("cs_t", [2, 8], F32)
        bm_t, b_bmt = pt("bm_t", [L, 48], F32)
        P.dma("sp", cs_t[:, 0, :], c_in, [], [b_cst])
        P.dma("sp", cs_t[:, 1, :], cc_in, [], [b_cst])
        P.op("act", lambda e: e.activation(out=cs_t, in_=cs_t, func=AF.Silu), [b_cst], [b_cst])
        for l in range(L):
            P.dma("sp", bm_t[:, l, :], b_mod[l], [], [b_bmt])
        wm = [pt("wm%d" % i, [8, 512], F32) for i in range(2)]
        for l in range(L):
            for j in range(12):
                wt_, wb_ = wm[j % 2]
                P.dma("sp", wt_, w_mod[l][:, j * 512:(j + 1) * 512].rearrange("(k p) m -> p k m", p=128), [], [wb_])
                for cc in range(4):
                    ch = j * 4 + cc
                    bank = ch % 4
                    for k in range(8):
                        P.op("pe", lambda e, wt_=wt_, k=k, cc=cc, bank=bank: e.matmul(
                            ps_ap[:, bank, 0:2], lhsT=wt_[:, k, cc * 128:(cc + 1) * 128], rhs=cs_t[:, :, k],
                            start=(k == 0), stop=(k == 7)), [wb_, b_cst], [psb[bank]], inc=(k == 7))
                    P.op("dve", lambda e, l=l, ch=ch, bank=bank: e.tensor_scalar(
                        out=modv[:, l, ch, :], in0=ps_ap[:, bank, 0:2], scalar1=bm_t[:, l, ch:ch + 1], scalar2=None,
                        op0=ALU.add), [psb[bank], b_bmt], [b_mod_])
        P.op("dve", lambda e: e.tensor_scalar(out=mod1, in0=modv, scalar1=1.0, scalar2=None, op0=ALU.add), [b_mod_], [b_mod1])

        def MOD(l, idx, s, plus1):
            src = mod1 if plus1 else modv
            return src[:, l, idx, s:s + 1]

        eb_t = [pt("ebt%d" % i, [23, 64], F32, parts=64) for i in range(2)]
        eb_o = [pt("ebo%d" % i, [23, 64], BF16, parts=64) for i in range(2)]
        for l in range(L):
            for h in range(8):
                (ti, tb), (oi, ob) = eb_t[h % 2], eb_o[h % 2]
                P.dma("sp", ti, rpbT[l, h].rearrange("e k q -> k e q"), [], [tb])
                P.op("act", lambda e, ti=ti, oi=oi: e.activation(out=oi, in_=ti, func=AF.Exp), [tb], [ob])
                P.dma("pool", EBT[l][h], oi.rearrange("p a b -> p (a b)"), [ob], [db("EBT%d" % l)], own=ob)
        P.barrier()
        ar.reset(PERSIST)

        def load_layer_small(l):
            st_uq, b1 = pt("st_uq", [2, 512], F32); st_qr, b2 = pt("st_qr", [2, 256], F32)
            P.dma("sp", st_uq, w_uq[l].rearrange("(k p) m -> p k m", p=128), [], [b1])
            P.dma("sp", st_qr, w_qr[l].rearrange("(k p) m -> p k m", p=128), [], [b2])
            P.op("pool", lambda e: e.memset(wqs, 0.0), [], [b_wqs])
            P.op("pool", lambda e: e.memset(wkr, 0.0), [], [b_wkr])
            uqv = st_uq.rearrange("p k (h d) -> p k h d", h=8); qrv = st_qr.rearrange("p k (h d) -> p k h d", h=8)
            P.op("dve", lambda e: e.tensor_copy(out=wq[:, :, :, 0:64], in_=uqv), [b1], [b_wq])
            P.op("dve", lambda e: e.tensor_copy(out=wq[:, :, :, 64:96], in_=qrv), [b2], [b_wq])
            P.op("dve", lambda e: e.tensor_copy(out=wqs[:, :, :, 64:80], in_=qrv[:, :, :, 16:32]), [b2], [b_wqs])
            P.op("dve", lambda e: e.tensor_copy(out=wqs[:, :, :, 80:96], in_=qrv[:, :, :, 0:16]), [b2], [b_wqs])
            st_kr, b3 = pt("st_kr", [8, 32], F32)
            P.dma("sp", st_kr, w_in[l][:, O_KR:O_KR + 32].rearrange("(k p) m -> p k m", p=128), [], [b3])
            P.op("dve", lambda e: e.tensor_copy(out=wkr[:, :, 0, 64:96], in_=st_kr), [b3], [b_wkr])
            P.op("dve", lambda e: e.tensor_copy(out=wkr[:, :, 1, 64:80], in_=st_kr[:, :, 16:32]), [b3], [b_wkr])
            P.op("dve", lambda e: e.tensor_copy(out=wkr[:, :, 1, 80:96], in_=st_kr[:, :, 0:16]), [b3], [b_wkr])
            st_uk, b4 = pt("st_uk", [512], F32); st_uv, b5 = pt("st_uv", [512], F32)
            P.dma("sp", st_uk, w_uk[l], [], [b4]); P.dma("sp", st_uv, w_uv[l], [], [b5])
            P.op("dve", lambda e: e.tensor_copy(out=wk, in_=st_uk.rearrange("p (h d) -> p h d", h=8)), [b4], [b_wk])
            P.op("dve", lambda e: e.tensor_copy(out=wuv, in_=st_uv), [b5], [b_wuv])

        wslots = []

        def load_piece(key):
            wd, kch, pc = WP[key]
            slot, sbuf_ = wslots[rr["lin"] % len(wslots)]
            rr["lin"] += 1
            P.dma("sp", slot[:, 0:kch * pc], wd, [wbuf_of(key)], [sbuf_])
            return slot[:, 0:kch * pc].rearrange("p (k m) -> p k m", k=kch), sbuf_, kch, pc

        def mm_chunk(piece, c0, rows, act, abuf, nt):
            sv, sbuf_, kch, pc = piece
            bank = rr["bank"] % 4; rr["bank"] += 1
            for k in range(kch):
                P.op("pe", lambda e, k=k: e.matmul(ps_ap[0:rows, bank, 0:nt], lhsT=sv[:, k, c0:c0 + rows], rhs=act[:, k, 0:nt],
                                                   start=(k == 0), stop=(k == kch - 1)), [sbuf_, bk(abuf, k)], [psb[bank]], inc=(k == kch - 1))
            return ps_ap[0:rows, bank, 0:nt], psb[bank]

        def linear_fm(keys, act, abuf, nt, consume):
            ci = 0
            for key in keys:
                piece = load_piece(key)
                for c0 in range(0, piece[3], 128):
                    rows = min(128, piece[3] - c0)
                    ps, pb = mm_chunk(piece, c0, rows, act, abuf, nt)
                    consume(ci, ps, pb, rows)
                    ci += 1

        def ln_fm(r, rbuf, nt, gap, bap, sq, sqbuf, st4, st4buf):
            for k in range(8):
                if k % 2 == 0:
                    P.op("act", lambda e, k=k: e.activation(out=sq[:, k, 0:nt], in_=r[:, k, 0:nt], func=AF.Square), [bk(rbuf, k)], [bk(sqbuf, k)])
                else:
                    P.op("dve", lambda e, k=k: e.tensor_tensor(out=sq[:, k, 0:nt], in0=r[:, k, 0:nt], in1=r[:, k, 0:nt], op=ALU.mult), [bk(rbuf, k)], [bk(sqbuf, k)])
            for k in range(8):
                P.op("pe", lambda e, k=k: e.matmul(ps_ap[:, 4, 0:nt], lhsT=ones, rhs=r[:, k, 0:nt], start=(k == 0), stop=(k == 7)),
                     [b_ones, bk(rbuf, k)], [psb[4]], inc=(k == 7))
            for k in range(8):
                P.op("pe", lambda e, k=k: e.matmul(ps_ap[:, 5, 0:nt], lhsT=ones, rhs=sq[:, k, 0:nt], start=(k == 0), stop=(k == 7)),
                     [b_ones, bk(sqbuf, k)], [psb[5]], inc=(k == 7))
            mean = st4[:, 0, 0:nt]; msq = st4[:, 1, 0:nt]; rstd = st4[:, 2, 0:nt]
            P.op("dve", lambda e: e.tensor_scalar(out=mean, in0=ps_ap[:, 4, 0:nt], scalar1=1.0 / D, scalar2=None, op0=ALU.mult), [psb[4]], [st4buf])
            P.op("dve", lambda e: e.tensor_tensor(out=msq, in0=mean, in1=mean, op=ALU.mult), [st4buf], [st4buf])
            P.op("dve", lambda e: e.scalar_tensor_tensor(out=rstd, in0=ps_ap[:, 5, 0:nt], scalar=1.0 / D, in1=msq, op0=ALU.mult, op1=ALU.subtract), [psb[5], st4buf], [st4buf])
            P.op("act", lambda e: e.activation(out=rstd, in_=rstd, func=AF.Ln, bias=epsb[:, 0:1], scale=1.0), [st4buf, b_eps], [st4buf])
            P.op("act", lambda e: e.activation(out=rstd, in_=rstd, func=AF.Exp, scale=-0.5), [st4buf], [st4buf])
            for k in range(8):
                rk = r[:, k, 0:nt]
                P.op("dve", lambda e, rk=rk: e.tensor_tensor(out=rk, in0=rk, in1=mean, op=ALU.subtract), [bk(rbuf, k), st4buf], [bk(rbuf, k)])
                P.op("dve", lambda e, rk=rk: e.tensor_tensor(out=rk, in0=rk, in1=rstd, op=ALU.mult), [bk(rbuf, k), st4buf], [bk(rbuf, k)])
                P.op("act", lambda e, rk=rk, k=k: e.activation(out=rk, in_=rk, func=AF.Identity, bias=bap[:, k:k + 1], scale=gap[:, k:k + 1]), [bk(rbuf, k), b_lnp, b_lnin], [bk(rbuf, k)])

        def rms_fm(src, sbuf_, kch, nt, gsc, dst, dbuf, sq, sqbuf, st4, st4buf):
            for k in range(kch):
                P.op("act", lambda e, k=k: e.activation(out=sq[:, k, 0:nt], in_=src[:, k, 0:nt], func=AF.Square), [sbuf_], [sqbuf])
            for k in range(kch):
                P.op("pe", lambda e, k=k: e.matmul(ps_ap[:, 5, 0:nt], lhsT=ones, rhs=sq[:, k, 0:nt], start=(k == 0), stop=(k == kch - 1)),
                     [b_ones, sqbuf], [psb[5]], inc=(k == kch - 1))
            rstd = st4[:, 3, 0:nt]
            P.op("act", lambda e: e.activation(out=rstd, in_=ps_ap[:, 5, 0:nt], func=AF.Ln, bias=epsb[:, 0:1], scale=1.0 / (kch * 128)), [psb[5], b_eps], [st4buf])
            P.op("act", lambda e: e.activation(out=rstd, in_=rstd, func=AF.Exp, scale=-0.5), [st4buf], [st4buf])
            for k in range(kch):
                P.op("dve", lambda e, k=k: e.scalar_tensor_tensor(out=dst[:, k, 0:nt], in0=src[:, k, 0:nt], scalar=gsc[:, k:k + 1], in1=rstd, op0=ALU.mult, op1=ALU.mult),
                     [sbuf_, st4buf, b_gq, b_gkv], [dbuf])

        att_state = {}
        SBANKS = (0, 1, 2, 4, 5, 6)

        def attend(QT, qbuf, kbs, scale, nq, ytile, ybuf, pre=None, after=None):
            ptl = att_state["ptl"]; rs, rsb = att_state["rs"]; bcs, bcsb = att_state["bc"]
            ob = att_state["obank"]; att_state["obank"] = 3 if ob == 7 else 7
            n = len(kbs)

            def S(i):
                KT, kb_, V, vb_, eb, ebb = kbs[i]
                bank = SBANKS[att_state["si"] % 6]; att_state["si"] += 1
                P.op("pe", lambda e: e.matmul(ps_ap[:, bank, 0:nq], lhsT=KT, rhs=QT, start=True, stop=True), [kb_, qbuf], [psb[bank]])
                pt_, pb_ = ptl[att_state["pi"] % len(ptl)]; att_state["pi"] += 1
                P.op("act", lambda e: e.activation(out=pt_[:, 0:nq], in_=ps_ap[:, bank, 0:nq], func=AF.Exp, scale=scale), [psb[bank]], [pb_])
                if eb is not None:
                    P.op("dve", lambda e: e.tensor_tensor(out=pt_[:, 0:nq], in0=pt_[:, 0:nq], in1=eb, op=ALU.mult), [pb_, ebb], [pb_])
                return pt_, pb_

            def PV(i, pt_, pb_):
                KT, kb_, V, vb_, eb, ebb = kbs[i]
                P.op("pe", lambda e: e.matmul(ps_ap[0:65, ob, 0:nq], lhsT=V, rhs=pt_[:, 0:nq], start=(i == 0), stop=(i == n - 1)), [vb_, pb_], [psb[ob]])

            pend = []
            for i in range(n):
                pend.append((i,) + S(i))
                if pre is not None and i == min(2, n - 1):
                    pre()
                    pre = None
                if len(pend) > 3:
                    PV(*pend.pop(0))

            def fin():
                while pend:
                    PV(*pend.pop(0))
                P.op("dve", lambda e: e.reciprocal(out=rs[64:65, 0:nq], in_=ps_ap[64:65, ob, 0:nq]), [psb[ob]], [rsb])
                b2 = SBANKS[att_state["si"] % 6]; att_state["si"] += 1
                P.op("pe", lambda e: e.matmul(ps_ap[0:64, b2, 0:nq], lhsT=ones[64:65, 0:64], rhs=rs[64:65, 0:nq], start=True, stop=True), [b_ones, rsb], [psb[b2]])
                P.op("act", lambda e: e.copy(out=bcs[0:64, 0:nq], in_=ps_ap[0:64, b2, 0:nq]), [psb[b2]], [bcsb])
                P.op("dve", lambda e: e.tensor_tensor(out=ytile, in0=ps_ap[0:64, ob, 0:nq], in1=bcs[0:64, 0:nq], op=ALU.mult), [psb[ob], bcsb], [ybuf])
                if after is not None:
                    after()
            return fin

        def phase1(l, xt, xbuf, nt, s, tok0, tidx, T1, store_x=True):
            lat = (s == 0)
            nsub = nt // 128
            A = T1["a"]; bA = T1["ab"]
            for k in range(8):
                P.op("dve", lambda e, k=k: e.tensor_scalar(out=A[:, k, 0:nt], in0=xt[:, k, 0:nt], scalar1=MOD(l, 8 + k, s, True), scalar2=MOD(l, k, s, False),
                                                           op0=ALU.mult, op1=ALU.add), [bk(xbuf, k), b_mod_, b_mod1], [bk(bA, k)])
            xdst = xview(l % 2, lat, tok0 // NT)
            if store_x:
                P.dma("pool", xdst, xt[:, :, 0:nt], ball(xbuf), [db("XL%d" % (l % 2) if lat else "XC%d" % (l % 2))], own=bk(xbuf, 0))
            zt_, bz = T1["z"]; fab, bfab = T1["fab"]

            def cons_f(ci, ps, pb, rows):
                g = ci
                copy_op(evac_eng(), zt_[:, 0:nt], ps, [pb], [bz])
                for sub in range(nsub):
                    P.op("pe", lambda e, sub=sub: e.matmul(ps_ap[:, 6, 0:256], lhsT=zt_[:, sub * 128:(sub + 1) * 128], rhs=cs128, start=True, stop=True), [bz, b_cs], [psb[6]])
                    ov = fab[:, sub, g]; iv = ps_ap[:, 6, 0:256]
                    copy_op(evac_eng(), ov, iv, [psb[6]], [bfab])
            linear_fm([(l, "f")], A, bA, nt, cons_f)
            for sub in range(nsub):
                if lat:
                    dst = FAB.rearrange("(g t) c -> t g c", g=4)[tok0 + sub * 128: tok0 + (sub + 1) * 128]
                    P.dma("pool", dst, fab[:, sub], [bfab], [db("FAB")], own=bfab)
                else:
                    P.dma("pool", FABc[sub * 128:(sub + 1) * 128], fab[:, sub], [bfab], [db("FABc")], own=bfab)
            for (wkey, key) in (("nq", "naq"), ("nk", "nak")):
                tl, tb = T1[key]

                def cons(ci, ps, pb, rows, tl=tl, tb=tb):
                    copy_op(evac_eng(), tl[:, ci, 0:nt], ps, [pb], [tb])
                linear_fm([(l, wkey)], A, bA, nt, cons)
                if key == "naq":
                    dst = NAQT[tok0 // NT].rearrange("p (k t) -> p k t", k=4) if lat else NAQTc[:, :, 0:nt].rearrange("k p t -> p k t")
                    P.dma("pool", dst, tl[:, :, 0:nt], [tb], [db("NAQT" if lat else "NAQTc")], own=tb)
                else:
                    dst = NAKW[tok0 // NT].rearrange("p (k t) -> p k t", k=4) if lat else NAKTc[:, :, 0:nt].rearrange("k p t -> p k t")
                    P.dma("pool", dst, tl[:, :, 0:nt], [tb], [db("NAKW" if lat else "NAKTc")], own=tb)
                    if lat and tidx == 0:
                        P.dma("pool", HK.rearrange("(k p) t -> p k t", p=128)[:, :, 0:256], tl[:, :, 0:256], [tb], [db("HK")], own=tb)
                    if lat and tidx == NTL - 1:
                        P.dma("pool", HK.rearrange("(k p) t -> p k t", p=128)[:, :, 256:512], tl[:, :, nt - 256:nt], [tb], [db("HK")], own=tb)
            wvv, bwv, _, _ = load_piece((l, "nv"))
            nav, bnav = T1["nav"]
            for sub in range(nsub):
                for k in range(8):
                    P.op("pe", lambda e, k=k, sub=sub: e.matmul(ps_ap[:, 7, :], lhsT=A[:, k, sub * 128:(sub + 1) * 128], rhs=wvv[:, k, :], start=(k == 0), stop=(k == 7)),
                         [bk(bA, k), bwv], [psb[7]], inc=(k == 7))
                copy_op(evac_eng(), nav[:, :, sub, 0:64], ps_ap[:, 7, :].rearrange("p (h d) -> p h d", h=8), [psb[7]], [bnav])
            cview = lambda Vd: Vd.rearrange("p (h b c) -> p h b c", h=8, c=65)
            if lat:
                P.dma("pool", NAVW[(tok0 // NT) * 128:(tok0 // NT + 1) * 128, :], nav.rearrange("p h b c -> p (h b c)"), [bnav], [db("NAVW")], own=bnav)
                if tidx == 0:
                    for s_ in range(2):
                        P.dma("pool", HV[s_ * 128:(s_ + 1) * 128].rearrange("p (h d) -> p h d", h=8), nav[:, :, s_, 0:64], [bnav], [db("HV")], own=bnav)
                if tidx == NTL - 1:
                    for s_ in range(2):
                        P.dma("pool", HV[256 + s_ * 128:256 + (s_ + 1) * 128].rearrange("p (h d) -> p h d", h=8), nav[:, :, nsub - 2 + s_, 0:64], [bnav], [db("HV")], own=bnav)
            else:
                P.dma("pool", cview(NAVc), nav[:, :, 0:nsub, :], [bnav], [db("NAVc")], own=bnav)
            cq, bcq = T1["cq"]; ckv, bckv = T1["ckv"]

            def cons_cq(ci, ps, pb, rows):
                copy_op(evac_eng(), cq[:, ci, 0:nt], ps, [pb], [bcq])
            linear_fm([(l, "cq")], A, bA, nt, cons_cq)

            def cons_ckv(ci, ps, pb, rows):
                copy_op(evac_eng(), ckv[:, 0, 0:nt], ps, [pb], [bckv])
            linear_fm([(l, "ckv")], A, bA, nt, cons_ckv)
            sq, bsq = T1["sq"]; st4, bst4 = T1["st4"]
            cqn, bcqn = T1["cqn"]; ckvn, bckvn = T1["ckvn"]
            rms_fm(cq, bcq, 2, nt, gq[:, l, :], cqn, bcqn, sq, bsq, st4, bst4)
            rms_fm(ckv, bckv, 1, nt, gkv[:, l, :], ckvn, bckvn, sq, bsq, st4, bst4)
            rope, brope = T1["rope"]
            if lat:
                P.dma("sp", rope[64:96, :, 0:nt], k_rope[:, :, tok0:tok0 + nt].rearrange("a p t -> p a t"), [], [brope])
            tmp, btmp = T1["tmp"]
            krr, bkrr = T1["krr"]
            for v in range(2 if lat else 1):
                for k in range(8):
                    P.op("pe", lambda e, k=k, v=v: e.matmul(ps_ap[0:96, 6 + v, 0:nt], lhsT=wkr[:, k, v, :], rhs=A[:, k, 0:nt], start=(k == 0), stop=(k == 7)),
                         [b_wkr, bk(bA, k)], [psb[6 + v]], inc=(k == 7))
            if lat:
                P.op("dve", lambda e: e.tensor_tensor(out=tmp[64:96, 0, 0:nt], in0=ps_ap[64:96, 6, 0:nt], in1=rope[64:96, 0, 0:nt], op=ALU.mult), [psb[6], brope], [btmp])
                P.op("dve", lambda e: e.tensor_tensor(out=tmp[64:96, 1, 0:nt], in0=ps_ap[64:96, 7, 0:nt], in1=rope[64:96, 1, 0:nt], op=ALU.mult), [psb[7], brope], [btmp])
                P.op("dve", lambda e: e.tensor_tensor(out=krr[64:96, 0:nt], in0=tmp[64:96, 0, 0:nt], in1=tmp[64:96, 1, 0:nt], op=ALU.add), [btmp], [bkrr])
            else:
                P.op("act", lambda e: e.copy(out=krr[64:96, 0:nt], in_=ps_ap[64:96, 6, 0:nt]), [psb[6]], [bkrr])
            for h in range(8):
                qt_, bq_ = T1["qt"][h % 2]
                for v in range(2 if lat else 1):
                    wsrc = wq if v == 0 else wqs
                    for k in range(2):
                        P.op("pe", lambda e, k=k, v=v, h=h, wsrc=wsrc: e.matmul(ps_ap[0:96, 6 + v, 0:nt], lhsT=wsrc[:, k, h, :], rhs=cqn[:, k, 0:nt], start=(k == 0), stop=(k == 1)),
                             [b_wq, b_wqs, bcqn], [psb[6 + v]], inc=(k == 1))
                if lat:
                    P.op("act", lambda e, qt_=qt_: e.copy(out=qt_[0:64, 0:nt], in_=ps_ap[0:64, 6, 0:nt]), [psb[6]], [bq_])
                    P.op("dve", lambda e: e.tensor_tensor(out=tmp[64:96, 0, 0:nt], in0=ps_ap[64:96, 6, 0:nt], in1=rope[64:96, 0, 0:nt], op=ALU.mult), [psb[6], brope], [btmp])
                    P.op("dve", lambda e: e.tensor_tensor(out=tmp[64:96, 1, 0:nt], in0=ps_ap[64:96, 7, 0:nt], in1=rope[64:96, 1, 0:nt], op=ALU.mult), [psb[7], brope], [btmp])
                    P.op("dve", lambda e, qt_=qt_: e.tensor_tensor(out=qt_[64:96, 0:nt], in0=tmp[64:96, 0, 0:nt], in1=tmp[64:96, 1, 0:nt], op=ALU.add), [btmp], [bq_])
                else:
                    P.op("act", lambda e, qt_=qt_: e.copy(out=qt_[0:96, 0:nt], in_=ps_ap[0:96, 6, 0:nt]), [psb[6]], [bq_])
                qd = MLAQT[h, :, tok0:tok0 + nt] if lat else MLAQTc[h, :, 0:nt]
                P.dma("pool", qd, qt_[0:96, 0:nt], [bq_], [db("MLAQT" if lat else "MLAQTc")], own=bq_)
                kt_, bk_ = T1["kt"][h % 2]
                bank = 4 + (h % 2)
                P.op("pe", lambda e, h=h, bank=bank: e.matmul(ps_ap[0:64, bank, 0:nt], lhsT=wk[:, h, :], rhs=ckvn[:, 0, 0:nt], start=True, stop=True), [b_wk, bckvn], [psb[bank]])
                P.op("dve", lambda e, kt_=kt_, bank=bank: e.tensor_copy(out=kt_[0:64, 0:nt], in_=ps_ap[0:64, bank, 0:nt]), [psb[bank]], [bk_])
                P.op("act", lambda e, kt_=kt_: e.copy(out=kt_[64:96, 0:nt], in_=krr[64:96, 0:nt]), [bkrr], [bk_])
                kd = MLAKT[h * 96:(h + 1) * 96, tok0:tok0 + nt] if lat else MLAKTc[h, :, 0:nt]
                P.dma("pool", kd, kt_[0:96, 0:nt], [bk_], [db("MLAKT" if lat else "MLAKTc")], own=bk_)
            mv, bmv = T1["mlav"]
            for sub in range(nsub):
                P.op("pe", lambda e, sub=sub: e.matmul(ps_ap[:, 7, :], lhsT=ckvn[:, 0, sub * 128:(sub + 1) * 128], rhs=wuv, start=True, stop=True), [bckvn, b_wuv], [psb[7]])
                copy_op(evac_eng(), mv[:, :, sub, 0:64], ps_ap[:, 7, :].rearrange("p (h d) -> p h d", h=8), [psb[7]], [bmv])
            if lat:
                P.dma("pool", MLAV[(tok0 // NT) * 128:(tok0 // NT + 1) * 128, :], mv.rearrange("p h b c -> p (h b c)"), [bmv], [db("MLAV")], own=bmv)
            else:
                P.dma("pool", cview(MLAVc), mv[:, :, 0:nsub, :], [bmv], [db("MLAVc")], own=bmv)

        def alloc_p1():
            T1 = {}
            T1["a"], T1["ab"] = ptk("a", [8, NT], BF16)
            T1["z"] = pt("z", [NT], BF16)
            T1["fab"] = pt("fab", [4, 4, 256], BF16)
            T1["naq"] = pt("naq", [4, NT], BF16); T1["nak"] = pt("nak", [4, NT], BF16)
            T1["nav"] = pt("nav", [8, 4, 65], BF16)
            P.op("pool", lambda e: e.memset(T1["nav"][0], 1.0), [], [T1["nav"][1]])
            T1["cq"] = pt("cq", [2, NT], F32); T1["ckv"] = pt("ckv", [1, NT], F32)
            T1["cqn"] = pt("cqn", [2, NT], BF16); T1["ckvn"] = pt("ckvn", [1, NT], BF16)
            T1["rope"] = pt("rope", [2, NT], F32); T1["tmp"] = pt("tmp", [2, NT], F32)
            T1["krr"] = pt("krr", [NT], BF16)
            T1["qt"] = [pt("qt%d" % i, [NT], BF16) for i in range(2)]
            T1["kt"] = [pt("kt%d" % i, [NT], BF16) for i in range(2)]
            T1["mlav"] = pt("mlav", [8, 4, 65], BF16)
            P.op("pool", lambda e: e.memset(T1["mlav"][0], 1.0), [], [T1["mlav"][1]])
            T1["sq"] = pt("sq1", [2, NT], F32); T1["st4"] = pt("st4_1", [4, NT], F32)
            return T1

        def input_ln(src_rows, nblk, xt, xbuf, W):
            for blk in range(nblk):
                xin, bxin = W["xin"][blk % 2]
                P.dma("sp", xin, src_rows[blk * 128:(blk + 1) * 128, :], [], [bxin])
                s4, bs4 = W["s4"]
                P.op("dve", lambda e, xin=xin: e.reduce_sum(out=s4[:, 0:1], in_=xin, axis=AX.X), [bxin], [bs4])
                P.op("dve", lambda e: e.tensor_scalar(out=s4[:, 1:2], in0=s4[:, 0:1], scalar1=-1.0 / D, scalar2=None, op0=ALU.mult), [bs4], [bs4])
                P.op("act", lambda e, xin=xin: e.activation(out=xin, in_=xin, func=AF.Identity, bias=s4[:, 1:2], scale=1.0), [bxin, bs4], [bxin])
                xsq, bxsq = W["xsq"]
                P.op("act", lambda e, xin=xin: e.activation(out=xsq, in_=xin, func=AF.Square), [bxin], [bxsq])
                P.op("dve", lambda e: e.reduce_sum(out=s4[:, 2:3], in_=xsq, axis=AX.X), [bxsq], [bs4])
                P.op("act", lambda e: e.activation(out=s4[:, 3:4], in_=s4[:, 2:3], func=AF.Ln, bias=epsb[:, 0:1], scale=1.0 / D), [bs4, b_eps], [bs4])
                P.op("act", lambda e: e.activation(out=s4[:, 3:4], in_=s4[:, 3:4], func=AF.Exp, scale=-0.5), [bs4], [bs4])
                P.op("act", lambda e, xin=xin: e.activation(out=xin, in_=xin, func=AF.Identity, scale=s4[:, 3:4]), [bxin, bs4], [bxin])
                for k in range(8):
                    bank = k % 4
                    P.op("pe", lambda e, xin=xin, k=k, bank=bank: e.transpose(ps_ap[:, bank, 0:128], xin[:, k * 128:(k + 1) * 128], ident), [bxin, b_ident], [psb[bank]])
                    P.op("act" if k % 2 else "dve",
                         (lambda e, k=k, bank=bank, blk=blk: e.activation(out=xt[:, k, blk * 128:(blk + 1) * 128], in_=ps_ap[:, bank, 0:128], func=AF.Identity,
                                                                       bias=lnin[:, 1, k:k + 1], scale=lnin[:, 0, k:k + 1])) if k % 2 else
                         (lambda e, k=k, bank=bank, blk=blk: e.tensor_scalar(out=xt[:, k, blk * 128:(blk + 1) * 128], in0=ps_ap[:, bank, 0:128], scalar1=lnin[:, 0, k:k + 1],
                                                                          scalar2=lnin[:, 1, k:k + 1], op0=ALU.mult, op1=ALU.add)),
                         [psb[bank], b_lnin], [bk(xbuf, k)])

        def exchange():
            P.barrier()
            cl = [(MLAKT[h * 96:(h + 1) * 96, :], MLAKTA[h], "MLAKT", "MLAKTA") for h in range(8)]
            cl += [(MLAV[t * 128:(t + 1) * 128, :], MLAVA[t], "MLAV", "MLAVA") for t in range(NTL)]
            cl += [(FAB[j * CH:(j + 1) * CH, :], FABA[j], "FAB", "FABA") for j in range(4 * HF)]
            cl += [(HK, HKA, "HK", "HKA"), (HV, HVA, "HV", "HVA")]
            for (src, dst, sn, dn) in cl:
                P.collective("AllGather", src.opt(), dst.opt(), GROUPS, [db(sn)], [db(dn)])
            P.barrier()
            ar.reset(PERSIST)
            hk = [pt("hk%d" % i, [4, 512], BF16) for i in range(2)]
            ho = [pt("ho%d" % i, [2, 256], BF16) for i in range(2)]
            hkv = HKA.rearrange("(r k p) t -> k p r t", r=4, k=4)
            for c in range(4):
                (hi, hb), (oi, ob) = hk[c % 2], ho[c % 2]
                P.dma("sp", hi, hkv[c], [db("HKA")], [hb])
                for half, (t0, so) in enumerate(((256, 0), (0, 4))):
                    P.op("dve", lambda e, hi=hi, oi=oi, half=half, t0=t0, so=so: e.tensor_scalar(out=oi[:, half, :], in0=hi[:, 0, t0:t0 + 256], scalar1=sel[:, so:so + 1], scalar2=None, op0=ALU.mult), [hb, b_sel], [ob])
                    for r in range(1, 4):
                        P.op("dve", lambda e, hi=hi, oi=oi, half=half, t0=t0, so=so, r=r: e.scalar_tensor_tensor(out=oi[:, half, :], in0=hi[:, r, t0:t0 + 256], scalar=sel[:, so + r:so + r + 1], in1=oi[:, half, :], op0=ALU.mult, op1=ALU.add), [hb, b_sel, ob], [ob])
                P.dma("pool", NAKH[c], oi.rearrange("p a b -> p (a b)"), [ob], [db("NAKH")], own=ob)
            hv = [pt("hv%d" % i, [4, 512], BF16) for i in range(2)]
            hvo = [pt("hvo%d" % i, [8, 65], BF16) for i in range(2)]
            for (oi, ob) in hvo:
                P.op("pool", lambda e, oi=oi: e.memset(oi, 1.0), [], [ob])
            hvv = HVA.rearrange("(r s p) c -> s p r c", r=4, s=4)
            for sblk in range(4):
                (hi, hb), (oi, ob) = hv[sblk % 2], hvo[sblk % 2]
                srcb = sblk + 2 if sblk < 2 else sblk - 2
                so = 0 if sblk < 2 else 4
                P.dma("sp", hi, hvv[srcb], [db("HVA")], [hb])
                ov = oi[:, :, 0:64]
                hv3 = lambda r, hi=hi: hi[:, r, :].rearrange("p (h d) -> p h d", h=8)
                P.op("dve", lambda e, ov=ov, hv3=hv3, so=so: e.tensor_scalar(out=ov, in0=hv3(0), scalar1=sel[:, so:so + 1], scalar2=None, op0=ALU.mult), [hb, b_sel], [ob])
                for r in range(1, 4):
                    P.op("dve", lambda e, ov=ov, hv3=hv3, so=so, r=r: e.scalar_tensor_tensor(out=ov, in0=hv3(r), scalar=sel[:, so + r:so + r + 1], in1=ov, op0=ALU.mult, op1=ALU.add), [hb, b_sel, ob], [ob])
                P.dma("pool", NAVH[sblk * 128:(sblk + 1) * 128, :], oi.rearrange("p h c -> p (h c)"), [ob], [db("NAVH")], own=ob)
            P.barrier()
            ar.reset(PERSIST)

        def alloc_att():
            att_state.clear()
            att_state.update(si=0, pi=0, obank=7)
            att_state["ptl"] = [pt("p%d" % i, [512], BF16) for i in range(6)]
            att_state["rs"] = pt("rs", [512], F32)
            att_state["bc"] = pt("bcs", [512], F32)

        def phase2_mla(l):
            ar.reset(PERSIST)
            alloc_att()
            NK = CTX + N
            NKB = NK // 128
            KT = [pt("KT%d" % i, [NK], BF16) for i in range(2)]
            VA = [pt("VA%d" % i, [NKB, 65], BF16) for i in range(2)]
            qts = [pt("q%d" % i, [512], BF16) for i in range(3)]
            ys = [pt("y%d" % i, [512], BF16) for i in range(2)]
            qi = 0
            prev_fin = [None]
            for h in range(8):
                (kt, kb_), (va, vb) = KT[h % 2], VA[h % 2]
                P.dma("sp", kt[0:96, 0:CTX], MLAKTc[h], [db("MLAKTc")], [kb_])
                P.dma("sp", va[:, 0:2, :].rearrange("p a b -> p (a b)"), MLAVc[:, h * 130:(h + 1) * 130], [db("MLAVc")], [vb])
                for r in range(4):
                    P.dma("sp", kt[0:96, CTX + r * T:CTX + (r + 1) * T], MLAKTA[h][r * 96:(r + 1) * 96, :], [db("MLAKTA")], [kb_])
                    for t in range(NTL):
                        b0 = 2 + r * NBL + 4 * t
                        P.dma("sp", va[:, b0:b0 + 4, :].rearrange("p a b -> p (a b)"), MLAVA[t][r * 128:(r + 1) * 128, h * 260:(h + 1) * 260], [db("MLAVA")], [vb])
                kbs_all = [(kt[0:96, j * 128:(j + 1) * 128], kb_, va[:, j, :], vb, None, None) for j in range(NKB)]
                c, po = h // 2, (h % 2) * 64
                for qb in range(NTL):
                    qt_, bq_ = qts[qi % 3]; y_, by_ = ys[qi % 2]; qi += 1
                    P.dma("sp", qt_[0:96, :], MLAQT[h, :, qb * 512:(qb + 1) * 512], [db("MLAQT")], [bq_])

                    def after(y_=y_, by_=by_, qb=qb, po=po, c=c):
                        P.dma("pool", YMLA[qb, po:po + 64, c * NT:(c + 1) * NT], y_[0:64, :], [by_], [db("YMLA")], own=by_)
                        issue_mid_casts(6)
                        issue_late_casts(6)
                    prev_fin[0] = attend(qt_[0:96, :], bq_, kbs_all, MLA_SCALE, 512, y_[0:64, :], by_, pre=prev_fin[0], after=after)
                if l == 0:
                    qt_, bq_ = qts[qi % 3]; y_, by_ = ys[qi % 2]; qi += 1
                    P.dma("sp", qt_[0:96, 0:CTX], MLAQTc[h], [db("MLAQTc")], [bq_])

                    def after(y_=y_, by_=by_, po=po, c=c):
                        P.dma("pool", YMLAc[c, po:po + 64, :], y_[0:64, 0:CTX], [by_], [db("YMLAc")], own=by_)
                    prev_fin[0] = attend(qt_[0:96, 0:CTX], bq_, kbs_all[0:2], MLA_SCALE, CTX, y_[0:64, 0:CTX], by_, pre=prev_fin[0], after=after)
            if prev_fin[0] is not None:
                prev_fin[0]()
            issue_mid_casts(10 ** 6)
            issue_late_casts(10 ** 6)
            P.barrier()

        def phase2_na(l):
            ar.reset(PERSIST)
            alloc_att()
            KW = [pt("KW%d" % i, [TW], BF16) for i in range(2)]
            QN = [pt("QN%d" % i, [T], BF16) for i in range(2)]
            KC = [pt("KC%d" % i, [CTX], BF16) for i in range(2)]
            QC = [pt("QC%d" % i, [CTX], BF16) for i in range(2)]
            VN = [pt("VN%d" % i, [NWB, 65], BF16) for i in range(2)]
            VC = [pt("VC%d" % i, [2, 65], BF16) for i in range(2)]
            EBF = pt("EBF", [8, 8, 64], BF16)
            EBY = [pt("EBY%d" % i, [8, 512], BF16) for i in range(6)]
            ys = [pt("y%d" % i, [512], BF16) for i in range(2)]
            yi = 0
            prev_fin = [None]
            for h in range(8):
                c, po = h // 2, (h % 2) * 64
                (kw, bkw), (qn, bqn), (kc, bkc), (qc, bqc) = KW[c % 2], QN[c % 2], KC[c % 2], QC[c % 2]
                (vn, bvn), (vc, bvc) = VN[h % 2], VC[h % 2]
                if h % 2 == 0:
                    for t in range(NTL):
                        P.dma("sp", kw[:, 256 + t * NT:256 + (t + 1) * NT], NAKW[t][:, c * NT:(c + 1) * NT], [db("NAKW")], [bkw])
                        P.dma("sp", qn[:, t * NT:(t + 1) * NT], NAQT[t][:, c * NT:(c + 1) * NT], [db("NAQT")], [bqn])
                    P.dma("sp", kw[:, 0:256], NAKH[c][:, 0:256], [db("NAKH")], [bkw])
                    P.dma("sp", kw[:, 256 + T:TW], NAKH[c][:, 256:512], [db("NAKH")], [bkw])
                    P.dma("sp", kc, NAKTc[c], [db("NAKTc")], [bkc])
                    if l == 0:
                        P.dma("sp", qc, NAQTc[c], [db("NAQTc")], [bqc])
                for t in range(NTL):
                    P.dma("sp", vn[:, 2 + 4 * t:6 + 4 * t, :].rearrange("p a b -> p (a b)"), NAVW[t * 128:(t + 1) * 128, h * 260:(h + 1) * 260], [db("NAVW")], [bvn])
                for sblk in range(4):
                    wb = sblk if sblk < 2 else 2 + NBL + (sblk - 2)
                    P.dma("sp", vn[:, wb, :], NAVH[sblk * 128:(sblk + 1) * 128, h * 65:(h + 1) * 65], [db("NAVH")], [bvn])
                P.dma("sp", vc.rearrange("p a b -> p (a b)"), NAVc[:, h * 130:(h + 1) * 130], [db("NAVc")], [bvc])
                ebf, bebf = EBF
                for kb in range(8):
                    for krl in range(2):
                        e0 = 15 - 2 * kb - krl
                        P.dma("sp", ebf[krl * 64:(krl + 1) * 64, kb, :, :].rearrange("p a b -> p (a b)"), EBT[l][h][:, e0 * 64:(e0 + 8) * 64], [db("EBT%d" % l)], [bebf])
                for ty in range(3):
                    eby, beby = EBY[3 * (h % 2) + ty]
                    P.op("dve", lambda e, eby=eby, ty=ty: e.tensor_tensor(out=eby.rearrange("p a (b c) -> p a b c", b=8), in0=ebf,
                                                                        in1=rv[:, ty].unsqueeze(3).broadcast_to([128, 8, 8, 64]), op=ALU.mult), [bebf, b_rv], [beby])
                ctxk = [(kc[po:po + 64, j * 128:(j + 1) * 128], bkc, vc[:, j, :], bvc, None, None) for j in range(2)]
                for qb in range(NQB):
                    ty = 0 if qb == 0 else (2 if qb == NQB - 1 else 1)
                    eby, beby = EBY[3 * (h % 2) + ty]
                    kbs = [(kw[po:po + 64, (4 * qb + j) * 128:(4 * qb + j + 1) * 128], bkw, vn[:, 4 * qb + j, :], bvn, eby[:, j, :], beby) for j in range(8)] + ctxk
                    y_, by_ = ys[yi % 2]; yi += 1
                    def after(y_=y_, by_=by_, qb=qb, po=po, c=c):
                        P.dma("pool", YNA[qb, po:po + 64, c * NT:(c + 1) * NT], y_[0:64, :], [by_], [db("YNA")], own=by_)
                    prev_fin[0] = attend(qn[po:po + 64, qb * 512:(qb + 1) * 512], bqn, kbs, NA_SCALE, 512, y_[0:64, :], by_, pre=prev_fin[0], after=after)
                if l == 0:
                    y_, by_ = ys[yi % 2]; yi += 1
                    def after(y_=y_, by_=by_, po=po, c=c):
                        P.dma("pool", YNAc[c, po:po + 64, :], y_[0:64, 0:CTX], [by_], [db("YNAc")], own=by_)
                    prev_fin[0] = attend(qc[po:po + 64, :], bqc, ctxk, NA_SCALE, CTX, y_[0:64, 0:CTX], by_, pre=prev_fin[0], after=after)
            if prev_fin[0] is not None:
                prev_fin[0]()
            P.barrier()

        def phase2_fourier(l):
            ar.reset(PERSIST)
            AB = pt("AB", [128, 256], BF16)
            GRE = pt("GRE", [N1, 128], BF16); GIM = pt("GIM", [N1, 128], BF16)
            YFS = [pt("YFS%d" % i, [32, N1], BF16) for i in range(2)]
            for g in range(4):
                gre, bgre = GRE; gim, bgim = GIM
                ab, bab = AB
                npc = CH // 128
                for r in range(4):
                    for hf in range(HF):
                        p0 = r * NBL + hf * npc
                        P.dma("sp", ab[p0:p0 + npc].rearrange("p a b -> p (a b)"), FABA[g * HF + hf][r * CH:(r + 1) * CH, :].rearrange("(a n) c -> a (n c)", n=128), [db("FABA")], [bab])
                for j4 in range(32):
                    for half in range(2):
                        bank = half
                        for jj in range(4):
                            j = j4 * 4 + jj
                            Al = ab[0:N1, :, j]; Bl = ab[0:N1, :, 128 + j]
                            o = ps_ap[:, bank, jj * N1:(jj + 1) * N1]
                            if half == 0:
                                P.op("pe", lambda e, o=o, Al=Al: e.matmul(o, lhsT=Al, rhs=c1[0:N1], start=True, stop=False), [bab, b_c1], [psb[bank]], inc=False)
                                P.op("pe", lambda e, o=o, Bl=Bl: e.matmul(o, lhsT=Bl, rhs=ns1[0:N1], start=False, stop=True), [bab, b_ns1], [psb[bank]], inc=(jj == 3))
                            else:
                                P.op("pe", lambda e, o=o, Bl=Bl: e.matmul(o, lhsT=Bl, rhs=c1[0:N1], start=True, stop=False), [bab, b_c1], [psb[bank]], inc=False)
                                P.op("pe", lambda e, o=o, Al=Al: e.matmul(o, lhsT=Al, rhs=s1[0:N1], start=False, stop=True), [bab, b_s1], [psb[bank]], inc=(jj == 3))
                        l0 = j4 * 4
                        G, bG = (gre, bgre) if half == 0 else (gim, bgim)
                        copy_op("act" if half else "dve", G[:, :, l0:l0 + 4], ps_ap[:, bank, 0:4 * N1].rearrange("p (j k) -> p k j", j=4), [psb[bank]], [bG])
                yfs, byfs = YFS[g % 2]
                for k1b in range(N1 // 16):
                    bank = 2 + (k1b % 2)
                    for kk in range(16):
                        k1 = k1b * 16 + kk
                        o = ps_ap[:, bank, kk * 32:(kk + 1) * 32]
                        P.op("pe", lambda e, o=o, k1=k1: e.matmul(o, lhsT=gre[:, k1, :], rhs=mre[:, k1, :], start=True, stop=False), [bgre, b_mre], [psb[bank]], inc=False)
                        P.op("pe", lambda e, o=o, k1=k1: e.matmul(o, lhsT=gim[:, k1, :], rhs=mimn[:, k1, :], start=False, stop=True), [bgim, b_mim], [psb[bank]], inc=(kk == 15))
                    copy_op("act" if k1b % 2 else "dve", yfs[:, :, k1b * 16:(k1b + 1) * 16], ps_ap[:, bank, :].rearrange("p (a b) -> p b a", a=16), [psb[bank]], [byfs])
                P.dma("pool", YF.rearrange("t p (c n) -> p t c n", c=4)[:, :, g, :], yfs.rearrange("p a b -> p (a b)").rearrange("p (t n) -> p t n", n=NT), [byfs], [db("YF")], own=byfs)
            if l == 0:
                fc, bfc = pt("fc", [2, 4, 256], BF16)
                P.dma("sp", fc, FABc.rearrange("(s p) g c -> p s g c", p=128), [db("FABc")], [bfc])
                yc, byc = pt("yc", [4, 256], BF16)
                for g in range(4):
                    steps = [(fc[:, nb, g, 0:128], cc256[:, nb, :]) for nb in range(2)] + [(fc[:, nb, g, 128:256], sc256[:, nb, :]) for nb in range(2)]
                    for i, (lh, rh) in enumerate(steps):
                        P.op("pe", lambda e, lh=lh, rh=rh, i=i: e.matmul(ps_ap[:, 4, 0:256], lhsT=lh, rhs=rh, start=(i == 0), stop=(i == 3)), [bfc, b_cc, b_sc], [psb[4]], inc=(i == 3))
                    copy_op("dve", yc[:, g, :], ps_ap[:, 4, 0:256], [psb[4]], [byc])
                P.dma("pool", YFc.rearrange("g p t -> p g t"), yc, [byc], [db("YFc")], own=byc)
            P.barrier()

        def alloc_p3():
            T3 = {}
            T3["x"] = [ptk("x3_%d" % i, [8, NT], F32) for i in range(1)]
            T3["mix"] = ptk("mix", [8, NT], F32)
            T3["r"] = ptk("r", [8, NT], F32)
            T3["a"] = ptk("a3", [8, NT], BF16)
            T3["mixb"] = ptk("mixb", [8, NT], BF16)
            T3["hid"] = ptk("hid", [22, NT], BF16)
            T3["y"] = [pt("y3_%d" % i, [4, NT], BF16) for i in range(3)]
            T3["st4"] = pt("st4_3", [4, NT], F32)
            T3["sg"] = [pt("sg%d" % i, [NT], BF16) for i in range(2)]
            T3["tmpf"] = [pt("tmpf%d" % i, [NT], F32) for i in range(2)]
            T3["ot"] = [pt("ot%d" % i, [1024], F32) for i in range(2)]
            return T3

        def phase3(l, nt, s, tok0, T3):
            lat = (s == 0)
            xt, xbuf = T3["x"][0]
            xsrc = xview(l % 2, lat, tok0 // NT)
            P.dma("sp", xt[:, :, 0:nt], xsrc, [db("XL%d" % (l % 2) if lat else "XC%d" % (l % 2))], xbuf)
            A, bA = T3["a"]
            for k in range(8):
                P.op("dve", lambda e, k=k: e.tensor_scalar(out=A[:, k, 0:nt], in0=xt[:, k, 0:nt], scalar1=MOD(l, 8 + k, s, True), scalar2=MOD(l, k, s, False),
                                                           op0=ALU.mult, op1=ALU.add), [xbuf[k], b_mod_, b_mod1], [bA[k]])
            ysrc = ((YF, "YF"), (YNA, "YNA"), (YMLA, "YMLA")) if lat else ((YFc, "YFc"), (YNAc, "YNAc"), (YMLAc, "YMLAc"))
            for gi in range(3):
                yt, by = T3["y"][gi]
                yd = ysrc[gi][0][tok0 // NT].rearrange("p (k t) -> p k t", k=4) if lat else ysrc[gi][0][:, :, 0:nt].rearrange("k p t -> p k t")
                P.dma("sp", yt[:, :, 0:nt], yd, [db(ysrc[gi][1])], [by])
            mix, bmix = T3["mix"]
            for gi in range(3):
                yt, by = T3["y"][gi]
                for hb in range(2):
                    pg = load_piece((l, "g", gi, hb)); pb_ = load_piece((l, "br", gi, hb))
                    for j in range(4):
                        mc = hb * 4 + j
                        ps, pb = mm_chunk(pg, j * 128, 128, A, bA, nt)
                        sg, bsg = T3["sg"][mc % 2]
                        P.op("act", lambda e, sg=sg, ps=ps: e.activation(out=sg[:, 0:nt], in_=ps, func=AF.Sigmoid), [pb], [bsg])
                        ps2, pb2 = mm_chunk(pb_, j * 128, 128, yt, by, nt)
                        if gi == 0:
                            P.op("dve", lambda e, sg=sg, ps2=ps2, mc=mc: e.tensor_tensor(out=mix[:, mc, 0:nt], in0=ps2, in1=sg[:, 0:nt], op=ALU.mult), [pb2, bsg], [bmix[mc]])
                        else:
                            tf, btf = T3["tmpf"][mc % 2]
                            P.op("dve", lambda e, sg=sg, ps2=ps2, tf=tf: e.tensor_tensor(out=tf[:, 0:nt], in0=ps2, in1=sg[:, 0:nt], op=ALU.mult), [pb2, bsg], [btf])
                            P.op("dve", lambda e, tf=tf, mc=mc: e.tensor_tensor(out=mix[:, mc, 0:nt], in0=mix[:, mc, 0:nt], in1=tf[:, 0:nt], op=ALU.add), [btf, bmix[mc]], [bmix[mc]])
            mixb, bmixb = T3["mixb"]
            for k in range(8):
                copy_op("act" if k % 2 else "dve", mixb[:, k, 0:nt], mix[:, k, 0:nt], [bmix[k]], [bmixb[k]])
            r, rb = T3["r"]

            def cons_o(ci, ps, pb, rows):
                P.op("act", lambda e: e.activation(out=r[:, ci, 0:nt], in_=xt[:, ci, 0:nt], func=AF.Identity, scale=ALPHA), [xbuf[ci]], [rb[ci]])
                P.op("dve", lambda e: e.scalar_tensor_tensor(out=r[:, ci, 0:nt], in0=ps, scalar=MOD(l, 16 + ci, s, True), in1=r[:, ci, 0:nt], op0=ALU.mult, op1=ALU.add), [pb, rb[ci], b_mod1], [rb[ci]])
            linear_fm([(l, "o", 0), (l, "o", 1)], mixb, bmixb, nt, cons_o)
            st4, bst4 = T3["st4"]
            ln_fm(r, rb, nt, lnp[:, l, 0, :], lnp[:, l, 1, :], mix, bmix, st4, bst4)
            for k in range(8):
                P.op("dve", lambda e, k=k: e.tensor_scalar(out=A[:, k, 0:nt], in0=r[:, k, 0:nt], scalar1=MOD(l, 32 + k, s, True), scalar2=MOD(l, 24 + k, s, False),
                                                           op0=ALU.mult, op1=ALU.add), [rb[k], b_mod_, b_mod1], [bA[k]])
            hid, bhid = T3["hid"]
            for j6 in range(6):
                pfg = load_piece((l, "fg", j6)); pfu = load_piece((l, "fu", j6))
                for j in range(pfg[3] // 128):
                    mc = j6 * 4 + j
                    ps, pb = mm_chunk(pfg, j * 128, 128, A, bA, nt)
                    sg, bsg = T3["sg"][mc % 2]
                    P.op("act", lambda e, sg=sg, ps=ps: e.activation(out=sg[:, 0:nt], in_=ps, func=AF.Silu), [pb], [bsg])
                    ps2, pb2 = mm_chunk(pfu, j * 128, 128, A, bA, nt)
                    P.op("dve", lambda e, sg=sg, ps2=ps2, mc=mc: e.tensor_tensor(out=hid[:, mc, 0:nt], in0=ps2, in1=sg[:, 0:nt], op=ALU.mult), [pb2, bsg], [bhid[mc]])

            def cons_d(ci, ps, pb, rows):
                P.op("act", lambda e: e.activation(out=r[:, ci, 0:nt], in_=r[:, ci, 0:nt], func=AF.Identity, scale=ALPHA), [rb[ci]], [rb[ci]])
                P.op("dve", lambda e: e.scalar_tensor_tensor(out=r[:, ci, 0:nt], in0=ps, scalar=MOD(l, 40 + ci, s, True), in1=r[:, ci, 0:nt], op0=ALU.mult, op1=ALU.add), [pb, rb[ci], b_mod1], [rb[ci]])
            linear_fm([(l, "fd", j) for j in range(4)], hid, bhid, nt, cons_d)
            ln_fm(r, rb, nt, lnp[:, l, 2, :], lnp[:, l, 3, :], mix, bmix, st4, bst4)
            return r, rb

        def write_out(r, rb, tok0, T3):
            for sub in range(4):
                ot, bot = T3["ot"][sub % 2]
                for k in range(8):
                    bank = 6 + (k // 4)
                    P.op("pe", lambda e, k=k, sub=sub, bank=bank: e.transpose(ps_ap[:, bank, (k % 4) * 128:(k % 4 + 1) * 128], r[:, k, sub * 128:(sub + 1) * 128], ident), [rb[k], b_ident], [psb[bank]])
                    if k % 4 == 3:
                        copy_op("act" if k == 3 else "dve", ot[:, (k // 4) * 512:(k // 4 + 1) * 512], ps_ap[:, bank, :], [psb[bank]], [bot])
                P.dma("pool", out_d[tok0 + sub * 128:tok0 + (sub + 1) * 128, :], ot, [bot], [db("out")], own=bot)

        def run_phase1_pass(l, from_input):
            ar.reset(PERSIST)
            load_layer_small(l)
            P.barrier()
            ar.reset(PERSIST)
            wslots[:] = [pt("wslot%d" % i, [6144], BF16) for i in range(3)]
            T1 = alloc_p1()
            xts = [ptk("xt%d" % i, [8, NT], F32) for i in range(2)]
            if from_input:
                W = {"xin": [pt("xin%d" % i, [1024], F32) for i in range(2)], "s4": pt("s4", [4], F32), "xsq": pt("xsq", [1024], F32)}
            for ti in range(NTL + 1):
                xt, xbuf = xts[ti % 2]
                lat = ti < NTL
                nt = NT if lat else CTX
                if from_input:
                    input_ln(x_in[ti * NT:(ti + 1) * NT] if lat else ctx_in, nt // 128, xt, xbuf, W)
                else:
                    P.dma("sp", xt[:, :, 0:nt], xview(l % 2, lat, ti), [db("XL%d" % (l % 2) if lat else "XC%d" % (l % 2))], ball(xbuf))
                phase1(l, xt, xbuf, nt, 0 if lat else 1, ti * NT if lat else 0, ti if lat else -1, T1, store_x=from_input)

        class _Stop(Exception):
            pass

        def chk(stage):
            if cfg.stop <= stage:
                raise _Stop()

        try:
            chk(0)
            if 'ONLYP3' in cfg.debug:
                lsel = 1 if 'L1' in cfg.debug else 0
                wslots[:] = [pt("wslot%d" % i, [6144], BF16) for i in range(3)]
                T3 = alloc_p3()
                for ti in range(NTL):
                    r, rb = phase3(lsel, NT, 0, ti * NT, T3)
                    if 'NOOUT' not in cfg.debug:
                        write_out(r, rb, ti * NT, T3)
                raise _Stop()
            run_phase1_pass(0, True)
            chk(1)
            for l in range(L):
                exchange()
                chk(2 + 10 * l)
                phase2_mla(l)
                chk(3 + 10 * l)
                phase2_na(l)
                chk(4 + 10 * l)
                phase2_fourier(l)
                chk(5 + 10 * l)
                ar.reset(PERSIST)
                wslots[:] = [pt("wslot%d" % i, [6144], BF16) for i in range(3)]
                T3 = alloc_p3()
                last = (l + 1 == L)
                for ti in range(NTL + (0 if last else 1)):
                    lat = ti < NTL
                    nt = NT if lat else CTX
                    r, rb = phase3(l, nt, 0 if lat else 1, ti * NT if lat else 0, T3)
                    if last:
                        if 'NOOUT' not in cfg.debug:
                            write_out(r, rb, ti * NT, T3)
                    else:
                        P.dma("pool", xview((l + 1) % 2, lat, ti), r[:, :, 0:nt], rb, [db("XL%d" % ((l + 1) % 2) if lat else "XC%d" % ((l + 1) % 2))], own=rb[0])
                chk(6 + 10 * l)
                if not last:
                    P.barrier()
                    run_phase1_pass(l + 1, False)
                chk(7 + 10 * l)
        except _Stop:
            pass
        P.barrier()
        P.emit()
    return nc, dbg


def _consts(cfg, q):
    bf = ml_dtypes.bfloat16
    R, T, N, N1 = cfg.R, cfg.T, cfg.N, cfg.N1
    out = {}
    out["k_ident"] = np.eye(128, dtype=np.float32)
    cl = np.outer(np.arange(128), np.arange(128)).astype(np.float64) * (2 * np.pi / 128)
    out["k_cs128"] = np.concatenate([np.cos(cl), np.sin(cl)], 1).astype(bf)
    a1 = np.outer(np.arange(N1), np.arange(N1)).astype(np.float64) * (2 * np.pi / N1)
    out["k_c1"] = np.cos(a1).astype(bf); out["k_s1"] = np.sin(a1).astype(bf); out["k_ns1"] = (-np.sin(a1)).astype(bf)
    sc = 1.0 / np.sqrt(N * 128.0)
    k1 = np.arange(N1)[None, :, None]; k2 = np.arange(32)[None, None, :]; n2 = np.arange(128)[:, None, None]
    k = T * q + k1 + N1 * k2
    ang = (n2 * k % N).astype(np.float64) * (2 * np.pi / N)
    out["k_mre"] = (np.cos(ang) * sc).reshape(128, N1 * 32).astype(bf)
    out["k_mimn"] = (-np.sin(ang) * sc).reshape(128, N1 * 32).astype(bf)
    ac = np.outer(np.arange(256), np.arange(256)).astype(np.float64) * (2 * np.pi / 256)
    scc = 1.0 / np.sqrt(256 * 128.0)
    out["k_cc256"] = (np.cos(ac) * scc).astype(bf); out["k_scn256"] = (-np.sin(ac) * scc).astype(bf)
    t = np.arange(T) + q * T
    rows = (t // GW).astype(np.float32); cols = (t % GW).astype(np.float32)
    inv = (np.float32(10000.0) ** (-np.arange(8, dtype=np.float32) / np.float32(8))).astype(np.float32)
    angr = np.concatenate([rows[:, None] * inv, cols[:, None] * inv], -1).astype(np.float32)
    co = np.cos(angr).T.astype(np.float32); si = np.sin(angr).T.astype(np.float32)
    out["k_rope"] = np.stack([np.concatenate([co, co], 0), np.concatenate([-si, si], 0)], 0).astype(np.float32)
    NR = 4 * R
    rvt = np.zeros((128, 3, 8, 8), np.float32)
    for ty, r0 in enumerate((R * q, R * q + 8, R * q + R - 8)):
        for p in range(128):
            for kb in range(8):
                krow = r0 - 4 + 2 * kb + p // 64
                for qi in range(8):
                    rs = min(max(r0 + qi - 4, 0), NR - 8)
                    rvt[p, ty, kb, qi] = 1.0 if rs <= krow < rs + 8 else 0.0
    out["k_rv"] = rvt.reshape(128, 192)
    sel = np.zeros((128, 8), np.float32)
    if q > 0:
        sel[:, q - 1] = 1.0
    if q < 3:
        sel[:, 4 + q + 1] = 1.0
    out["k_sel"] = sel
    return out


def _rpb_table(rpb):
    Ln, H = rpb.shape[0], rpb.shape[1]
    tab = np.full((Ln, H, 23, 64, 64), MASKVAL, np.float32)
    qc = np.arange(64)[None, :]; kc = np.arange(64)[:, None]
    cs = np.clip(qc - 8, 0, 48)
    valid = (kc >= cs) & (kc < cs + 16)
    dc = np.clip(kc - qc + 15, 0, 30)
    for e in range(23):
        dr = 11 - e
        if abs(dr) <= 7:
            g = rpb[:, :, dr + 7, :][:, :, dc]
            tab[:, :, e] = np.where(valid[None, None], g, np.float32(MASKVAL))
    return tab


def _pk(v, k):
    v = np.asarray(v, np.float32)
    sh = v.shape[:-1]
    return np.ascontiguousarray(np.swapaxes(v.reshape(sh + (k, 128)), -1, -2))


_CACHE = {}


def run_cfg(inputs, cfg):
    key = (cfg.R, cfg.debug, cfg.stop)
    if key not in _CACHE:
        _CACHE[key] = build_program(cfg)
    nc, dbg = _CACHE[key]
    T = cfg.T
    f32 = lambda a: np.ascontiguousarray(np.asarray(a, np.float32))
    shared = {
        "c_ctx": _pk(inputs["c_ctx"], 8), "ln_in_g": _pk(inputs["ln_in_g"], 8), "ln_in_b": _pk(inputs["ln_in_b"], 8),
        "w_mod": f32(inputs["w_mod"]), "b_mod": _pk(inputs["b_mod"], 48), "w_in": f32(inputs["w_in"]),
        "mla_q_norm_g": _pk(inputs["mla_q_norm_g"], 2), "mla_kv_norm_g": _pk(inputs["mla_kv_norm_g"], 1),
        "w_uq": f32(inputs["w_uq"]), "w_qr": f32(inputs["w_qr"]), "w_uk": f32(inputs["w_uk"]), "w_uv": f32(inputs["w_uv"]),
        "rpbT": _rpb_table(f32(inputs["na_rpb"])),
        "w_branch": f32(inputs["w_branch"]).reshape(L, 1536, D), "w_out": f32(inputs["w_out"]),
        "ln1_g": _pk(inputs["ln1_g"], 8), "ln1_b": _pk(inputs["ln1_b"], 8), "ln2_g": _pk(inputs["ln2_g"], 8), "ln2_b": _pk(inputs["ln2_b"], 8),
        "w_ffn_gate": f32(inputs["w_ffn_gate"]), "w_ffn_up": f32(inputs["w_ffn_up"]), "w_ffn_down": f32(inputs["w_ffn_down"]),
    }
    x = np.asarray(inputs["x"], np.float32); ctx = np.asarray(inputs["ctx"], np.float32); c = np.asarray(inputs["c"], np.float32)
    in_maps = []
    for i in range(8):
        b, q = i // 4, i % 4
        m = dict(shared)
        m["x"] = np.ascontiguousarray(x[b, q * T:(q + 1) * T]); m["ctx"] = np.ascontiguousarray(ctx[b]); m["c"] = _pk(c[b], 8)
        m.update(_consts(cfg, q))
        in_maps.append(m)
    res = run_bass_kernel_spmd(nc, in_maps, core_ids=list(range(8)))
    out = np.empty((2, 4 * T, D), np.float32)
    for i in range(8):
        b, q = i // 4, i % 4
        out[b, q * T:(q + 1) * T] = np.asarray(res.results[i]["out"], np.float32)
    return out, res


def kernel(**inputs):
    out, _ = run_cfg(inputs, Cfg(64))
    return out
```

```python
from contextlib import ExitStack
import ml_dtypes
from concourse.bass_utils import run_bass_kernel_spmd
import numpy as np
import concourse.bass as bass
import concourse.mybir as mybir

F32 = mybir.dt.float32
BF16 = mybir.dt.bfloat16
U8 = mybir.dt.uint8
ALU = mybir.AluOpType
AF = mybir.ActivationFunctionType
AX = mybir.AxisListType


class Buf:
    __slots__ = ("name", "w", "r", "dsem", "dcnt")

    def __init__(self, name):
        self.name = name
        self.w = None
        self.r = {}
        self.dsem = None
        self.dcnt = 0


class Prog:
    ENGS = ("pe", "act", "dve", "pool", "sp")

    def __init__(self, nc, stack):
        self.nc = nc
        self.stack = stack
        self.lists = {e: [] for e in self.ENGS}
        self.sems = {}
        self.cnt = {}
        self.waited = {e: {} for e in self.ENGS}
        for e in self.ENGS:
            self._newsem("eng_" + e)
        self._newsem("coll")
        self.pool_keys = []
        self.pool_idx = 0
        self.attached = []

    def _newsem(self, key):
        self.sems[key] = self.stack.enter_context(self.nc.semaphore(key))
        self.cnt[key] = 0
        return key

    def buf(self, name):
        return Buf(name)

    def _deps(self, reads, writes):
        deps = {}
        def add(t):
            if t is None:
                return
            k, v = t
            if deps.get(k, 0) < v:
                deps[k] = v
        for b in reads:
            add(b.w)
        for b in writes:
            add(b.w)
            for k, v in b.r.items():
                add((k, v))
        return deps

    def _waits(self, eng, deps):
        ws = []
        wd = self.waited[eng]
        for k, v in deps.items():
            if wd.get(k, 0) < v:
                wd[k] = v
                ws.append((k, v))
        return ws

    def _mark(self, tok, reads, writes):
        k, v = tok
        for b in reads:
            if b.r.get(k, 0) < v:
                b.r[k] = v
        for b in writes:
            b.w = tok
            b.r = {}

    def _dsem(self, b):
        if b.dsem is None:
            if self.pool_idx >= len(self.pool_keys):
                self.pool_keys.append(self._newsem("dq%d" % len(self.pool_keys)))
            b.dsem = self.pool_keys[self.pool_idx]
            self.pool_idx += 1
            self.attached.append(b)
        return b.dsem

    def op(self, eng, fn, reads=(), writes=(), inc=True):
        deps = self._deps(reads, writes)
        if eng == "pe":
            deps.pop("eng_pe", None)
        ws = self._waits(eng, deps)
        key = "eng_" + eng
        if inc:
            self.cnt[key] += 1
            tok = (key, self.cnt[key])
        else:
            tok = (key, self.cnt[key] + 1)
        self.lists[eng].append((ws, fn, (key, 1) if inc else None))
        self._mark(tok, reads, writes)
        if inc:
            self.waited[eng][key] = max(self.waited[eng].get(key, 0), 0)
        return tok

    def dma(self, q, out_ap, in_ap, reads, writes, own=None, **kw):
        if own is None:
            own = writes[0]
        key = self._dsem(own)
        ws = self._waits(q, self._deps(reads, writes))
        self.cnt[key] += 16
        tok = (key, self.cnt[key])
        fn = lambda e, o=out_ap, i=in_ap, kw=kw: e.dma_start(out=o, in_=i, **kw)
        self.lists[q].append((ws, fn, (key, 16)))
        self._mark(tok, reads, writes)
        return tok

    def collective(self, kind, in_ap, out_ap, groups, reads, writes):
        key = "coll"
        ws = self._waits("pool", self._deps(reads, writes))
        self.cnt[key] += 1
        tok = (key, self.cnt[key])
        fn = lambda e: e.collective_compute(kind, ALU.bypass, replica_groups=groups,
                                            ins=[in_ap], outs=[out_ap])
        self.lists["pool"].append((ws, fn, (key, 1)))
        self._mark(tok, reads, writes)
        return tok

    def barrier(self):
        for e in self.ENGS:
            ws = self._waits(e, {k: v for k, v in self.cnt.items() if v > 0})
            if ws:
                self.lists[e].append((ws, None, None))
        for b in self.attached:
            b.dsem = None
        self.attached = []
        self.pool_idx = 0

    def emit(self):
        nc = self.nc
        sems = self.sems
        lists = self.lists
        with nc.Block() as block:
            def run(eng_obj, lst):
                for ws, fn, inc in lst:
                    for k, v in ws:
                        eng_obj.wait_ge(sems[k], v)
                    if fn is not None:
                        ins = fn(eng_obj)
                        if inc is not None:
                            ins.then_inc(sems[inc[0]], inc[1])

            @block.tensor
            def _(e):
                run(e, lists["pe"])

            @block.scalar
            def _(e):
                run(e, lists["act"])

            @block.vector
            def _(e):
                run(e, lists["dve"])

            @block.gpsimd
            def _(e):
                run(e, lists["pool"])

            @block.sync
            def _(e):
                run(e, lists["sp"])


class Arena:
    def __init__(self, ap_u8, limit):
        self.ap = ap_u8
        self.limit = limit
        self.off = 0

    def reset(self, off=0):
        self.off = off

    def alloc(self, shape, dt, parts=128):
        esz = 4 if dt == F32 else 2
        n = int(np.prod(shape))
        nbytes = (n * esz + 63) // 64 * 64
        assert self.off + nbytes <= self.limit, ("arena overflow", self.off, nbytes, self.limit)
        a = self.ap[0:parts, self.off:self.off + n * esz].bitcast(dt)
        self.off += nbytes
        if len(shape) == 2:
            a = a.rearrange("p (a b) -> p a b", a=shape[0])
        elif len(shape) == 3:
            a = a.rearrange("p (a b c) -> p a b c", a=shape[0], b=shape[1])
        return a
D = 1024
CTX = 256
GW = 64
FH = 2816
L = 2
EPS = 1e-5
ALPHA = (2 * L) ** 0.25
NA_SCALE = 64 ** -0.5
MLA_SCALE = 96 ** -0.5
MASKVAL = -30000.0
O_F, O_NQ, O_NK, O_NV, O_CQ, O_CKV, O_KR, O_G = 0, 512, 1024, 1536, 2048, 2304, 2432, 2464
IN_DIM = 5536


class Cfg:
    def __init__(self, R=64, debug=(), stop=99):
        self.stop = stop
        self.R = R
        self.T = R * GW
        self.N = 4 * self.T
        self.N1 = self.N // 128
        self.NT = 512
        self.NTL = self.T // 512
        self.NQB = R // 8
        self.debug = tuple(debug)


def build_program(cfg):
    R, T, N, N1, NT, NTL, NQB = cfg.R, cfg.T, cfg.N, cfg.N1, cfg.NT, cfg.NTL, cfg.NQB
    TW = T + 512
    nc = bass.Bass("TRN2", target_bir_lowering=False)

    def din(name, shape, dt=F32):
        return nc.dram_tensor(name, list(shape), dt, kind="ExternalInput").ap()

    dbg = {}

    def dscr(name, shape, dt=BF16):
        if name in cfg.debug and name not in ('NOOUT', 'ONLYP3', 'L1'):
            t = nc.dram_tensor(name, list(shape), dt, kind="ExternalOutput")
            dbg[name] = t
        else:
            t = nc.dram_tensor(name, list(shape), dt)
        return t.ap()

    x_in = din("x", [T, D]); ctx_in = din("ctx", [CTX, D]); c_in = din("c", [128, 8]); cc_in = din("c_ctx", [128, 8])
    lng_in = din("ln_in_g", [128, 8]); lnb_in = din("ln_in_b", [128, 8])
    w_mod = din("w_mod", [L, D, 6 * D]); b_mod = din("b_mod", [L, 128, 48]); w_in = din("w_in", [L, D, IN_DIM])
    gq_in = din("mla_q_norm_g", [L, 128, 2]); gkv_in = din("mla_kv_norm_g", [L, 128, 1])
    w_uq = din("w_uq", [L, 256, 512]); w_qr = din("w_qr", [L, 256, 256]); w_uk = din("w_uk", [L, 128, 512]); w_uv = din("w_uv", [L, 128, 512])
    rpbT = din("rpbT", [L, 8, 23, 64, 64])
    w_br = din("w_branch", [L, 1536, D]); w_out = din("w_out", [L, D, D])
    ln1g = din("ln1_g", [L, 128, 8]); ln1b = din("ln1_b", [L, 128, 8]); ln2g = din("ln2_g", [L, 128, 8]); ln2b = din("ln2_b", [L, 128, 8])
    w_fg = din("w_ffn_gate", [L, D, FH]); w_fu = din("w_ffn_up", [L, D, FH]); w_fd = din("w_ffn_down", [L, FH, D])
    k_id = din("k_ident", [128, 128]); k_cs = din("k_cs128", [128, 256], BF16)
    k_c1 = din("k_c1", [N1, N1], BF16); k_s1 = din("k_s1", [N1, N1], BF16); k_ns1 = din("k_ns1", [N1, N1], BF16)
    k_mre = din("k_mre", [128, N1 * 32], BF16); k_mim = din("k_mimn", [128, N1 * 32], BF16)
    k_cc = din("k_cc256", [256, 256], BF16); k_sc = din("k_scn256", [256, 256], BF16)
    k_rope = din("k_rope", [2, 32, T]); k_rv = din("k_rv", [128, 3 * 64]); k_sel = din("k_sel", [128, 8])
    out_d = nc.dram_tensor("out", [T, D], F32, kind="ExternalOutput").ap()

    WP = {}
    CASTS = []

    def wpiece(key, src2d, r0, kch, c0, pc):
        WP[key] = (dscr("W_%s" % "_".join(str(x) for x in key), [128, kch * pc]), kch, pc)
        CASTS.append((key, src2d, r0, kch, c0, pc))

    for l in range(L):
        for nm, c0, pc in (("f", O_F, 512), ("nq", O_NQ, 512), ("nk", O_NK, 512), ("nv", O_NV, 512), ("cq", O_CQ, 256), ("ckv", O_CKV, 128)):
            wpiece((l, nm), w_in[l], 0, 8, c0, pc)
        for gi in range(3):
            for hb in range(2):
                wpiece((l, "g", gi, hb), w_in[l], 0, 8, O_G + gi * D + hb * 512, 512)
                wpiece((l, "br", gi, hb), w_br[l], gi * 512, 4, hb * 512, 512)
        for hb in range(2):
            wpiece((l, "o", hb), w_out[l], 0, 8, hb * 512, 512)
        for j in range(6):
            pc = 512 if j < 5 else 256
            wpiece((l, "fg", j), w_fg[l], 0, 8, j * 512, pc)
            wpiece((l, "fu", j), w_fu[l], 0, 8, j * 512, pc)
        for j in range(4):
            wpiece((l, "fd", j), w_fd[l], 0, 22, j * 256, 256)
    XL = [dscr("XL%d" % i, [NTL, 128, 8 * NT], F32) for i in range(2)]
    XC = [dscr("XC%d" % i, [128, 8 * CTX], F32) for i in range(2)]
    NAQT = dscr("NAQT", [NTL, 128, 4 * NT]); NAKW = dscr("NAKW", [NTL, 128, 4 * NT]); NAKH = dscr("NAKH", [4, 128, 512]); NBL = T // 128; NWB = TW // 128
    NAVW = dscr("NAVW", [NTL * 128, 2080]); NAVH = dscr("NAVH", [4 * 128, 520])
    HK = dscr("HK", [512, 512]); HKA = dscr("HKA", [4 * 512, 512]); HV = dscr("HV", [512, 512]); HVA = dscr("HVA", [4 * 512, 512])
    MLAQT = dscr("MLAQT", [8, 96, T]); MLAKT = dscr("MLAKT", [768, T]); MLAKTA = [dscr("MLAKTA%d" % h, [4 * 96, T]) for h in range(8)]
    MLAV = dscr("MLAV", [NTL * 128, 2080])
    MLAVA = [dscr("MLAVA%d" % t, [4 * 128, 2080]) for t in range(NTL)]
    HF = 2 if T * 512 > (1 << 20) else 1; CH = T // HF
    FAB = dscr("FAB", [4 * T, 256]); FABA = [dscr("FABA%d" % j, [4 * CH, 256]) for j in range(4 * HF)]
    NAQTc = dscr("NAQTc", [4, 128, CTX]); NAKTc = dscr("NAKTc", [4, 128, CTX]); NAVc = dscr("NAVc", [128, 8 * 130])
    MLAQTc = dscr("MLAQTc", [8, 96, CTX]); MLAKTc = dscr("MLAKTc", [8, 96, CTX]); MLAVc = dscr("MLAVc", [128, 8 * 130])
    FABc = dscr("FABc", [CTX, 4, 256])
    YF = dscr("YF", [NTL, 128, 4 * NT]); YNA = dscr("YNA", [NTL, 128, 4 * NT]); YMLA = dscr("YMLA", [NTL, 128, 4 * NT])
    YFc = dscr("YFc", [4, 128, CTX]); YNAc = dscr("YNAc", [4, 128, CTX]); YMLAc = dscr("YMLAc", [4, 128, CTX])
    EBT = [dscr("EBT%d" % l, [8, 64, 23 * 64]) for l in range(L)]

    GROUPS = [[0, 1, 2, 3], [4, 5, 6, 7]]
    ARENA = 196 * 1024

    with ExitStack() as st:
        P = Prog(nc, st)
        ar_ap = st.enter_context(nc.sbuf_tensor("arena", [128, ARENA], U8))
        ps_ap = st.enter_context(nc.psum_tensor("ps", [128, 8, 512], F32))
        ar = Arena(ar_ap, ARENA)
        psb = [P.buf("ps%d" % i) for i in range(8)]
        PS = lambda i: ps_ap[:, i, :]
        dram = {}

        def db(name):
            if name not in dram:
                dram[name] = P.buf(name)
            return dram[name]

        rr = {"lin": 0, "ev": 0, "bank": 0}

        def xview(i, lat, ti):
            if lat:
                return XL[i][ti].rearrange("p (k t) -> p k t", k=8)
            return XC[i].rearrange("p (k t) -> p k t", k=8)

        def evac_eng():
            rr["ev"] ^= 1
            return "act" if rr["ev"] else "dve"

        def copy_op(eng, out, in_, rd, wr):
            if eng == "act":
                P.op("act", lambda e: e.copy(out=out, in_=in_), rd, wr)
            else:
                P.op(eng, lambda e: e.tensor_copy(out=out, in_=in_), rd, wr)

        def pt(name, shape, dt, parts=128):
            return ar.alloc(shape, dt, parts=parts), P.buf(name)

        def ptk(name, shape, dt):
            return ar.alloc(shape, dt), [P.buf("%s_%d" % (name, k)) for k in range(shape[0])]

        def bk(b, k):
            return b[k] if isinstance(b, list) else b

        def ball(b):
            return list(b) if isinstance(b, list) else [b]

        ident, b_ident = pt("ident", [128], F32)
        ones, b_ones = pt("ones", [128], F32)
        epsb, b_eps = pt("eps", [1], F32)
        cs128, b_cs = pt("cs128", [256], BF16)
        c1, b_c1 = pt("c1", [N1], BF16); s1, b_s1 = pt("s1", [N1], BF16); ns1, b_ns1 = pt("ns1", [N1], BF16)
        mre, b_mre = pt("mre", [N1, 32], BF16); mimn, b_mim = pt("mimn", [N1, 32], BF16)
        cc256, b_cc = pt("cc256", [2, 256], BF16); sc256, b_sc = pt("sc256", [2, 256], BF16)
        rv, b_rv = pt("rv", [3, 8, 8], F32); sel, b_sel = pt("sel", [8], F32)
        lnin, b_lnin = pt("lnin", [2, 8], F32)
        modv, b_mod_ = pt("modv", [L, 48, 2], F32); mod1, b_mod1 = pt("mod1", [L, 48, 2], F32)
        lnp, b_lnp = pt("lnp", [L, 4, 8], F32)
        gq, b_gq = pt("gq", [L, 2], F32); gkv, b_gkv = pt("gkv", [L, 1], F32)
        wkr, b_wkr = pt("wkr", [8, 2, 96], BF16)
        wq, b_wq = pt("wq", [2, 8, 96], BF16); wqs, b_wqs = pt("wqs", [2, 8, 96], BF16)
        wk, b_wk = pt("wk", [8, 64], BF16); wuv, b_wuv = pt("wuv", [512], BF16)
        PERSIST = ar.off

        P.dma("sp", ident, k_id, [], [b_ident])
        P.op("pool", lambda e: e.memset(ones, 1.0), [], [b_ones])
        P.op("pool", lambda e: e.memset(epsb, EPS), [], [b_eps])
        P.dma("sp", cs128, k_cs, [], [b_cs])
        P.dma("sp", c1[0:N1], k_c1, [], [b_c1]); P.dma("sp", s1[0:N1], k_s1, [], [b_s1]); P.dma("sp", ns1[0:N1], k_ns1, [], [b_ns1])
        P.dma("sp", mre, k_mre.rearrange("p (a b) -> p a b", b=32), [], [b_mre])
        P.dma("sp", mimn, k_mim.rearrange("p (a b) -> p a b", b=32), [], [b_mim])
        P.dma("sp", cc256, k_cc.rearrange("(a p) k -> p a k", p=128), [], [b_cc])
        P.dma("sp", sc256, k_sc.rearrange("(a p) k -> p a k", p=128), [], [b_sc])
        P.dma("sp", rv, k_rv.rearrange("p (t a b) -> p t a b", t=3, a=8), [], [b_rv])
        P.dma("sp", sel, k_sel, [], [b_sel])
        P.dma("sp", lnin[:, 0, :], lng_in, [], [b_lnin])
        P.dma("sp", lnin[:, 1, :], lnb_in, [], [b_lnin])
        for l in range(L):
            for i, src in enumerate((ln1g, ln1b, ln2g, ln2b)):
                P.dma("sp", lnp[:, l, i, :], src[l], [], [b_lnp])
            P.dma("sp", gq[:, l, :], gq_in[l], [], [b_gq])
            P.dma("sp", gkv[:, l, :], gkv_in[l], [], [b_gkv])

        b_wcast = P.buf("wcast")
        b_wcast1 = P.buf("wcast1"); b_wcast3 = P.buf("wcast3")
        late_casts = []
        mid_casts = []
        P1KEYS = ("f", "nq", "nk", "nv", "cq", "ckv")

        def wbuf_of(key):
            if key[0] == 1:
                return b_wcast1
            return b_wcast if key[1] in P1KEYS else b_wcast3

        for (key, src2d, r0, kch, c0, pc) in CASTS:
            dstp = WP[key][0]
            for k in range(kch):
                args = (dstp[:, k * pc:(k + 1) * pc], src2d[r0 + k * 128:r0 + (k + 1) * 128, c0:c0 + pc])
                if key[0] == 0 and key[1] in P1KEYS:
                    P.dma("pool", args[0], args[1], [], [b_wcast])
                elif key[0] == 0:
                    mid_casts.append(args)
                else:
                    late_casts.append(args)

        def issue_late_casts(n):
            for _ in range(min(n, len(late_casts))):
                o, i = late_casts.pop(0)
                P.dma("pool", o, i, [], [b_wcast1])

        def issue_mid_casts(n):
            for _ in range(min(n, len(mid_casts))):
                o, i = mid_casts.pop(0)
                P.dma("pool", o, i, [], [b_wcast3])

        ar.reset(PERSIST)
        cs_t, b_cst = pt("cs_t", [2, 8], F32)
        bm_t, b_bmt = pt("bm_t", [L, 48], F32)
        P.dma("sp", cs_t[:, 0, :], c_in, [], [b_cst])
        P.dma("sp", cs_t[:, 1, :], cc_in, [], [b_cst])
        P.op("act", lambda e: e.activation(out=cs_t, in_=cs_t, func=AF.Silu), [b_cst], [b_cst])
        for l in range(L):
            P.dma("sp", bm_t[:, l, :], b_mod[l], [], [b_bmt])
        wm = [pt("wm%d" % i, [8, 512], F32) for i in range(2)]
        for l in range(L):
            for j in range(12):
                wt_, wb_ = wm[j % 2]
                P.dma("sp", wt_, w_mod[l][:, j * 512:(j + 1) * 512].rearrange("(k p) m -> p k m", p=128), [], [wb_])
                for cc in range(4):
                    ch = j * 4 + cc
                    bank = ch % 4
                    for k in range(8):
                        P.op("pe", lambda e, wt_=wt_, k=k, cc=cc, bank=bank: e.matmul(
                            ps_ap[:, bank, 0:2], lhsT=wt_[:, k, cc * 128:(cc + 1) * 128], rhs=cs_t[:, :, k],
                            start=(k == 0), stop=(k == 7)), [wb_, b_cst], [psb[bank]], inc=(k == 7))
                    P.op("dve", lambda e, l=l, ch=ch, bank=bank: e.tensor_scalar(
                        out=modv[:, l, ch, :], in0=ps_ap[:, bank, 0:2], scalar1=bm_t[:, l, ch:ch + 1], scalar2=None,
                        op0=ALU.add), [psb[bank], b_bmt], [b_mod_])
        P.op("dve", lambda e: e.tensor_scalar(out=mod1, in0=modv, scalar1=1.0, scalar2=None, op0=ALU.add), [b_mod_], [b_mod1])

        def MOD(l, idx, s, plus1):
            src = mod1 if plus1 else modv
            return src[:, l, idx, s:s + 1]

        eb_t = [pt("ebt%d" % i, [23, 64], F32, parts=64) for i in range(2)]
        eb_o = [pt("ebo%d" % i, [23, 64], BF16, parts=64) for i in range(2)]
        for l in range(L):
            for h in range(8):
                (ti, tb), (oi, ob) = eb_t[h % 2], eb_o[h % 2]
                P.dma("sp", ti, rpbT[l, h].rearrange("e k q -> k e q"), [], [tb])
                P.op("act", lambda e, ti=ti, oi=oi: e.activation(out=oi, in_=ti, func=AF.Exp), [tb], [ob])
                P.dma("pool", EBT[l][h], oi.rearrange("p a b -> p (a b)"), [ob], [db("EBT%d" % l)], own=ob)
        P.barrier()
        ar.reset(PERSIST)

        def load_layer_small(l):
            st_uq, b1 = pt("st_uq", [2, 512], F32); st_qr, b2 = pt("st_qr", [2, 256], F32)
            P.dma("sp", st_uq, w_uq[l].rearrange("(k p) m -> p k m", p=128), [], [b1])
            P.dma("sp", st_qr, w_qr[l].rearrange("(k p) m -> p k m", p=128), [], [b2])
            P.op("pool", lambda e: e.memset(wqs, 0.0), [], [b_wqs])
            P.op("pool", lambda e: e.memset(wkr, 0.0), [], [b_wkr])
            uqv = st_uq.rearrange("p k (h d) -> p k h d", h=8); qrv = st_qr.rearrange("p k (h d) -> p k h d", h=8)
            P.op("dve", lambda e: e.tensor_copy(out=wq[:, :, :, 0:64], in_=uqv), [b1], [b_wq])
            P.op("dve", lambda e: e.tensor_copy(out=wq[:, :, :, 64:96], in_=qrv), [b2], [b_wq])
            P.op("dve", lambda e: e.tensor_copy(out=wqs[:, :, :, 64:80], in_=qrv[:, :, :, 16:32]), [b2], [b_wqs])
            P.op("dve", lambda e: e.tensor_copy(out=wqs[:, :, :, 80:96], in_=qrv[:, :, :, 0:16]), [b2], [b_wqs])
            st_kr, b3 = pt("st_kr", [8, 32], F32)
            P.dma("sp", st_kr, w_in[l][:, O_KR:O_KR + 32].rearrange("(k p) m -> p k m", p=128), [], [b3])
            P.op("dve", lambda e: e.tensor_copy(out=wkr[:, :, 0, 64:96], in_=st_kr), [b3], [b_wkr])
            P.op("dve", lambda e: e.tensor_copy(out=wkr[:, :, 1, 64:80], in_=st_kr[:, :, 16:32]), [b3], [b_wkr])
            P.op("dve", lambda e: e.tensor_copy(out=wkr[:, :, 1, 80:96], in_=st_kr[:, :, 0:16]), [b3], [b_wkr])
            st_uk, b4 = pt("st_uk", [512], F32); st_uv, b5 = pt("st_uv", [512], F32)
            P.dma("sp", st_uk, w_uk[l], [], [b4]); P.dma("sp", st_uv, w_uv[l], [], [b5])
            P.op("dve", lambda e: e.tensor_copy(out=wk, in_=st_uk.rearrange("p (h d) -> p h d", h=8)), [b4], [b_wk])
            P.op("dve", lambda e: e.tensor_copy(out=wuv, in_=st_uv), [b5], [b_wuv])

        wslots = []

        def load_piece(key):
            wd, kch, pc = WP[key]
            slot, sbuf_ = wslots[rr["lin"] % len(wslots)]
            rr["lin"] += 1
            P.dma("sp", slot[:, 0:kch * pc], wd, [wbuf_of(key)], [sbuf_])
            return slot[:, 0:kch * pc].rearrange("p (k m) -> p k m", k=kch), sbuf_, kch, pc

        def mm_chunk(piece, c0, rows, act, abuf, nt):
            sv, sbuf_, kch, pc = piece
            bank = rr["bank"] % 4; rr["bank"] += 1
            for k in range(kch):
                P.op("pe", lambda e, k=k: e.matmul(ps_ap[0:rows, bank, 0:nt], lhsT=sv[:, k, c0:c0 + rows], rhs=act[:, k, 0:nt],
                                                   start=(k == 0), stop=(k == kch - 1)), [sbuf_, bk(abuf, k)], [psb[bank]], inc=(k == kch - 1))
            return ps_ap[0:rows, bank, 0:nt], psb[bank]

        def linear_fm(keys, act, abuf, nt, consume):
            ci = 0
            for key in keys:
                piece = load_piece(key)
                for c0 in range(0, piece[3], 128):
                    rows = min(128, piece[3] - c0)
                    ps, pb = mm_chunk(piece, c0, rows, act, abuf, nt)
                    consume(ci, ps, pb, rows)
                    ci += 1

        def ln_fm(r, rbuf, nt, gap, bap, sq, sqbuf, st4, st4buf):
            for k in range(8):
                if k % 2 == 0:
                    P.op("act", lambda e, k=k: e.activation(out=sq[:, k, 0:nt], in_=r[:, k, 0:nt], func=AF.Square), [bk(rbuf, k)], [bk(sqbuf, k)])
                else:
                    P.op("dve", lambda e, k=k: e.tensor_tensor(out=sq[:, k, 0:nt], in0=r[:, k, 0:nt], in1=r[:, k, 0:nt], op=ALU.mult), [bk(rbuf, k)], [bk(sqbuf, k)])
            for k in range(8):
                P.op("pe", lambda e, k=k: e.matmul(ps_ap[:, 4, 0:nt], lhsT=ones, rhs=r[:, k, 0:nt], start=(k == 0), stop=(k == 7)),
                     [b_ones, bk(rbuf, k)], [psb[4]], inc=(k == 7))
            for k in range(8):
                P.op("pe", lambda e, k=k: e.matmul(ps_ap[:, 5, 0:nt], lhsT=ones, rhs=sq[:, k, 0:nt], start=(k == 0), stop=(k == 7)),
                     [b_ones, bk(sqbuf, k)], [psb[5]], inc=(k == 7))
            mean = st4[:, 0, 0:nt]; msq = st4[:, 1, 0:nt]; rstd = st4[:, 2, 0:nt]
            P.op("dve", lambda e: e.tensor_scalar(out=mean, in0=ps_ap[:, 4, 0:nt], scalar1=1.0 / D, scalar2=None, op0=ALU.mult), [psb[4]], [st4buf])
            P.op("dve", lambda e: e.tensor_tensor(out=msq, in0=mean, in1=mean, op=ALU.mult), [st4buf], [st4buf])
            P.op("dve", lambda e: e.scalar_tensor_tensor(out=rstd, in0=ps_ap[:, 5, 0:nt], scalar=1.0 / D, in1=msq, op0=ALU.mult, op1=ALU.subtract), [psb[5], st4buf], [st4buf])
            P.op("act", lambda e: e.activation(out=rstd, in_=rstd, func=AF.Ln, bias=epsb[:, 0:1], scale=1.0), [st4buf, b_eps], [st4buf])
            P.op("act", lambda e: e.activation(out=rstd, in_=rstd, func=AF.Exp, scale=-0.5), [st4buf], [st4buf])
            for k in range(8):
                rk = r[:, k, 0:nt]
                P.op("dve", lambda e, rk=rk: e.tensor_tensor(out=rk, in0=rk, in1=mean, op=ALU.subtract), [bk(rbuf, k), st4buf], [bk(rbuf, k)])
                P.op("dve", lambda e, rk=rk: e.tensor_tensor(out=rk, in0=rk, in1=rstd, op=ALU.mult), [bk(rbuf, k), st4buf], [bk(rbuf, k)])
                P.op("act", lambda e, rk=rk, k=k: e.activation(out=rk, in_=rk, func=AF.Identity, bias=bap[:, k:k + 1], scale=gap[:, k:k + 1]), [bk(rbuf, k), b_lnp, b_lnin], [bk(rbuf, k)])

        def rms_fm(src, sbuf_, kch, nt, gsc, dst, dbuf, sq, sqbuf, st4, st4buf):
            for k in range(kch):
                P.op("act", lambda e, k=k: e.activation(out=sq[:, k, 0:nt], in_=src[:, k, 0:nt], func=AF.Square), [sbuf_], [sqbuf])
            for k in range(kch):
                P.op("pe", lambda e, k=k: e.matmul(ps_ap[:, 5, 0:nt], lhsT=ones, rhs=sq[:, k, 0:nt], start=(k == 0), stop=(k == kch - 1)),
                     [b_ones, sqbuf], [psb[5]], inc=(k == kch - 1))
            rstd = st4[:, 3, 0:nt]
            P.op("act", lambda e: e.activation(out=rstd, in_=ps_ap[:, 5, 0:nt], func=AF.Ln, bias=epsb[:, 0:1], scale=1.0 / (kch * 128)), [psb[5], b_eps], [st4buf])
            P.op("act", lambda e: e.activation(out=rstd, in_=rstd, func=AF.Exp, scale=-0.5), [st4buf], [st4buf])
            for k in range(kch):
                P.op("dve", lambda e, k=k: e.scalar_tensor_tensor(out=dst[:, k, 0:nt], in0=src[:, k, 0:nt], scalar=gsc[:, k:k + 1], in1=rstd, op0=ALU.mult, op1=ALU.mult),
                     [sbuf_, st4buf, b_gq, b_gkv], [dbuf])

        att_state = {}
        SBANKS = (0, 1, 2, 4, 5, 6)

        def attend(QT, qbuf, kbs, scale, nq, ytile, ybuf, pre=None, after=None):
            ptl = att_state["ptl"]; rs, rsb = att_state["rs"]; bcs, bcsb = att_state["bc"]
            ob = att_state["obank"]; att_state["obank"] = 3 if ob == 7 else 7
            n = len(kbs)

            def S(i):
                KT, kb_, V, vb_, eb, ebb = kbs[i]
                bank = SBANKS[att_state["si"] % 6]; att_state["si"] += 1
                P.op("pe", lambda e: e.matmul(ps_ap[:, bank, 0:nq], lhsT=KT, rhs=QT, start=True, stop=True), [kb_, qbuf], [psb[bank]])
                pt_, pb_ = ptl[att_state["pi"] % len(ptl)]; att_state["pi"] += 1
                P.op("act", lambda e: e.activation(out=pt_[:, 0:nq], in_=ps_ap[:, bank, 0:nq], func=AF.Exp, scale=scale), [psb[bank]], [pb_])
                if eb is not None:
                    P.op("dve", lambda e: e.tensor_tensor(out=pt_[:, 0:nq], in0=pt_[:, 0:nq], in1=eb, op=ALU.mult), [pb_, ebb], [pb_])
                return pt_, pb_

            def PV(i, pt_, pb_):
                KT, kb_, V, vb_, eb, ebb = kbs[i]
                P.op("pe", lambda e: e.matmul(ps_ap[0:65, ob, 0:nq], lhsT=V, rhs=pt_[:, 0:nq], start=(i == 0), stop=(i == n - 1)), [vb_, pb_], [psb[ob]])

            pend = []
            for i in range(n):
                pend.append((i,) + S(i))
                if pre is not None and i == min(2, n - 1):
                    pre[0]()
                if pre is not None and i == min(7, n - 1):
                    pre[1]()
                    pre = None
                if len(pend) > 3:
                    PV(*pend.pop(0))

            def fin_a():
                while pend:
                    PV(*pend.pop(0))
                P.op("act", lambda e: e.copy(out=rs[64:65, 0:nq], in_=ps_ap[64:65, ob, 0:nq]), [psb[ob]], [rsb])

            def fin_b():
                b2 = SBANKS[att_state["si"] % 6]; att_state["si"] += 1
                P.op("pe", lambda e: e.matmul(ps_ap[0:64, b2, 0:nq], lhsT=ones[64:65, 0:64], rhs=rs[64:65, 0:nq], start=True, stop=True), [b_ones, rsb], [psb[b2]])
                P.op("dve", lambda e: e.reciprocal(out=bcs[0:64, 0:nq], in_=ps_ap[0:64, b2, 0:nq]), [psb[b2]], [bcsb])
                P.op("dve", lambda e: e.tensor_tensor(out=ytile, in0=ps_ap[0:64, ob, 0:nq], in1=bcs[0:64, 0:nq], op=ALU.mult), [psb[ob], bcsb], [ybuf])
                if after is not None:
                    after()
            return (fin_a, fin_b)

        def phase1(l, xt, xbuf, nt, s, tok0, tidx, T1, store_x=True):
            lat = (s == 0)
            nsub = nt // 128
            A = T1["a"]; bA = T1["ab"]
            for k in range(8):
                P.op("dve", lambda e, k=k: e.tensor_scalar(out=A[:, k, 0:nt], in0=xt[:, k, 0:nt], scalar1=MOD(l, 8 + k, s, True), scalar2=MOD(l, k, s, False),
                                                           op0=ALU.mult, op1=ALU.add), [bk(xbuf, k), b_mod_, b_mod1], [bk(bA, k)])
            xdst = xview(l % 2, lat, tok0 // NT)
            if store_x:
                P.dma("pool", xdst, xt[:, :, 0:nt], ball(xbuf), [db("XL%d" % (l % 2) if lat else "XC%d" % (l % 2))], own=bk(xbuf, 0))
            zt_, bz = T1["z"]; fab, bfab = T1["fab"]

            def cons_f(ci, ps, pb, rows):
                g = ci
                copy_op(evac_eng(), zt_[:, 0:nt], ps, [pb], [bz])
                for sub in range(nsub):
                    P.op("pe", lambda e, sub=sub: e.matmul(ps_ap[:, 6, 0:256], lhsT=zt_[:, sub * 128:(sub + 1) * 128], rhs=cs128, start=True, stop=True), [bz, b_cs], [psb[6]])
                    ov = fab[:, sub, g]; iv = ps_ap[:, 6, 0:256]
                    copy_op(evac_eng(), ov, iv, [psb[6]], [bfab])
            linear_fm([(l, "f")], A, bA, nt, cons_f)
            for sub in range(nsub):
                if lat:
                    dst = FAB.rearrange("(g t) c -> t g c", g=4)[tok0 + sub * 128: tok0 + (sub + 1) * 128]
                    P.dma("pool", dst, fab[:, sub], [bfab], [db("FAB")], own=bfab)
                else:
                    P.dma("pool", FABc[sub * 128:(sub + 1) * 128], fab[:, sub], [bfab], [db("FABc")], own=bfab)
            for (wkey, key) in (("nq", "naq"), ("nk", "nak")):
                tl, tb = T1[key]

                def cons(ci, ps, pb, rows, tl=tl, tb=tb):
                    copy_op(evac_eng(), tl[:, ci, 0:nt], ps, [pb], [tb])
                linear_fm([(l, wkey)], A, bA, nt, cons)
                if key == "naq":
                    dst = NAQT[tok0 // NT].rearrange("p (k t) -> p k t", k=4) if lat else NAQTc[:, :, 0:nt].rearrange("k p t -> p k t")
                    P.dma("pool", dst, tl[:, :, 0:nt], [tb], [db("NAQT" if lat else "NAQTc")], own=tb)
                else:
                    dst = NAKW[tok0 // NT].rearrange("p (k t) -> p k t", k=4) if lat else NAKTc[:, :, 0:nt].rearrange("k p t -> p k t")
                    P.dma("pool", dst, tl[:, :, 0:nt], [tb], [db("NAKW" if lat else "NAKTc")], own=tb)
                    if lat and tidx == 0:
                        P.dma("pool", HK.rearrange("(k p) t -> p k t", p=128)[:, :, 0:256], tl[:, :, 0:256], [tb], [db("HK")], own=tb)
                    if lat and tidx == NTL - 1:
                        P.dma("pool", HK.rearrange("(k p) t -> p k t", p=128)[:, :, 256:512], tl[:, :, nt - 256:nt], [tb], [db("HK")], own=tb)
            wvv, bwv, _, _ = load_piece((l, "nv"))
            nav, bnav = T1["nav"]
            for sub in range(nsub):
                for k in range(8):
                    P.op("pe", lambda e, k=k, sub=sub: e.matmul(ps_ap[:, 7, :], lhsT=A[:, k, sub * 128:(sub + 1) * 128], rhs=wvv[:, k, :], start=(k == 0), stop=(k == 7)),
                         [bk(bA, k), bwv], [psb[7]], inc=(k == 7))
                copy_op(evac_eng(), nav[:, :, sub, 0:64], ps_ap[:, 7, :].rearrange("p (h d) -> p h d", h=8), [psb[7]], [bnav])
            cview = lambda Vd: Vd.rearrange("p (h b c) -> p h b c", h=8, c=65)
            if lat:
                P.dma("pool", NAVW[(tok0 // NT) * 128:(tok0 // NT + 1) * 128, :], nav.rearrange("p h b c -> p (h b c)"), [bnav], [db("NAVW")], own=bnav)
                if tidx == 0:
                    for s_ in range(2):
                        P.dma("pool", HV[s_ * 128:(s_ + 1) * 128].rearrange("p (h d) -> p h d", h=8), nav[:, :, s_, 0:64], [bnav], [db("HV")], own=bnav)
                if tidx == NTL - 1:
                    for s_ in range(2):
                        P.dma("pool", HV[256 + s_ * 128:256 + (s_ + 1) * 128].rearrange("p (h d) -> p h d", h=8), nav[:, :, nsub - 2 + s_, 0:64], [bnav], [db("HV")], own=bnav)
            else:
                P.dma("pool", cview(NAVc), nav[:, :, 0:nsub, :], [bnav], [db("NAVc")], own=bnav)
            cq, bcq = T1["cq"]; ckv, bckv = T1["ckv"]

            def cons_cq(ci, ps, pb, rows):
                copy_op(evac_eng(), cq[:, ci, 0:nt], ps, [pb], [bcq])
            linear_fm([(l, "cq")], A, bA, nt, cons_cq)

            def cons_ckv(ci, ps, pb, rows):
                copy_op(evac_eng(), ckv[:, 0, 0:nt], ps, [pb], [bckv])
            linear_fm([(l, "ckv")], A, bA, nt, cons_ckv)
            sq, bsq = T1["sq"]; st4, bst4 = T1["st4"]
            cqn, bcqn = T1["cqn"]; ckvn, bckvn = T1["ckvn"]
            rms_fm(cq, bcq, 2, nt, gq[:, l, :], cqn, bcqn, sq, bsq, st4, bst4)
            rms_fm(ckv, bckv, 1, nt, gkv[:, l, :], ckvn, bckvn, sq, bsq, st4, bst4)
            rope, brope = T1["rope"]
            if lat:
                P.dma("sp", rope[64:96, :, 0:nt], k_rope[:, :, tok0:tok0 + nt].rearrange("a p t -> p a t"), [], [brope])
            tmp, btmp = T1["tmp"]
            krr, bkrr = T1["krr"]
            for v in range(2 if lat else 1):
                for k in range(8):
                    P.op("pe", lambda e, k=k, v=v: e.matmul(ps_ap[0:96, 6 + v, 0:nt], lhsT=wkr[:, k, v, :], rhs=A[:, k, 0:nt], start=(k == 0), stop=(k == 7)),
                         [b_wkr, bk(bA, k)], [psb[6 + v]], inc=(k == 7))
            if lat:
                P.op("dve", lambda e: e.tensor_tensor(out=tmp[64:96, 0, 0:nt], in0=ps_ap[64:96, 6, 0:nt], in1=rope[64:96, 0, 0:nt], op=ALU.mult), [psb[6], brope], [btmp])
                P.op("dve", lambda e: e.tensor_tensor(out=tmp[64:96, 1, 0:nt], in0=ps_ap[64:96, 7, 0:nt], in1=rope[64:96, 1, 0:nt], op=ALU.mult), [psb[7], brope], [btmp])
                P.op("dve", lambda e: e.tensor_tensor(out=krr[64:96, 0:nt], in0=tmp[64:96, 0, 0:nt], in1=tmp[64:96, 1, 0:nt], op=ALU.add), [btmp], [bkrr])
            else:
                P.op("act", lambda e: e.copy(out=krr[64:96, 0:nt], in_=ps_ap[64:96, 6, 0:nt]), [psb[6]], [bkrr])
            for h in range(8):
                qt_, bq_ = T1["qt"][h % 2]
                for v in range(2 if lat else 1):
                    wsrc = wq if v == 0 else wqs
                    for k in range(2):
                        P.op("pe", lambda e, k=k, v=v, h=h, wsrc=wsrc: e.matmul(ps_ap[0:96, 6 + v, 0:nt], lhsT=wsrc[:, k, h, :], rhs=cqn[:, k, 0:nt], start=(k == 0), stop=(k == 1)),
                             [b_wq, b_wqs, bcqn], [psb[6 + v]], inc=(k == 1))
                if lat:
                    P.op("act", lambda e, qt_=qt_: e.copy(out=qt_[0:64, 0:nt], in_=ps_ap[0:64, 6, 0:nt]), [psb[6]], [bq_])
                    P.op("dve", lambda e: e.tensor_tensor(out=tmp[64:96, 0, 0:nt], in0=ps_ap[64:96, 6, 0:nt], in1=rope[64:96, 0, 0:nt], op=ALU.mult), [psb[6], brope], [btmp])
                    P.op("dve", lambda e: e.tensor_tensor(out=tmp[64:96, 1, 0:nt], in0=ps_ap[64:96, 7, 0:nt], in1=rope[64:96, 1, 0:nt], op=ALU.mult), [psb[7], brope], [btmp])
                    P.op("dve", lambda e, qt_=qt_: e.tensor_tensor(out=qt_[64:96, 0:nt], in0=tmp[64:96, 0, 0:nt], in1=tmp[64:96, 1, 0:nt], op=ALU.add), [btmp], [bq_])
                else:
                    P.op("act", lambda e, qt_=qt_: e.copy(out=qt_[0:96, 0:nt], in_=ps_ap[0:96, 6, 0:nt]), [psb[6]], [bq_])
                qd = MLAQT[h, :, tok0:tok0 + nt] if lat else MLAQTc[h, :, 0:nt]
                P.dma("pool", qd, qt_[0:96, 0:nt], [bq_], [db("MLAQT" if lat else "MLAQTc")], own=bq_)
                kt_, bk_ = T1["kt"][h % 2]
                bank = 4 + (h % 2)
                P.op("pe", lambda e, h=h, bank=bank: e.matmul(ps_ap[0:64, bank, 0:nt], lhsT=wk[:, h, :], rhs=ckvn[:, 0, 0:nt], start=True, stop=True), [b_wk, bckvn], [psb[bank]])
                P.op("dve", lambda e, kt_=kt_, bank=bank: e.tensor_copy(out=kt_[0:64, 0:nt], in_=ps_ap[0:64, bank, 0:nt]), [psb[bank]], [bk_])
                P.op("act", lambda e, kt_=kt_: e.copy(out=kt_[64:96, 0:nt], in_=krr[64:96, 0:nt]), [bkrr], [bk_])
                kd = MLAKT[h * 96:(h + 1) * 96, tok0:tok0 + nt] if lat else MLAKTc[h, :, 0:nt]
                P.dma("pool", kd, kt_[0:96, 0:nt], [bk_], [db("MLAKT" if lat else "MLAKTc")], own=bk_)
            mv, bmv = T1["mlav"]
            for sub in range(nsub):
                P.op("pe", lambda e, sub=sub: e.matmul(ps_ap[:, 7, :], lhsT=ckvn[:, 0, sub * 128:(sub + 1) * 128], rhs=wuv, start=True, stop=True), [bckvn, b_wuv], [psb[7]])
                copy_op(evac_eng(), mv[:, :, sub, 0:64], ps_ap[:, 7, :].rearrange("p (h d) -> p h d", h=8), [psb[7]], [bmv])
            if lat:
                P.dma("pool", MLAV[(tok0 // NT) * 128:(tok0 // NT + 1) * 128, :], mv.rearrange("p h b c -> p (h b c)"), [bmv], [db("MLAV")], own=bmv)
            else:
                P.dma("pool", cview(MLAVc), mv[:, :, 0:nsub, :], [bmv], [db("MLAVc")], own=bmv)

        def alloc_p1():
            T1 = {}
            T1["a"], T1["ab"] = ptk("a", [8, NT], BF16)
            T1["z"] = pt("z", [NT], BF16)
            T1["fab"] = pt("fab", [4, 4, 256], BF16)
            T1["naq"] = pt("naq", [4, NT], BF16); T1["nak"] = pt("nak", [4, NT], BF16)
            T1["nav"] = pt("nav", [8, 4, 65], BF16)
            P.op("pool", lambda e: e.memset(T1["nav"][0], 1.0), [], [T1["nav"][1]])
            T1["cq"] = pt("cq", [2, NT], F32); T1["ckv"] = pt("ckv", [1, NT], F32)
            T1["cqn"] = pt("cqn", [2, NT], BF16); T1["ckvn"] = pt("ckvn", [1, NT], BF16)
            T1["rope"] = pt("rope", [2, NT], F32); T1["tmp"] = pt("tmp", [2, NT], F32)
            T1["krr"] = pt("krr", [NT], BF16)
            T1["qt"] = [pt("qt%d" % i, [NT], BF16) for i in range(2)]
            T1["kt"] = [pt("kt%d" % i, [NT], BF16) for i in range(2)]
            T1["mlav"] = pt("mlav", [8, 4, 65], BF16)
            P.op("pool", lambda e: e.memset(T1["mlav"][0], 1.0), [], [T1["mlav"][1]])
            T1["sq"] = pt("sq1", [2, NT], F32); T1["st4"] = pt("st4_1", [4, NT], F32)
            return T1

        def input_ln(src_rows, nblk, xt, xbuf, W):
            for blk in range(nblk):
                xin, bxin = W["xin"][blk % 2]
                P.dma("sp", xin, src_rows[blk * 128:(blk + 1) * 128, :], [], [bxin])
                s4, bs4 = W["s4"]
                P.op("dve", lambda e, xin=xin: e.reduce_sum(out=s4[:, 0:1], in_=xin, axis=AX.X), [bxin], [bs4])
                P.op("dve", lambda e: e.tensor_scalar(out=s4[:, 1:2], in0=s4[:, 0:1], scalar1=-1.0 / D, scalar2=None, op0=ALU.mult), [bs4], [bs4])
                P.op("act", lambda e, xin=xin: e.activation(out=xin, in_=xin, func=AF.Identity, bias=s4[:, 1:2], scale=1.0), [bxin, bs4], [bxin])
                xsq, bxsq = W["xsq"]
                P.op("act", lambda e, xin=xin: e.activation(out=xsq, in_=xin, func=AF.Square), [bxin], [bxsq])
                P.op("dve", lambda e: e.reduce_sum(out=s4[:, 2:3], in_=xsq, axis=AX.X), [bxsq], [bs4])
                P.op("act", lambda e: e.activation(out=s4[:, 3:4], in_=s4[:, 2:3], func=AF.Ln, bias=epsb[:, 0:1], scale=1.0 / D), [bs4, b_eps], [bs4])
                P.op("act", lambda e: e.activation(out=s4[:, 3:4], in_=s4[:, 3:4], func=AF.Exp, scale=-0.5), [bs4], [bs4])
                P.op("act", lambda e, xin=xin: e.activation(out=xin, in_=xin, func=AF.Identity, scale=s4[:, 3:4]), [bxin, bs4], [bxin])
                for k in range(8):
                    bank = k % 4
                    P.op("pe", lambda e, xin=xin, k=k, bank=bank: e.transpose(ps_ap[:, bank, 0:128], xin[:, k * 128:(k + 1) * 128], ident), [bxin, b_ident], [psb[bank]])
                    P.op("act" if k % 2 else "dve",
                         (lambda e, k=k, bank=bank, blk=blk: e.activation(out=xt[:, k, blk * 128:(blk + 1) * 128], in_=ps_ap[:, bank, 0:128], func=AF.Identity,
                                                                       bias=lnin[:, 1, k:k + 1], scale=lnin[:, 0, k:k + 1])) if k % 2 else
                         (lambda e, k=k, bank=bank, blk=blk: e.tensor_scalar(out=xt[:, k, blk * 128:(blk + 1) * 128], in0=ps_ap[:, bank, 0:128], scalar1=lnin[:, 0, k:k + 1],
                                                                          scalar2=lnin[:, 1, k:k + 1], op0=ALU.mult, op1=ALU.add)),
                         [psb[bank], b_lnin], [bk(xbuf, k)])

        def exchange():
            P.barrier()
            cl = [(MLAKT[h * 96:(h + 1) * 96, :], MLAKTA[h], "MLAKT", "MLAKTA") for h in range(8)]
            cl += [(MLAV[t * 128:(t + 1) * 128, :], MLAVA[t], "MLAV", "MLAVA") for t in range(NTL)]
            cl += [(FAB[j * CH:(j + 1) * CH, :], FABA[j], "FAB", "FABA") for j in range(4 * HF)]
            cl += [(HK, HKA, "HK", "HKA"), (HV, HVA, "HV", "HVA")]
            for (src, dst, sn, dn) in cl:
                P.collective("AllGather", src.opt(), dst.opt(), GROUPS, [db(sn)], [db(dn)])
            P.barrier()
            ar.reset(PERSIST)
            hk = [pt("hk%d" % i, [4, 512], BF16) for i in range(2)]
            ho = [pt("ho%d" % i, [2, 256], BF16) for i in range(2)]
            hkv = HKA.rearrange("(r k p) t -> k p r t", r=4, k=4)
            for c in range(4):
                (hi, hb), (oi, ob) = hk[c % 2], ho[c % 2]
                P.dma("sp", hi, hkv[c], [db("HKA")], [hb])
                for half, (t0, so) in enumerate(((256, 0), (0, 4))):
                    P.op("dve", lambda e, hi=hi, oi=oi, half=half, t0=t0, so=so: e.tensor_scalar(out=oi[:, half, :], in0=hi[:, 0, t0:t0 + 256], scalar1=sel[:, so:so + 1], scalar2=None, op0=ALU.mult), [hb, b_sel], [ob])
                    for r in range(1, 4):
                        P.op("dve", lambda e, hi=hi, oi=oi, half=half, t0=t0, so=so, r=r: e.scalar_tensor_tensor(out=oi[:, half, :], in0=hi[:, r, t0:t0 + 256], scalar=sel[:, so + r:so + r + 1], in1=oi[:, half, :], op0=ALU.mult, op1=ALU.add), [hb, b_sel, ob], [ob])
                P.dma("pool", NAKH[c], oi.rearrange("p a b -> p (a b)"), [ob], [db("NAKH")], own=ob)
            hv = [pt("hv%d" % i, [4, 512], BF16) for i in range(2)]
            hvo = [pt("hvo%d" % i, [8, 65], BF16) for i in range(2)]
            for (oi, ob) in hvo:
                P.op("pool", lambda e, oi=oi: e.memset(oi, 1.0), [], [ob])
            hvv = HVA.rearrange("(r s p) c -> s p r c", r=4, s=4)
            for sblk in range(4):
                (hi, hb), (oi, ob) = hv[sblk % 2], hvo[sblk % 2]
                srcb = sblk + 2 if sblk < 2 else sblk - 2
                so = 0 if sblk < 2 else 4
                P.dma("sp", hi, hvv[srcb], [db("HVA")], [hb])
                ov = oi[:, :, 0:64]
                hv3 = lambda r, hi=hi: hi[:, r, :].rearrange("p (h d) -> p h d", h=8)
                P.op("dve", lambda e, ov=ov, hv3=hv3, so=so: e.tensor_scalar(out=ov, in0=hv3(0), scalar1=sel[:, so:so + 1], scalar2=None, op0=ALU.mult), [hb, b_sel], [ob])
                for r in range(1, 4):
                    P.op("dve", lambda e, ov=ov, hv3=hv3, so=so, r=r: e.scalar_tensor_tensor(out=ov, in0=hv3(r), scalar=sel[:, so + r:so + r + 1], in1=ov, op0=ALU.mult, op1=ALU.add), [hb, b_sel, ob], [ob])
                P.dma("pool", NAVH[sblk * 128:(sblk + 1) * 128, :], oi.rearrange("p h c -> p (h c)"), [ob], [db("NAVH")], own=ob)
            P.barrier()
            ar.reset(PERSIST)

        def alloc_att():
            att_state.clear()
            att_state.update(si=0, pi=0, obank=7)
            att_state["ptl"] = [pt("p%d" % i, [512], BF16) for i in range(6)]
            att_state["rs"] = pt("rs", [512], F32)
            att_state["bc"] = pt("bcs", [512], F32)

        def phase2_mla(l):
            ar.reset(PERSIST)
            alloc_att()
            NK = CTX + N
            NKB = NK // 128
            KT = [pt("KT%d" % i, [NK], BF16) for i in range(2)]
            VA = [pt("VA%d" % i, [NKB, 65], BF16) for i in range(2)]
            qts = [pt("q%d" % i, [512], BF16) for i in range(3)]
            ys = [pt("y%d" % i, [512], BF16) for i in range(2)]
            qi = 0
            prev_fin = [None]
            for h in range(8):
                (kt, kb_), (va, vb) = KT[h % 2], VA[h % 2]
                P.dma("sp", kt[0:96, 0:CTX], MLAKTc[h], [db("MLAKTc")], [kb_])
                P.dma("sp", va[:, 0:2, :].rearrange("p a b -> p (a b)"), MLAVc[:, h * 130:(h + 1) * 130], [db("MLAVc")], [vb])
                for r in range(4):
                    P.dma("sp", kt[0:96, CTX + r * T:CTX + (r + 1) * T], MLAKTA[h][r * 96:(r + 1) * 96, :], [db("MLAKTA")], [kb_])
                    for t in range(NTL):
                        b0 = 2 + r * NBL + 4 * t
                        P.dma("sp", va[:, b0:b0 + 4, :].rearrange("p a b -> p (a b)"), MLAVA[t][r * 128:(r + 1) * 128, h * 260:(h + 1) * 260], [db("MLAVA")], [vb])
                kbs_all = [(kt[0:96, j * 128:(j + 1) * 128], kb_, va[:, j, :], vb, None, None) for j in range(NKB)]
                c, po = h // 2, (h % 2) * 64
                for qb in range(NTL):
                    qt_, bq_ = qts[qi % 3]; y_, by_ = ys[qi % 2]; qi += 1
                    P.dma("sp", qt_[0:96, :], MLAQT[h, :, qb * 512:(qb + 1) * 512], [db("MLAQT")], [bq_])

                    def after(y_=y_, by_=by_, qb=qb, po=po, c=c):
                        P.dma("pool", YMLA[qb, po:po + 64, c * NT:(c + 1) * NT], y_[0:64, :], [by_], [db("YMLA")], own=by_)
                        issue_mid_casts(6)
                        issue_late_casts(6)
                    prev_fin[0] = attend(qt_[0:96, :], bq_, kbs_all, MLA_SCALE, 512, y_[0:64, :], by_, pre=prev_fin[0], after=after)
                if l == 0:
                    qt_, bq_ = qts[qi % 3]; y_, by_ = ys[qi % 2]; qi += 1
                    P.dma("sp", qt_[0:96, 0:CTX], MLAQTc[h], [db("MLAQTc")], [bq_])

                    def after(y_=y_, by_=by_, po=po, c=c):
                        P.dma("pool", YMLAc[c, po:po + 64, :], y_[0:64, 0:CTX], [by_], [db("YMLAc")], own=by_)
                    prev_fin[0] = attend(qt_[0:96, 0:CTX], bq_, kbs_all[0:2], MLA_SCALE, CTX, y_[0:64, 0:CTX], by_, pre=prev_fin[0], after=after)
            if prev_fin[0] is not None:
                prev_fin[0][0](); prev_fin[0][1]()
            issue_mid_casts(10 ** 6)
            issue_late_casts(10 ** 6)
            P.barrier()

        def phase2_na(l):
            ar.reset(PERSIST)
            alloc_att()
            KW = [pt("KW%d" % i, [TW], BF16) for i in range(2)]
            QN = [pt("QN%d" % i, [T], BF16) for i in range(2)]
            KC = [pt("KC%d" % i, [CTX], BF16) for i in range(2)]
            QC = [pt("QC%d" % i, [CTX], BF16) for i in range(2)]
            VN = [pt("VN%d" % i, [NWB, 65], BF16) for i in range(2)]
            VC = [pt("VC%d" % i, [2, 65], BF16) for i in range(2)]
            EBF = pt("EBF", [8, 8, 64], BF16)
            EBY = [pt("EBY%d" % i, [8, 512], BF16) for i in range(6)]
            ys = [pt("y%d" % i, [512], BF16) for i in range(2)]
            yi = 0
            prev_fin = [None]
            for h in range(8):
                c, po = h // 2, (h % 2) * 64
                (kw, bkw), (qn, bqn), (kc, bkc), (qc, bqc) = KW[c % 2], QN[c % 2], KC[c % 2], QC[c % 2]
                (vn, bvn), (vc, bvc) = VN[h % 2], VC[h % 2]
                if h % 2 == 0:
                    for t in range(NTL):
                        P.dma("sp", kw[:, 256 + t * NT:256 + (t + 1) * NT], NAKW[t][:, c * NT:(c + 1) * NT], [db("NAKW")], [bkw])
                        P.dma("sp", qn[:, t * NT:(t + 1) * NT], NAQT[t][:, c * NT:(c + 1) * NT], [db("NAQT")], [bqn])
                    P.dma("sp", kw[:, 0:256], NAKH[c][:, 0:256], [db("NAKH")], [bkw])
                    P.dma("sp", kw[:, 256 + T:TW], NAKH[c][:, 256:512], [db("NAKH")], [bkw])
                    P.dma("sp", kc, NAKTc[c], [db("NAKTc")], [bkc])
                    if l == 0:
                        P.dma("sp", qc, NAQTc[c], [db("NAQTc")], [bqc])
                for t in range(NTL):
                    P.dma("sp", vn[:, 2 + 4 * t:6 + 4 * t, :].rearrange("p a b -> p (a b)"), NAVW[t * 128:(t + 1) * 128, h * 260:(h + 1) * 260], [db("NAVW")], [bvn])
                for sblk in range(4):
                    wb = sblk if sblk < 2 else 2 + NBL + (sblk - 2)
                    P.dma("sp", vn[:, wb, :], NAVH[sblk * 128:(sblk + 1) * 128, h * 65:(h + 1) * 65], [db("NAVH")], [bvn])
                P.dma("sp", vc.rearrange("p a b -> p (a b)"), NAVc[:, h * 130:(h + 1) * 130], [db("NAVc")], [bvc])
                ebf, bebf = EBF
                for kb in range(8):
                    for krl in range(2):
                        e0 = 15 - 2 * kb - krl
                        P.dma("sp", ebf[krl * 64:(krl + 1) * 64, kb, :, :].rearrange("p a b -> p (a b)"), EBT[l][h][:, e0 * 64:(e0 + 8) * 64], [db("EBT%d" % l)], [bebf])
                for ty in range(3):
                    eby, beby = EBY[3 * (h % 2) + ty]
                    P.op("dve", lambda e, eby=eby, ty=ty: e.tensor_tensor(out=eby.rearrange("p a (b c) -> p a b c", b=8), in0=ebf,
                                                                        in1=rv[:, ty].unsqueeze(3).broadcast_to([128, 8, 8, 64]), op=ALU.mult), [bebf, b_rv], [beby])
                ctxk = [(kc[po:po + 64, j * 128:(j + 1) * 128], bkc, vc[:, j, :], bvc, None, None) for j in range(2)]
                for qb in range(NQB):
                    ty = 0 if qb == 0 else (2 if qb == NQB - 1 else 1)
                    eby, beby = EBY[3 * (h % 2) + ty]
                    kbs = [(kw[po:po + 64, (4 * qb + j) * 128:(4 * qb + j + 1) * 128], bkw, vn[:, 4 * qb + j, :], bvn, eby[:, j, :], beby) for j in range(8)] + ctxk
                    y_, by_ = ys[yi % 2]; yi += 1
                    def after(y_=y_, by_=by_, qb=qb, po=po, c=c):
                        P.dma("pool", YNA[qb, po:po + 64, c * NT:(c + 1) * NT], y_[0:64, :], [by_], [db("YNA")], own=by_)
                    prev_fin[0] = attend(qn[po:po + 64, qb * 512:(qb + 1) * 512], bqn, kbs, NA_SCALE, 512, y_[0:64, :], by_, pre=prev_fin[0], after=after)
                if l == 0:
                    y_, by_ = ys[yi % 2]; yi += 1
                    def after(y_=y_, by_=by_, po=po, c=c):
                        P.dma("pool", YNAc[c, po:po + 64, :], y_[0:64, 0:CTX], [by_], [db("YNAc")], own=by_)
                    prev_fin[0] = attend(qc[po:po + 64, :], bqc, ctxk, NA_SCALE, CTX, y_[0:64, 0:CTX], by_, pre=prev_fin[0], after=after)
            if prev_fin[0] is not None:
                prev_fin[0][0](); prev_fin[0][1]()
            P.barrier()

        def phase2_fourier(l):
            ar.reset(PERSIST)
            AB = pt("AB", [128, 256], BF16)
            GRE = pt("GRE", [N1, 128], BF16); GIM = pt("GIM", [N1, 128], BF16)
            YFS = [pt("YFS%d" % i, [32, N1], BF16) for i in range(2)]
            for g in range(4):
                gre, bgre = GRE; gim, bgim = GIM
                ab, bab = AB
                npc = CH // 128
                for r in range(4):
                    for hf in range(HF):
                        p0 = r * NBL + hf * npc
                        P.dma("sp", ab[p0:p0 + npc].rearrange("p a b -> p (a b)"), FABA[g * HF + hf][r * CH:(r + 1) * CH, :].rearrange("(a n) c -> a (n c)", n=128), [db("FABA")], [bab])
                for j4 in range(32):
                    for half in range(2):
                        bank = half
                        for jj in range(4):
                            j = j4 * 4 + jj
                            Al = ab[0:N1, :, j]; Bl = ab[0:N1, :, 128 + j]
                            o = ps_ap[:, bank, jj * N1:(jj + 1) * N1]
                            if half == 0:
                                P.op("pe", lambda e, o=o, Al=Al: e.matmul(o, lhsT=Al, rhs=c1[0:N1], start=True, stop=False), [bab, b_c1], [psb[bank]], inc=False)
                                P.op("pe", lambda e, o=o, Bl=Bl: e.matmul(o, lhsT=Bl, rhs=ns1[0:N1], start=False, stop=True), [bab, b_ns1], [psb[bank]], inc=(jj == 3))
                            else:
                                P.op("pe", lambda e, o=o, Bl=Bl: e.matmul(o, lhsT=Bl, rhs=c1[0:N1], start=True, stop=False), [bab, b_c1], [psb[bank]], inc=False)
                                P.op("pe", lambda e, o=o, Al=Al: e.matmul(o, lhsT=Al, rhs=s1[0:N1], start=False, stop=True), [bab, b_s1], [psb[bank]], inc=(jj == 3))
                        l0 = j4 * 4
                        G, bG = (gre, bgre) if half == 0 else (gim, bgim)
                        copy_op("act" if half else "dve", G[:, :, l0:l0 + 4], ps_ap[:, bank, 0:4 * N1].rearrange("p (j k) -> p k j", j=4), [psb[bank]], [bG])
                yfs, byfs = YFS[g % 2]
                for k1b in range(N1 // 16):
                    bank = 2 + (k1b % 2)
                    for kk in range(16):
                        k1 = k1b * 16 + kk
                        o = ps_ap[:, bank, kk * 32:(kk + 1) * 32]
                        P.op("pe", lambda e, o=o, k1=k1: e.matmul(o, lhsT=gre[:, k1, :], rhs=mre[:, k1, :], start=True, stop=False), [bgre, b_mre], [psb[bank]], inc=False)
                        P.op("pe", lambda e, o=o, k1=k1: e.matmul(o, lhsT=gim[:, k1, :], rhs=mimn[:, k1, :], start=False, stop=True), [bgim, b_mim], [psb[bank]], inc=(kk == 15))
                    copy_op("act" if k1b % 2 else "dve", yfs[:, :, k1b * 16:(k1b + 1) * 16], ps_ap[:, bank, :].rearrange("p (a b) -> p b a", a=16), [psb[bank]], [byfs])
                P.dma("pool", YF.rearrange("t p (c n) -> p t c n", c=4)[:, :, g, :], yfs.rearrange("p a b -> p (a b)").rearrange("p (t n) -> p t n", n=NT), [byfs], [db("YF")], own=byfs)
            if l == 0:
                fc, bfc = pt("fc", [2, 4, 256], BF16)
                P.dma("sp", fc, FABc.rearrange("(s p) g c -> p s g c", p=128), [db("FABc")], [bfc])
                yc, byc = pt("yc", [4, 256], BF16)
                for g in range(4):
                    steps = [(fc[:, nb, g, 0:128], cc256[:, nb, :]) for nb in range(2)] + [(fc[:, nb, g, 128:256], sc256[:, nb, :]) for nb in range(2)]
                    for i, (lh, rh) in enumerate(steps):
                        P.op("pe", lambda e, lh=lh, rh=rh, i=i: e.matmul(ps_ap[:, 4, 0:256], lhsT=lh, rhs=rh, start=(i == 0), stop=(i == 3)), [bfc, b_cc, b_sc], [psb[4]], inc=(i == 3))
                    copy_op("dve", yc[:, g, :], ps_ap[:, 4, 0:256], [psb[4]], [byc])
                P.dma("pool", YFc.rearrange("g p t -> p g t"), yc, [byc], [db("YFc")], own=byc)
            P.barrier()

        def alloc_p3():
            T3 = {}
            T3["x"] = [ptk("x3_%d" % i, [8, NT], F32) for i in range(1)]
            T3["mix"] = ptk("mix", [8, NT], F32)
            T3["r"] = ptk("r", [8, NT], F32)
            T3["a"] = ptk("a3", [8, NT], BF16)
            T3["mixb"] = ptk("mixb", [8, NT], BF16)
            T3["hid"] = ptk("hid", [22, NT], BF16)
            T3["y"] = [pt("y3_%d" % i, [4, NT], BF16) for i in range(3)]
            T3["st4"] = pt("st4_3", [4, NT], F32)
            T3["sg"] = [pt("sg%d" % i, [NT], BF16) for i in range(2)]
            T3["tmpf"] = [pt("tmpf%d" % i, [NT], F32) for i in range(2)]
            T3["ot"] = [pt("ot%d" % i, [1024], F32) for i in range(2)]
            return T3

        def phase3(l, nt, s, tok0, T3):
            lat = (s == 0)
            xt, xbuf = T3["x"][0]
            xsrc = xview(l % 2, lat, tok0 // NT)
            P.dma("sp", xt[:, :, 0:nt], xsrc, [db("XL%d" % (l % 2) if lat else "XC%d" % (l % 2))], xbuf)
            A, bA = T3["a"]
            for k in range(8):
                P.op("dve", lambda e, k=k: e.tensor_scalar(out=A[:, k, 0:nt], in0=xt[:, k, 0:nt], scalar1=MOD(l, 8 + k, s, True), scalar2=MOD(l, k, s, False),
                                                           op0=ALU.mult, op1=ALU.add), [xbuf[k], b_mod_, b_mod1], [bA[k]])
            ysrc = ((YF, "YF"), (YNA, "YNA"), (YMLA, "YMLA")) if lat else ((YFc, "YFc"), (YNAc, "YNAc"), (YMLAc, "YMLAc"))
            for gi in range(3):
                yt, by = T3["y"][gi]
                yd = ysrc[gi][0][tok0 // NT].rearrange("p (k t) -> p k t", k=4) if lat else ysrc[gi][0][:, :, 0:nt].rearrange("k p t -> p k t")
                P.dma("sp", yt[:, :, 0:nt], yd, [db(ysrc[gi][1])], [by])
            mix, bmix = T3["mix"]
            for gi in range(3):
                yt, by = T3["y"][gi]
                for hb in range(2):
                    pg = load_piece((l, "g", gi, hb)); pb_ = load_piece((l, "br", gi, hb))
                    for j in range(4):
                        mc = hb * 4 + j
                        ps, pb = mm_chunk(pg, j * 128, 128, A, bA, nt)
                        sg, bsg = T3["sg"][mc % 2]
                        P.op("act", lambda e, sg=sg, ps=ps: e.activation(out=sg[:, 0:nt], in_=ps, func=AF.Sigmoid), [pb], [bsg])
                        ps2, pb2 = mm_chunk(pb_, j * 128, 128, yt, by, nt)
                        if gi == 0:
                            P.op("dve", lambda e, sg=sg, ps2=ps2, mc=mc: e.tensor_tensor(out=mix[:, mc, 0:nt], in0=ps2, in1=sg[:, 0:nt], op=ALU.mult), [pb2, bsg], [bmix[mc]])
                        else:
                            tf, btf = T3["tmpf"][mc % 2]
                            P.op("dve", lambda e, sg=sg, ps2=ps2, tf=tf: e.tensor_tensor(out=tf[:, 0:nt], in0=ps2, in1=sg[:, 0:nt], op=ALU.mult), [pb2, bsg], [btf])
                            P.op("dve", lambda e, tf=tf, mc=mc: e.tensor_tensor(out=mix[:, mc, 0:nt], in0=mix[:, mc, 0:nt], in1=tf[:, 0:nt], op=ALU.add), [btf, bmix[mc]], [bmix[mc]])
            mixb, bmixb = T3["mixb"]
            for k in range(8):
                copy_op("act" if k % 2 else "dve", mixb[:, k, 0:nt], mix[:, k, 0:nt], [bmix[k]], [bmixb[k]])
            r, rb = T3["r"]

            def cons_o(ci, ps, pb, rows):
                P.op("act", lambda e: e.activation(out=r[:, ci, 0:nt], in_=xt[:, ci, 0:nt], func=AF.Identity, scale=ALPHA), [xbuf[ci]], [rb[ci]])
                P.op("dve", lambda e: e.scalar_tensor_tensor(out=r[:, ci, 0:nt], in0=ps, scalar=MOD(l, 16 + ci, s, True), in1=r[:, ci, 0:nt], op0=ALU.mult, op1=ALU.add), [pb, rb[ci], b_mod1], [rb[ci]])
            linear_fm([(l, "o", 0), (l, "o", 1)], mixb, bmixb, nt, cons_o)
            st4, bst4 = T3["st4"]
            ln_fm(r, rb, nt, lnp[:, l, 0, :], lnp[:, l, 1, :], mix, bmix, st4, bst4)
            for k in range(8):
                P.op("dve", lambda e, k=k: e.tensor_scalar(out=A[:, k, 0:nt], in0=r[:, k, 0:nt], scalar1=MOD(l, 32 + k, s, True), scalar2=MOD(l, 24 + k, s, False),
                                                           op0=ALU.mult, op1=ALU.add), [rb[k], b_mod_, b_mod1], [bA[k]])
            hid, bhid = T3["hid"]
            for j6 in range(6):
                pfg = load_piece((l, "fg", j6)); pfu = load_piece((l, "fu", j6))
                for j in range(pfg[3] // 128):
                    mc = j6 * 4 + j
                    ps, pb = mm_chunk(pfg, j * 128, 128, A, bA, nt)
                    sg, bsg = T3["sg"][mc % 2]
                    P.op("act", lambda e, sg=sg, ps=ps: e.activation(out=sg[:, 0:nt], in_=ps, func=AF.Silu), [pb], [bsg])
                    ps2, pb2 = mm_chunk(pfu, j * 128, 128, A, bA, nt)
                    P.op("dve", lambda e, sg=sg, ps2=ps2, mc=mc: e.tensor_tensor(out=hid[:, mc, 0:nt], in0=ps2, in1=sg[:, 0:nt], op=ALU.mult), [pb2, bsg], [bhid[mc]])

            def cons_d(ci, ps, pb, rows):
                P.op("act", lambda e: e.activation(out=r[:, ci, 0:nt], in_=r[:, ci, 0:nt], func=AF.Identity, scale=ALPHA), [rb[ci]], [rb[ci]])
                P.op("dve", lambda e: e.scalar_tensor_tensor(out=r[:, ci, 0:nt], in0=ps, scalar=MOD(l, 40 + ci, s, True), in1=r[:, ci, 0:nt], op0=ALU.mult, op1=ALU.add), [pb, rb[ci], b_mod1], [rb[ci]])
            linear_fm([(l, "fd", j) for j in range(4)], hid, bhid, nt, cons_d)
            ln_fm(r, rb, nt, lnp[:, l, 2, :], lnp[:, l, 3, :], mix, bmix, st4, bst4)
            return r, rb

        def write_out(r, rb, tok0, T3):
            for sub in range(4):
                ot, bot = T3["ot"][sub % 2]
                for k in range(8):
                    bank = 6 + (k // 4)
                    P.op("pe", lambda e, k=k, sub=sub, bank=bank: e.transpose(ps_ap[:, bank, (k % 4) * 128:(k % 4 + 1) * 128], r[:, k, sub * 128:(sub + 1) * 128], ident), [rb[k], b_ident], [psb[bank]])
                    if k % 4 == 3:
                        copy_op("act" if k == 3 else "dve", ot[:, (k // 4) * 512:(k // 4 + 1) * 512], ps_ap[:, bank, :], [psb[bank]], [bot])
                P.dma("pool", out_d[tok0 + sub * 128:tok0 + (sub + 1) * 128, :], ot, [bot], [db("out")], own=bot)

        def run_phase1_pass(l, from_input):
            ar.reset(PERSIST)
            load_layer_small(l)
            P.barrier()
            ar.reset(PERSIST)
            wslots[:] = [pt("wslot%d" % i, [6144], BF16) for i in range(3)]
            T1 = alloc_p1()
            xts = [ptk("xt%d" % i, [8, NT], F32) for i in range(2)]
            if from_input:
                W = {"xin": [pt("xin%d" % i, [1024], F32) for i in range(2)], "s4": pt("s4", [4], F32), "xsq": pt("xsq", [1024], F32)}
            for ti in range(NTL + 1):
                xt, xbuf = xts[ti % 2]
                lat = ti < NTL
                nt = NT if lat else CTX
                if from_input:
                    input_ln(x_in[ti * NT:(ti + 1) * NT] if lat else ctx_in, nt // 128, xt, xbuf, W)
                else:
                    P.dma("sp", xt[:, :, 0:nt], xview(l % 2, lat, ti), [db("XL%d" % (l % 2) if lat else "XC%d" % (l % 2))], ball(xbuf))
                phase1(l, xt, xbuf, nt, 0 if lat else 1, ti * NT if lat else 0, ti if lat else -1, T1, store_x=from_input)

        class _Stop(Exception):
            pass

        def chk(stage):
            if cfg.stop <= stage:
                raise _Stop()

        try:
            chk(0)
            if 'ONLYP3' in cfg.debug:
                lsel = 1 if 'L1' in cfg.debug else 0
                wslots[:] = [pt("wslot%d" % i, [6144], BF16) for i in range(3)]
                T3 = alloc_p3()
                for ti in range(NTL):
                    r, rb = phase3(lsel, NT, 0, ti * NT, T3)
                    if 'NOOUT' not in cfg.debug:
                        write_out(r, rb, ti * NT, T3)
                raise _Stop()
            run_phase1_pass(0, True)
            chk(1)
            for l in range(L):
                exchange()
                chk(2 + 10 * l)
                phase2_mla(l)
                chk(3 + 10 * l)
                phase2_na(l)
                chk(4 + 10 * l)
                phase2_fourier(l)
                chk(5 + 10 * l)
                ar.reset(PERSIST)
                wslots[:] = [pt("wslot%d" % i, [6144], BF16) for i in range(3)]
                T3 = alloc_p3()
                last = (l + 1 == L)
                for ti in range(NTL + (0 if last else 1)):
                    lat = ti < NTL
                    nt = NT if lat else CTX
                    r, rb = phase3(l, nt, 0 if lat else 1, ti * NT if lat else 0, T3)
                    if last:
                        if 'NOOUT' not in cfg.debug:
                            write_out(r, rb, ti * NT, T3)
                    else:
                        P.dma("pool", xview((l + 1) % 2, lat, ti), r[:, :, 0:nt], rb, [db("XL%d" % ((l + 1) % 2) if lat else "XC%d" % ((l + 1) % 2))], own=rb[0])
                chk(6 + 10 * l)
                if not last:
                    P.barrier()
                    run_phase1_pass(l + 1, False)
                chk(7 + 10 * l)
        except _Stop:
            pass
        P.barrier()
        P.emit()
    return nc, dbg


def _consts(cfg, q):
    bf = ml_dtypes.bfloat16
    R, T, N, N1 = cfg.R, cfg.T, cfg.N, cfg.N1
    out = {}
    out["k_ident"] = np.eye(128, dtype=np.float32)
    cl = np.outer(np.arange(128), np.arange(128)).astype(np.float64) * (2 * np.pi / 128)
    out["k_cs128"] = np.concatenate([np.cos(cl), np.sin(cl)], 1).astype(bf)
    a1 = np.outer(np.arange(N1), np.arange(N1)).astype(np.float64) * (2 * np.pi / N1)
    out["k_c1"] = np.cos(a1).astype(bf); out["k_s1"] = np.sin(a1).astype(bf); out["k_ns1"] = (-np.sin(a1)).astype(bf)
    sc = 1.0 / np.sqrt(N * 128.0)
    k1 = np.arange(N1)[None, :, None]; k2 = np.arange(32)[None, None, :]; n2 = np.arange(128)[:, None, None]
    k = T * q + k1 + N1 * k2
    ang = (n2 * k % N).astype(np.float64) * (2 * np.pi / N)
    out["k_mre"] = (np.cos(ang) * sc).reshape(128, N1 * 32).astype(bf)
    out["k_mimn"] = (-np.sin(ang) * sc).reshape(128, N1 * 32).astype(bf)
    ac = np.outer(np.arange(256), np.arange(256)).astype(np.float64) * (2 * np.pi / 256)
    scc = 1.0 / np.sqrt(256 * 128.0)
    out["k_cc256"] = (np.cos(ac) * scc).astype(bf); out["k_scn256"] = (-np.sin(ac) * scc).astype(bf)
    t = np.arange(T) + q * T
    rows = (t // GW).astype(np.float32); cols = (t % GW).astype(np.float32)
    inv = (np.float32(10000.0) ** (-np.arange(8, dtype=np.float32) / np.float32(8))).astype(np.float32)
    angr = np.concatenate([rows[:, None] * inv, cols[:, None] * inv], -1).astype(np.float32)
    co = np.cos(angr).T.astype(np.float32); si = np.sin(angr).T.astype(np.float32)
    out["k_rope"] = np.stack([np.concatenate([co, co], 0), np.concatenate([-si, si], 0)], 0).astype(np.float32)
    NR = 4 * R
    rvt = np.zeros((128, 3, 8, 8), np.float32)
    for ty, r0 in enumerate((R * q, R * q + 8, R * q + R - 8)):
        for p in range(128):
            for kb in range(8):
                krow = r0 - 4 + 2 * kb + p // 64
                for qi in range(8):
                    rs = min(max(r0 + qi - 4, 0), NR - 8)
                    rvt[p, ty, kb, qi] = 1.0 if rs <= krow < rs + 8 else 0.0
    out["k_rv"] = rvt.reshape(128, 192)
    sel = np.zeros((128, 8), np.float32)
    if q > 0:
        sel[:, q - 1] = 1.0
    if q < 3:
        sel[:, 4 + q + 1] = 1.0
    out["k_sel"] = sel
    return out


def _rpb_table(rpb):
    Ln, H = rpb.shape[0], rpb.shape[1]
    tab = np.full((Ln, H, 23, 64, 64), MASKVAL, np.float32)
    qc = np.arange(64)[None, :]; kc = np.arange(64)[:, None]
    cs = np.clip(qc - 8, 0, 48)
    valid = (kc >= cs) & (kc < cs + 16)
    dc = np.clip(kc - qc + 15, 0, 30)
    for e in range(23):
        dr = 11 - e
        if abs(dr) <= 7:
            g = rpb[:, :, dr + 7, :][:, :, dc]
            tab[:, :, e] = np.where(valid[None, None], g, np.float32(MASKVAL))
    return tab


def _pk(v, k):
    v = np.asarray(v, np.float32)
    sh = v.shape[:-1]
    return np.ascontiguousarray(np.swapaxes(v.reshape(sh + (k, 128)), -1, -2))


_CACHE = {}


def run_cfg(inputs, cfg):
    key = (cfg.R, cfg.debug, cfg.stop)
    if key not in _CACHE:
        _CACHE[key] = build_program(cfg)
    nc, dbg = _CACHE[key]
    T = cfg.T
    f32 = lambda a: np.ascontiguousarray(np.asarray(a, np.float32))
    shared = {
        "c_ctx": _pk(inputs["c_ctx"], 8), "ln_in_g": _pk(inputs["ln_in_g"], 8), "ln_in_b": _pk(inputs["ln_in_b"], 8),
        "w_mod": f32(inputs["w_mod"]), "b_mod": _pk(inputs["b_mod"], 48), "w_in": f32(inputs["w_in"]),
        "mla_q_norm_g": _pk(inputs["mla_q_norm_g"], 2), "mla_kv_norm_g": _pk(inputs["mla_kv_norm_g"], 1),
        "w_uq": f32(inputs["w_uq"]), "w_qr": f32(inputs["w_qr"]), "w_uk": f32(inputs["w_uk"]), "w_uv": f32(inputs["w_uv"]),
        "rpbT": _rpb_table(f32(inputs["na_rpb"])),
        "w_branch": f32(inputs["w_branch"]).reshape(L, 1536, D), "w_out": f32(inputs["w_out"]),
        "ln1_g": _pk(inputs["ln1_g"], 8), "ln1_b": _pk(inputs["ln1_b"], 8), "ln2_g": _pk(inputs["ln2_g"], 8), "ln2_b": _pk(inputs["ln2_b"], 8),
        "w_ffn_gate": f32(inputs["w_ffn_gate"]), "w_ffn_up": f32(inputs["w_ffn_up"]), "w_ffn_down": f32(inputs["w_ffn_down"]),
    }
    x = np.asarray(inputs["x"], np.float32); ctx = np.asarray(inputs["ctx"], np.float32); c = np.asarray(inputs["c"], np.float32)
    in_maps = []
    for i in range(8):
        b, q = i // 4, i % 4
        m = dict(shared)
        m["x"] = np.ascontiguousarray(x[b, q * T:(q + 1) * T]); m["ctx"] = np.ascontiguousarray(ctx[b]); m["c"] = _pk(c[b], 8)
        m.update(_consts(cfg, q))
        in_maps.append(m)
    res = run_bass_kernel_spmd(nc, in_maps, core_ids=list(range(8)))
    out = np.empty((2, 4 * T, D), np.float32)
    for i in range(8):
        b, q = i // 4, i % 4
        out[b, q * T:(q + 1) * T] = np.asarray(res.results[i]["out"], np.float32)
    return out, res


def kernel(**inputs):
    out, _ = run_cfg(inputs, Cfg(64))
    return out
```
